# Optimizing a Trainium2 kernel written in Bass

```python
import math
import jax, jax.numpy as jnp
from jax import lax
import numpy as np

D_MODEL = 1024
BATCH = 2
SEQ = 8192
DEPTH = 2

CTX_LEN = 256
GRID_W = 64
N_MOD = 9
D_FF = 2816
EPS = 1e-6
NEG_INF = -1e30
Q_BLOCK = 128
ROPE_THETA = 10000.0

NA_HEADS = 4
NA_HEAD_DIM = 64
NA_WIN_ROWS = 8
NA_WIN_COLS = 16
NA_QCOLS = NA_WIN_COLS
NA_BAND_COLS = 2 * NA_WIN_COLS
NA_WIDTH = NA_HEADS * NA_HEAD_DIM

POOL_WINDOWS = (2, 4, 8, 16)
POOL_GROUPS = len(POOL_WINDOWS)
POOL_GROUP_DIM = 64
POOL_WIDTH = POOL_GROUPS * POOL_GROUP_DIM

DIFF_HEADS = 4
DIFF_QK_DIM = 64
DIFF_V_DIM = 2 * DIFF_QK_DIM
DIFF_QK_WIDTH = DIFF_HEADS * 2 * DIFF_QK_DIM
DIFF_WIDTH = DIFF_HEADS * DIFF_V_DIM

MIX_WIDTH = NA_WIDTH + POOL_WIDTH + DIFF_WIDTH
IN_WIDTH = 3 * NA_WIDTH + POOL_WIDTH + 2 * DIFF_QK_WIDTH + DIFF_WIDTH
IN_SPLITS = (NA_WIDTH, 2 * NA_WIDTH, 3 * NA_WIDTH, 3 * NA_WIDTH + POOL_WIDTH,
             3 * NA_WIDTH + POOL_WIDTH + DIFF_QK_WIDTH, 3 * NA_WIDTH + POOL_WIDTH + 2 * DIFF_QK_WIDTH)

kernel_name = 'hybrid_natten_pool_diffattn_dit'


def rms_norm(x, g):
    xf = x.astype(jnp.float32)
    y = xf * lax.rsqrt(jnp.mean(xf * xf, axis=-1, keepdims=True) + EPS)
    return (y * g.astype(jnp.float32)).astype(x.dtype)


def sandwich_in(x, mods, idx, g_pre):
    shift, scale = mods[3 * idx], mods[3 * idx + 1]
    return rms_norm(x, g_pre) * (1 + scale) + shift


def sandwich_out(x, y, mods, idx, g_post, res_w):
    gate = mods[3 * idx + 2]
    return x + res_w * gate * rms_norm(y, g_post)


def swiglu(h, w1, w2):
    a, b = jnp.split(h @ w1, 2, axis=-1)
    return (jax.nn.silu(a) * b) @ w2


def axial_rope_tables(n_tokens, dim):
    t = jnp.arange(n_tokens, dtype=jnp.int32)
    row = (t // GRID_W).astype(jnp.float32)
    col = (t % GRID_W).astype(jnp.float32)
    n_freq = dim // 4
    inv_freq = jnp.power(ROPE_THETA, -jnp.arange(n_freq, dtype=jnp.float32) / n_freq)
    ang = jnp.concatenate([row[:, None] * inv_freq, col[:, None] * inv_freq], axis=-1)
    return jnp.cos(ang), jnp.sin(ang)


def apply_rope(x, cos, sin):
    x1, x2 = jnp.split(x, 2, axis=-1)
    c = cos[None, :, None, :].astype(x.dtype)
    s = sin[None, :, None, :].astype(x.dtype)
    return jnp.concatenate([x1 * c - x2 * s, x1 * s + x2 * c], axis=-1)


def dense_attention(q, k, v):
    s = jnp.einsum('bqhd,bkhd->bhqk', q, k, preferred_element_type=jnp.float32)
    p = jax.nn.softmax(s, axis=-1).astype(v.dtype)
    return jnp.einsum('bhqk,bkhd->bqhd', p, v)


def neighbourhood_attention(q, k, v, k_ctx, v_ctx, rpb):
    B, S, H, dh = q.shape
    rows = S // GRID_W
    win_r = min(NA_WIN_ROWS, rows)
    n_cb = GRID_W // NA_QCOLS
    r = jnp.arange(rows)
    row_idx = jnp.clip(r - win_r // 2, 0, rows - win_r)[:, None] + jnp.arange(win_r)[None, :]
    cb = jnp.arange(n_cb)
    band_idx = (jnp.clip(cb * NA_QCOLS - NA_WIN_COLS // 2, 0, GRID_W - NA_BAND_COLS)[:, None]
                + jnp.arange(NA_BAND_COLS)[None, :])
    q_col = cb[:, None] * NA_QCOLS + jnp.arange(NA_QCOLS)[None, :]
    win_c0 = jnp.clip(q_col - NA_WIN_COLS // 2, 0, GRID_W - NA_WIN_COLS)[:, :, None]
    key_col = band_idx[:, None, :]
    in_win = (key_col >= win_c0) & (key_col < win_c0 + NA_WIN_COLS)
    d_row = (row_idx - r[:, None]) + NA_WIN_ROWS - 1
    d_col = jnp.clip(key_col - q_col[:, :, None], 1 - NA_WIN_COLS, NA_WIN_COLS - 1) + NA_WIN_COLS - 1
    bias = rpb.astype(jnp.float32)[:, d_row[:, None, None, :, None], d_col[None, :, :, None, :]]
    bias = jnp.where(in_win[None, None, :, :, None, :], bias, NEG_INF)

    qg = q.reshape(B, rows, n_cb, NA_QCOLS, H, dh)
    kg = k.reshape(B, rows, GRID_W, H, dh)
    vg = v.reshape(B, rows, GRID_W, H, dh)
    gi_r = row_idx[:, None, :, None]
    gi_c = band_idx[None, :, None, :]
    kb = kg[:, gi_r, gi_c]
    vb = vg[:, gi_r, gi_c]
    n_loc = win_r * NA_BAND_COLS
    s_loc = jnp.einsum('brnqhd,brnkwhd->bhrnqkw', qg, kb, preferred_element_type=jnp.float32) + bias
    s_loc = s_loc.reshape(B, H, rows, n_cb, NA_QCOLS, n_loc)
    s_ctx = jnp.einsum('brnqhd,bchd->bhrnqc', qg, k_ctx, preferred_element_type=jnp.float32)
    p = jax.nn.softmax(jnp.concatenate([s_loc, s_ctx], axis=-1), axis=-1).astype(v.dtype)
    p_loc = p[..., :n_loc].reshape(B, H, rows, n_cb, NA_QCOLS, win_r, NA_BAND_COLS)
    p_ctx = p[..., n_loc:]
    out = (jnp.einsum('bhrnqkw,brnkwhd->brnqhd', p_loc, vb)
           + jnp.einsum('bhrnqc,bchd->brnqhd', p_ctx, v_ctx))
    return out.reshape(B, S, H * dh)


def pool_mix(u, w, scale):
    B, L, _ = u.shape
    ug = u.reshape(B, L, POOL_GROUPS, POOL_GROUP_DIM).astype(jnp.float32)
    cs = jnp.concatenate([jnp.zeros_like(ug[:, :1]), jnp.cumsum(ug, axis=1)], axis=1)
    half = jnp.array(POOL_WINDOWS, dtype=jnp.int32) // 2
    t = jnp.arange(L, dtype=jnp.int32)[:, None]
    lo = jnp.clip(t - half[None, :], 0, L)
    hi = jnp.clip(t + half[None, :], 0, L)
    gi = jnp.arange(POOL_GROUPS)[None, :]
    win_mean = (cs[:, hi, gi] - cs[:, lo, gi]) / (hi - lo).astype(jnp.float32)[None, :, :, None]
    pooled = (win_mean - ug).astype(u.dtype)
    y = jnp.einsum('blgc,gcd->blgd', pooled, w)
    return y.reshape(B, L, POOL_WIDTH) * scale


def diff_attention(q1, q2, k1, k2, v, lam):
    B, L, H, d = q1.shape
    nb = L // Q_BLOCK

    def to_blocks(t):
        return t.reshape(B, nb, Q_BLOCK, H, d).transpose(1, 0, 2, 3, 4)

    def one_block(qb):
        qb1, qb2 = qb
        a1 = jax.nn.softmax(jnp.einsum('bqhd,bkhd->bhqk', qb1, k1, preferred_element_type=jnp.float32), axis=-1)
        a2 = jax.nn.softmax(jnp.einsum('bqhd,bkhd->bhqk', qb2, k2, preferred_element_type=jnp.float32), axis=-1)
        p = (a1 - lam * a2).astype(v.dtype)
        return jnp.einsum('bhqk,bkhe->bqhe', p, v)

    out = lax.map(one_block, (to_blocks(q1), to_blocks(q2)))
    return out.transpose(1, 0, 2, 3, 4).reshape(B, L, H, v.shape[-1])


def project_heads(h, w_in):
    naq, nak, nav, pin, dq, dk, dv = jnp.split(h @ w_in, IN_SPLITS, axis=-1)
    B, L = h.shape[:2]
    dq = dq.reshape(B, L, DIFF_HEADS, 2, DIFF_QK_DIM)
    dk = dk.reshape(B, L, DIFF_HEADS, 2, DIFF_QK_DIM)
    return (naq.reshape(B, L, NA_HEADS, NA_HEAD_DIM) * (NA_HEAD_DIM ** -0.5),
            nak.reshape(B, L, NA_HEADS, NA_HEAD_DIM),
            nav.reshape(B, L, NA_HEADS, NA_HEAD_DIM),
            pin,
            dq[..., 0, :] * (DIFF_QK_DIM ** -0.5), dq[..., 1, :] * (DIFF_QK_DIM ** -0.5),
            dk[..., 0, :], dk[..., 1, :],
            dv.reshape(B, L, DIFF_HEADS, DIFF_V_DIM))


def token_mixing(h_lat, h_ctx, w_in, w_out, rpb, pool_w, pool_scale, lam_vec, subln_g, lam_init, cos, sin, with_ctx_out):
    B, S, _ = h_lat.shape
    C = h_ctx.shape[1]
    lf = lam_vec.astype(jnp.float32)
    lam = jnp.exp(jnp.sum(lf[0] * lf[1])) - jnp.exp(jnp.sum(lf[2] * lf[3])) + lam_init
    aq, ak, av, pin, q1, q2, k1, k2, v = project_heads(h_lat, w_in)
    aqc, akc, avc, pinc, q1c, q2c, k1c, k2c, vc = project_heads(h_ctx, w_in)

    a_lat = neighbourhood_attention(aq, ak, av, akc, avc, rpb)
    b_lat = pool_mix(pin, pool_w, pool_scale)
    q1, q2, k1, k2 = [apply_rope(t, cos, sin) for t in (q1, q2, k1, k2)]
    c_lat = diff_attention(q1, q2,
                           jnp.concatenate([k1, k1c], axis=1),
                           jnp.concatenate([k2, k2c], axis=1),
                           jnp.concatenate([v, vc], axis=1), lam)
    c_lat = (rms_norm(c_lat, subln_g) * (1 - lam_init)).reshape(B, S, DIFF_WIDTH)
    y_lat = jnp.concatenate([a_lat, b_lat, c_lat], axis=-1) @ w_out
    if not with_ctx_out:
        return y_lat, None

    a_ctx = dense_attention(aqc, akc, avc).reshape(B, C, NA_WIDTH)
    b_ctx = pool_mix(pinc, pool_w, pool_scale)
    c_ctx = diff_attention(q1c, q2c, k1c, k2c, vc, lam)
    c_ctx = (rms_norm(c_ctx, subln_g) * (1 - lam_init)).reshape(B, C, DIFF_WIDTH)
    y_ctx = jnp.concatenate([a_ctx, b_ctx, c_ctx], axis=-1) @ w_out
    return y_lat, y_ctx


def setup_inputs(seed: int = 0) -> dict:
    key = jax.random.key(seed)
    ks = jax.random.split(key, 16)
    f32 = jnp.float32
    nrm = lambda k, shape: jax.random.normal(k, shape, dtype=f32)
    n_rpb_r, n_rpb_c = 2 * NA_WIN_ROWS - 1, 2 * NA_WIN_COLS - 1
    return {
        'x': nrm(ks[0], (BATCH, SEQ, D_MODEL)),
        'c': nrm(ks[1], (BATCH, D_MODEL)),
        'ctx': nrm(ks[2], (BATCH, CTX_LEN, D_MODEL)),
        'c_ctx': nrm(ks[3], (D_MODEL,)),
        'w_ada': nrm(ks[4], (DEPTH, D_MODEL, N_MOD * D_MODEL)) * D_MODEL ** -0.5,
        'b_ada': nrm(ks[5], (DEPTH, N_MOD * D_MODEL)) * 0.02,
        'norm_g': 1.0 + 0.05 * nrm(ks[6], (DEPTH, 6, D_MODEL)),
        'ffn_w1': nrm(ks[7], (DEPTH, 2, D_MODEL, 2 * D_FF)) * D_MODEL ** -0.5,
        'ffn_w2': nrm(ks[8], (DEPTH, 2, D_FF, D_MODEL)) * D_FF ** -0.5,
        'w_in': nrm(ks[9], (DEPTH, D_MODEL, IN_WIDTH)) * D_MODEL ** -0.5,
        'w_out': nrm(ks[10], (DEPTH, MIX_WIDTH, D_MODEL)) * MIX_WIDTH ** -0.5,
        'na_rpb': nrm(ks[11], (DEPTH, NA_HEADS, n_rpb_r, n_rpb_c)) * 0.2,
        'pool_w': nrm(ks[12], (DEPTH, POOL_GROUPS, POOL_GROUP_DIM, POOL_GROUP_DIM)) * POOL_GROUP_DIM ** -0.5,
        'pool_scale': 1.0 + 0.1 * nrm(ks[13], (DEPTH, POOL_WIDTH)),
        'diff_lambda': nrm(ks[14], (DEPTH, 4, DIFF_QK_DIM)) * 0.1,
        'diff_subln_g': 1.0 + 0.05 * nrm(ks[15], (DEPTH, DIFF_V_DIM)),
    }


def reference(x, c, ctx, c_ctx, w_ada, b_ada, norm_g, ffn_w1, ffn_w2, w_in, w_out, na_rpb, pool_w, pool_scale, diff_lambda, diff_subln_g):
    S = x.shape[1]
    cos, sin = axial_rope_tables(S, DIFF_QK_DIM)
    silu_c = jax.nn.silu(c)
    silu_cc = jax.nn.silu(c_ctx)
    x_lat, x_ctx = x, ctx
    for layer in range(DEPTH):
        last = layer == DEPTH - 1
        m_lat = jnp.split((silu_c @ w_ada[layer] + b_ada[layer])[:, None, :], N_MOD, axis=-1)
        m_ctx = jnp.split(silu_cc @ w_ada[layer] + b_ada[layer], N_MOD, axis=-1)
        g = norm_g[layer]
        lam_init = 0.8 - 0.6 * math.exp(-0.3 * layer)

        x_lat = sandwich_out(x_lat, swiglu(sandwich_in(x_lat, m_lat, 0, g[0]), ffn_w1[layer, 0], ffn_w2[layer, 0]), m_lat, 0, g[1], 0.5)
        x_ctx = sandwich_out(x_ctx, swiglu(sandwich_in(x_ctx, m_ctx, 0, g[0]), ffn_w1[layer, 0], ffn_w2[layer, 0]), m_ctx, 0, g[1], 0.5)

        h_lat = sandwich_in(x_lat, m_lat, 1, g[2])
        h_ctx = sandwich_in(x_ctx, m_ctx, 1, g[2])
        y_lat, y_ctx = token_mixing(h_lat, h_ctx, w_in[layer], w_out[layer], na_rpb[layer], pool_w[layer], pool_scale[layer],
                                    diff_lambda[layer], diff_subln_g[layer], lam_init, cos, sin, not last)
        x_lat = sandwich_out(x_lat, y_lat, m_lat, 1, g[3], 1.0)

        x_lat = sandwich_out(x_lat, swiglu(sandwich_in(x_lat, m_lat, 2, g[4]), ffn_w1[layer, 1], ffn_w2[layer, 1]), m_lat, 2, g[5], 0.5)
        if not last:
            x_ctx = sandwich_out(x_ctx, y_ctx, m_ctx, 1, g[3], 1.0)
            x_ctx = sandwich_out(x_ctx, swiglu(sandwich_in(x_ctx, m_ctx, 2, g[4]), ffn_w1[layer, 1], ffn_w2[layer, 1]), m_ctx, 2, g[5], 0.5)
    return x_lat
```

```python
import math
import numpy as np
import ml_dtypes
from contextlib import ExitStack
import concourse.bass as bass
import concourse.mybir as mybir
from concourse.bass_utils import run_bass_kernel_spmd

F32 = mybir.dt.float32
BF16 = mybir.dt.bfloat16
AF = mybir.ActivationFunctionType
ALU = mybir.AluOpType
AX = mybir.AxisListType
NPBF = ml_dtypes.bfloat16

D = 1024
DFF = 2816
NFC = 22
SEQ = 8192
CTX = 256
GRID_W = 64
EPS = 1e-6
NEG = -30000.0
EXP_SHIFT = -40.0


class Emitter:
    ENGS = ('pe', 'act', 'dve', 'pool', 'sp')

    def __init__(self, nc, stack, n_dma_sems=16):
        self.nc = nc
        self._stack = stack
        self.cccount = 0
        self.prog = {e: [] for e in self.ENGS}
        self.count = {e: 0 for e in self.ENGS}
        self.waited = {e: {} for e in self.ENGS}
        self.dcount = [0] * n_dma_sems
        self.dnext_q = {e: 0 for e in self.ENGS}
        self.lastw = {}
        self.readers = {}
        self.semobj = {}
        for e in self.ENGS:
            self.semobj[('c', e)] = stack.enter_context(nc.semaphore('c_' + e))
        for i in range(n_dma_sems):
            self.semobj[('d', i)] = stack.enter_context(nc.semaphore('d%d' % i))
        self.ninst = 0

    def _deps(self, eng, reads, writes):
        deps = {}
        own = ('c', eng)

        def add(k, v):
            if deps.get(k, 0) < v:
                deps[k] = v
        skip_own = (eng == 'pe')
        for r in reads:
            t = self.lastw.get(r)
            if t is not None and not (skip_own and t[0] == own):
                add(*t)
        for w in writes:
            t = self.lastw.get(w)
            if t is not None and not (skip_own and t[0] == own):
                add(*t)
            for k, v in self.readers.get(w, {}).items():
                if not (skip_own and k == own):
                    add(k, v)
        waits = []
        wd = self.waited[eng]
        for k, v in deps.items():
            if wd.get(k, 0) < v:
                wd[k] = v
                waits.append((k, v))
        return waits

    def _commit(self, tok, reads, writes):
        for w in writes:
            self.lastw[w] = tok
            self.readers[w] = {}
        for r in reads:
            d = self.readers.setdefault(r, {})
            if d.get(tok[0], 0) < tok[1]:
                d[tok[0]] = tok[1]

    def op(self, eng, fn, reads=(), writes=(), inc=True):
        writes = list(writes) + [r for r in reads if r.startswith('ps') and r not in writes]
        waits = self._deps(eng, reads, writes)
        tok = (('c', eng), self.count[eng] + 1)
        if inc:
            self.count[eng] += 1
        self.prog[eng].append((waits, fn, (tok[0], 1) if inc else None))
        self._commit(tok, reads, writes)
        self.ninst += 1
        return tok

    def dma(self, eng, fn, reads=(), writes=()):
        waits = self._deps(eng, reads, writes)
        half = len(self.dcount) // 2
        base = 0 if eng == 'pool' else half
        i = base + self.dnext_q[eng]
        self.dnext_q[eng] = (self.dnext_q[eng] + 1) % half
        k = ('d', i)
        wd = self.waited[eng]
        if wd.get(k, 0) < self.dcount[i]:
            wd[k] = self.dcount[i]
            waits.append((k, self.dcount[i]))
        self.dcount[i] += 16
        tok = (k, self.dcount[i])
        self.prog[eng].append((waits, fn, (k, 16)))
        self._commit(tok, reads, writes)
        self.ninst += 1
        return tok

    def coll(self, fn, reads=(), writes=()):
        eng = 'pool'
        waits = self._deps(eng, reads, writes)
        k = ('cc', 0)
        if k not in self.semobj:
            self.semobj[k] = self._stack.enter_context(self.nc.semaphore('cc_sem'))
            self.cccount = 0
        wd = self.waited[eng]
        if wd.get(k, 0) < self.cccount:
            wd[k] = self.cccount
            waits.append((k, self.cccount))
        self.cccount += 1
        tok = (k, self.cccount)
        self.prog[eng].append((waits, fn, (k, 1)))
        self._commit(tok, reads, writes)
        self.ninst += 1
        return tok

    def finish(self, eng='sp'):
        toks = [(('c', e), self.count[e]) for e in self.ENGS if self.count[e]]
        toks += [(('d', i), c) for i, c in enumerate(self.dcount) if c]
        if self.cccount:
            toks.append((('cc', 0), self.cccount))
        waits = []
        wd = self.waited[eng]
        for k, v in toks:
            if wd.get(k, 0) < v:
                wd[k] = v
                waits.append((k, v))
        self.prog[eng].append((waits, None, None))

    def emit(self, block):
        def mk(e):
            def body(engh):
                for waits, fn, inc in self.prog[e]:
                    for k, v in waits:
                        engh.wait_ge(self.semobj[k], v)
                    if fn is not None:
                        if fn[0] == '__call__':
                            ins = fn[1](engh)
                        else:
                            ins = getattr(engh, fn[0])(*fn[1], **fn[2])
                        if inc is not None:
                            ins.then_inc(self.semobj[inc[0]], inc[1])
            return body
        block.tensor(mk('pe'))
        block.scalar(mk('act'))
        block.vector(mk('dve'))
        block.gpsimd(mk('pool'))
        block.sync(mk('sp'))


class Ctx:
    def __init__(self):
        self.nc = bass.Bass("TRN2", target_bir_lowering=False)
        self.st = ExitStack()
        self.em = Emitter(self.nc, self.st)
        self.uid = 0

    def sb(self, name, shape, dt):
        return self.st.enter_context(self.nc.sbuf_tensor("s_" + name, list(shape), dt))

    def ps(self, name, shape, dt=F32):
        return self.st.enter_context(self.nc.psum_tensor("p_" + name, list(shape), dt))

    def din(self, name, shape, dt):
        return self.nc.dram_tensor(name, list(shape), dt, kind="ExternalInput").ap()

    def dout(self, name, shape, dt):
        return self.nc.dram_tensor(name, list(shape), dt, kind="ExternalOutput").ap()

    def done(self):
        self.em.finish('sp')
        with self.nc.Block() as block:
            self.em.emit(block)
        self.st.close()
        return self.nc


def I(name, *a, **kw):
    return (name, a, kw)


def mm_group(em, out_ap, pairs, reads, wres, extra_writes=()):
    n = len(pairs)
    for i, (l, r) in enumerate(pairs):
        em.op('pe', I('matmul', out_ap, lhsT=l, rhs=r, start=(i == 0), stop=(i == n - 1)),
              reads=reads, writes=[wres] + list(extra_writes), inc=(i == n - 1))


class Common:
    def __init__(self, c, need_mods_from_wada=True):
        self.c = c
        em = c.em
        self.ones = c.sb("ones_bf", [128, 128], BF16)
        em.op('dve', I('memset', self.ones[:], 1.0), writes=['ones'])
        self.dummy = c.sb("dummy", [128, 1], F32)
        self.epst = c.sb("epst", [128, 1], F32)
        em.op('dve', I('memset', self.epst[:], EPS), writes=['epst'])
        self.modsT = c.sb("modsT", [128, 72, 2], F32)
        self.normg = c.sb("normgT", [128, 6, 8], F32)
        self.A = c.sb("coefA", [128, 3, 8, 2], F32)
        self.G = c.sb("coefG", [128, 3, 8, 2], F32)

    def load_normg(self, normg_d):
        self.c.em.dma('sp', I('dma_start', out=self.normg[:], in_=normg_d), writes=['normg'])

    def compute_mods(self, cvec_d, wada_d, badaT_d, ps_mods, fb, alloc=None, psname='ps_mods'):
        c, em = self.c, self.c.em
        if alloc is None:
            alloc = lambda name, shape, dt: c.sb(name, shape, dt)[:]
        cv = alloc("cvec", [128, 8, 2], F32)
        scv = alloc("scvec", [128, 8, 2], BF16)
        bad = alloc("badaT", [128, 72], F32)
        em.dma('sp', I('dma_start', out=cv, in_=cvec_d), writes=['cv'])
        em.dma('sp', I('dma_start', out=bad, in_=badaT_d), writes=['bad'])
        em.op('act', I('activation', out=scv, in_=cv, func=AF.Silu), reads=['cv'], writes=['scv'])
        wbuf = [flatview(fb.gT, i * 8 * 512, [128, 8, 512]) for i in range(2)]
        wv = wada_d.rearrange("(k p) n -> p k n", p=128)
        for m in range(18):
            wb = wbuf[m % 2]
            em.dma('pool', I('dma_start', out=wb, in_=wv[:, :, m * 512:(m + 1) * 512]),
                   writes=['wada%d' % (m % 2)])
            for fc in range(4):
                mm_group(em, ps_mods[:, m * 4 + fc, :],
                         [(wb[:, k, fc * 128:(fc + 1) * 128], scv[:, k, :]) for k in range(8)],
                         reads=['wada%d' % (m % 2), 'scv'], wres=psname)
        em.op('dve', I('memset', self.dummy[:], 0.0), writes=['dummy', 'gT', 'wada0', 'wada1'])
        for g in range(2):
            em.op('dve', I('tensor_tensor', out=self.modsT[:, :, g], in0=ps_mods[:, 0:72, g], in1=bad, op=ALU.add),
                  reads=[psname, 'bad'], writes=['modsT'])

    def load_mods(self, modsT_d):
        self.c.em.dma('sp', I('dma_start', out=self.modsT[:], in_=modsT_d), writes=['modsT'])

    def compute_coefs(self):
        em = self.c.em
        for idx, res_w in ((0, 0.5), (1, 1.0), (2, 0.5)):
            for g in range(2):
                em.op('dve', I('scalar_tensor_tensor',
                    out=self.A[:, idx, :, g], in0=self.modsT[:, (3 * idx + 1) * 8:(3 * idx + 2) * 8, g], scalar=1.0,
                    in1=self.normg[:, 2 * idx, :], op0=ALU.add, op1=ALU.mult),
                    reads=['modsT', 'normg'], writes=['coefA'])
                em.op('dve', I('scalar_tensor_tensor',
                    out=self.G[:, idx, :, g], in0=self.modsT[:, (3 * idx + 2) * 8:(3 * idx + 3) * 8, g], scalar=res_w,
                    in1=self.normg[:, 2 * idx + 1, :], op0=ALU.mult, op1=ALU.mult),
                    reads=['modsT', 'normg'], writes=['coefG'])

    def shift(self, idx, k, g):
        return self.modsT[:, 3 * idx * 8 + k, g:g + 1]


class FFNBufs:
    def __init__(self, c, tw, alloc=None, gT=None):
        if alloc is None:
            alloc = lambda name, shape, dt: c.sb(name, shape, dt)[:]
        self.tw = tw
        self.hT = alloc("hT", [128, 8, tw], BF16)
        self.gT = gT if gT is not None else alloc("gT", [128, NFC, tw], BF16)
        assert NFC * tw >= 2 * 8 * 512
        self.w1b = [alloc("w1b%d" % i, [128, 2 * 8 * 256], BF16) for i in range(2)]
        self.w2b = [alloc("w2b%d" % i, [128, NFC, 256], BF16) for i in range(2)]
        self.ysb = alloc("ysb", [128, 8, tw], F32)
        self.sq = alloc("sq", [128, 8, tw], BF16)
        self.tmp = [alloc("tmpf%d" % i, [128, 512], F32) for i in range(2)]
        self.rstd = alloc("rstd", [128, 512], F32)
        self.sa = [alloc("sa%d" % i, [128, 512], BF16) for i in range(2)]
        self.n_tmp = 0


def rms_rstd(c, cm, src_fn, w, ps_ss, sq, rstd, src_reads, tag):
    em = c.em
    for k in range(8):
        em.op('act', I('activation', out=sq[:, k, :w], in_=src_fn(k), func=AF.Square),
              reads=src_reads, writes=['sq'])
    mm_group(em, ps_ss[:, :w], [(cm.ones[:], sq[:, k, :w]) for k in range(8)], reads=['sq', 'ones'], wres=tag)
    em.op('act', I('activation', out=rstd[:, :w], in_=ps_ss[:, :w], func=AF.Sqrt, scale=1.0 / D, bias=cm.epst[:]),
          reads=[tag, 'epst'], writes=['rstd'])
    em.op('dve', I('reciprocal', out=rstd[:, :w], in_=rstd[:, :w]), reads=['rstd'], writes=['rstd'])


def sandwich_in(c, cm, fb, xT, idx, subs, ps_ss, dst, dst_res, ss_res='ps_ss'):
    em = c.em
    for (t0, w, g, off) in subs:
        rms_rstd(c, cm, lambda k: xT[:, k, t0:t0 + w], w, ps_ss, fb.sq, fb.rstd, xres(t0, w), ss_res)
        for k in range(8):
            tmp = fb.tmp[fb.n_tmp % 2]
            tr = 'tmpf%d' % (fb.n_tmp % 2)
            fb.n_tmp += 1
            em.op('dve', I('tensor_tensor', out=tmp[:, :w], in0=xT[:, k, t0:t0 + w], in1=fb.rstd[:, :w], op=ALU.mult),
                  reads=xres(t0, w) + ['rstd'], writes=[tr])
            em.op('act', I('activation', out=dst[:, k, off:off + w], in_=tmp[:, :w], func=AF.Identity,
                                                               scale=cm.A[:, idx, k, g:g + 1], bias=cm.shift(idx, k, g)),
                  reads=[tr, 'coefA', 'modsT'], writes=[dst_res])


def y_evac(c, fb, k, off, w, yp, yres):
    em = c.em
    em.op('dve', I('tensor_copy', out=fb.ysb[:, k, off:off + w], in_=yp), reads=[yres], writes=['ysb'])
    em.op('pool', I('tensor_tensor', out=fb.sq[:, k, off:off + w], in0=fb.ysb[:, k, off:off + w], in1=fb.ysb[:, k, off:off + w], op=ALU.mult),
          reads=['ysb'], writes=['sq'])


def sandwich_out(c, cm, fb, xT, idx, t0, w, g, off, ps_ss, ss_res='ps_ss'):
    em = c.em
    mm_group(em, ps_ss[:, :w], [(cm.ones[:], fb.sq[:, k, off:off + w]) for k in range(8)], reads=['sq', 'ones'], wres=ss_res)
    em.op('act', I('activation', out=fb.rstd[:, :w], in_=ps_ss[:, :w], func=AF.Sqrt, scale=1.0 / D, bias=cm.epst[:]),
          reads=[ss_res, 'epst'], writes=['rstd'])
    em.op('dve', I('reciprocal', out=fb.rstd[:, :w], in_=fb.rstd[:, :w]), reads=['rstd'], writes=['rstd'])
    for k in range(8):
        tmp = fb.tmp[fb.n_tmp % 2]
        tr = 'tmpf%d' % (fb.n_tmp % 2)
        fb.n_tmp += 1
        em.op('dve', I('scalar_tensor_tensor', out=tmp[:, :w], in0=fb.ysb[:, k, off:off + w], scalar=cm.G[:, idx, k, g:g + 1],
                                                                     in1=fb.rstd[:, :w], op0=ALU.mult, op1=ALU.mult),
              reads=['ysb', 'rstd', 'coefG'], writes=[tr])
        em.op('dve', I('tensor_tensor', out=xT[:, k, t0:t0 + w], in0=xT[:, k, t0:t0 + w], in1=tmp[:, :w], op=ALU.add),
              reads=[tr] + xres(t0, w), writes=xres(t0, w))


def ffn(c, cm, fb, xT, idx, tiles, w1_d, w2_d, pss, psnames=None):
    em = c.em
    ps_ss, ps_a, ps_b, ps_y = pss['ss'], pss['a'], pss['b'], pss['y']
    if psnames is None:
        psnames = {'ss': 'ps_ss', 'a': ['ps_a0', 'ps_a1'], 'b': ['ps_b0', 'ps_b1'], 'y': ['ps_y0', 'ps_y1']}
    w1v = w1_d.rearrange("(k p) (two f) -> p k two f", p=128, two=2)
    w2v = w2_d.rearrange("(f p) d -> p f d", p=128)
    nw1 = 0
    nw2 = 0
    nsa = 0
    ny = 0
    for tile in tiles:
        subs = []
        off = 0
        for (t0, w, g) in tile:
            subs.append((t0, w, g, off))
            off += w
        sandwich_in(c, cm, fb, xT, idx, subs, ps_ss, fb.hT, 'hT', ss_res=psnames['ss'])
        for fp in range(NFC // 2):
            wb = fb.w1b[nw1 % 2].rearrange("p (a k f) -> p a k f", a=2, k=8)
            wr = 'w1b%d' % (nw1 % 2)
            nw1 += 1
            for two in range(2):
                em.dma('pool', I('dma_start', out=wb[:, two, :, :], in_=w1v[:, :, two, fp * 256:(fp + 1) * 256]), writes=[wr])
            for fi in range(2):
                fc = fp * 2 + fi
                for (t0, w, g, off) in subs:
                    pa = ps_a[nsa % 2]
                    pb = ps_b[nsa % 2]
                    ra, rb = psnames['a'][nsa % 2], psnames['b'][nsa % 2]
                    sa = fb.sa[nsa % 2]
                    rs = 'sa%d' % (nsa % 2)
                    nsa += 1
                    mm_group(em, pa[:, :w], [(wb[:, 0, k, fi * 128:(fi + 1) * 128], fb.hT[:, k, off:off + w]) for k in range(8)],
                             reads=[wr, 'hT'], wres=ra)
                    mm_group(em, pb[:, :w], [(wb[:, 1, k, fi * 128:(fi + 1) * 128], fb.hT[:, k, off:off + w]) for k in range(8)],
                             reads=[wr, 'hT'], wres=rb)
                    em.op('act', I('activation', out=sa[:, :w], in_=pa[:, :w], func=AF.Silu),
                          reads=[ra], writes=[rs])
                    em.op('dve', I('tensor_tensor',
                        out=fb.gT[:, fc, off:off + w], in0=sa[:, :w], in1=pb[:, :w], op=ALU.mult),
                        reads=[rs, rb], writes=['gT'])
        for piece in range(4):
            wb = fb.w2b[nw2 % 2]
            wr = 'w2b%d' % (nw2 % 2)
            nw2 += 1
            em.dma('pool', I('dma_start', out=wb, in_=w2v[:, :, piece * 256:(piece + 1) * 256]), writes=[wr])
            for (t0, w, g, off) in subs:
                for kk in range(2):
                    k = piece * 2 + kk
                    py = ps_y[ny % 2]
                    ry = psnames['y'][ny % 2]
                    ny += 1
                    mm_group(em, py[:, :w], [(wb[:, f, kk * 128:(kk + 1) * 128], fb.gT[:, f, off:off + w]) for f in range(NFC)],
                             reads=[wr, 'gT'], wres=ry)
                    y_evac(c, fb, k, off, w, py[:, :w], ry)
        for (t0, w, g, off) in subs:
            sandwich_out(c, cm, fb, xT, idx, t0, w, g, off, ps_ss, ss_res=psnames['ss'])


def xres(t0, w):
    return ['x%d' % i for i in range(t0 // 128, (t0 + w + 127) // 128)]


def flatview(ap3, n0, shape):
    flat = ap3.rearrange("p a b -> p (a b)")
    n = 1
    for s in shape[1:]:
        n *= s
    v = flat[:, n0:n0 + n]
    if len(shape) == 2:
        return v
    if len(shape) == 3:
        return v.rearrange("p (a b) -> p a b", a=shape[1])
    return v.rearrange("p (a b c) -> p a b c", a=shape[1], b=shape[2])


def proj_phase(c, cm, fb, xT, tiles, win_d, ropeC_d, ropeS_d, pm_d, pss, qT_o, kvT_o, v_o, alloc=None, psnames=None):
    em = c.em
    ps_ss, ps_a, ps_b, ps_y = pss['ss'], pss['a'], pss['b'], pss['y']
    tw = fb.tw
    winv = win_d.rearrange("(k p) n -> p k n", p=128)
    if alloc is None:
        alloc = lambda name, shape, dt: c.sb(name, shape, dt)[:]
    if psnames is None:
        psnames = {'ss': 'ps_ss', 'a': ['ps_a0', 'ps_a1'], 'b': ['ps_b0', 'ps_b1'], 'y': ['ps_y0', 'ps_y1']}
    pmat = alloc("pmat", [128, 128], BF16)
    em.dma('sp', I('dma_start', out=pmat, in_=pm_d), writes=['pmat'])
    ct = [alloc("ropec%d" % i, [128, 512], F32) for i in range(2)]
    sn = [alloc("ropes%d" % i, [128, 512], F32) for i in range(2)]
    qb = [alloc("qb%d" % i, [128, 512], BF16) for i in range(2)]
    t2 = [alloc("t2_%d" % i, [128, 512], F32) for i in range(2)]
    qst = flatview(fb.gT, 0, [128, 6, tw])
    kvst = flatview(fb.gT, 6 * tw, [128, 8, tw])
    vst = flatview(fb.gT, 14 * tw, [128, tw // 128, 768])
    cnt = {'w': 0, 'p': 0, 'r': 0, 't': 0}

    def next_ps():
        i = cnt['p'] % 4
        cnt['p'] += 1
        return ([ps_a[0], ps_a[1], ps_b[0], ps_b[1]][i], (psnames['a'] + psnames['b'])[i])

    for tile in tiles:
        subs = []
        off = 0
        for (t0, w, g) in tile:
            subs.append((t0, w, g, off))
            off += w
        tww = off
        sandwich_in(c, cm, fb, xT, 1, subs, ps_ss, fb.hT, 'hT', ss_res=psnames['ss'])
        for piece in range(5):
            wb4 = fb.w1b[cnt['w'] % 2]
            wr = 'w1b%d' % (cnt['w'] % 2)
            cnt['w'] += 1
            wb = wb4.rearrange("p (k n) -> p k n", k=8)
            em.dma('pool', I('dma_start', out=wb, in_=winv[:, :, piece * 512:(piece + 1) * 512]), writes=[wr])
            for (t0, w, g, off) in subs:
                if piece == 0:
                    fm = [(0, 'q', 0, 0.125, False), (1, 'q', 1, 0.125, False), (2, 'kv', 0, 1.0, False), (3, 'kv', 1, 1.0, False)]
                elif piece == 1:
                    fm = [(2, 'kv', 2, 1.0, False), (3, 'kv', 3, 1.0, False)]
                elif piece == 2:
                    fm = [(i, 'q', 2 + i, 0.125, True) for i in range(4)]
                elif piece == 3:
                    fm = [(i, 'kv', 4 + i, 1.0, True) for i in range(4)]
                else:
                    fm = []
                if piece in (2, 3):
                    ci = cnt['r'] % 2
                    cnt['r'] += 1
                    em.dma('sp', I('dma_start', out=ct[ci][:, :w], in_=ropeC_d[:, t0:t0 + w]), writes=['ropec%d' % ci])
                    em.dma('sp', I('dma_start', out=sn[ci][:, :w], in_=ropeS_d[:, t0:t0 + w]), writes=['ropes%d' % ci])
                for (lc, kind, oc, scale, rope) in fm:
                    pp, pr = next_ps()
                    mm_group(em, pp[:, :w], [(wb[:, k, lc * 128:(lc + 1) * 128], fb.hT[:, k, off:off + w]) for k in range(8)],
                             reads=[wr, 'hT'], wres=pr)
                    dst = (qst if kind == 'q' else kvst)[:, oc, off:off + w]
                    if not rope:
                        em.op('act', I('activation', out=dst, in_=pp[:, :w], func=AF.Copy, scale=scale),
                              reads=[pr], writes=['gT'])
                    else:
                        ti = cnt['t'] % 2
                        cnt['t'] += 1
                        em.op('act', I('activation', out=qb[ti][:, :w], in_=pp[:, :w], func=AF.Copy, scale=scale),
                              reads=[pr], writes=['qb%d' % ti])
                        py = ps_y[ti]
                        pyr = psnames['y'][ti]
                        mm_group(em, py[:, :w], [(pmat, qb[ti][:, :w])], reads=['pmat', 'qb%d' % ti], wres=pyr)
                        tmp = fb.tmp[ti]
                        em.op('dve', I('scalar_tensor_tensor',
                            out=tmp[:, :w], in0=pp[:, :w], scalar=scale, in1=ct[ci][:, :w], op0=ALU.mult, op1=ALU.mult),
                            reads=[pr, 'ropec%d' % ci], writes=['tmpf%d' % ti])
                        em.op('dve', I('tensor_tensor', out=t2[ti][:, :w], in0=py[:, :w], in1=sn[ci][:, :w], op=ALU.mult),
                              reads=[pyr, 'ropes%d' % ci], writes=['t2_%d' % ti])
                        em.op('pool', I('tensor_tensor', out=dst, in0=tmp[:, :w], in1=t2[ti][:, :w], op=ALU.add),
                              reads=['tmpf%d' % ti, 't2_%d' % ti], writes=['gT'])
                if piece in (1, 4):
                    c0, ncol, vo = (0, 256, 0) if piece == 1 else (0, 512, 256)
                    for tcn in range(w // 128):
                        pp, pr = next_ps()
                        tk = off + tcn * 128
                        mm_group(em, pp[:, :ncol], [(fb.hT[:, k, tk:tk + 128], wb[:, k, c0:c0 + ncol]) for k in range(8)],
                                 reads=[wr, 'hT'], wres=pr)
                        em.op('act', I('activation', out=vst[:, tk // 128, vo:vo + ncol], in_=pp[:, :ncol], func=AF.Copy),
                              reads=[pr], writes=['gT'])
        tile0 = tile[0][0]
        em.dma('sp', I('dma_start', out=qT_o[:, :, tile0:tile0 + tww], in_=qst[:, :, :tww]), reads=['gT'], writes=['qT_o'])
        em.dma('sp', I('dma_start', out=kvT_o[:, :, tile0:tile0 + tww], in_=kvst[:, :, :tww]), reads=['gT'], writes=['kvT_o'])
        em.dma('sp', I('dma_start',
            out=v_o[tile0:tile0 + tww, :].rearrange("(c p) n -> p c n", p=128), in_=vst[:, :tww // 128, :]), reads=['gT'], writes=['v_o'])


def make_tiles(n_lat, n_ctx, tw):
    subs = []
    t = 0
    while t < n_lat:
        w = min(512, n_lat - t)
        subs.append((t, w, 0))
        t += w
    t = 0
    while t < n_ctx:
        w = min(512, n_ctx - t)
        subs.append((n_lat + t, w, 1))
        t += w
    tiles = []
    cur = []
    room = tw
    for (t0, w, g) in subs:
        while w > 0:
            take = min(w, room)
            cur.append((t0, take, g))
            t0 += take
            w -= take
            room -= take
            if room == 0:
                tiles.append(cur)
                cur = []
                room = tw
    if cur:
        tiles.append(cur)
    return tiles


def build_part_a(n_lat=2048, n_ctx=256, tw=768, upto=3):
    NT = n_lat + n_ctx
    c = Ctx()
    em = c.em
    xT_d = c.din("xT", [128, 8, NT], F32)
    cvec_d = c.din("cvec", [128, 8, 2], F32)
    wada_d = c.din("w_ada", [D, 9 * D], F32)
    bada_d = c.din("badaT", [128, 72], F32)
    normg_d = c.din("normgT", [128, 6, 8], F32)
    w1_d = c.din("w1", [D, 2 * DFF], F32)
    w2_d = c.din("w2", [DFF, D], F32)
    win_d = c.din("w_in", [D, 2560], F32)
    ropeC_d = c.din("ropeC", [128, NT], F32)
    ropeS_d = c.din("ropeS", [128, NT], F32)
    pm_d = c.din("pmat", [128, 128], BF16)
    x1T_o = c.dout("x1T", [128, 8, NT], F32)
    mods_o = c.dout("modsT", [128, 72, 2], F32)
    qT_o = c.dout("qT", [128, 6, NT], BF16)
    kvT_o = c.dout("kvT", [128, 8, NT], BF16)
    v_o = c.dout("vtok", [NT, 768], BF16)

    xT = c.sb("xT_sb", [128, 8, NT], F32)
    cm = Common(c)
    fb = FFNBufs(c, tw)
    pss = {'ss': c.ps("ps_ss", [128, 512]), 'a': [c.ps("ps_a%d" % i, [128, 512]) for i in range(2)],
           'b': [c.ps("ps_b%d" % i, [128, 512]) for i in range(2)], 'y': [c.ps("ps_y%d" % i, [128, 512]) for i in range(2)]}
    ps_mods = c.ps("ps_mods", [128, 256, 2])
    tiles = make_tiles(n_lat, n_ctx, tw)
    for t in range(0, NT, 512):
        w = min(512, NT - t)
        em.dma('sp', I('dma_start', out=xT[:, :, t:t + w], in_=xT_d[:, :, t:t + w]), writes=xres(t, w))
    cm.load_normg(normg_d)
    cm.compute_mods(cvec_d, wada_d, bada_d, ps_mods, fb)
    cm.compute_coefs()
    em.dma('sp', I('dma_start', out=mods_o, in_=cm.modsT[:]), reads=['modsT'], writes=['mods_o'])
    if upto >= 2:
        ffn(c, cm, fb, xT, 0, tiles, w1_d, w2_d, pss)
    em.dma('sp', I('dma_start', out=x1T_o, in_=xT[:]), reads=xres(0, NT), writes=['x1T_o'])
    if upto >= 3:
        proj_phase(c, cm, fb, xT, tiles, win_d, ropeC_d, ropeS_d, pm_d, pss, qT_o, kvT_o, v_o)
    print("part A instructions:", em.ninst)
    return c.done()


def rope_tables(pos):
    pos = np.asarray(pos)
    row = (pos // GRID_W).astype(np.float32)
    col = (pos % GRID_W).astype(np.float32)
    n_freq = 16
    inv_freq = np.power(np.float32(10000.0), -np.arange(n_freq, dtype=np.float32) / np.float32(n_freq)).astype(np.float32)
    ang = np.concatenate([row[:, None] * inv_freq, col[:, None] * inv_freq], axis=-1).astype(np.float32)
    return np.cos(ang).astype(np.float32), np.sin(ang).astype(np.float32)


def rope_feature_major(cos, sin, n_ctx):
    n = cos.shape[0]
    C = np.ones((128, n + n_ctx), np.float32)
    S = np.zeros((128, n + n_ctx), np.float32)
    p = np.arange(128)
    C[:, :n] = cos.T[p % 32]
    sign = np.where((p % 64) < 32, -1.0, 1.0).astype(np.float32)
    S[:, :n] = sin.T[p % 32] * sign[:, None]
    return C, S


def rope_pmat():
    pm = np.zeros((128, 128), np.float32)
    for po in range(128):
        pi = po + 32 if (po % 64) < 32 else po - 32
        pm[pi, po] = 1.0
    return pm.astype(NPBF)


class Arena:
    def __init__(self, c, nelem):
        self.c = c
        self.t = c.sb("arena", [128, nelem], BF16)
        self.n = nelem
        self.off = 0
        self.gen = 0

    def alloc(self, name, shape, dt):
        n = 1
        for s in shape[1:]:
            n *= s
        if dt == F32:
            n *= 2
        self.off = (self.off + 15) // 16 * 16
        assert self.off + n <= self.n, "arena overflow %s: need %d have %d" % (name, self.off + n, self.n)
        v = self.t[:, self.off:self.off + n]
        self.off += n
        if dt == F32:
            v = v.bitcast(F32)
        if len(shape) == 3:
            v = v.rearrange("p (a b) -> p a b", a=shape[1])
        elif len(shape) == 4:
            v = v.rearrange("p (a b c) -> p a b c", a=shape[1], b=shape[2])
        return v

    def reset(self, to_zero=False):
        self.c.em.barrier()
        if to_zero:
            self.base = 0
        self.off = getattr(self, 'base', 0)
        self.gen += 1

    def set_base(self):
        self.base = self.off


def _barrier(self):
    toks = [(('c', e), self.count[e]) for e in self.ENGS if self.count[e]]
    toks += [(('d', i), cc) for i, cc in enumerate(self.dcount) if cc]
    if self.cccount:
        toks.append((('cc', 0), self.cccount))
    for eng in self.ENGS:
        waits = []
        wd = self.waited[eng]
        for k, v in toks:
            if wd.get(k, 0) < v:
                wd[k] = v
                waits.append((k, v))
        if waits:
            self.prog[eng].append((waits, None, None))


Emitter.barrier = _barrier


def na_attention(c, ar, P, mixT, q_d, kT_src, v_src, n_kc_tot, qchunks, bias_d, ident, col0, shiftt):
    em = c.em
    NK = n_kc_tot * 128
    kT = ar.alloc("na_kT", [128, 2, NK], BF16)
    vv = ar.alloc("na_v", [128, n_kc_tot, 4, 65], BF16)
    nq = len(qchunks)
    qT = ar.alloc("na_q", [128, 2, nq * 128], BF16)
    g = ar.gen
    rk, rv, rq = 'na_kT%d' % g, 'na_v%d' % g, 'na_q%d' % g
    em.dma('sp', I('dma_start', out=kT, in_=kT_src), writes=[rk])
    em.op('dve', I('memset', vv[:, :, :, 64:65], 1.0), writes=[rv])
    for h in range(4):
        em.dma('sp', I('dma_start', out=vv[:, :, h, 0:64], in_=v_src[:, h * 64:(h + 1) * 64].rearrange("(c p) e -> p c e", p=128)), writes=[rv])
    q0 = qchunks[0][0]
    em.dma('sp', I('dma_start', out=qT, in_=q_d[:, 0:2, q0:q0 + nq * 128]), writes=[rq])
    bias = [ar.alloc("na_bias%d" % i, [128, 4, 6, 128], F32) for i in range(2)]
    ssb = [ar.alloc("na_s%d" % i, [128, 6, 128], F32) for i in range(2)]
    pT = [ar.alloc("na_p%d" % i, [128, 8, 128], BF16) for i in range(2)]
    atok = ar.alloc("na_atok", [128, 256], BF16)
    rec = ar.alloc("na_rec", [128, 4], F32)
    cnt = 0
    for qi, (qcol, kcs, bvar, nb) in enumerate(qchunks):
        bi = qi % 2
        if bvar is not None:
            for h in range(4):
                em.dma('sp', I('dma_start', out=bias[bi][:, h, :, :], in_=bias_d[bvar, h].rearrange("j k q -> k j q")), writes=['na_bias%d_%d' % (bi, g)])
        nk = len(kcs)
        for h in range(4):
            hp = (h % 2) * 64
            hc = h // 2
            si = cnt % 2
            cnt += 1
            SX, SY = P['S'][si][:, 0, :].rearrange("p (j q) -> p j q", j=4), P['S'][si][:, 1, :].rearrange("p (j q) -> p j q", j=4)
            rsx, rsy = 'psS%d_0' % si, 'psS%d_1' % si
            qap = qT[hp:hp + 64, hc, qi * 128:(qi + 1) * 128]

            def sdst(jj):
                return (SX[:, jj, :], rsx) if jj < 4 else (SY[:, jj - 4, :], rsy)
            for jj, kc in enumerate(kcs):
                d, r = sdst(jj)
                mm_group(em, d, [(kT[hp:hp + 64, hc, kc * 128:(kc + 1) * 128], qap)], reads=[rk, rq], wres=r)
            sb_ = ssb[si]
            pt = pT[si]
            rs_, rp_ = 'na_s%d_%d' % (si, g), 'na_p%d_%d' % (si, g)
            if nb > 0:
                n1 = min(nb, 4)
                em.op('dve', I('tensor_tensor', out=sb_[:, 0:n1, :], in0=SX[:, 0:n1, :], in1=bias[bi][:, h, 0:n1, :], op=ALU.add),
                      reads=[rsx, 'na_bias%d_%d' % (bi, g)], writes=[rs_])
                if nb > 4:
                    em.op('dve', I('tensor_tensor', out=sb_[:, 4:nb, :], in0=SY[:, 0:nb - 4, :], in1=bias[bi][:, h, 4:nb, :], op=ALU.add),
                          reads=[rsy, 'na_bias%d_%d' % (bi, g)], writes=[rs_])
                em.op('act', I('activation', out=pt[:, 0:nb, :], in_=sb_[:, 0:nb, :], func=AF.Exp, bias=shiftt[:]), reads=[rs_, 'shiftt'], writes=[rp_])
            j = nb
            while j < nk:
                if j < 4:
                    e_ = min(nk, 4)
                    em.op('act', I('activation', out=pt[:, j:e_, :], in_=SX[:, j:e_, :], func=AF.Exp, bias=shiftt[:]), reads=[rsx, 'shiftt'], writes=[rp_])
                else:
                    e_ = nk
                    em.op('act', I('activation', out=pt[:, j:e_, :], in_=SY[:, j - 4:e_ - 4, :], func=AF.Exp, bias=shiftt[:]), reads=[rsy, 'shiftt'], writes=[rp_])
                j = e_
            O = P['O'][:, 0, 0:4 * 65].rearrange("p (h e) -> p h e", h=4)
            mm_group(em, O[:, h, :], [(pt[:, jj, :], vv[:, kc, h, :]) for jj, kc in enumerate(kcs)], reads=[rp_, rv], wres='psU1_0')
        em.op('dve', I('reciprocal', out=rec[:, :], in_=O[:, :, 64]), reads=['psU1_0'], writes=['na_rec%d' % g])
        em.op('dve', I('tensor_tensor', out=atok[:, :].rearrange("p (h e) -> p h e", h=4), in0=O[:, :, 0:64],
                       in1=rec[:, :].unsqueeze(2).to_broadcast([128, 4, 64]), op=ALU.mult),
              reads=['psU1_0', 'na_rec%d' % g], writes=['na_atok%d' % g])
        T = P['T']
        for ch in range(2):
            em.op('pe', I('transpose', out=T[:, ch * 128:(ch + 1) * 128], in_=atok[:, ch * 128:(ch + 1) * 128], identity=ident[:]),
                  reads=['na_atok%d' % g, 'ident'], writes=['psU2_0'])
        col = col0 + qi * 128
        em.op('act', I('activation', out=mixT[:, 0:2, col:col + 128], in_=T[:, 0:256].rearrange("p (c q) -> p c q", c=2), func=AF.Copy),
              reads=['psU2_0'], writes=['mix%d' % (col // 128)])


def pool_mixer(c, ar, P, mixT, pin_src, rcnt_src, n_tok, pwbd, pscale, col0):
    em = c.em
    g = ar.gen
    W = 512
    ubuf = [ar.alloc("pl_u%d" % i, [128, 2, W + 16], BF16) for i in range(2)]
    rc = [ar.alloc("pl_rc%d" % i, [128, 2, W], F32) for i in range(2)]
    s2 = ar.alloc("pl_s2", [128, 2, W + 16], F32)
    s4 = ar.alloc("pl_s4", [128, 2, W + 16], F32)
    s8 = ar.alloc("pl_s8", [128, 2, W + 16], F32)
    s16 = ar.alloc("pl_s16", [128, 2, W + 16], F32)
    pm = ar.alloc("pl_pm", [128, 2, W], F32)
    pb = ar.alloc("pl_pb", [128, 2, W], BF16)
    it = 0
    for t0 in range(0, n_tok, W):
        w = min(W, n_tok - t0)
        u = ubuf[it % 2]
        r = rc[it % 2]
        ru, rr = 'pl_u%d_%d' % (it % 2, g), 'pl_rc%d_%d' % (it % 2, g)
        it += 1
        em.dma('sp', I('dma_start', out=u[:, :, 0:w + 16], in_=pin_src[:, :, t0:t0 + w + 16]), writes=[ru])
        em.dma('sp', I('dma_start', out=r[:, :, 0:w], in_=rcnt_src[:, :, t0:t0 + w]), writes=[rr])
        L = w + 16
        em.op('pool', I('tensor_tensor', out=s2[:, :, 1:L], in0=u[:, :, 0:L - 1], in1=u[:, :, 1:L], op=ALU.add), reads=[ru], writes=['pl_s2_%d' % g])
        em.op('pool', I('tensor_tensor', out=s4[:, :, 2:L - 1], in0=s2[:, :, 1:L - 2], in1=s2[:, :, 3:L], op=ALU.add), reads=['pl_s2_%d' % g], writes=['pl_s4_%d' % g])
        em.op('pool', I('tensor_tensor', out=s8[:, :, 4:L - 3], in0=s4[:, :, 2:L - 5], in1=s4[:, :, 6:L - 1], op=ALU.add), reads=['pl_s4_%d' % g], writes=['pl_s8_%d' % g])
        em.op('pool', I('tensor_tensor', out=s16[:, :, 8:L - 7], in0=s8[:, :, 4:L - 11], in1=s8[:, :, 12:L - 3], op=ALU.add), reads=['pl_s8_%d' % g], writes=['pl_s16_%d' % g])
        for (ch, p0, lvl, lr) in ((0, 0, s2, 'pl_s2_%d' % g), (0, 64, s4, 'pl_s4_%d' % g), (1, 0, s8, 'pl_s8_%d' % g), (1, 64, s16, 'pl_s16_%d' % g)):
            em.op('dve', I('tensor_tensor', out=pm[p0:p0 + 64, ch, 0:w], in0=lvl[p0:p0 + 64, ch, 8:8 + w], in1=r[p0:p0 + 64, ch, 0:w], op=ALU.mult),
                  reads=[lr, rr], writes=['pl_pm_%d' % g])
            em.op('dve', I('tensor_tensor', out=pb[p0:p0 + 64, ch, 0:w], in0=pm[p0:p0 + 64, ch, 0:w], in1=u[p0:p0 + 64, ch, 8:8 + w], op=ALU.subtract),
                  reads=['pl_pm_%d' % g, ru], writes=['pl_pb_%d' % g])
        for ch in range(2):
            pp = P['S'][ch][:, 0, :]
            pr = 'psS%d_0' % ch
            mm_group(em, pp[:, :w], [(pwbd[:, ch, :], pb[:, ch, 0:w])], reads=['pl_pb_%d' % g, 'pwbd'], wres=pr)
            col = col0 + t0
            em.op('act', I('activation', out=mixT[:, 2 + ch, col:col + w], in_=pp[:, :w], func=AF.Copy, scale=pscale[:, ch:ch + 1]),
                  reads=[pr, 'pscale'], writes=['mix%d' % i for i in range(col // 128, (col + w) // 128)])


def diff_attention(c, ar, P, mixT, q_d, qtiles, kT_src, v_src, n_kc, lamt, gsub, ident, shiftt, epst, heads=range(4), loaders=None):
    em = c.em
    g = ar.gen
    NK = n_kc * 128
    kT = [ar.alloc("df_kT%d" % i, [128, NK], BF16) for i in range(2)]
    vv = [ar.alloc("df_v%d" % i, [128, n_kc, 129], BF16) for i in range(2)]
    qmax = max(w for _, w in qtiles)
    qb = [ar.alloc("df_q%d" % i, [128, qmax], BF16) for i in range(2)]
    p12 = [ar.alloc("df_p%d" % i, [128, 2, 512], BF16) for i in range(2)]
    rr = ar.alloc("df_r", [128, 2, 4], F32)
    tt = ar.alloc("df_t", [128, 4, 128], F32)
    oo = ar.alloc("df_o", [128, 4, 128], F32)
    osq = ar.alloc("df_osq", [128, 4, 128], F32)
    ss = ar.alloc("df_ss", [128, 4], F32)
    cb = ar.alloc("df_cb", [128, 4, 128], BF16)
    for i in range(2):
        em.op('dve', I('memset', vv[i][:, :, 128:129], 1.0), writes=['df_v%d_%d' % (i, g)])
    nq = 0
    ns = 0
    for hi, h in enumerate(heads):
        kb, vb = kT[hi % 2], vv[hi % 2]
        rk, rv = 'df_kT%d_%d' % (hi % 2, g), 'df_v%d_%d' % (hi % 2, g)
        if loaders is not None:
            loaders(h, kb, vb, rk, rv)
        else:
            em.dma('sp', I('dma_start', out=kb, in_=kT_src[:, h, :]), writes=[rk])
            step = 16
            for c0 in range(0, n_kc, step):
                c1 = min(n_kc, c0 + step)
                em.dma('sp', I('dma_start', out=vb[:, c0:c1, 0:128],
                               in_=v_src[c0 * 128:c1 * 128, h * 128:(h + 1) * 128].rearrange("(c p) e -> p c e", p=128)), writes=[rv])
        for (t0, w) in qtiles:
            qq = qb[nq % 2]
            rq = 'df_q%d_%d' % (nq % 2, g)
            nq += 1
            em.dma('sp', I('dma_start', out=qq[:, :w], in_=q_d[:, 2 + h, t0:t0 + w]), writes=[rq])
            nqc = w // 128
            U1 = P['U1'][:].rearrange("p a (c e) -> p (a c) e", c=2)
            U2 = P['U2'][:].rearrange("p a (c e) -> p (a c) e", c=2)
            def issue_S(kc, slot):
                S = P['S'][slot]
                rs = ['psS%d_0' % slot, 'psS%d_1' % slot]
                for half in range(2):
                    hp = half * 64
                    mm_group(em, S[:, half, :w], [(kb[hp:hp + 64, kc * 128:(kc + 1) * 128], qq[hp:hp + 64, :w])], reads=[rk, rq], wres=rs[half])

            def issue_exp_pv(kc, slot):
                S = P['S'][slot]
                rs = ['psS%d_0' % slot, 'psS%d_1' % slot]
                pt = p12[slot]
                rp = 'df_p%d_%d' % (slot, g)
                em.op('act', I('activation', out=pt[:, :, :w], in_=S[:, :, :w], func=AF.Exp, bias=shiftt[:]), reads=rs + ['shiftt'], writes=[rp])
                for half, (U, ru) in enumerate(((U1, 'psU1'), (U2, 'psU2'))):
                    for qc in range(nqc):
                        em.op('pe', I('matmul', U[:, qc, 0:129], lhsT=pt[:, half, qc * 128:(qc + 1) * 128], rhs=vb[:, kc, :],
                                      start=(kc == 0 and qc % 2 == 0), stop=(kc == n_kc - 1), skip_group_check=True),
                              reads=[rp, rv], writes=['%s_%d' % (ru, qc // 2)], inc=(qc == nqc - 1))
            issue_S(0, ns % 2)
            for kc in range(n_kc):
                slot = ns % 2
                ns += 1
                if kc + 1 < n_kc:
                    issue_S(kc + 1, ns % 2)
                issue_exp_pv(kc, slot)
            ru1 = ['psU1_0', 'psU1_1'][:(nqc + 1) // 2]
            ru2 = ['psU2_0', 'psU2_1'][:(nqc + 1) // 2]
            rg = 'df_ep%d' % g
            em.op('dve', I('reciprocal', out=rr[:, 0, :nqc], in_=U1[:, :nqc, 128]), reads=ru1, writes=[rg])
            em.op('dve', I('reciprocal', out=rr[:, 1, :nqc], in_=U2[:, :nqc, 128]), reads=ru2, writes=[rg])
            em.op('dve', I('tensor_scalar', out=rr[:, 1, :nqc], in0=rr[:, 1, :nqc], scalar1=lamt[:, 0:1], scalar2=None, op0=ALU.mult), reads=[rg, 'lamt'], writes=[rg])
            em.op('dve', I('tensor_tensor', out=tt[:, :nqc, :], in0=U2[:, :nqc, 0:128], in1=rr[:, 1, :nqc].unsqueeze(2).to_broadcast([128, nqc, 128]), op=ALU.mult),
                  reads=ru2 + [rg], writes=[rg])
            em.op('dve', I('tensor_tensor', out=oo[:, :nqc, :], in0=U1[:, :nqc, 0:128], in1=rr[:, 0, :nqc].unsqueeze(2).to_broadcast([128, nqc, 128]), op=ALU.mult),
                  reads=ru1 + [rg], writes=[rg])
            em.op('dve', I('tensor_tensor', out=oo[:, :nqc, :], in0=oo[:, :nqc, :], in1=tt[:, :nqc, :], op=ALU.subtract), reads=[rg], writes=[rg])
            em.op('dve', I('tensor_tensor', out=osq[:, :nqc, :], in0=oo[:, :nqc, :], in1=oo[:, :nqc, :], op=ALU.mult), reads=[rg], writes=[rg])
            em.op('dve', I('reduce_sum', out=ss[:, :nqc], in_=osq[:, :nqc, :], axis=AX.X), reads=[rg], writes=[rg])
            em.op('act', I('activation', out=ss[:, :nqc], in_=ss[:, :nqc], func=AF.Sqrt, scale=1.0 / 128, bias=epst[:]), reads=[rg, 'epst'], writes=[rg])
            em.op('dve', I('reciprocal', out=ss[:, :nqc], in_=ss[:, :nqc]), reads=[rg], writes=[rg])
            em.op('dve', I('tensor_tensor', out=oo[:, :nqc, :], in0=oo[:, :nqc, :], in1=ss[:, :nqc].unsqueeze(2).to_broadcast([128, nqc, 128]), op=ALU.mult),
                  reads=[rg], writes=[rg])
            em.op('dve', I('tensor_tensor', out=cb[:, :nqc, :], in0=oo[:, :nqc, :], in1=gsub[:, :].unsqueeze(1).to_broadcast([128, nqc, 128]), op=ALU.mult),
                  reads=[rg, 'gsub'], writes=['df_cb%d' % g])
            T = P['T']
            for qc in range(nqc):
                em.op('pe', I('transpose', out=T[:, qc * 128:(qc + 1) * 128], in_=cb[:, qc, :], identity=ident[:]),
                      reads=['df_cb%d' % g, 'ident'], writes=['psU2_0'])
            em.op('act', I('activation', out=mixT[:, 4 + h, t0:t0 + w], in_=T[:, 0:w], func=AF.Copy),
                  reads=['psU2_0'], writes=['mix%d' % i for i in range(t0 // 128, (t0 + w) // 128)])


def na_local_chunks(i, nq):
    if i == 0:
        return 0, 6
    if i == nq - 1:
        return i - 1, 6
    return i, 5


def build_part_b(n_lat=2048, n_ctx=256, tw=768, n_kc_diff=66, na_variants=None, lam_init=0.2, ctx_out=True, n_halo_kc=None):
    NT = n_lat + n_ctx
    nq = n_lat // 128
    if n_halo_kc is None:
        n_halo_kc = nq + 4
    if na_variants is None:
        na_variants = [0] * nq
    c = Ctx()
    em = c.em
    x1T_d = c.din("x1T", [128, 8, NT], F32)
    mods_d = c.din("modsT", [128, 72, 2], F32)
    normg_d = c.din("normgT", [128, 6, 8], F32)
    q_d = c.din("qT", [128, 6, NT], BF16)
    nakT_d = c.din("na_kT", [128, 2, (n_halo_kc + n_ctx // 128) * 128], BF16)
    nav_d = c.din("na_v", [(n_halo_kc + n_ctx // 128) * 128, 256], BF16)
    nakTc_d = c.din("na_kTc", [128, 2, n_ctx], BF16)
    navc_d = c.din("na_vc", [n_ctx, 256], BF16)
    nvar = max(na_variants) + 1
    bias_d = c.din("na_bias", [nvar, 4, 6, 128, 128], F32)
    pin_d = c.din("pinT", [128, 2, n_lat + 16], BF16)
    rcnt_d = c.din("rcnt", [128, 2, n_lat], F32)
    pinc_d = c.din("pinTc", [128, 2, n_ctx + 16], BF16)
    rcntc_d = c.din("rcntc", [128, 2, n_ctx], F32)
    pwbd_d = c.din("pwbd", [128, 2, 128], BF16)
    pscale_d = c.din("pscaleT", [128, 2], F32)
    dkT_d = c.din("dkT", [128, 4, n_kc_diff * 128], BF16)
    dv_d = c.din("dv", [n_kc_diff * 128, 512], BF16)
    dlam_d = c.din("dlam", [128, 256], F32)
    subg_d = c.din("subg", [128, 128], F32)
    ident_d = c.din("ident", [128, 128], BF16)
    wout_d = c.din("w_out", [D, D], F32)
    w1_d = c.din("w1", [D, 2 * DFF], F32)
    w2_d = c.din("w2", [DFF, D], F32)
    x2T_o = c.dout("x2T", [128, 8, NT], F32)

    xT = c.sb("xT_sb", [128, 8, NT], F32)
    mixT_t = c.sb("mixT", [128, 8, NT], BF16)
    mixT = mixT_t[:]
    cm = Common(c)
    ident = c.sb("ident", [128, 128], BF16)
    pwbd = c.sb("pwbd", [128, 2, 128], BF16)
    pscale = c.sb("pscale", [128, 2], F32)
    dlam = c.sb("dlam", [128, 256], F32)
    lamw = c.sb("lamw", [128, 4], F32)
    lamt = c.sb("lamt", [128, 1], F32)
    gsub = c.sb("gsub", [128, 128], F32)
    shiftt = c.sb("shiftt", [128, 1], F32)
    S0 = c.ps("S0", [128, 2, 512])
    S1 = c.ps("S1", [128, 2, 512])
    U1 = c.ps("U1", [128, 2, 512])
    U2 = c.ps("U2", [128, 2, 512])
    P = {'S': [S0, S1], 'U1': U1, 'U2': U2, 'O': U1, 'T': U2[:, 0, :].bitcast(BF16)}
    arena_n = (c.nc.sbuf_bytes_remaining - 2048) // 2 // 16 * 16
    ar = Arena(c, arena_n)
    print("arena elems", arena_n)

    for t in range(0, NT, 512):
        w = min(512, NT - t)
        em.dma('sp', I('dma_start', out=xT[:, :, t:t + w], in_=x1T_d[:, :, t:t + w]), writes=xres(t, w))
    cm.load_normg(normg_d)
    cm.load_mods(mods_d)
    cm.compute_coefs()
    em.dma('sp', I('dma_start', out=ident[:], in_=ident_d), writes=['ident'])
    em.dma('sp', I('dma_start', out=pwbd[:], in_=pwbd_d), writes=['pwbd'])
    em.dma('sp', I('dma_start', out=pscale[:], in_=pscale_d), writes=['pscale'])
    em.dma('sp', I('dma_start', out=dlam[:], in_=dlam_d), writes=['dlam'])
    em.dma('sp', I('dma_start', out=gsub[:], in_=subg_d), writes=['gsub'])
    em.op('dve', I('memset', shiftt[:], EXP_SHIFT), writes=['shiftt'])
    em.op('dve', I('tensor_scalar', out=gsub[:], in0=gsub[:], scalar1=float(1.0 - lam_init), scalar2=None, op0=ALU.mult), reads=['gsub'], writes=['gsub'])
    em.op('dve', I('tensor_tensor', out=dlam[:, 0:64], in0=dlam[:, 0:64], in1=dlam[:, 64:128], op=ALU.mult), reads=['dlam'], writes=['dlam'])
    em.op('dve', I('tensor_tensor', out=dlam[:, 128:192], in0=dlam[:, 128:192], in1=dlam[:, 192:256], op=ALU.mult), reads=['dlam'], writes=['dlam'])
    em.op('dve', I('reduce_sum', out=lamw[:, 0:2], in_=dlam[:].rearrange("p (a b) -> p a b", a=2)[:, :, 0:64], axis=AX.X), reads=['dlam'], writes=['lamw'])
    em.op('act', I('activation', out=lamw[:, 2:4], in_=lamw[:, 0:2], func=AF.Exp), reads=['lamw'], writes=['lamw'])
    em.op('dve', I('tensor_tensor', out=lamt[:], in0=lamw[:, 2:3], in1=lamw[:, 3:4], op=ALU.subtract), reads=['lamw'], writes=['lamt'])
    em.op('dve', I('tensor_scalar', out=lamt[:], in0=lamt[:], scalar1=float(lam_init), scalar2=None, op0=ALU.add), reads=['lamt'], writes=['lamt'])

    nkc_tot = n_halo_kc + n_ctx // 128
    ctx_kcs = [n_halo_kc + i for i in range(n_ctx // 128)]
    qch = []
    for i in range(nq):
        k0, nl = na_local_chunks(i, nq)
        qch.append((i * 128, [k0 + j for j in range(nl)] + ctx_kcs, na_variants[i], nl))
    na_attention(c, ar, P, mixT, q_d, nakT_d, nav_d, nkc_tot, qch, bias_d, ident, 0, shiftt)
    if ctx_out:
        ar.reset()
        qch = [(n_lat + i * 128, list(range(n_ctx // 128)), None, 0) for i in range(n_ctx // 128)]
        na_attention(c, ar, P, mixT, q_d, nakTc_d, navc_d, n_ctx // 128, qch, bias_d, ident, n_lat, shiftt)
    ar.reset()
    pool_mixer(c, ar, P, mixT, pin_d, rcnt_d, n_lat, pwbd, pscale, 0)
    if ctx_out:
        ar.reset()
        pool_mixer(c, ar, P, mixT, pinc_d, rcntc_d, n_ctx, pwbd, pscale, n_lat)
    ar.reset()
    qtiles = [(t, min(512, n_lat - t)) for t in range(0, n_lat, 512)]
    diff_attention(c, ar, P, mixT, q_d, qtiles, dkT_d, dv_d, n_kc_diff, lamt, gsub, ident, shiftt, cm.epst)
    if ctx_out:
        ar.reset()
        ctx0 = n_kc_diff - n_ctx // 128
        qtiles = [(n_lat, n_ctx)]
        diff_attention(c, ar, P, mixT, q_d, qtiles, dkT_d[:, :, ctx0 * 128:], dv_d[ctx0 * 128:, :], n_ctx // 128, lamt, gsub, ident, shiftt, cm.epst)
    ar.reset()
    n_mix = NT if ctx_out else n_lat
    wo = ar.alloc("wo", [128, 8, D], BF16)
    em.dma('pool', I('dma_start', out=wo, in_=wout_d.rearrange("(k p) n -> p k n", p=128)), writes=['wo'])

    class YB:
        pass
    yb = YB()
    yb.ysb = ar.alloc("ysb", [128, 8, 512], F32)
    yb.sq = ar.alloc("sq", [128, 8, 512], BF16)
    yb.tmp = [ar.alloc("tmpf%d" % i, [128, 512], F32) for i in range(2)]
    yb.rstd = ar.alloc("rstd", [128, 512], F32)
    yb.n_tmp = 0
    pss = {'ss': U2[:, 1, :], 'a': [S0[:, 0, :], S0[:, 1, :]], 'b': [S1[:, 0, :], S1[:, 1, :]], 'y': [U1[:, 0, :], U1[:, 1, :]]}
    ny = 0
    for (t0, w, g) in [s_ for tile in make_tiles(n_lat, n_ctx if ctx_out else 0, 512) for s_ in tile]:
        for k in range(8):
            py = pss['y'][ny % 2]
            ry = 'psU1_%d' % (ny % 2)
            ny += 1
            mm_group(em, py[:, :w], [(wo[:, kk, k * 128:(k + 1) * 128], mixT[:, kk, t0:t0 + w]) for kk in range(8)],
                     reads=['wo'] + ['mix%d' % i for i in range(t0 // 128, (t0 + w) // 128)], wres=ry)
            y_evac(c, yb, k, 0, w, py[:, :w], ry)
        sandwich_out(c, cm, yb, xT, 1, t0, w, g, 0, pss['ss'], ss_res='psU2_1')
    ar.reset()
    gT = mixT_t[:].rearrange("p a b -> p (a b)")[:, 0:NFC * tw].rearrange("p (a b) -> p a b", a=NFC) if 8 * NT >= NFC * tw else None
    fb = FFNBufs(c, tw, alloc=ar.alloc, gT=gT)
    tiles = make_tiles(n_lat, n_ctx if ctx_out else 0, tw)
    ffn(c, cm, fb, xT, 2, tiles, w1_d, w2_d, pss, psnames={'ss': 'psU2_1', 'a': ['psS0_0', 'psS0_1'], 'b': ['psS1_0', 'psS1_1'], 'y': ['psU1_0', 'psU1_1']})
    em.dma('sp', I('dma_start', out=x2T_o, in_=xT[:]), reads=xres(0, NT), writes=['x2T_o'])
    print("part B instructions:", em.ninst)
    return c.done()


def na_bias_tiles(rpb, rows_total, q_row0, key_row0, nj=6):
    kr = np.arange(2)[:, None, None, None]
    kc = np.arange(64)[None, :, None, None]
    qr = np.arange(2)[None, None, :, None]
    qc = np.arange(64)[None, None, None, :]
    q_row = q_row0 + qr
    rs = np.clip(q_row - 4, 0, rows_total - 8)
    cs = np.clip(qc - 8, 0, 64 - 16)
    out = np.full((4, nj, 2, 64, 2, 64), NEG, np.float32)
    for j in range(nj):
        key_row = key_row0 + 2 * j + kr
        valid = (key_row >= rs) & (key_row < rs + 8) & (kc >= cs) & (kc < cs + 16) & (key_row >= 0) & (key_row < rows_total)
        valid = np.broadcast_to(valid, (2, 64, 2, 64))
        dr = np.clip(np.broadcast_to(key_row - q_row + 7, (2, 64, 2, 64)), 0, 14)
        dc = np.clip(np.broadcast_to(kc - qc, (2, 64, 2, 64)), -15, 15) + 15
        for h in range(4):
            out[h, j] = np.where(valid, rpb[h][dr, dc], np.float32(NEG))
    return out.reshape(4, nj, 128, 128)


def pool_rcount(t_global, L):
    n = len(t_global)
    out = np.zeros((128, 2, n), np.float32)
    for gi, wdw in enumerate((2, 4, 8, 16)):
        half = wdw // 2
        lo = np.clip(t_global - half, 0, L)
        hi = np.clip(t_global + half, 0, L)
        rc = (1.0 / (hi - lo).astype(np.float32)).astype(np.float32)
        out[(gi % 2) * 64:(gi % 2) * 64 + 64, gi // 2, :] = rc[None, :]
    return out


def halo_cols(arrT, t0, n, halo, L):
    out = np.zeros(arrT.shape[:-1] + (n + 2 * halo,), arrT.dtype)
    a = max(0, t0 - halo)
    b = min(L, t0 + n + halo)
    out[..., a - (t0 - halo):b - (t0 - halo)] = arrT[..., a:b]
    return out


def pool_blockdiag(pool_w):
    out = np.zeros((128, 2, 128), np.float32)
    for gi in range(4):
        p0 = (gi % 2) * 64
        out[p0:p0 + 64, gi // 2, p0:p0 + 64] = pool_w[gi]
    return out.astype(NPBF)


N_LAT = 2048
NT_FULL = N_LAT + CTX
_PROGS = {}


def _fm(a):
    T, F = a.shape
    return np.ascontiguousarray(a.reshape(T, F // 128, 128).transpose(2, 1, 0))


def _unfm(aT):
    return np.ascontiguousarray(aT.transpose(2, 1, 0)).reshape(aT.shape[2], -1)


def _lay_vec(v):
    return np.ascontiguousarray(v.reshape(-1, 128).T)


def _prog(key, fn):
    if key not in _PROGS:
        _PROGS[key] = fn()
    return _PROGS[key]


def kernel_unfused(x, c, ctx, c_ctx, w_ada, b_ada, norm_g, ffn_w1, ffn_w2, w_in, w_out, na_rpb, pool_w, pool_scale, diff_lambda,
           diff_subln_g):
    f32 = np.float32
    x = np.asarray(x, f32)
    ctx = np.asarray(ctx, f32)
    cvals = np.asarray(c, f32)
    c_ctx = np.asarray(c_ctx, f32)
    ncores = 8
    depth = w_ada.shape[0]
    rows_total = SEQ // GRID_W
    nq = N_LAT // 128
    variants = [1, 2] + [0] * (nq - 4) + [3, 4]
    xT = []
    for i in range(ncores):
        b, j = i // 4, i % 4
        xt = np.concatenate([x[b, j * N_LAT:(j + 1) * N_LAT], ctx[b]], axis=0)
        xT.append(_fm(xt))
    ropes = []
    for i in range(ncores):
        j = i % 4
        cos, sin = rope_tables(np.arange(j * N_LAT, (j + 1) * N_LAT))
        ropes.append(rope_feature_major(cos, sin, CTX))
    pmat = rope_pmat()
    ident = np.eye(128, dtype=f32).astype(NPBF)
    TW = 512
    for l in range(depth):
        last = (l == depth - 1)
        lam_init = 0.8 - 0.6 * math.exp(-0.3 * l)
        nca = _prog(('A',), lambda: build_part_a(N_LAT, CTX, TW))
        normgT = np.ascontiguousarray(np.asarray(norm_g[l], f32).reshape(6, 8, 128).transpose(2, 0, 1))
        badaT = np.ascontiguousarray(np.asarray(b_ada[l], f32).reshape(72, 128).T)
        wada_l = np.ascontiguousarray(np.asarray(w_ada[l], f32))
        w1a = np.ascontiguousarray(np.asarray(ffn_w1[l, 0], f32))
        w2a = np.ascontiguousarray(np.asarray(ffn_w2[l, 0], f32))
        win_l = np.ascontiguousarray(np.asarray(w_in[l], f32))
        in_maps = []
        for i in range(ncores):
            b = i // 4
            cvec = np.ascontiguousarray(np.stack([_lay_vec(cvals[b]), _lay_vec(c_ctx)], axis=-1))
            in_maps.append({"xT": xT[i], "cvec": cvec, "w_ada": wada_l, "badaT": badaT, "normgT": normgT, "w1": w1a, "w2": w2a,
                            "w_in": win_l, "ropeC": ropes[i][0], "ropeS": ropes[i][1], "pmat": pmat})
        ra = run_bass_kernel_spmd(nca, in_maps, core_ids=list(range(ncores))).results
        ncb = _prog(('B', l), lambda: build_part_b(N_LAT, CTX, TW, n_kc_diff=(SEQ + CTX) // 128, na_variants=variants,
                                                  lam_init=lam_init, ctx_out=not last))
        w1b_ = np.ascontiguousarray(np.asarray(ffn_w1[l, 1], f32))
        w2b_ = np.ascontiguousarray(np.asarray(ffn_w2[l, 1], f32))
        wout_l = np.ascontiguousarray(np.asarray(w_out[l], f32))
        pwbd = pool_blockdiag(np.asarray(pool_w[l], f32))
        pscaleT = np.ascontiguousarray(np.asarray(pool_scale[l], f32).reshape(2, 128).T)
        dlam = np.ascontiguousarray(np.broadcast_to(np.asarray(diff_lambda[l], f32).reshape(1, 256), (128, 256)))
        subg = np.ascontiguousarray(np.broadcast_to(np.asarray(diff_subln_g[l], f32)[None, :], (128, 128)))
        rpb = np.asarray(na_rpb[l], f32)
        in_maps = []
        for b in range(2):
            cores = [b * 4 + j for j in range(4)]
            kv_lat = np.concatenate([ra[i]["kvT"][:, :, :N_LAT] for i in cores], axis=2)
            kv_ctx = ra[cores[0]]["kvT"][:, :, N_LAT:]
            v_lat = np.concatenate([ra[i]["vtok"][:N_LAT] for i in cores], axis=0)
            v_ctx = ra[cores[0]]["vtok"][N_LAT:]
            dkT = np.ascontiguousarray(np.concatenate([kv_lat[:, 4:8], kv_ctx[:, 4:8]], axis=2))
            dv = np.ascontiguousarray(np.concatenate([v_lat[:, 256:], v_ctx[:, 256:]], axis=0))
            nakTc = np.ascontiguousarray(kv_ctx[:, 0:2])
            navc = np.ascontiguousarray(v_ctx[:, 0:256])
            pinTc = halo_cols(np.ascontiguousarray(kv_ctx[:, 2:4]), 0, CTX, 8, CTX)
            rcntc = pool_rcount(np.arange(CTX), CTX)
            for j in range(4):
                i = cores[j]
                t_start = j * N_LAT
                r0 = t_start // GRID_W
                hk0 = (r0 - 4) * GRID_W
                nhk = (nq + 4) * 128
                na_kT = np.concatenate([halo_cols(kv_lat[:, 0:2], hk0, nhk, 0, SEQ), nakTc], axis=2)
                na_v = np.concatenate([halo_cols(v_lat[:, 0:256].T, hk0, nhk, 0, SEQ).T, navc], axis=0)
                bias = np.full((5, 4, 6, 128, 128), NEG, f32)
                for vi, ci in ((0, 2), (1, 0), (2, 1), (3, nq - 2), (4, nq - 1)):
                    k0, nl = na_local_chunks(ci, nq)
                    bias[vi] = na_bias_tiles(rpb, rows_total, r0 + 2 * ci, r0 - 4 + 2 * k0)
                in_maps.append({
                    "x1T": ra[i]["x1T"], "modsT": ra[i]["modsT"], "normgT": normgT, "qT": ra[i]["qT"],
                    "na_kT": np.ascontiguousarray(na_kT), "na_v": np.ascontiguousarray(na_v), "na_kTc": nakTc, "na_vc": navc,
                    "na_bias": bias,
                    "pinT": halo_cols(kv_lat[:, 2:4], t_start, N_LAT, 8, SEQ), "rcnt": pool_rcount(np.arange(t_start, t_start + N_LAT), SEQ),
                    "pinTc": pinTc, "rcntc": rcntc, "pwbd": pwbd, "pscaleT": pscaleT,
                    "dkT": dkT, "dv": dv, "dlam": dlam, "subg": subg, "ident": ident,
                    "w_out": wout_l, "w1": w1b_, "w2": w2b_,
                })
        rb = run_bass_kernel_spmd(ncb, in_maps, core_ids=list(range(ncores))).results
        xT = [rb[i]["x2T"] for i in range(ncores)]
    out = np.zeros((2, SEQ, D), f32)
    for i in range(ncores):
        b, j = i // 4, i % 4
        out[b, j * N_LAT:(j + 1) * N_LAT] = _unfm(xT[i][:, :, :N_LAT])
    return out


def build_fused(n_lat=2048, n_ctx=256, tw=512, depth=2, group=4, dbg_ctx_out=False):
    NT = n_lat + n_ctx
    nq = n_lat // 128
    n_halo_kc = nq + 4
    nkc_na = n_halo_kc + n_ctx // 128
    n_kc_diff = (group * n_lat + n_ctx) // 128
    variants = [1, 2] + [0] * (nq - 4) + [3, 4] if nq > 4 else list(range(1, nq + 1))
    nvar = max(variants) + 1
    c = Ctx()
    em = c.em
    nc = c.nc
    xT_d = c.din("xT", [128, 8, NT], F32)
    cvec_d = c.din("cvec", [128, 8, 2], F32)
    ropeC_d = c.din("ropeC", [128, NT], F32)
    ropeS_d = c.din("ropeS", [128, NT], F32)
    pm_d = c.din("pmat", [128, 128], BF16)
    ident_d = c.din("ident", [128, 128], BF16)
    rcnt_d = c.din("rcnt", [128, 2, n_lat], F32)
    rcntc_d = c.din("rcntc", [128, 2, n_ctx], F32)
    L = []
    for l in range(depth):
        L.append(dict(
            wada=c.din("w_ada%d" % l, [D, 9 * D], F32), bada=c.din("badaT%d" % l, [128, 72], F32),
            normg=c.din("normgT%d" % l, [128, 6, 8], F32),
            w1a=c.din("w1a%d" % l, [D, 2 * DFF], F32), w2a=c.din("w2a%d" % l, [DFF, D], F32),
            w1b=c.din("w1b%d" % l, [D, 2 * DFF], F32), w2b=c.din("w2b%d" % l, [DFF, D], F32),
            win=c.din("w_in%d" % l, [D, 2560], F32), wout=c.din("w_out%d" % l, [D, D], F32),
            bias=c.din("na_bias%d" % l, [nvar, 4, 6, 128, 128], F32),
            pwbd=c.din("pwbd%d" % l, [128, 2, 128], BF16), pscale=c.din("pscaleT%d" % l, [128, 2], F32),
            dlam=c.din("dlam%d" % l, [128, 256], F32), subg=c.din("subg%d" % l, [128, 128], F32),
        ))
    outT_o = c.dout("outT", [128, 8, n_lat], F32)

    def dram(name, shape, dt):
        return nc.dram_tensor(name, list(shape), dt, kind="Internal").ap()

    xT = c.sb("xT_sb", [128, 8, NT], F32)
    cm = Common(c)
    ident = c.sb("ident", [128, 128], BF16)
    pwbd = c.sb("pwbd", [128, 2, 128], BF16)
    pscale = c.sb("pscale", [128, 2], F32)
    dlam = c.sb("dlam", [128, 256], F32)
    lamw = c.sb("lamw", [128, 4], F32)
    lamt = c.sb("lamt", [128, 1], F32)
    gsub = c.sb("gsub", [128, 128], F32)
    shiftt = c.sb("shiftt", [128, 1], F32)
    zt = c.sb("zeros", [128, 2048], BF16)
    S0 = c.ps("S0", [128, 2, 512])
    S1 = c.ps("S1", [128, 2, 512])
    U1 = c.ps("U1", [128, 2, 512])
    U2 = c.ps("U2", [128, 2, 512])
    P = {'S': [S0, S1], 'U1': U1, 'U2': U2, 'O': U1, 'T': U2[:, 0, :].bitcast(BF16)}
    pss = {'ss': U2[:, 1, :], 'a': [S0[:, 0, :], S0[:, 1, :]], 'b': [S1[:, 0, :], S1[:, 1, :]], 'y': [U1[:, 0, :], U1[:, 1, :]]}
    psn = {'ss': 'psU2_1', 'a': ['psS0_0', 'psS0_1'], 'b': ['psS1_0', 'psS1_1'], 'y': ['psU1_0', 'psU1_1']}
    ps_mods = U2[:, 0, :].rearrange("p (a b) -> p a b", b=2)
    arena_n = (nc.sbuf_bytes_remaining - 2048) // 2 // 16 * 16
    ar = Arena(c, arena_n)
    print("fused arena elems", arena_n)

    for t in range(0, NT, 512):
        w = min(512, NT - t)
        em.dma('sp', I('dma_start', out=xT[:, :, t:t + w], in_=xT_d[:, :, t:t + w]), writes=xres(t, w))
    em.dma('sp', I('dma_start', out=ident[:], in_=ident_d), writes=['ident'])
    em.op('dve', I('memset', shiftt[:], EXP_SHIFT), writes=['shiftt'])
    em.op('dve', I('memset', zt[:], 0.0), writes=['zeros'])
    tiles_all = make_tiles(n_lat, n_ctx, tw)
    wsel_d = c.din("wsel", [128, 2 * group], F32)
    wsel = c.sb("wsel", [128, 2 * group], F32)
    em.dma('sp', I('dma_start', out=wsel[:], in_=wsel_d), writes=['wsel'])

    for l in range(depth):
        W = L[l]
        last = (l == depth - 1)
        ctx_out = (not last) or dbg_ctx_out
        lam_init = 0.8 - 0.6 * math.exp(-0.3 * l)
        ar.reset(to_zero=True)
        em.dma('sp', I('dma_start', out=cm.normg[:], in_=W['normg']), writes=['normg'])
        fb = FFNBufs(c, tw, alloc=ar.alloc)
        cm.compute_mods(cvec_d, W['wada'], W['bada'], ps_mods, fb, alloc=ar.alloc, psname='psU2_0')
        cm.compute_coefs()
        ffn(c, cm, fb, xT, 0, tiles_all, W['w1a'], W['w2a'], pss, psnames=psn)
        qT_l = dram("qT_l%d" % l, [128, 6, NT], BF16)
        kvT_l = dram("kvT_l%d" % l, [128, 8 * NT], BF16)
        v_l = dram("v_l%d" % l, [NT, 768], BF16)
        kvT_l3 = kvT_l.rearrange("p (c t) -> p c t", c=8)
        proj_phase(c, cm, fb, xT, tiles_all, W['win'], ropeC_d, ropeS_d, pm_d, pss, qT_l, kvT_l3, v_l, alloc=ar.alloc, psnames=psn)
        rg = [[g0 * group + j for j in range(group)] for g0 in range(8 // group)]
        kedge_loc = dram("kedge_loc%d" % l, [128, 4 * 512], BF16)
        kedge_loc3 = kedge_loc.rearrange("p (c t) -> p c t", c=4)
        vedge_loc = dram("vedge_loc%d" % l, [512, 256], BF16)
        em.dma('sp', I('dma_start', out=kedge_loc3[:, :, 0:256], in_=kvT_l3[:, 0:4, 0:256]), reads=['kvT_o'], writes=['kedge_loc'])
        em.dma('sp', I('dma_start', out=kedge_loc3[:, :, 256:512], in_=kvT_l3[:, 0:4, n_lat - 256:n_lat]), reads=['kvT_o'], writes=['kedge_loc'])
        em.dma('sp', I('dma_start', out=vedge_loc[0:256, :], in_=v_l[0:256, 0:256]), reads=['v_o'], writes=['vedge_loc'])
        em.dma('sp', I('dma_start', out=vedge_loc[256:512, :], in_=v_l[n_lat - 256:n_lat, 0:256]), reads=['v_o'], writes=['vedge_loc'])
        kedge_g = dram("kedge_g%d" % l, [group * 128, 4 * 512], BF16)
        vedge_g = dram("vedge_g%d" % l, [group * 512, 256], BF16)
        em.coll(I('collective_compute', "AllGather", ALU.bypass, replica_groups=rg, ins=[kedge_loc.opt()], outs=[kedge_g.opt()]),
                reads=['kedge_loc'], writes=['kedge_g'])
        em.coll(I('collective_compute', "AllGather", ALU.bypass, replica_groups=rg, ins=[vedge_loc.opt()], outs=[vedge_g.opt()]),
                reads=['vedge_loc'], writes=['vedge_g'])
        dk_g, dv_g = [], []
        for h in range(4):
            dk_loc = dram("dk_loc%d_%d" % (l, h), [128, n_lat], BF16)
            dv_loc = dram("dv_loc%d_%d" % (l, h), [n_lat, 128], BF16)
            em.dma('sp', I('dma_start', out=dk_loc, in_=kvT_l3[:, 4 + h, 0:n_lat]), reads=['kvT_o'], writes=['dk_loc%d' % h])
            em.dma('sp', I('dma_start', out=dv_loc, in_=v_l[0:n_lat, 256 + h * 128:256 + (h + 1) * 128]), reads=['v_o'], writes=['dv_loc%d' % h])
            dkg = dram("dk_g%d_%d" % (l, h), [group * 128, n_lat], BF16)
            dvg = dram("dv_g%d_%d" % (l, h), [group * n_lat, 128], BF16)
            em.coll(I('collective_compute', "AllGather", ALU.bypass, replica_groups=rg, ins=[dk_loc.opt()], outs=[dkg.opt()]),
                    reads=['dk_loc%d' % h], writes=['dk_g%d' % h])
            em.coll(I('collective_compute', "AllGather", ALU.bypass, replica_groups=rg, ins=[dv_loc.opt()], outs=[dvg.opt()]),
                    reads=['dv_loc%d' % h], writes=['dv_g%d' % h])
            dk_g.append(dkg)
            dv_g.append(dvg)
        ar.reset(to_zero=True)
        ke = ar.alloc("ke", [128, group, 2048], BF16)
        ve = ar.alloc("ve", [128, group, 4, 256], BF16)
        kp = ar.alloc("kp", [128, 2048], BF16)
        kn = ar.alloc("kn", [128, 2048], BF16)
        vp = ar.alloc("vp", [128, 4, 256], BF16)
        vn = ar.alloc("vn", [128, 4, 256], BF16)
        em.dma('sp', I('dma_start', out=ke, in_=kedge_g.rearrange("(r p) n -> p r n", p=128)), reads=['kedge_g'], writes=['ke'])
        for r in range(group):
            em.dma('sp', I('dma_start', out=ve[:, r, :, :], in_=vedge_g[r * 512:(r + 1) * 512, :].rearrange("(a p) n -> p a n", p=128)),
                   reads=['vedge_g'], writes=['ve'])
        for (dst, dres, src, sres, w0) in ((kp, 'kp', lambda r: ke[:, r, :], 'ke', 0), (kn, 'kn', lambda r: ke[:, r, :], 'ke', group),
                                           (vp, 'vp', lambda r: ve[:, r, :, :], 've', 0), (vn, 'vn', lambda r: ve[:, r, :, :], 've', group)):
            em.op('dve', I('tensor_scalar', out=dst, in0=src(0), scalar1=wsel[:, w0:w0 + 1], scalar2=None, op0=ALU.mult),
                  reads=[sres, 'wsel'], writes=[dres])
            for r in range(1, group):
                em.op('dve', I('scalar_tensor_tensor', out=dst, in0=src(r), scalar=wsel[:, w0 + r:w0 + r + 1], in1=dst, op0=ALU.mult, op1=ALU.add),
                      reads=[sres, 'wsel', dres], writes=[dres])
        kp3 = kp.rearrange("p (c t) -> p c t", c=4)
        kn3 = kn.rearrange("p (c t) -> p c t", c=4)
        na_kT_asm = dram("na_kT_asm%d" % l, [128, 2, nkc_na * 128], BF16)
        na_v_asm = dram("na_v_asm%d" % l, [nkc_na * 128, 256], BF16)
        pin_asm = dram("pin_asm%d" % l, [128, 2, n_lat + 16], BF16)
        pinc_asm = dram("pinc_asm%d" % l, [128, 2, n_ctx + 16], BF16)
        em.dma('sp', I('dma_start', out=na_kT_asm[:, :, 0:256], in_=kp3[:, 0:2, 256:512]), reads=['kp'], writes=['na_kT_asm'])
        em.dma('sp', I('dma_start', out=na_kT_asm[:, :, 256 + n_lat:512 + n_lat], in_=kn3[:, 0:2, 0:256]), reads=['kn'], writes=['na_kT_asm'])
        em.dma('sp', I('dma_start', out=na_v_asm[0:256, :].rearrange("(a p) n -> p a n", p=128), in_=vp[:, 2:4, :]), reads=['vp'], writes=['na_v_asm'])
        em.dma('sp', I('dma_start', out=na_v_asm[256 + n_lat:512 + n_lat, :].rearrange("(a p) n -> p a n", p=128), in_=vn[:, 0:2, :]), reads=['vn'], writes=['na_v_asm'])
        em.dma('sp', I('dma_start', out=pin_asm[:, :, 0:8], in_=kp3[:, 2:4, 504:512]), reads=['kp'], writes=['pin_asm'])
        em.dma('sp', I('dma_start', out=pin_asm[:, :, 8 + n_lat:16 + n_lat], in_=kn3[:, 2:4, 0:8]), reads=['kn'], writes=['pin_asm'])
        em.dma('sp', I('dma_start', out=na_kT_asm[:, :, 256:256 + n_lat], in_=kvT_l3[:, 0:2, 0:n_lat]), reads=['kvT_o'], writes=['na_kT_asm'])
        em.dma('sp', I('dma_start', out=na_kT_asm[:, :, 512 + n_lat:], in_=kvT_l3[:, 0:2, n_lat:NT]), reads=['kvT_o'], writes=['na_kT_asm'])
        em.dma('sp', I('dma_start', out=na_v_asm[256:256 + n_lat, :], in_=v_l[0:n_lat, 0:256]), reads=['v_o'], writes=['na_v_asm'])
        em.dma('sp', I('dma_start', out=na_v_asm[512 + n_lat:, :], in_=v_l[n_lat:NT, 0:256]), reads=['v_o'], writes=['na_v_asm'])
        em.dma('sp', I('dma_start', out=pin_asm[:, :, 8:8 + n_lat], in_=kvT_l3[:, 2:4, 0:n_lat]), reads=['kvT_o'], writes=['pin_asm'])
        if ctx_out:
            for (a0, a1) in ((0, 8), (8 + n_ctx, 16 + n_ctx)):
                em.dma('sp', I('dma_start', out=pinc_asm[:, :, a0:a1], in_=zt[:, 0:16].rearrange("p (c t) -> p c t", c=2)), reads=['zeros'], writes=['pinc_asm'])
            em.dma('sp', I('dma_start', out=pinc_asm[:, :, 8:8 + n_ctx], in_=kvT_l3[:, 2:4, n_lat:NT]), reads=['kvT_o'], writes=['pinc_asm'])
        ar.reset(to_zero=True)
        mixT = ar.alloc("mixT", [128, 8, NT], BF16)
        ar.set_base()
        em.dma('sp', I('dma_start', out=pwbd[:], in_=W['pwbd']), writes=['pwbd'])
        em.dma('sp', I('dma_start', out=pscale[:], in_=W['pscale']), writes=['pscale'])
        em.dma('sp', I('dma_start', out=dlam[:], in_=W['dlam']), writes=['dlam'])
        em.dma('sp', I('dma_start', out=gsub[:], in_=W['subg']), writes=['gsub'])
        em.op('dve', I('tensor_scalar', out=gsub[:], in0=gsub[:], scalar1=float(1.0 - lam_init), scalar2=None, op0=ALU.mult), reads=['gsub'], writes=['gsub'])
        em.op('dve', I('tensor_tensor', out=dlam[:, 0:64], in0=dlam[:, 0:64], in1=dlam[:, 64:128], op=ALU.mult), reads=['dlam'], writes=['dlam'])
        em.op('dve', I('tensor_tensor', out=dlam[:, 128:192], in0=dlam[:, 128:192], in1=dlam[:, 192:256], op=ALU.mult), reads=['dlam'], writes=['dlam'])
        em.op('dve', I('reduce_sum', out=lamw[:, 0:2], in_=dlam[:].rearrange("p (a b) -> p a b", a=2)[:, :, 0:64], axis=AX.X), reads=['dlam'], writes=['lamw'])
        em.op('act', I('activation', out=lamw[:, 2:4], in_=lamw[:, 0:2], func=AF.Exp), reads=['lamw'], writes=['lamw'])
        em.op('dve', I('tensor_tensor', out=lamt[:], in0=lamw[:, 2:3], in1=lamw[:, 3:4], op=ALU.subtract), reads=['lamw'], writes=['lamt'])
        em.op('dve', I('tensor_scalar', out=lamt[:], in0=lamt[:], scalar1=float(lam_init), scalar2=None, op0=ALU.add), reads=['lamt'], writes=['lamt'])
        em.barrier()
        ctx_kcs = [n_halo_kc + i for i in range(n_ctx // 128)]
        qch = []
        for i in range(nq):
            k0, nl = na_local_chunks(i, nq)
            qch.append((i * 128, [k0 + j for j in range(nl)] + ctx_kcs, variants[i], nl))
        na_attention(c, ar, P, mixT, qT_l, na_kT_asm, na_v_asm, nkc_na, qch, W['bias'], ident, 0, shiftt)
        if ctx_out:
            ar.reset()
            qch = [(n_lat + i * 128, list(range(n_ctx // 128)), None, 0) for i in range(n_ctx // 128)]
            na_attention(c, ar, P, mixT, qT_l, kvT_l3[:, 0:2, n_lat:NT], v_l[n_lat:NT, 0:256], n_ctx // 128, qch, W['bias'], ident, n_lat, shiftt)
        ar.reset()
        pool_mixer(c, ar, P, mixT, pin_asm, rcnt_d, n_lat, pwbd, pscale, 0)
        if ctx_out:
            ar.reset()
            pool_mixer(c, ar, P, mixT, pinc_asm, rcntc_d, n_ctx, pwbd, pscale, n_lat)
        ar.reset()
        lat_kc = n_lat // 128

        def load_all(h, kb, vb, rk, rv):
            for r in range(group):
                em.dma('sp', I('dma_start', out=kb[:, r * n_lat:(r + 1) * n_lat], in_=dk_g[h][r * 128:(r + 1) * 128, :]), reads=['dk_g%d' % h], writes=[rk])
                em.dma('sp', I('dma_start', out=vb[:, r * lat_kc:(r + 1) * lat_kc, 0:128],
                               in_=dv_g[h][r * n_lat:(r + 1) * n_lat, :].rearrange("(c p) e -> p c e", p=128)),
                       reads=['dv_g%d' % h], writes=[rv])
            em.dma('sp', I('dma_start', out=kb[:, group * n_lat:], in_=kvT_l3[:, 4 + h, n_lat:NT]), reads=['kvT_o'], writes=[rk])
            em.dma('sp', I('dma_start', out=vb[:, group * lat_kc:, 0:128],
                           in_=v_l[n_lat:NT, 256 + h * 128:256 + (h + 1) * 128].rearrange("(c p) e -> p c e", p=128)), reads=['v_o'], writes=[rv])

        def load_ctx(h, kb, vb, rk, rv):
            em.dma('sp', I('dma_start', out=kb, in_=kvT_l3[:, 4 + h, n_lat:NT]), reads=['kvT_o'], writes=[rk])
            em.dma('sp', I('dma_start', out=vb[:, :, 0:128],
                           in_=v_l[n_lat:NT, 256 + h * 128:256 + (h + 1) * 128].rearrange("(c p) e -> p c e", p=128)), reads=['v_o'], writes=[rv])
        qtiles = [(t, min(512, n_lat - t)) for t in range(0, n_lat, 512)]
        diff_attention(c, ar, P, mixT, qT_l, qtiles, None, None, n_kc_diff, lamt, gsub, ident, shiftt, cm.epst, loaders=load_all)
        if ctx_out:
            ar.reset()
            diff_attention(c, ar, P, mixT, qT_l, [(n_lat, n_ctx)], None, None, n_ctx // 128, lamt, gsub, ident, shiftt, cm.epst, loaders=load_ctx)
        ar.reset()
        wo = ar.alloc("wo", [128, 8, D], BF16)
        em.dma('pool', I('dma_start', out=wo, in_=W['wout'].rearrange("(k p) n -> p k n", p=128)), writes=['wo'])

        class YB:
            pass
        yb = YB()
        yb.ysb = ar.alloc("ysb", [128, 8, 512], F32)
        yb.sq = ar.alloc("sq", [128, 8, 512], BF16)
        yb.tmp = [ar.alloc("tmpf%d" % i, [128, 512], F32) for i in range(2)]
        yb.rstd = ar.alloc("rstd", [128, 512], F32)
        yb.n_tmp = 0
        ny = 0
        for (t0, w, g) in [s_ for tile in make_tiles(n_lat, n_ctx if ctx_out else 0, 512) for s_ in tile]:
            for k in range(8):
                py = pss['y'][ny % 2]
                ry = psn['y'][ny % 2]
                ny += 1
                mm_group(em, py[:, :w], [(wo[:, kk, k * 128:(k + 1) * 128], mixT[:, kk, t0:t0 + w]) for kk in range(8)],
                         reads=['wo'] + ['mix%d' % i for i in range(t0 // 128, (t0 + w) // 128)], wres=ry)
                y_evac(c, yb, k, 0, w, py[:, :w], ry)
            sandwich_out(c, cm, yb, xT, 1, t0, w, g, 0, pss['ss'], ss_res=psn['ss'])
        ar.reset(to_zero=True)
        fb = FFNBufs(c, tw, alloc=ar.alloc)
        ffn(c, cm, fb, xT, 2, make_tiles(n_lat, n_ctx if ctx_out else 0, tw), W['w1b'], W['w2b'], pss, psnames=psn)
    em.dma('sp', I('dma_start', out=outT_o, in_=xT[:, :, 0:n_lat]), reads=xres(0, n_lat), writes=['outT_o'])
    print("fused instructions:", em.ninst)
    return c.done()


def fused_inputs(x, c, ctx, c_ctx, w_ada, b_ada, norm_g, ffn_w1, ffn_w2, w_in, w_out, na_rpb, pool_w, pool_scale, diff_lambda,
                 diff_subln_g, n_lat):
    f32 = np.float32
    x = np.asarray(x, f32)
    ctx = np.asarray(ctx, f32)
    cvals = np.asarray(c, f32)
    c_ctx = np.asarray(c_ctx, f32)
    seq = x.shape[1]
    n_ctx = ctx.shape[1]
    group = seq // n_lat
    ncores = x.shape[0] * group
    depth = w_ada.shape[0]
    rows_total = seq // GRID_W
    nq = n_lat // 128
    if nq > 4:
        vmap = ((0, 2), (1, 0), (2, 1), (3, nq - 2), (4, nq - 1))
    else:
        vmap = tuple((i + 1, i) for i in range(nq))
    shared = {"pmat": rope_pmat(), "ident": np.eye(128, dtype=f32).astype(NPBF), "rcntc": pool_rcount(np.arange(n_ctx), n_ctx)}
    for l in range(depth):
        shared["w_ada%d" % l] = np.ascontiguousarray(np.asarray(w_ada[l], f32))
        shared["badaT%d" % l] = np.ascontiguousarray(np.asarray(b_ada[l], f32).reshape(72, 128).T)
        shared["normgT%d" % l] = np.ascontiguousarray(np.asarray(norm_g[l], f32).reshape(6, 8, 128).transpose(2, 0, 1))
        shared["w1a%d" % l] = np.ascontiguousarray(np.asarray(ffn_w1[l, 0], f32))
        shared["w2a%d" % l] = np.ascontiguousarray(np.asarray(ffn_w2[l, 0], f32))
        shared["w1b%d" % l] = np.ascontiguousarray(np.asarray(ffn_w1[l, 1], f32))
        shared["w2b%d" % l] = np.ascontiguousarray(np.asarray(ffn_w2[l, 1], f32))
        shared["w_in%d" % l] = np.ascontiguousarray(np.asarray(w_in[l], f32))
        shared["w_out%d" % l] = np.ascontiguousarray(np.asarray(w_out[l], f32))
        shared["pwbd%d" % l] = pool_blockdiag(np.asarray(pool_w[l], f32))
        shared["pscaleT%d" % l] = np.ascontiguousarray(np.asarray(pool_scale[l], f32).reshape(2, 128).T)
        shared["dlam%d" % l] = np.ascontiguousarray(np.broadcast_to(np.asarray(diff_lambda[l], f32).reshape(1, 256), (128, 256)))
        shared["subg%d" % l] = np.ascontiguousarray(np.broadcast_to(np.asarray(diff_subln_g[l], f32)[None, :], (128, 128)))
    in_maps = []
    for i in range(ncores):
        b, j = i // group, i % group
        t_start = j * n_lat
        r0 = t_start // GRID_W
        m = dict(shared)
        m["xT"] = _fm(np.concatenate([x[b, t_start:t_start + n_lat], ctx[b]], axis=0))
        m["cvec"] = np.ascontiguousarray(np.stack([_lay_vec(cvals[b]), _lay_vec(c_ctx)], axis=-1))
        cos, sin = rope_tables(np.arange(t_start, t_start + n_lat))
        m["ropeC"], m["ropeS"] = rope_feature_major(cos, sin, n_ctx)
        m["rcnt"] = pool_rcount(np.arange(t_start, t_start + n_lat), seq)
        ws = np.zeros((128, 2 * group), f32)
        if j > 0:
            ws[:, j - 1] = 1.0
        if j < group - 1:
            ws[:, group + j + 1] = 1.0
        m["wsel"] = ws
        for l in range(depth):
            rpb = np.asarray(na_rpb[l], f32)
            bias = np.full((len(vmap) + (1 if nq <= 4 else 0), 4, 6, 128, 128), NEG, f32)
            for vi, ci in vmap:
                k0, nl = na_local_chunks(ci, nq)
                bias[vi] = na_bias_tiles(rpb, rows_total, r0 + 2 * ci, r0 - 4 + 2 * k0)
            m["na_bias%d" % l] = bias
        in_maps.append(m)
    return in_maps, ncores, group


def kernel_fused(n_lat=N_LAT, **inputs):
    in_maps, ncores, group = fused_inputs(n_lat=n_lat, **inputs)
    n_ctx = inputs["ctx"].shape[1]
    seq = inputs["x"].shape[1]
    tw = 512 if n_lat >= 2048 else 384
    nc = _prog(('F', n_lat, n_ctx), lambda: build_fused(n_lat, n_ctx, tw, depth=inputs["w_ada"].shape[0], group=group))
    res = run_bass_kernel_spmd(nc, in_maps, core_ids=list(range(ncores))).results
    out = np.zeros((inputs["x"].shape[0], seq, D), np.float32)
    for i in range(ncores):
        b, j = i // group, i % group
        out[b, j * n_lat:(j + 1) * n_lat] = _unfm(res[i]["outT"])
    return out


def kernel(x, c, ctx, c_ctx, w_ada, b_ada, norm_g, ffn_w1, ffn_w2, w_in, w_out, na_rpb, pool_w, pool_scale, diff_lambda,
           diff_subln_g):
    return kernel_fused(n_lat=N_LAT, x=x, c=c, ctx=ctx, c_ctx=c_ctx, w_ada=w_ada, b_ada=b_ada, norm_g=norm_g, ffn_w1=ffn_w1,
                        ffn_w2=ffn_w2, w_in=w_in, w_out=w_out, na_rpb=na_rpb, pool_w=pool_w, pool_scale=pool_scale,
                        diff_lambda=diff_lambda, diff_subln_g=diff_subln_g)
```

```python
import math
import numpy as np
import ml_dtypes
from contextlib import ExitStack
import concourse.bass as bass
import concourse.mybir as mybir
from concourse.bass_utils import run_bass_kernel_spmd

F32 = mybir.dt.float32
BF16 = mybir.dt.bfloat16
AF = mybir.ActivationFunctionType
ALU = mybir.AluOpType
AX = mybir.AxisListType
NPBF = ml_dtypes.bfloat16

D = 1024
DFF = 2816
NFC = 22
SEQ = 8192
CTX = 256
GRID_W = 64
EPS = 1e-6
NEG = -30000.0
EXP_SHIFT = -40.0


class Emitter:
    ENGS = ('pe', 'act', 'dve', 'pool', 'sp')

    def __init__(self, nc, stack, n_dma_sems=16):
        self.nc = nc
        self._stack = stack
        self.cccount = 0
        self.prog = {e: [] for e in self.ENGS}
        self.count = {e: 0 for e in self.ENGS}
        self.waited = {e: {} for e in self.ENGS}
        self.dcount = [0] * n_dma_sems
        self.dnext_q = {e: 0 for e in self.ENGS}
        self.lastw = {}
        self.readers = {}
        self.semobj = {}
        for e in self.ENGS:
            self.semobj[('c', e)] = stack.enter_context(nc.semaphore('c_' + e))
        for i in range(n_dma_sems):
            self.semobj[('d', i)] = stack.enter_context(nc.semaphore('d%d' % i))
        self.ninst = 0

    def _deps(self, eng, reads, writes):
        deps = {}
        own = ('c', eng)

        def add(k, v):
            if deps.get(k, 0) < v:
                deps[k] = v
        skip_own = (eng == 'pe')
        for r in reads:
            t = self.lastw.get(r)
            if t is not None and not (skip_own and t[0] == own):
                add(*t)
        for w in writes:
            t = self.lastw.get(w)
            if t is not None and not (skip_own and t[0] == own):
                add(*t)
            for k, v in self.readers.get(w, {}).items():
                if not (skip_own and k == own):
                    add(k, v)
        waits = []
        wd = self.waited[eng]
        for k, v in deps.items():
            if wd.get(k, 0) < v:
                wd[k] = v
                waits.append((k, v))
        return waits

    def _commit(self, tok, reads, writes):
        for w in writes:
            self.lastw[w] = tok
            self.readers[w] = {}
        for r in reads:
            d = self.readers.setdefault(r, {})
            if d.get(tok[0], 0) < tok[1]:
                d[tok[0]] = tok[1]

    def op(self, eng, fn, reads=(), writes=(), inc=True):
        writes = list(writes) + [r for r in reads if r.startswith('ps') and r not in writes]
        waits = self._deps(eng, reads, writes)
        tok = (('c', eng), self.count[eng] + 1)
        if inc:
            self.count[eng] += 1
        self.prog[eng].append((waits, fn, (tok[0], 1) if inc else None))
        self._commit(tok, reads, writes)
        self.ninst += 1
        return tok

    def dma(self, eng, fn, reads=(), writes=()):
        waits = self._deps(eng, reads, writes)
        half = len(self.dcount) // 2
        base = 0 if eng == 'pool' else half
        i = base + self.dnext_q[eng]
        self.dnext_q[eng] = (self.dnext_q[eng] + 1) % half
        k = ('d', i)
        wd = self.waited[eng]
        if wd.get(k, 0) < self.dcount[i]:
            wd[k] = self.dcount[i]
            waits.append((k, self.dcount[i]))
        self.dcount[i] += 16
        tok = (k, self.dcount[i])
        self.prog[eng].append((waits, fn, (k, 16)))
        self._commit(tok, reads, writes)
        self.ninst += 1
        return tok

    def coll(self, fn, reads=(), writes=()):
        eng = 'pool'
        waits = self._deps(eng, reads, writes)
        k = ('cc', 0)
        if k not in self.semobj:
            self.semobj[k] = self._stack.enter_context(self.nc.semaphore('cc_sem'))
            self.cccount = 0
        wd = self.waited[eng]
        if wd.get(k, 0) < self.cccount:
            wd[k] = self.cccount
            waits.append((k, self.cccount))
        self.cccount += 1
        tok = (k, self.cccount)
        self.prog[eng].append((waits, fn, (k, 1)))
        self._commit(tok, reads, writes)
        self.ninst += 1
        return tok

    def finish(self, eng='sp'):
        toks = [(('c', e), self.count[e]) for e in self.ENGS if self.count[e]]
        toks += [(('d', i), c) for i, c in enumerate(self.dcount) if c]
        if self.cccount:
            toks.append((('cc', 0), self.cccount))
        waits = []
        wd = self.waited[eng]
        for k, v in toks:
            if wd.get(k, 0) < v:
                wd[k] = v
                waits.append((k, v))
        self.prog[eng].append((waits, None, None))

    def emit(self, block):
        def mk(e):
            def body(engh):
                for waits, fn, inc in self.prog[e]:
                    for k, v in waits:
                        engh.wait_ge(self.semobj[k], v)
                    if fn is not None:
                        if fn[0] == '__call__':
                            ins = fn[1](engh)
                        else:
                            ins = getattr(engh, fn[0])(*fn[1], **fn[2])
                        if inc is not None:
                            ins.then_inc(self.semobj[inc[0]], inc[1])
            return body
        block.tensor(mk('pe'))
        block.scalar(mk('act'))
        block.vector(mk('dve'))
        block.gpsimd(mk('pool'))
        block.sync(mk('sp'))


class Ctx:
    def __init__(self):
        self.nc = bass.Bass("TRN2", target_bir_lowering=False)
        self.st = ExitStack()
        self.em = Emitter(self.nc, self.st)
        self.uid = 0

    def sb(self, name, shape, dt):
        return self.st.enter_context(self.nc.sbuf_tensor("s_" + name, list(shape), dt))

    def ps(self, name, shape, dt=F32):
        return self.st.enter_context(self.nc.psum_tensor("p_" + name, list(shape), dt))

    def din(self, name, shape, dt):
        return self.nc.dram_tensor(name, list(shape), dt, kind="ExternalInput").ap()

    def dout(self, name, shape, dt):
        return self.nc.dram_tensor(name, list(shape), dt, kind="ExternalOutput").ap()

    def done(self):
        self.em.finish('sp')
        with self.nc.Block() as block:
            self.em.emit(block)
        self.st.close()
        return self.nc


def I(name, *a, **kw):
    return (name, a, kw)


def mm_group(em, out_ap, pairs, reads, wres, extra_writes=()):
    n = len(pairs)
    for i, (l, r) in enumerate(pairs):
        em.op('pe', I('matmul', out_ap, lhsT=l, rhs=r, start=(i == 0), stop=(i == n - 1)),
              reads=reads, writes=[wres] + list(extra_writes), inc=(i == n - 1))


class Common:
    def __init__(self, c, need_mods_from_wada=True):
        self.c = c
        em = c.em
        self.ones = c.sb("ones_bf", [128, 128], BF16)
        em.op('dve', I('memset', self.ones[:], 1.0), writes=['ones'])
        self.dummy = c.sb("dummy", [128, 1], F32)
        self.epst = c.sb("epst", [128, 1], F32)
        em.op('dve', I('memset', self.epst[:], EPS), writes=['epst'])
        self.modsT = c.sb("modsT", [128, 72, 2], F32)
        self.normg = c.sb("normgT", [128, 6, 8], F32)
        self.A = c.sb("coefA", [128, 3, 8, 2], F32)
        self.G = c.sb("coefG", [128, 3, 8, 2], F32)

    def load_normg(self, normg_d):
        self.c.em.dma('sp', I('dma_start', out=self.normg[:], in_=normg_d), writes=['normg'])

    def compute_mods(self, cvec_d, wada_d, badaT_d, ps_mods, fb, alloc=None, psname='ps_mods'):
        c, em = self.c, self.c.em
        if alloc is None:
            alloc = lambda name, shape, dt: c.sb(name, shape, dt)[:]
        cv = alloc("cvec", [128, 8, 2], F32)
        scv = alloc("scvec", [128, 8, 2], BF16)
        bad = alloc("badaT", [128, 72], F32)
        em.dma('sp', I('dma_start', out=cv, in_=cvec_d), writes=['cv'])
        em.dma('sp', I('dma_start', out=bad, in_=badaT_d), writes=['bad'])
        em.op('act', I('activation', out=scv, in_=cv, func=AF.Silu), reads=['cv'], writes=['scv'])
        wbuf = [flatview(fb.gT, i * 8 * 512, [128, 8, 512]) for i in range(2)]
        wv = wada_d.rearrange("(k p) n -> p k n", p=128)
        for m in range(18):
            wb = wbuf[m % 2]
            em.dma('pool', I('dma_start', out=wb, in_=wv[:, :, m * 512:(m + 1) * 512]),
                   writes=['wada%d' % (m % 2)])
            for fc in range(4):
                mm_group(em, ps_mods[:, m * 4 + fc, :],
                         [(wb[:, k, fc * 128:(fc + 1) * 128], scv[:, k, :]) for k in range(8)],
                         reads=['wada%d' % (m % 2), 'scv'], wres=psname)
        em.op('dve', I('memset', self.dummy[:], 0.0), writes=['dummy', 'gT', 'wada0', 'wada1'])
        for g in range(2):
            em.op('dve', I('tensor_tensor', out=self.modsT[:, :, g], in0=ps_mods[:, 0:72, g], in1=bad, op=ALU.add),
                  reads=[psname, 'bad'], writes=['modsT'])

    def load_mods(self, modsT_d):
        self.c.em.dma('sp', I('dma_start', out=self.modsT[:], in_=modsT_d), writes=['modsT'])

    def compute_coefs(self):
        em = self.c.em
        for idx, res_w in ((0, 0.5), (1, 1.0), (2, 0.5)):
            for g in range(2):
                em.op('dve', I('scalar_tensor_tensor',
                    out=self.A[:, idx, :, g], in0=self.modsT[:, (3 * idx + 1) * 8:(3 * idx + 2) * 8, g], scalar=1.0,
                    in1=self.normg[:, 2 * idx, :], op0=ALU.add, op1=ALU.mult),
                    reads=['modsT', 'normg'], writes=['coefA'])
                em.op('dve', I('scalar_tensor_tensor',
                    out=self.G[:, idx, :, g], in0=self.modsT[:, (3 * idx + 2) * 8:(3 * idx + 3) * 8, g], scalar=res_w,
                    in1=self.normg[:, 2 * idx + 1, :], op0=ALU.mult, op1=ALU.mult),
                    reads=['modsT', 'normg'], writes=['coefG'])

    def shift(self, idx, k, g):
        return self.modsT[:, 3 * idx * 8 + k, g:g + 1]


class FFNBufs:
    def __init__(self, c, tw, alloc=None, gT=None):
        if alloc is None:
            alloc = lambda name, shape, dt: c.sb(name, shape, dt)[:]
        self.tw = tw
        self.hT = alloc("hT", [128, 8, tw], BF16)
        self.gT = gT if gT is not None else alloc("gT", [128, NFC, tw], BF16)
        assert NFC * tw >= 2 * 8 * 512
        self.w1b = [alloc("w1b%d" % i, [128, 2 * 8 * 256], BF16) for i in range(2)]
        self.w2b = [alloc("w2b%d" % i, [128, NFC, 256], BF16) for i in range(2)]
        self.ysb = alloc("ysb", [128, 8, tw], F32)
        self.sq = alloc("sq", [128, 8, tw], BF16)
        self.tmp = [alloc("tmpf%d" % i, [128, 512], F32) for i in range(2)]
        self.rstd = alloc("rstd", [128, 512], F32)
        self.sa = [alloc("sa%d" % i, [128, 512], BF16) for i in range(2)]
        self.n_tmp = 0


def rms_rstd(c, cm, src_fn, w, ps_ss, sq, rstd, src_reads, tag):
    em = c.em
    for k in range(8):
        em.op('act', I('activation', out=sq[:, k, :w], in_=src_fn(k), func=AF.Square),
              reads=src_reads, writes=['sq'])
    mm_group(em, ps_ss[:, :w], [(cm.ones[:], sq[:, k, :w]) for k in range(8)], reads=['sq', 'ones'], wres=tag)
    em.op('act', I('activation', out=rstd[:, :w], in_=ps_ss[:, :w], func=AF.Sqrt, scale=1.0 / D, bias=cm.epst[:]),
          reads=[tag, 'epst'], writes=['rstd'])
    em.op('dve', I('reciprocal', out=rstd[:, :w], in_=rstd[:, :w]), reads=['rstd'], writes=['rstd'])


def sandwich_in(c, cm, fb, xT, idx, subs, ps_ss, dst, dst_res, ss_res='ps_ss'):
    em = c.em
    for (t0, w, g, off) in subs:
        rms_rstd(c, cm, lambda k: xT[:, k, t0:t0 + w], w, ps_ss, fb.sq, fb.rstd, xres(t0, w), ss_res)
        for k in range(8):
            tmp = fb.tmp[fb.n_tmp % 2]
            tr = 'tmpf%d' % (fb.n_tmp % 2)
            fb.n_tmp += 1
            em.op('dve', I('tensor_tensor', out=tmp[:, :w], in0=xT[:, k, t0:t0 + w], in1=fb.rstd[:, :w], op=ALU.mult),
                  reads=xres(t0, w) + ['rstd'], writes=[tr])
            em.op('act', I('activation', out=dst[:, k, off:off + w], in_=tmp[:, :w], func=AF.Identity,
                                                               scale=cm.A[:, idx, k, g:g + 1], bias=cm.shift(idx, k, g)),
                  reads=[tr, 'coefA', 'modsT'], writes=[dst_res])


def y_evac(c, fb, k, off, w, yp, yres):
    em = c.em
    em.op('dve', I('tensor_copy', out=fb.ysb[:, k, off:off + w], in_=yp), reads=[yres], writes=['ysb'])
    em.op('dve', I('tensor_tensor', out=fb.sq[:, k, off:off + w], in0=fb.ysb[:, k, off:off + w], in1=fb.ysb[:, k, off:off + w], op=ALU.mult),
          reads=['ysb'], writes=['sq'])


def sandwich_out(c, cm, fb, xT, idx, t0, w, g, off, ps_ss, ss_res='ps_ss'):
    em = c.em
    mm_group(em, ps_ss[:, :w], [(cm.ones[:], fb.sq[:, k, off:off + w]) for k in range(8)], reads=['sq', 'ones'], wres=ss_res)
    em.op('act', I('activation', out=fb.rstd[:, :w], in_=ps_ss[:, :w], func=AF.Sqrt, scale=1.0 / D, bias=cm.epst[:]),
          reads=[ss_res, 'epst'], writes=['rstd'])
    em.op('dve', I('reciprocal', out=fb.rstd[:, :w], in_=fb.rstd[:, :w]), reads=['rstd'], writes=['rstd'])
    for k in range(8):
        tmp = fb.tmp[fb.n_tmp % 2]
        tr = 'tmpf%d' % (fb.n_tmp % 2)
        fb.n_tmp += 1
        em.op('dve', I('scalar_tensor_tensor', out=tmp[:, :w], in0=fb.ysb[:, k, off:off + w], scalar=cm.G[:, idx, k, g:g + 1],
                                                                     in1=fb.rstd[:, :w], op0=ALU.mult, op1=ALU.mult),
              reads=['ysb', 'rstd', 'coefG'], writes=[tr])
        em.op('dve', I('tensor_tensor', out=xT[:, k, t0:t0 + w], in0=xT[:, k, t0:t0 + w], in1=tmp[:, :w], op=ALU.add),
              reads=[tr] + xres(t0, w), writes=xres(t0, w))


def ffn(c, cm, fb, xT, idx, tiles, w1_d, w2_d, pss, psnames=None):
    em = c.em
    ps_ss, ps_a, ps_b, ps_y = pss['ss'], pss['a'], pss['b'], pss['y']
    if psnames is None:
        psnames = {'ss': 'ps_ss', 'a': ['ps_a0', 'ps_a1'], 'b': ['ps_b0', 'ps_b1'], 'y': ['ps_y0', 'ps_y1']}
    w1v = w1_d.rearrange("(k p) (two f) -> p k two f", p=128, two=2)
    w2v = w2_d.rearrange("(f p) d -> p f d", p=128)
    nw1 = 0
    nw2 = 0
    nsa = 0
    ny = 0
    for tile in tiles:
        subs = []
        off = 0
        for (t0, w, g) in tile:
            subs.append((t0, w, g, off))
            off += w
        sandwich_in(c, cm, fb, xT, idx, subs, ps_ss, fb.hT, 'hT', ss_res=psnames['ss'])
        for fp in range(NFC // 2):
            wb = fb.w1b[nw1 % 2].rearrange("p (a k f) -> p a k f", a=2, k=8)
            wr = 'w1b%d' % (nw1 % 2)
            nw1 += 1
            for two in range(2):
                em.dma('pool', I('dma_start', out=wb[:, two, :, :], in_=w1v[:, :, two, fp * 256:(fp + 1) * 256]), writes=[wr])
            for fi in range(2):
                fc = fp * 2 + fi
                for (t0, w, g, off) in subs:
                    pa = ps_a[nsa % 2]
                    pb = ps_b[nsa % 2]
                    ra, rb = psnames['a'][nsa % 2], psnames['b'][nsa % 2]
                    sa = fb.sa[nsa % 2]
                    rs = 'sa%d' % (nsa % 2)
                    nsa += 1
                    mm_group(em, pa[:, :w], [(wb[:, 0, k, fi * 128:(fi + 1) * 128], fb.hT[:, k, off:off + w]) for k in range(8)],
                             reads=[wr, 'hT'], wres=ra)
                    mm_group(em, pb[:, :w], [(wb[:, 1, k, fi * 128:(fi + 1) * 128], fb.hT[:, k, off:off + w]) for k in range(8)],
                             reads=[wr, 'hT'], wres=rb)
                    em.op('act', I('activation', out=sa[:, :w], in_=pa[:, :w], func=AF.Silu),
                          reads=[ra], writes=[rs])
                    em.op('dve', I('tensor_tensor',
                        out=fb.gT[:, fc, off:off + w], in0=sa[:, :w], in1=pb[:, :w], op=ALU.mult),
                        reads=[rs, rb], writes=['gT'])
        for piece in range(4):
            wb = fb.w2b[nw2 % 2]
            wr = 'w2b%d' % (nw2 % 2)
            nw2 += 1
            em.dma('pool', I('dma_start', out=wb, in_=w2v[:, :, piece * 256:(piece + 1) * 256]), writes=[wr])
            for (t0, w, g, off) in subs:
                for kk in range(2):
                    k = piece * 2 + kk
                    py = ps_y[ny % 2]
                    ry = psnames['y'][ny % 2]
                    ny += 1
                    mm_group(em, py[:, :w], [(wb[:, f, kk * 128:(kk + 1) * 128], fb.gT[:, f, off:off + w]) for f in range(NFC)],
                             reads=[wr, 'gT'], wres=ry)
                    y_evac(c, fb, k, off, w, py[:, :w], ry)
        for (t0, w, g, off) in subs:
            sandwich_out(c, cm, fb, xT, idx, t0, w, g, off, ps_ss, ss_res=psnames['ss'])


def xres(t0, w):
    return ['x%d' % i for i in range(t0 // 128, (t0 + w + 127) // 128)]


def flatview(ap3, n0, shape):
    flat = ap3.rearrange("p a b -> p (a b)")
    n = 1
    for s in shape[1:]:
        n *= s
    v = flat[:, n0:n0 + n]
    if len(shape) == 2:
        return v
    if len(shape) == 3:
        return v.rearrange("p (a b) -> p a b", a=shape[1])
    return v.rearrange("p (a b c) -> p a b c", a=shape[1], b=shape[2])


def proj_phase(c, cm, fb, xT, tiles, win_d, ropeC_d, ropeS_d, pm_d, pss, qT_o, kvT_o, v_o, alloc=None, psnames=None):
    em = c.em
    ps_ss, ps_a, ps_b, ps_y = pss['ss'], pss['a'], pss['b'], pss['y']
    tw = fb.tw
    winv = win_d.rearrange("(k p) n -> p k n", p=128)
    if alloc is None:
        alloc = lambda name, shape, dt: c.sb(name, shape, dt)[:]
    if psnames is None:
        psnames = {'ss': 'ps_ss', 'a': ['ps_a0', 'ps_a1'], 'b': ['ps_b0', 'ps_b1'], 'y': ['ps_y0', 'ps_y1']}
    pmat = alloc("pmat", [128, 128], BF16)
    em.dma('sp', I('dma_start', out=pmat, in_=pm_d), writes=['pmat'])
    ct = [alloc("ropec%d" % i, [128, 512], F32) for i in range(2)]
    sn = [alloc("ropes%d" % i, [128, 512], F32) for i in range(2)]
    qb = [alloc("qb%d" % i, [128, 512], BF16) for i in range(2)]
    t2 = [alloc("t2_%d" % i, [128, 512], F32) for i in range(2)]
    qst = flatview(fb.gT, 0, [128, 6, tw])
    kvst = flatview(fb.gT, 6 * tw, [128, 8, tw])
    vst = flatview(fb.gT, 14 * tw, [128, tw // 128, 768])
    cnt = {'w': 0, 'p': 0, 'r': 0, 't': 0}

    def next_ps():
        i = cnt['p'] % 4
        cnt['p'] += 1
        return ([ps_a[0], ps_a[1], ps_b[0], ps_b[1]][i], (psnames['a'] + psnames['b'])[i])

    for tile in tiles:
        subs = []
        off = 0
        for (t0, w, g) in tile:
            subs.append((t0, w, g, off))
            off += w
        tww = off
        sandwich_in(c, cm, fb, xT, 1, subs, ps_ss, fb.hT, 'hT', ss_res=psnames['ss'])
        for piece in range(5):
            wb4 = fb.w1b[cnt['w'] % 2]
            wr = 'w1b%d' % (cnt['w'] % 2)
            cnt['w'] += 1
            wb = wb4.rearrange("p (k n) -> p k n", k=8)
            em.dma('pool', I('dma_start', out=wb, in_=winv[:, :, piece * 512:(piece + 1) * 512]), writes=[wr])
            for (t0, w, g, off) in subs:
                if piece == 0:
                    fm = [(0, 'q', 0, 0.125, False), (1, 'q', 1, 0.125, False), (2, 'kv', 0, 1.0, False), (3, 'kv', 1, 1.0, False)]
                elif piece == 1:
                    fm = [(2, 'kv', 2, 1.0, False), (3, 'kv', 3, 1.0, False)]
                elif piece == 2:
                    fm = [(i, 'q', 2 + i, 0.125, True) for i in range(4)]
                elif piece == 3:
                    fm = [(i, 'kv', 4 + i, 1.0, True) for i in range(4)]
                else:
                    fm = []
                if piece in (2, 3):
                    ci = cnt['r'] % 2
                    cnt['r'] += 1
                    em.dma('sp', I('dma_start', out=ct[ci][:, :w], in_=ropeC_d[:, t0:t0 + w]), writes=['ropec%d' % ci])
                    em.dma('sp', I('dma_start', out=sn[ci][:, :w], in_=ropeS_d[:, t0:t0 + w]), writes=['ropes%d' % ci])
                for (lc, kind, oc, scale, rope) in fm:
                    pp, pr = next_ps()
                    mm_group(em, pp[:, :w], [(wb[:, k, lc * 128:(lc + 1) * 128], fb.hT[:, k, off:off + w]) for k in range(8)],
                             reads=[wr, 'hT'], wres=pr)
                    dst = (qst if kind == 'q' else kvst)[:, oc, off:off + w]
                    if not rope:
                        em.op('act', I('activation', out=dst, in_=pp[:, :w], func=AF.Copy, scale=scale),
                              reads=[pr], writes=['gT'])
                    else:
                        ti = cnt['t'] % 2
                        cnt['t'] += 1
                        em.op('act', I('activation', out=qb[ti][:, :w], in_=pp[:, :w], func=AF.Copy, scale=scale),
                              reads=[pr], writes=['qb%d' % ti])
                        py = ps_y[ti]
                        pyr = psnames['y'][ti]
                        mm_group(em, py[:, :w], [(pmat, qb[ti][:, :w])], reads=['pmat', 'qb%d' % ti], wres=pyr)
                        tmp = fb.tmp[ti]
                        em.op('dve', I('scalar_tensor_tensor',
                            out=tmp[:, :w], in0=pp[:, :w], scalar=scale, in1=ct[ci][:, :w], op0=ALU.mult, op1=ALU.mult),
                            reads=[pr, 'ropec%d' % ci], writes=['tmpf%d' % ti])
                        em.op('dve', I('tensor_tensor', out=t2[ti][:, :w], in0=py[:, :w], in1=sn[ci][:, :w], op=ALU.mult),
                              reads=[pyr, 'ropes%d' % ci], writes=['t2_%d' % ti])
                        em.op('dve', I('tensor_tensor', out=dst, in0=tmp[:, :w], in1=t2[ti][:, :w], op=ALU.add),
                              reads=['tmpf%d' % ti, 't2_%d' % ti], writes=['gT'])
                if piece in (1, 4):
                    c0, ncol, vo = (0, 256, 0) if piece == 1 else (0, 512, 256)
                    for tcn in range(w // 128):
                        pp, pr = next_ps()
                        tk = off + tcn * 128
                        mm_group(em, pp[:, :ncol], [(fb.hT[:, k, tk:tk + 128], wb[:, k, c0:c0 + ncol]) for k in range(8)],
                                 reads=[wr, 'hT'], wres=pr)
                        em.op('act', I('activation', out=vst[:, tk // 128, vo:vo + ncol], in_=pp[:, :ncol], func=AF.Copy),
                              reads=[pr], writes=['gT'])
        tile0 = tile[0][0]
        em.dma('sp', I('dma_start', out=qT_o[:, :, tile0:tile0 + tww], in_=qst[:, :, :tww]), reads=['gT'], writes=['qT_o'])
        em.dma('sp', I('dma_start', out=kvT_o[:, :, tile0:tile0 + tww], in_=kvst[:, :, :tww]), reads=['gT'], writes=['kvT_o'])
        em.dma('sp', I('dma_start',
            out=v_o[tile0:tile0 + tww, :].rearrange("(c p) n -> p c n", p=128), in_=vst[:, :tww // 128, :]), reads=['gT'], writes=['v_o'])


def make_tiles(n_lat, n_ctx, tw):
    subs = []
    t = 0
    while t < n_lat:
        w = min(512, n_lat - t)
        subs.append((t, w, 0))
        t += w
    t = 0
    while t < n_ctx:
        w = min(512, n_ctx - t)
        subs.append((n_lat + t, w, 1))
        t += w
    tiles = []
    cur = []
    room = tw
    for (t0, w, g) in subs:
        while w > 0:
            take = min(w, room)
            cur.append((t0, take, g))
            t0 += take
            w -= take
            room -= take
            if room == 0:
                tiles.append(cur)
                cur = []
                room = tw
    if cur:
        tiles.append(cur)
    return tiles


def build_part_a(n_lat=2048, n_ctx=256, tw=768, upto=3):
    NT = n_lat + n_ctx
    c = Ctx()
    em = c.em
    xT_d = c.din("xT", [128, 8, NT], F32)
    cvec_d = c.din("cvec", [128, 8, 2], F32)
    wada_d = c.din("w_ada", [D, 9 * D], F32)
    bada_d = c.din("badaT", [128, 72], F32)
    normg_d = c.din("normgT", [128, 6, 8], F32)
    w1_d = c.din("w1", [D, 2 * DFF], F32)
    w2_d = c.din("w2", [DFF, D], F32)
    win_d = c.din("w_in", [D, 2560], F32)
    ropeC_d = c.din("ropeC", [128, NT], F32)
    ropeS_d = c.din("ropeS", [128, NT], F32)
    pm_d = c.din("pmat", [128, 128], BF16)
    x1T_o = c.dout("x1T", [128, 8, NT], F32)
    mods_o = c.dout("modsT", [128, 72, 2], F32)
    qT_o = c.dout("qT", [128, 6, NT], BF16)
    kvT_o = c.dout("kvT", [128, 8, NT], BF16)
    v_o = c.dout("vtok", [NT, 768], BF16)

    xT = c.sb("xT_sb", [128, 8, NT], F32)
    cm = Common(c)
    fb = FFNBufs(c, tw)
    pss = {'ss': c.ps("ps_ss", [128, 512]), 'a': [c.ps("ps_a%d" % i, [128, 512]) for i in range(2)],
           'b': [c.ps("ps_b%d" % i, [128, 512]) for i in range(2)], 'y': [c.ps("ps_y%d" % i, [128, 512]) for i in range(2)]}
    ps_mods = c.ps("ps_mods", [128, 256, 2])
    tiles = make_tiles(n_lat, n_ctx, tw)
    for t in range(0, NT, 512):
        w = min(512, NT - t)
        em.dma('sp', I('dma_start', out=xT[:, :, t:t + w], in_=xT_d[:, :, t:t + w]), writes=xres(t, w))
    cm.load_normg(normg_d)
    cm.compute_mods(cvec_d, wada_d, bada_d, ps_mods, fb)
    cm.compute_coefs()
    em.dma('sp', I('dma_start', out=mods_o, in_=cm.modsT[:]), reads=['modsT'], writes=['mods_o'])
    if upto >= 2:
        ffn(c, cm, fb, xT, 0, tiles, w1_d, w2_d, pss)
    em.dma('sp', I('dma_start', out=x1T_o, in_=xT[:]), reads=xres(0, NT), writes=['x1T_o'])
    if upto >= 3:
        proj_phase(c, cm, fb, xT, tiles, win_d, ropeC_d, ropeS_d, pm_d, pss, qT_o, kvT_o, v_o)
    print("part A instructions:", em.ninst)
    return c.done()


def rope_tables(pos):
    pos = np.asarray(pos)
    row = (pos // GRID_W).astype(np.float32)
    col = (pos % GRID_W).astype(np.float32)
    n_freq = 16
    inv_freq = np.power(np.float32(10000.0), -np.arange(n_freq, dtype=np.float32) / np.float32(n_freq)).astype(np.float32)
    ang = np.concatenate([row[:, None] * inv_freq, col[:, None] * inv_freq], axis=-1).astype(np.float32)
    return np.cos(ang).astype(np.float32), np.sin(ang).astype(np.float32)


def rope_feature_major(cos, sin, n_ctx):
    n = cos.shape[0]
    C = np.ones((128, n + n_ctx), np.float32)
    S = np.zeros((128, n + n_ctx), np.float32)
    p = np.arange(128)
    C[:, :n] = cos.T[p % 32]
    sign = np.where((p % 64) < 32, -1.0, 1.0).astype(np.float32)
    S[:, :n] = sin.T[p % 32] * sign[:, None]
    return C, S


def rope_pmat():
    pm = np.zeros((128, 128), np.float32)
    for po in range(128):
        pi = po + 32 if (po % 64) < 32 else po - 32
        pm[pi, po] = 1.0
    return pm.astype(NPBF)


class Arena:
    def __init__(self, c, nelem):
        self.c = c
        self.t = c.sb("arena", [128, nelem], BF16)
        self.n = nelem
        self.off = 0
        self.gen = 0

    def alloc(self, name, shape, dt):
        n = 1
        for s in shape[1:]:
            n *= s
        if dt == F32:
            n *= 2
        self.off = (self.off + 15) // 16 * 16
        assert self.off + n <= self.n, "arena overflow %s: need %d have %d" % (name, self.off + n, self.n)
        v = self.t[:, self.off:self.off + n]
        self.off += n
        if dt == F32:
            v = v.bitcast(F32)
        if len(shape) == 3:
            v = v.rearrange("p (a b) -> p a b", a=shape[1])
        elif len(shape) == 4:
            v = v.rearrange("p (a b c) -> p a b c", a=shape[1], b=shape[2])
        return v

    def reset(self, to_zero=False):
        self.c.em.barrier()
        if to_zero:
            self.base = 0
        self.off = getattr(self, 'base', 0)
        self.gen += 1

    def set_base(self):
        self.base = self.off


def _barrier(self):
    toks = [(('c', e), self.count[e]) for e in self.ENGS if self.count[e]]
    toks += [(('d', i), cc) for i, cc in enumerate(self.dcount) if cc]
    if self.cccount:
        toks.append((('cc', 0), self.cccount))
    for eng in self.ENGS:
        waits = []
        wd = self.waited[eng]
        for k, v in toks:
            if wd.get(k, 0) < v:
                wd[k] = v
                waits.append((k, v))
        if waits:
            self.prog[eng].append((waits, None, None))


Emitter.barrier = _barrier


def na_attention(c, ar, P, mixT, q_d, kT_src, v_src, n_kc_tot, qchunks, bias_d, ident, col0, shiftt):
    em = c.em
    NK = n_kc_tot * 128
    kT = ar.alloc("na_kT", [128, 2, NK], BF16)
    vv = ar.alloc("na_v", [128, n_kc_tot, 4, 65], BF16)
    nq = len(qchunks)
    qT = ar.alloc("na_q", [128, 2, nq * 128], BF16)
    g = ar.gen
    rk, rv, rq = 'na_kT%d' % g, 'na_v%d' % g, 'na_q%d' % g
    em.dma('sp', I('dma_start', out=kT, in_=kT_src), writes=[rk])
    em.op('dve', I('memset', vv[:, :, :, 64:65], 1.0), writes=[rv])
    for h in range(4):
        em.dma('sp', I('dma_start', out=vv[:, :, h, 0:64], in_=v_src[:, h * 64:(h + 1) * 64].rearrange("(c p) e -> p c e", p=128)), writes=[rv])
    q0 = qchunks[0][0]
    em.dma('sp', I('dma_start', out=qT, in_=q_d[:, 0:2, q0:q0 + nq * 128]), writes=[rq])
    bias = [ar.alloc("na_bias%d" % i, [128, 4, 6, 128], F32) for i in range(2)]
    ssb = [ar.alloc("na_s%d" % i, [128, 6, 128], F32) for i in range(2)]
    pT = [ar.alloc("na_p%d" % i, [128, 8, 128], BF16) for i in range(2)]
    atok = ar.alloc("na_atok", [128, 256], BF16)
    rec = ar.alloc("na_rec", [128, 4], F32)
    cnt = 0
    for qi, (qcol, kcs, bvar, nb) in enumerate(qchunks):
        bi = qi % 2
        if bvar is not None:
            for h in range(4):
                em.dma('sp', I('dma_start', out=bias[bi][:, h, :, :], in_=bias_d[bvar, h].rearrange("j k q -> k j q")), writes=['na_bias%d_%d' % (bi, g)])
        nk = len(kcs)
        for h in range(4):
            hp = (h % 2) * 64
            hc = h // 2
            si = cnt % 2
            cnt += 1
            SX, SY = P['S'][si][:, 0, :].rearrange("p (j q) -> p j q", j=4), P['S'][si][:, 1, :].rearrange("p (j q) -> p j q", j=4)
            rsx, rsy = 'psS%d_0' % si, 'psS%d_1' % si
            qap = qT[hp:hp + 64, hc, qi * 128:(qi + 1) * 128]

            def sdst(jj):
                return (SX[:, jj, :], rsx) if jj < 4 else (SY[:, jj - 4, :], rsy)
            for jj, kc in enumerate(kcs):
                d, r = sdst(jj)
                mm_group(em, d, [(kT[hp:hp + 64, hc, kc * 128:(kc + 1) * 128], qap)], reads=[rk, rq], wres=r)
            sb_ = ssb[si]
            pt = pT[si]
            rs_, rp_ = 'na_s%d_%d' % (si, g), 'na_p%d_%d' % (si, g)
            if nb > 0:
                n1 = min(nb, 4)
                em.op('dve', I('tensor_tensor', out=sb_[:, 0:n1, :], in0=SX[:, 0:n1, :], in1=bias[bi][:, h, 0:n1, :], op=ALU.add),
                      reads=[rsx, 'na_bias%d_%d' % (bi, g)], writes=[rs_])
                if nb > 4:
                    em.op('dve', I('tensor_tensor', out=sb_[:, 4:nb, :], in0=SY[:, 0:nb - 4, :], in1=bias[bi][:, h, 4:nb, :], op=ALU.add),
                          reads=[rsy, 'na_bias%d_%d' % (bi, g)], writes=[rs_])
                em.op('act', I('activation', out=pt[:, 0:nb, :], in_=sb_[:, 0:nb, :], func=AF.Exp, bias=shiftt[:]), reads=[rs_, 'shiftt'], writes=[rp_])
            j = nb
            while j < nk:
                if j < 4:
                    e_ = min(nk, 4)
                    em.op('act', I('activation', out=pt[:, j:e_, :], in_=SX[:, j:e_, :], func=AF.Exp, bias=shiftt[:]), reads=[rsx, 'shiftt'], writes=[rp_])
                else:
                    e_ = nk
                    em.op('act', I('activation', out=pt[:, j:e_, :], in_=SY[:, j - 4:e_ - 4, :], func=AF.Exp, bias=shiftt[:]), reads=[rsy, 'shiftt'], writes=[rp_])
                j = e_
            O = P['O'][:, 0, 0:4 * 65].rearrange("p (h e) -> p h e", h=4)
            mm_group(em, O[:, h, :], [(pt[:, jj, :], vv[:, kc, h, :]) for jj, kc in enumerate(kcs)], reads=[rp_, rv], wres='psU1_0')
        em.op('dve', I('reciprocal', out=rec[:, :], in_=O[:, :, 64]), reads=['psU1_0'], writes=['na_rec%d' % g])
        em.op('dve', I('tensor_tensor', out=atok[:, :].rearrange("p (h e) -> p h e", h=4), in0=O[:, :, 0:64],
                       in1=rec[:, :].unsqueeze(2).to_broadcast([128, 4, 64]), op=ALU.mult),
              reads=['psU1_0', 'na_rec%d' % g], writes=['na_atok%d' % g])
        T = P['T']
        for ch in range(2):
            em.op('pe', I('transpose', out=T[:, ch * 128:(ch + 1) * 128], in_=atok[:, ch * 128:(ch + 1) * 128], identity=ident[:]),
                  reads=['na_atok%d' % g, 'ident'], writes=['psU2_0'])
        col = col0 + qi * 128
        em.op('act', I('activation', out=mixT[:, 0:2, col:col + 128], in_=T[:, 0:256].rearrange("p (c q) -> p c q", c=2), func=AF.Copy),
              reads=['psU2_0'], writes=['mix%d' % (col // 128)])


def pool_mixer(c, ar, P, mixT, pin_src, rcnt_src, n_tok, pwbd, pscale, col0):
    em = c.em
    g = ar.gen
    W = 512
    ubuf = [ar.alloc("pl_u%d" % i, [128, 2, W + 16], BF16) for i in range(2)]
    rc = [ar.alloc("pl_rc%d" % i, [128, 2, W], F32) for i in range(2)]
    s2 = ar.alloc("pl_s2", [128, 2, W + 16], F32)
    s4 = ar.alloc("pl_s4", [128, 2, W + 16], F32)
    s8 = ar.alloc("pl_s8", [128, 2, W + 16], F32)
    s16 = ar.alloc("pl_s16", [128, 2, W + 16], F32)
    pm = ar.alloc("pl_pm", [128, 2, W], F32)
    pb = ar.alloc("pl_pb", [128, 2, W], BF16)
    it = 0
    for t0 in range(0, n_tok, W):
        w = min(W, n_tok - t0)
        u = ubuf[it % 2]
        r = rc[it % 2]
        ru, rr = 'pl_u%d_%d' % (it % 2, g), 'pl_rc%d_%d' % (it % 2, g)
        it += 1
        em.dma('sp', I('dma_start', out=u[:, :, 0:w + 16], in_=pin_src[:, :, t0:t0 + w + 16]), writes=[ru])
        em.dma('sp', I('dma_start', out=r[:, :, 0:w], in_=rcnt_src[:, :, t0:t0 + w]), writes=[rr])
        L = w + 16
        em.op('dve', I('tensor_tensor', out=s2[:, :, 1:L], in0=u[:, :, 0:L - 1], in1=u[:, :, 1:L], op=ALU.add), reads=[ru], writes=['pl_s2_%d' % g])
        em.op('dve', I('tensor_tensor', out=s4[:, :, 2:L - 1], in0=s2[:, :, 1:L - 2], in1=s2[:, :, 3:L], op=ALU.add), reads=['pl_s2_%d' % g], writes=['pl_s4_%d' % g])
        em.op('dve', I('tensor_tensor', out=s8[:, :, 4:L - 3], in0=s4[:, :, 2:L - 5], in1=s4[:, :, 6:L - 1], op=ALU.add), reads=['pl_s4_%d' % g], writes=['pl_s8_%d' % g])
        em.op('dve', I('tensor_tensor', out=s16[:, :, 8:L - 7], in0=s8[:, :, 4:L - 11], in1=s8[:, :, 12:L - 3], op=ALU.add), reads=['pl_s8_%d' % g], writes=['pl_s16_%d' % g])
        for (ch, p0, lvl, lr) in ((0, 0, s2, 'pl_s2_%d' % g), (0, 64, s4, 'pl_s4_%d' % g), (1, 0, s8, 'pl_s8_%d' % g), (1, 64, s16, 'pl_s16_%d' % g)):
            em.op('dve', I('tensor_tensor', out=pm[p0:p0 + 64, ch, 0:w], in0=lvl[p0:p0 + 64, ch, 8:8 + w], in1=r[p0:p0 + 64, ch, 0:w], op=ALU.mult),
                  reads=[lr, rr], writes=['pl_pm_%d' % g])
            em.op('dve', I('tensor_tensor', out=pb[p0:p0 + 64, ch, 0:w], in0=pm[p0:p0 + 64, ch, 0:w], in1=u[p0:p0 + 64, ch, 8:8 + w], op=ALU.subtract),
                  reads=['pl_pm_%d' % g, ru], writes=['pl_pb_%d' % g])
        for ch in range(2):
            pp = P['S'][ch][:, 0, :]
            pr = 'psS%d_0' % ch
            mm_group(em, pp[:, :w], [(pwbd[:, ch, :], pb[:, ch, 0:w])], reads=['pl_pb_%d' % g, 'pwbd'], wres=pr)
            col = col0 + t0
            em.op('act', I('activation', out=mixT[:, 2 + ch, col:col + w], in_=pp[:, :w], func=AF.Copy, scale=pscale[:, ch:ch + 1]),
                  reads=[pr, 'pscale'], writes=['mix%d' % i for i in range(col // 128, (col + w) // 128)])


def diff_attention(c, ar, P, mixT, q_d, qtiles, kT_src, v_src, n_kc, lamt, gsub, ident, shiftt, epst, heads=range(4), loaders=None):
    em = c.em
    g = ar.gen
    NK = n_kc * 128
    kT = [ar.alloc("df_kT%d" % i, [128, NK], BF16) for i in range(2)]
    vv = [ar.alloc("df_v%d" % i, [128, n_kc, 129], BF16) for i in range(2)]
    qmax = max(w for _, w in qtiles)
    qb = [ar.alloc("df_q%d" % i, [128, qmax], BF16) for i in range(2)]
    p12 = [ar.alloc("df_p%d" % i, [128, 2, 512], BF16) for i in range(2)]
    rr = ar.alloc("df_r", [128, 2, 4], F32)
    tt = ar.alloc("df_t", [128, 4, 128], F32)
    oo = ar.alloc("df_o", [128, 4, 128], F32)
    osq = ar.alloc("df_osq", [128, 4, 128], F32)
    ss = ar.alloc("df_ss", [128, 4], F32)
    cb = ar.alloc("df_cb", [128, 4, 128], BF16)
    for i in range(2):
        em.op('dve', I('memset', vv[i][:, :, 128:129], 1.0), writes=['df_v%d_%d' % (i, g)])
    nq = 0
    ns = 0
    for hi, h in enumerate(heads):
        kb, vb = kT[hi % 2], vv[hi % 2]
        rk, rv = 'df_kT%d_%d' % (hi % 2, g), 'df_v%d_%d' % (hi % 2, g)
        if loaders is not None:
            loaders(h, kb, vb, rk, rv)
        else:
            em.dma('sp', I('dma_start', out=kb, in_=kT_src[:, h, :]), writes=[rk])
            step = 16
            for c0 in range(0, n_kc, step):
                c1 = min(n_kc, c0 + step)
                em.dma('sp', I('dma_start', out=vb[:, c0:c1, 0:128],
                               in_=v_src[c0 * 128:c1 * 128, h * 128:(h + 1) * 128].rearrange("(c p) e -> p c e", p=128)), writes=[rv])
        for (t0, w) in qtiles:
            qq = qb[nq % 2]
            rq = 'df_q%d_%d' % (nq % 2, g)
            nq += 1
            em.dma('sp', I('dma_start', out=qq[:, :w], in_=q_d[:, 2 + h, t0:t0 + w]), writes=[rq])
            nqc = w // 128
            U1 = P['U1'][:].rearrange("p a (c e) -> p (a c) e", c=2)
            U2 = P['U2'][:].rearrange("p a (c e) -> p (a c) e", c=2)
            def issue_S(kc, slot):
                S = P['S'][slot]
                rs = ['psS%d_0' % slot, 'psS%d_1' % slot]
                for half in range(2):
                    hp = half * 64
                    mm_group(em, S[:, half, :w], [(kb[hp:hp + 64, kc * 128:(kc + 1) * 128], qq[hp:hp + 64, :w])], reads=[rk, rq], wres=rs[half])

            def issue_exp_pv(kc, slot):
                S = P['S'][slot]
                rs = ['psS%d_0' % slot, 'psS%d_1' % slot]
                pt = p12[slot]
                rp = 'df_p%d_%d' % (slot, g)
                em.op('act', I('activation', out=pt[:, :, :w], in_=S[:, :, :w], func=AF.Exp, bias=shiftt[:]), reads=rs + ['shiftt'], writes=[rp])
                for half, (U, ru) in enumerate(((U1, 'psU1'), (U2, 'psU2'))):
                    for qc in range(nqc):
                        em.op('pe', I('matmul', U[:, qc, 0:129], lhsT=pt[:, half, qc * 128:(qc + 1) * 128], rhs=vb[:, kc, :],
                                      start=(kc == 0 and qc % 2 == 0), stop=(kc == n_kc - 1), skip_group_check=True),
                              reads=[rp, rv], writes=['%s_%d' % (ru, qc // 2)], inc=(qc == nqc - 1))
            issue_S(0, ns % 2)
            for kc in range(n_kc):
                slot = ns % 2
                ns += 1
                if kc + 1 < n_kc:
                    issue_S(kc + 1, ns % 2)
                issue_exp_pv(kc, slot)
            ru1 = ['psU1_0', 'psU1_1'][:(nqc + 1) // 2]
            ru2 = ['psU2_0', 'psU2_1'][:(nqc + 1) // 2]
            rg = 'df_ep%d' % g
            em.op('dve', I('reciprocal', out=rr[:, 0, :nqc], in_=U1[:, :nqc, 128]), reads=ru1, writes=[rg])
            em.op('dve', I('reciprocal', out=rr[:, 1, :nqc], in_=U2[:, :nqc, 128]), reads=ru2, writes=[rg])
            em.op('dve', I('tensor_scalar', out=rr[:, 1, :nqc], in0=rr[:, 1, :nqc], scalar1=lamt[:, 0:1], scalar2=None, op0=ALU.mult), reads=[rg, 'lamt'], writes=[rg])
            em.op('dve', I('tensor_tensor', out=tt[:, :nqc, :], in0=U2[:, :nqc, 0:128], in1=rr[:, 1, :nqc].unsqueeze(2).to_broadcast([128, nqc, 128]), op=ALU.mult),
                  reads=ru2 + [rg], writes=[rg])
            em.op('dve', I('tensor_tensor', out=oo[:, :nqc, :], in0=U1[:, :nqc, 0:128], in1=rr[:, 0, :nqc].unsqueeze(2).to_broadcast([128, nqc, 128]), op=ALU.mult),
                  reads=ru1 + [rg], writes=[rg])
            em.op('dve', I('tensor_tensor', out=oo[:, :nqc, :], in0=oo[:, :nqc, :], in1=tt[:, :nqc, :], op=ALU.subtract), reads=[rg], writes=[rg])
            em.op('dve', I('tensor_tensor', out=osq[:, :nqc, :], in0=oo[:, :nqc, :], in1=oo[:, :nqc, :], op=ALU.mult), reads=[rg], writes=[rg])
            em.op('dve', I('reduce_sum', out=ss[:, :nqc], in_=osq[:, :nqc, :], axis=AX.X), reads=[rg], writes=[rg])
            em.op('act', I('activation', out=ss[:, :nqc], in_=ss[:, :nqc], func=AF.Sqrt, scale=1.0 / 128, bias=epst[:]), reads=[rg, 'epst'], writes=[rg])
            em.op('dve', I('reciprocal', out=ss[:, :nqc], in_=ss[:, :nqc]), reads=[rg], writes=[rg])
            em.op('dve', I('tensor_tensor', out=oo[:, :nqc, :], in0=oo[:, :nqc, :], in1=ss[:, :nqc].unsqueeze(2).to_broadcast([128, nqc, 128]), op=ALU.mult),
                  reads=[rg], writes=[rg])
            em.op('dve', I('tensor_tensor', out=cb[:, :nqc, :], in0=oo[:, :nqc, :], in1=gsub[:, :].unsqueeze(1).to_broadcast([128, nqc, 128]), op=ALU.mult),
                  reads=[rg, 'gsub'], writes=['df_cb%d' % g])
            T = P['T']
            for qc in range(nqc):
                em.op('pe', I('transpose', out=T[:, qc * 128:(qc + 1) * 128], in_=cb[:, qc, :], identity=ident[:]),
                      reads=['df_cb%d' % g, 'ident'], writes=['psU2_0'])
            em.op('act', I('activation', out=mixT[:, 4 + h, t0:t0 + w], in_=T[:, 0:w], func=AF.Copy),
                  reads=['psU2_0'], writes=['mix%d' % i for i in range(t0 // 128, (t0 + w) // 128)])


def na_local_chunks(i, nq):
    if i == 0:
        return 0, 6
    if i == nq - 1:
        return i - 1, 6
    return i, 5


def build_part_b(n_lat=2048, n_ctx=256, tw=768, n_kc_diff=66, na_variants=None, lam_init=0.2, ctx_out=True, n_halo_kc=None):
    NT = n_lat + n_ctx
    nq = n_lat // 128
    if n_halo_kc is None:
        n_halo_kc = nq + 4
    if na_variants is None:
        na_variants = [0] * nq
    c = Ctx()
    em = c.em
    x1T_d = c.din("x1T", [128, 8, NT], F32)
    mods_d = c.din("modsT", [128, 72, 2], F32)
    normg_d = c.din("normgT", [128, 6, 8], F32)
    q_d = c.din("qT", [128, 6, NT], BF16)
    nakT_d = c.din("na_kT", [128, 2, (n_halo_kc + n_ctx // 128) * 128], BF16)
    nav_d = c.din("na_v", [(n_halo_kc + n_ctx // 128) * 128, 256], BF16)
    nakTc_d = c.din("na_kTc", [128, 2, n_ctx], BF16)
    navc_d = c.din("na_vc", [n_ctx, 256], BF16)
    nvar = max(na_variants) + 1
    bias_d = c.din("na_bias", [nvar, 4, 6, 128, 128], F32)
    pin_d = c.din("pinT", [128, 2, n_lat + 16], BF16)
    rcnt_d = c.din("rcnt", [128, 2, n_lat], F32)
    pinc_d = c.din("pinTc", [128, 2, n_ctx + 16], BF16)
    rcntc_d = c.din("rcntc", [128, 2, n_ctx], F32)
    pwbd_d = c.din("pwbd", [128, 2, 128], BF16)
    pscale_d = c.din("pscaleT", [128, 2], F32)
    dkT_d = c.din("dkT", [128, 4, n_kc_diff * 128], BF16)
    dv_d = c.din("dv", [n_kc_diff * 128, 512], BF16)
    dlam_d = c.din("dlam", [128, 256], F32)
    subg_d = c.din("subg", [128, 128], F32)
    ident_d = c.din("ident", [128, 128], BF16)
    wout_d = c.din("w_out", [D, D], F32)
    w1_d = c.din("w1", [D, 2 * DFF], F32)
    w2_d = c.din("w2", [DFF, D], F32)
    x2T_o = c.dout("x2T", [128, 8, NT], F32)

    xT = c.sb("xT_sb", [128, 8, NT], F32)
    mixT_t = c.sb("mixT", [128, 8, NT], BF16)
    mixT = mixT_t[:]
    cm = Common(c)
    ident = c.sb("ident", [128, 128], BF16)
    pwbd = c.sb("pwbd", [128, 2, 128], BF16)
    pscale = c.sb("pscale", [128, 2], F32)
    dlam = c.sb("dlam", [128, 256], F32)
    lamw = c.sb("lamw", [128, 4], F32)
    lamt = c.sb("lamt", [128, 1], F32)
    gsub = c.sb("gsub", [128, 128], F32)
    shiftt = c.sb("shiftt", [128, 1], F32)
    S0 = c.ps("S0", [128, 2, 512])
    S1 = c.ps("S1", [128, 2, 512])
    U1 = c.ps("U1", [128, 2, 512])
    U2 = c.ps("U2", [128, 2, 512])
    P = {'S': [S0, S1], 'U1': U1, 'U2': U2, 'O': U1, 'T': U2[:, 0, :].bitcast(BF16)}
    arena_n = (c.nc.sbuf_bytes_remaining - 2048) // 2 // 16 * 16
    ar = Arena(c, arena_n)
    print("arena elems", arena_n)

    for t in range(0, NT, 512):
        w = min(512, NT - t)
        em.dma('sp', I('dma_start', out=xT[:, :, t:t + w], in_=x1T_d[:, :, t:t + w]), writes=xres(t, w))
    cm.load_normg(normg_d)
    cm.load_mods(mods_d)
    cm.compute_coefs()
    em.dma('sp', I('dma_start', out=ident[:], in_=ident_d), writes=['ident'])
    em.dma('sp', I('dma_start', out=pwbd[:], in_=pwbd_d), writes=['pwbd'])
    em.dma('sp', I('dma_start', out=pscale[:], in_=pscale_d), writes=['pscale'])
    em.dma('sp', I('dma_start', out=dlam[:], in_=dlam_d), writes=['dlam'])
    em.dma('sp', I('dma_start', out=gsub[:], in_=subg_d), writes=['gsub'])
    em.op('dve', I('memset', shiftt[:], EXP_SHIFT), writes=['shiftt'])
    em.op('dve', I('tensor_scalar', out=gsub[:], in0=gsub[:], scalar1=float(1.0 - lam_init), scalar2=None, op0=ALU.mult), reads=['gsub'], writes=['gsub'])
    em.op('dve', I('tensor_tensor', out=dlam[:, 0:64], in0=dlam[:, 0:64], in1=dlam[:, 64:128], op=ALU.mult), reads=['dlam'], writes=['dlam'])
    em.op('dve', I('tensor_tensor', out=dlam[:, 128:192], in0=dlam[:, 128:192], in1=dlam[:, 192:256], op=ALU.mult), reads=['dlam'], writes=['dlam'])
    em.op('dve', I('reduce_sum', out=lamw[:, 0:2], in_=dlam[:].rearrange("p (a b) -> p a b", a=2)[:, :, 0:64], axis=AX.X), reads=['dlam'], writes=['lamw'])
    em.op('act', I('activation', out=lamw[:, 2:4], in_=lamw[:, 0:2], func=AF.Exp), reads=['lamw'], writes=['lamw'])
    em.op('dve', I('tensor_tensor', out=lamt[:], in0=lamw[:, 2:3], in1=lamw[:, 3:4], op=ALU.subtract), reads=['lamw'], writes=['lamt'])
    em.op('dve', I('tensor_scalar', out=lamt[:], in0=lamt[:], scalar1=float(lam_init), scalar2=None, op0=ALU.add), reads=['lamt'], writes=['lamt'])

    nkc_tot = n_halo_kc + n_ctx // 128
    ctx_kcs = [n_halo_kc + i for i in range(n_ctx // 128)]
    qch = []
    for i in range(nq):
        k0, nl = na_local_chunks(i, nq)
        qch.append((i * 128, [k0 + j for j in range(nl)] + ctx_kcs, na_variants[i], nl))
    na_attention(c, ar, P, mixT, q_d, nakT_d, nav_d, nkc_tot, qch, bias_d, ident, 0, shiftt)
    if ctx_out:
        ar.reset()
        qch = [(n_lat + i * 128, list(range(n_ctx // 128)), None, 0) for i in range(n_ctx // 128)]
        na_attention(c, ar, P, mixT, q_d, nakTc_d, navc_d, n_ctx // 128, qch, bias_d, ident, n_lat, shiftt)
    ar.reset()
    pool_mixer(c, ar, P, mixT, pin_d, rcnt_d, n_lat, pwbd, pscale, 0)
    if ctx_out:
        ar.reset()
        pool_mixer(c, ar, P, mixT, pinc_d, rcntc_d, n_ctx, pwbd, pscale, n_lat)
    ar.reset()
    qtiles = [(t, min(512, n_lat - t)) for t in range(0, n_lat, 512)]
    diff_attention(c, ar, P, mixT, q_d, qtiles, dkT_d, dv_d, n_kc_diff, lamt, gsub, ident, shiftt, cm.epst)
    if ctx_out:
        ar.reset()
        ctx0 = n_kc_diff - n_ctx // 128
        qtiles = [(n_lat, n_ctx)]
        diff_attention(c, ar, P, mixT, q_d, qtiles, dkT_d[:, :, ctx0 * 128:], dv_d[ctx0 * 128:, :], n_ctx // 128, lamt, gsub, ident, shiftt, cm.epst)
    ar.reset()
    n_mix = NT if ctx_out else n_lat
    wo = ar.alloc("wo", [128, 8, D], BF16)
    em.dma('pool', I('dma_start', out=wo, in_=wout_d.rearrange("(k p) n -> p k n", p=128)), writes=['wo'])

    class YB:
        pass
    yb = YB()
    yb.ysb = ar.alloc("ysb", [128, 8, 512], F32)
    yb.sq = ar.alloc("sq", [128, 8, 512], BF16)
    yb.tmp = [ar.alloc("tmpf%d" % i, [128, 512], F32) for i in range(2)]
    yb.rstd = ar.alloc("rstd", [128, 512], F32)
    yb.n_tmp = 0
    pss = {'ss': U2[:, 1, :], 'a': [S0[:, 0, :], S0[:, 1, :]], 'b': [S1[:, 0, :], S1[:, 1, :]], 'y': [U1[:, 0, :], U1[:, 1, :]]}
    ny = 0
    for (t0, w, g) in [s_ for tile in make_tiles(n_lat, n_ctx if ctx_out else 0, 512) for s_ in tile]:
        for k in range(8):
            py = pss['y'][ny % 2]
            ry = 'psU1_%d' % (ny % 2)
            ny += 1
            mm_group(em, py[:, :w], [(wo[:, kk, k * 128:(k + 1) * 128], mixT[:, kk, t0:t0 + w]) for kk in range(8)],
                     reads=['wo'] + ['mix%d' % i for i in range(t0 // 128, (t0 + w) // 128)], wres=ry)
            y_evac(c, yb, k, 0, w, py[:, :w], ry)
        sandwich_out(c, cm, yb, xT, 1, t0, w, g, 0, pss['ss'], ss_res='psU2_1')
    ar.reset()
    gT = mixT_t[:].rearrange("p a b -> p (a b)")[:, 0:NFC * tw].rearrange("p (a b) -> p a b", a=NFC) if 8 * NT >= NFC * tw else None
    fb = FFNBufs(c, tw, alloc=ar.alloc, gT=gT)
    tiles = make_tiles(n_lat, n_ctx if ctx_out else 0, tw)
    ffn(c, cm, fb, xT, 2, tiles, w1_d, w2_d, pss, psnames={'ss': 'psU2_1', 'a': ['psS0_0', 'psS0_1'], 'b': ['psS1_0', 'psS1_1'], 'y': ['psU1_0', 'psU1_1']})
    em.dma('sp', I('dma_start', out=x2T_o, in_=xT[:]), reads=xres(0, NT), writes=['x2T_o'])
    print("part B instructions:", em.ninst)
    return c.done()


def na_bias_tiles(rpb, rows_total, q_row0, key_row0, nj=6):
    kr = np.arange(2)[:, None, None, None]
    kc = np.arange(64)[None, :, None, None]
    qr = np.arange(2)[None, None, :, None]
    qc = np.arange(64)[None, None, None, :]
    q_row = q_row0 + qr
    rs = np.clip(q_row - 4, 0, rows_total - 8)
    cs = np.clip(qc - 8, 0, 64 - 16)
    out = np.full((4, nj, 2, 64, 2, 64), NEG, np.float32)
    for j in range(nj):
        key_row = key_row0 + 2 * j + kr
        valid = (key_row >= rs) & (key_row < rs + 8) & (kc >= cs) & (kc < cs + 16) & (key_row >= 0) & (key_row < rows_total)
        valid = np.broadcast_to(valid, (2, 64, 2, 64))
        dr = np.clip(np.broadcast_to(key_row - q_row + 7, (2, 64, 2, 64)), 0, 14)
        dc = np.clip(np.broadcast_to(kc - qc, (2, 64, 2, 64)), -15, 15) + 15
        for h in range(4):
            out[h, j] = np.where(valid, rpb[h][dr, dc], np.float32(NEG))
    return out.reshape(4, nj, 128, 128)


def pool_rcount(t_global, L):
    n = len(t_global)
    out = np.zeros((128, 2, n), np.float32)
    for gi, wdw in enumerate((2, 4, 8, 16)):
        half = wdw // 2
        lo = np.clip(t_global - half, 0, L)
        hi = np.clip(t_global + half, 0, L)
        rc = (1.0 / (hi - lo).astype(np.float32)).astype(np.float32)
        out[(gi % 2) * 64:(gi % 2) * 64 + 64, gi // 2, :] = rc[None, :]
    return out


def halo_cols(arrT, t0, n, halo, L):
    out = np.zeros(arrT.shape[:-1] + (n + 2 * halo,), arrT.dtype)
    a = max(0, t0 - halo)
    b = min(L, t0 + n + halo)
    out[..., a - (t0 - halo):b - (t0 - halo)] = arrT[..., a:b]
    return out


def pool_blockdiag(pool_w):
    out = np.zeros((128, 2, 128), np.float32)
    for gi in range(4):
        p0 = (gi % 2) * 64
        out[p0:p0 + 64, gi // 2, p0:p0 + 64] = pool_w[gi]
    return out.astype(NPBF)


N_LAT = 2048
NT_FULL = N_LAT + CTX
_PROGS = {}


def _fm(a):
    T, F = a.shape
    return np.ascontiguousarray(a.reshape(T, F // 128, 128).transpose(2, 1, 0))


def _unfm(aT):
    return np.ascontiguousarray(aT.transpose(2, 1, 0)).reshape(aT.shape[2], -1)


def _lay_vec(v):
    return np.ascontiguousarray(v.reshape(-1, 128).T)


def _prog(key, fn):
    if key not in _PROGS:
        _PROGS[key] = fn()
    return _PROGS[key]


def kernel_unfused(x, c, ctx, c_ctx, w_ada, b_ada, norm_g, ffn_w1, ffn_w2, w_in, w_out, na_rpb, pool_w, pool_scale, diff_lambda,
           diff_subln_g):
    f32 = np.float32
    x = np.asarray(x, f32)
    ctx = np.asarray(ctx, f32)
    cvals = np.asarray(c, f32)
    c_ctx = np.asarray(c_ctx, f32)
    ncores = 8
    depth = w_ada.shape[0]
    rows_total = SEQ // GRID_W
    nq = N_LAT // 128
    variants = [1, 2] + [0] * (nq - 4) + [3, 4]
    xT = []
    for i in range(ncores):
        b, j = i // 4, i % 4
        xt = np.concatenate([x[b, j * N_LAT:(j + 1) * N_LAT], ctx[b]], axis=0)
        xT.append(_fm(xt))
    ropes = []
    for i in range(ncores):
        j = i % 4
        cos, sin = rope_tables(np.arange(j * N_LAT, (j + 1) * N_LAT))
        ropes.append(rope_feature_major(cos, sin, CTX))
    pmat = rope_pmat()
    ident = np.eye(128, dtype=f32).astype(NPBF)
    TW = 512
    for l in range(depth):
        last = (l == depth - 1)
        lam_init = 0.8 - 0.6 * math.exp(-0.3 * l)
        nca = _prog(('A',), lambda: build_part_a(N_LAT, CTX, TW))
        normgT = np.ascontiguousarray(np.asarray(norm_g[l], f32).reshape(6, 8, 128).transpose(2, 0, 1))
        badaT = np.ascontiguousarray(np.asarray(b_ada[l], f32).reshape(72, 128).T)
        wada_l = np.ascontiguousarray(np.asarray(w_ada[l], f32))
        w1a = np.ascontiguousarray(np.asarray(ffn_w1[l, 0], f32))
        w2a = np.ascontiguousarray(np.asarray(ffn_w2[l, 0], f32))
        win_l = np.ascontiguousarray(np.asarray(w_in[l], f32))
        in_maps = []
        for i in range(ncores):
            b = i // 4
            cvec = np.ascontiguousarray(np.stack([_lay_vec(cvals[b]), _lay_vec(c_ctx)], axis=-1))
            in_maps.append({"xT": xT[i], "cvec": cvec, "w_ada": wada_l, "badaT": badaT, "normgT": normgT, "w1": w1a, "w2": w2a,
                            "w_in": win_l, "ropeC": ropes[i][0], "ropeS": ropes[i][1], "pmat": pmat})
        ra = run_bass_kernel_spmd(nca, in_maps, core_ids=list(range(ncores))).results
        ncb = _prog(('B', l), lambda: build_part_b(N_LAT, CTX, TW, n_kc_diff=(SEQ + CTX) // 128, na_variants=variants,
                                                  lam_init=lam_init, ctx_out=not last))
        w1b_ = np.ascontiguousarray(np.asarray(ffn_w1[l, 1], f32))
        w2b_ = np.ascontiguousarray(np.asarray(ffn_w2[l, 1], f32))
        wout_l = np.ascontiguousarray(np.asarray(w_out[l], f32))
        pwbd = pool_blockdiag(np.asarray(pool_w[l], f32))
        pscaleT = np.ascontiguousarray(np.asarray(pool_scale[l], f32).reshape(2, 128).T)
        dlam = np.ascontiguousarray(np.broadcast_to(np.asarray(diff_lambda[l], f32).reshape(1, 256), (128, 256)))
        subg = np.ascontiguousarray(np.broadcast_to(np.asarray(diff_subln_g[l], f32)[None, :], (128, 128)))
        rpb = np.asarray(na_rpb[l], f32)
        in_maps = []
        for b in range(2):
            cores = [b * 4 + j for j in range(4)]
            kv_lat = np.concatenate([ra[i]["kvT"][:, :, :N_LAT] for i in cores], axis=2)
            kv_ctx = ra[cores[0]]["kvT"][:, :, N_LAT:]
            v_lat = np.concatenate([ra[i]["vtok"][:N_LAT] for i in cores], axis=0)
            v_ctx = ra[cores[0]]["vtok"][N_LAT:]
            dkT = np.ascontiguousarray(np.concatenate([kv_lat[:, 4:8], kv_ctx[:, 4:8]], axis=2))
            dv = np.ascontiguousarray(np.concatenate([v_lat[:, 256:], v_ctx[:, 256:]], axis=0))
            nakTc = np.ascontiguousarray(kv_ctx[:, 0:2])
            navc = np.ascontiguousarray(v_ctx[:, 0:256])
            pinTc = halo_cols(np.ascontiguousarray(kv_ctx[:, 2:4]), 0, CTX, 8, CTX)
            rcntc = pool_rcount(np.arange(CTX), CTX)
            for j in range(4):
                i = cores[j]
                t_start = j * N_LAT
                r0 = t_start // GRID_W
                hk0 = (r0 - 4) * GRID_W
                nhk = (nq + 4) * 128
                na_kT = np.concatenate([halo_cols(kv_lat[:, 0:2], hk0, nhk, 0, SEQ), nakTc], axis=2)
                na_v = np.concatenate([halo_cols(v_lat[:, 0:256].T, hk0, nhk, 0, SEQ).T, navc], axis=0)
                bias = np.full((5, 4, 6, 128, 128), NEG, f32)
                for vi, ci in ((0, 2), (1, 0), (2, 1), (3, nq - 2), (4, nq - 1)):
                    k0, nl = na_local_chunks(ci, nq)
                    bias[vi] = na_bias_tiles(rpb, rows_total, r0 + 2 * ci, r0 - 4 + 2 * k0)
                in_maps.append({
                    "x1T": ra[i]["x1T"], "modsT": ra[i]["modsT"], "normgT": normgT, "qT": ra[i]["qT"],
                    "na_kT": np.ascontiguousarray(na_kT), "na_v": np.ascontiguousarray(na_v), "na_kTc": nakTc, "na_vc": navc,
                    "na_bias": bias,
                    "pinT": halo_cols(kv_lat[:, 2:4], t_start, N_LAT, 8, SEQ), "rcnt": pool_rcount(np.arange(t_start, t_start + N_LAT), SEQ),
                    "pinTc": pinTc, "rcntc": rcntc, "pwbd": pwbd, "pscaleT": pscaleT,
                    "dkT": dkT, "dv": dv, "dlam": dlam, "subg": subg, "ident": ident,
                    "w_out": wout_l, "w1": w1b_, "w2": w2b_,
                })
        rb = run_bass_kernel_spmd(ncb, in_maps, core_ids=list(range(ncores))).results
        xT = [rb[i]["x2T"] for i in range(ncores)]
    out = np.zeros((2, SEQ, D), f32)
    for i in range(ncores):
        b, j = i // 4, i % 4
        out[b, j * N_LAT:(j + 1) * N_LAT] = _unfm(xT[i][:, :, :N_LAT])
    return out


def build_fused(n_lat=2048, n_ctx=256, tw=512, depth=2, group=4, dbg_ctx_out=False):
    NT = n_lat + n_ctx
    nq = n_lat // 128
    n_halo_kc = nq + 4
    nkc_na = n_halo_kc + n_ctx // 128
    n_kc_diff = (group * n_lat + n_ctx) // 128
    variants = [1, 2] + [0] * (nq - 4) + [3, 4] if nq > 4 else list(range(1, nq + 1))
    nvar = max(variants) + 1
    c = Ctx()
    em = c.em
    nc = c.nc
    xT_d = c.din("xT", [128, 8, NT], F32)
    cvec_d = c.din("cvec", [128, 8, 2], F32)
    ropeC_d = c.din("ropeC", [128, NT], F32)
    ropeS_d = c.din("ropeS", [128, NT], F32)
    pm_d = c.din("pmat", [128, 128], BF16)
    ident_d = c.din("ident", [128, 128], BF16)
    rcnt_d = c.din("rcnt", [128, 2, n_lat], F32)
    rcntc_d = c.din("rcntc", [128, 2, n_ctx], F32)
    L = []
    for l in range(depth):
        L.append(dict(
            wada=c.din("w_ada%d" % l, [D, 9 * D], F32), bada=c.din("badaT%d" % l, [128, 72], F32),
            normg=c.din("normgT%d" % l, [128, 6, 8], F32),
            w1a=c.din("w1a%d" % l, [D, 2 * DFF], F32), w2a=c.din("w2a%d" % l, [DFF, D], F32),
            w1b=c.din("w1b%d" % l, [D, 2 * DFF], F32), w2b=c.din("w2b%d" % l, [DFF, D], F32),
            win=c.din("w_in%d" % l, [D, 2560], F32), wout=c.din("w_out%d" % l, [D, D], F32),
            bias=c.din("na_bias%d" % l, [nvar, 4, 6, 128, 128], F32),
            pwbd=c.din("pwbd%d" % l, [128, 2, 128], BF16), pscale=c.din("pscaleT%d" % l, [128, 2], F32),
            dlam=c.din("dlam%d" % l, [128, 256], F32), subg=c.din("subg%d" % l, [128, 128], F32),
        ))
    outT_o = c.dout("outT", [128, 8, n_lat], F32)

    def dram(name, shape, dt):
        return nc.dram_tensor(name, list(shape), dt, kind="Internal").ap()

    xT = c.sb("xT_sb", [128, 8, NT], F32)
    cm = Common(c)
    ident = c.sb("ident", [128, 128], BF16)
    pwbd = c.sb("pwbd", [128, 2, 128], BF16)
    pscale = c.sb("pscale", [128, 2], F32)
    dlam = c.sb("dlam", [128, 256], F32)
    lamw = c.sb("lamw", [128, 4], F32)
    lamt = c.sb("lamt", [128, 1], F32)
    gsub = c.sb("gsub", [128, 128], F32)
    shiftt = c.sb("shiftt", [128, 1], F32)
    zt = c.sb("zeros", [128, 2048], BF16)
    S0 = c.ps("S0", [128, 2, 512])
    S1 = c.ps("S1", [128, 2, 512])
    U1 = c.ps("U1", [128, 2, 512])
    U2 = c.ps("U2", [128, 2, 512])
    P = {'S': [S0, S1], 'U1': U1, 'U2': U2, 'O': U1, 'T': U2[:, 0, :].bitcast(BF16)}
    pss = {'ss': U2[:, 1, :], 'a': [S0[:, 0, :], S0[:, 1, :]], 'b': [S1[:, 0, :], S1[:, 1, :]], 'y': [U1[:, 0, :], U1[:, 1, :]]}
    psn = {'ss': 'psU2_1', 'a': ['psS0_0', 'psS0_1'], 'b': ['psS1_0', 'psS1_1'], 'y': ['psU1_0', 'psU1_1']}
    ps_mods = U2[:, 0, :].rearrange("p (a b) -> p a b", b=2)
    arena_n = (nc.sbuf_bytes_remaining - 2048) // 2 // 16 * 16
    ar = Arena(c, arena_n)
    print("fused arena elems", arena_n)

    for t in range(0, NT, 512):
        w = min(512, NT - t)
        em.dma('sp', I('dma_start', out=xT[:, :, t:t + w], in_=xT_d[:, :, t:t + w]), writes=xres(t, w))
    em.dma('sp', I('dma_start', out=ident[:], in_=ident_d), writes=['ident'])
    em.op('dve', I('memset', shiftt[:], EXP_SHIFT), writes=['shiftt'])
    em.op('dve', I('memset', zt[:], 0.0), writes=['zeros'])
    tiles_all = make_tiles(n_lat, n_ctx, tw)
    wsel_d = c.din("wsel", [128, 2 * group], F32)
    wsel = c.sb("wsel", [128, 2 * group], F32)
    em.dma('sp', I('dma_start', out=wsel[:], in_=wsel_d), writes=['wsel'])

    for l in range(depth):
        W = L[l]
        last = (l == depth - 1)
        ctx_out = (not last) or dbg_ctx_out
        lam_init = 0.8 - 0.6 * math.exp(-0.3 * l)
        ar.reset(to_zero=True)
        em.dma('sp', I('dma_start', out=cm.normg[:], in_=W['normg']), writes=['normg'])
        fb = FFNBufs(c, tw, alloc=ar.alloc)
        cm.compute_mods(cvec_d, W['wada'], W['bada'], ps_mods, fb, alloc=ar.alloc, psname='psU2_0')
        cm.compute_coefs()
        ffn(c, cm, fb, xT, 0, tiles_all, W['w1a'], W['w2a'], pss, psnames=psn)
        qT_l = dram("qT_l%d" % l, [128, 6, NT], BF16)
        kvT_l = dram("kvT_l%d" % l, [128, 8 * NT], BF16)
        v_l = dram("v_l%d" % l, [NT, 768], BF16)
        kvT_l3 = kvT_l.rearrange("p (c t) -> p c t", c=8)
        proj_phase(c, cm, fb, xT, tiles_all, W['win'], ropeC_d, ropeS_d, pm_d, pss, qT_l, kvT_l3, v_l, alloc=ar.alloc, psnames=psn)
        rg = [[g0 * group + j for j in range(group)] for g0 in range(8 // group)]
        kedge_loc = dram("kedge_loc%d" % l, [128, 4 * 512], BF16)
        kedge_loc3 = kedge_loc.rearrange("p (c t) -> p c t", c=4)
        vedge_loc = dram("vedge_loc%d" % l, [512, 256], BF16)
        em.dma('sp', I('dma_start', out=kedge_loc3[:, :, 0:256], in_=kvT_l3[:, 0:4, 0:256]), reads=['kvT_o'], writes=['kedge_loc'])
        em.dma('sp', I('dma_start', out=kedge_loc3[:, :, 256:512], in_=kvT_l3[:, 0:4, n_lat - 256:n_lat]), reads=['kvT_o'], writes=['kedge_loc'])
        em.dma('sp', I('dma_start', out=vedge_loc[0:256, :], in_=v_l[0:256, 0:256]), reads=['v_o'], writes=['vedge_loc'])
        em.dma('sp', I('dma_start', out=vedge_loc[256:512, :], in_=v_l[n_lat - 256:n_lat, 0:256]), reads=['v_o'], writes=['vedge_loc'])
        kedge_g = dram("kedge_g%d" % l, [group * 128, 4 * 512], BF16)
        vedge_g = dram("vedge_g%d" % l, [group * 512, 256], BF16)
        em.coll(I('collective_compute', "AllGather", ALU.bypass, replica_groups=rg, ins=[kedge_loc.opt()], outs=[kedge_g.opt()]),
                reads=['kedge_loc'], writes=['kedge_g'])
        em.coll(I('collective_compute', "AllGather", ALU.bypass, replica_groups=rg, ins=[vedge_loc.opt()], outs=[vedge_g.opt()]),
                reads=['vedge_loc'], writes=['vedge_g'])
        dk_g, dv_g = [], []
        for h in range(4):
            dk_loc = dram("dk_loc%d_%d" % (l, h), [128, n_lat], BF16)
            dv_loc = dram("dv_loc%d_%d" % (l, h), [n_lat, 128], BF16)
            em.dma('sp', I('dma_start', out=dk_loc, in_=kvT_l3[:, 4 + h, 0:n_lat]), reads=['kvT_o'], writes=['dk_loc%d' % h])
            em.dma('sp', I('dma_start', out=dv_loc, in_=v_l[0:n_lat, 256 + h * 128:256 + (h + 1) * 128]), reads=['v_o'], writes=['dv_loc%d' % h])
            dkg = dram("dk_g%d_%d" % (l, h), [group * 128, n_lat], BF16)
            dvg = dram("dv_g%d_%d" % (l, h), [group * n_lat, 128], BF16)
            em.coll(I('collective_compute', "AllGather", ALU.bypass, replica_groups=rg, ins=[dk_loc.opt()], outs=[dkg.opt()]),
                    reads=['dk_loc%d' % h], writes=['dk_g%d' % h])
            em.coll(I('collective_compute', "AllGather", ALU.bypass, replica_groups=rg, ins=[dv_loc.opt()], outs=[dvg.opt()]),
                    reads=['dv_loc%d' % h], writes=['dv_g%d' % h])
            dk_g.append(dkg)
            dv_g.append(dvg)
        ar.reset(to_zero=True)
        ke = ar.alloc("ke", [128, group, 2048], BF16)
        ve = ar.alloc("ve", [128, group, 4, 256], BF16)
        kp = ar.alloc("kp", [128, 2048], BF16)
        kn = ar.alloc("kn", [128, 2048], BF16)
        vp = ar.alloc("vp", [128, 4, 256], BF16)
        vn = ar.alloc("vn", [128, 4, 256], BF16)
        em.dma('sp', I('dma_start', out=ke, in_=kedge_g.rearrange("(r p) n -> p r n", p=128)), reads=['kedge_g'], writes=['ke'])
        for r in range(group):
            em.dma('sp', I('dma_start', out=ve[:, r, :, :], in_=vedge_g[r * 512:(r + 1) * 512, :].rearrange("(a p) n -> p a n", p=128)),
                   reads=['vedge_g'], writes=['ve'])
        for (dst, dres, src, sres, w0) in ((kp, 'kp', lambda r: ke[:, r, :], 'ke', 0), (kn, 'kn', lambda r: ke[:, r, :], 'ke', group),
                                           (vp, 'vp', lambda r: ve[:, r, :, :], 've', 0), (vn, 'vn', lambda r: ve[:, r, :, :], 've', group)):
            em.op('dve', I('tensor_scalar', out=dst, in0=src(0), scalar1=wsel[:, w0:w0 + 1], scalar2=None, op0=ALU.mult),
                  reads=[sres, 'wsel'], writes=[dres])
            for r in range(1, group):
                em.op('dve', I('scalar_tensor_tensor', out=dst, in0=src(r), scalar=wsel[:, w0 + r:w0 + r + 1], in1=dst, op0=ALU.mult, op1=ALU.add),
                      reads=[sres, 'wsel', dres], writes=[dres])
        kp3 = kp.rearrange("p (c t) -> p c t", c=4)
        kn3 = kn.rearrange("p (c t) -> p c t", c=4)
        na_kT_asm = dram("na_kT_asm%d" % l, [128, 2, nkc_na * 128], BF16)
        na_v_asm = dram("na_v_asm%d" % l, [nkc_na * 128, 256], BF16)
        pin_asm = dram("pin_asm%d" % l, [128, 2, n_lat + 16], BF16)
        pinc_asm = dram("pinc_asm%d" % l, [128, 2, n_ctx + 16], BF16)
        em.dma('sp', I('dma_start', out=na_kT_asm[:, :, 0:256], in_=kp3[:, 0:2, 256:512]), reads=['kp'], writes=['na_kT_asm'])
        em.dma('sp', I('dma_start', out=na_kT_asm[:, :, 256 + n_lat:512 + n_lat], in_=kn3[:, 0:2, 0:256]), reads=['kn'], writes=['na_kT_asm'])
        em.dma('sp', I('dma_start', out=na_v_asm[0:256, :].rearrange("(a p) n -> p a n", p=128), in_=vp[:, 2:4, :]), reads=['vp'], writes=['na_v_asm'])
        em.dma('sp', I('dma_start', out=na_v_asm[256 + n_lat:512 + n_lat, :].rearrange("(a p) n -> p a n", p=128), in_=vn[:, 0:2, :]), reads=['vn'], writes=['na_v_asm'])
        em.dma('sp', I('dma_start', out=pin_asm[:, :, 0:8], in_=kp3[:, 2:4, 504:512]), reads=['kp'], writes=['pin_asm'])
        em.dma('sp', I('dma_start', out=pin_asm[:, :, 8 + n_lat:16 + n_lat], in_=kn3[:, 2:4, 0:8]), reads=['kn'], writes=['pin_asm'])
        em.dma('sp', I('dma_start', out=na_kT_asm[:, :, 256:256 + n_lat], in_=kvT_l3[:, 0:2, 0:n_lat]), reads=['kvT_o'], writes=['na_kT_asm'])
        em.dma('sp', I('dma_start', out=na_kT_asm[:, :, 512 + n_lat:], in_=kvT_l3[:, 0:2, n_lat:NT]), reads=['kvT_o'], writes=['na_kT_asm'])
        em.dma('sp', I('dma_start', out=na_v_asm[256:256 + n_lat, :], in_=v_l[0:n_lat, 0:256]), reads=['v_o'], writes=['na_v_asm'])
        em.dma('sp', I('dma_start', out=na_v_asm[512 + n_lat:, :], in_=v_l[n_lat:NT, 0:256]), reads=['v_o'], writes=['na_v_asm'])
        em.dma('sp', I('dma_start', out=pin_asm[:, :, 8:8 + n_lat], in_=kvT_l3[:, 2:4, 0:n_lat]), reads=['kvT_o'], writes=['pin_asm'])
        if ctx_out:
            for (a0, a1) in ((0, 8), (8 + n_ctx, 16 + n_ctx)):
                em.dma('sp', I('dma_start', out=pinc_asm[:, :, a0:a1], in_=zt[:, 0:16].rearrange("p (c t) -> p c t", c=2)), reads=['zeros'], writes=['pinc_asm'])
            em.dma('sp', I('dma_start', out=pinc_asm[:, :, 8:8 + n_ctx], in_=kvT_l3[:, 2:4, n_lat:NT]), reads=['kvT_o'], writes=['pinc_asm'])
        ar.reset(to_zero=True)
        mixT = ar.alloc("mixT", [128, 8, NT], BF16)
        ar.set_base()
        em.dma('sp', I('dma_start', out=pwbd[:], in_=W['pwbd']), writes=['pwbd'])
        em.dma('sp', I('dma_start', out=pscale[:], in_=W['pscale']), writes=['pscale'])
        em.dma('sp', I('dma_start', out=dlam[:], in_=W['dlam']), writes=['dlam'])
        em.dma('sp', I('dma_start', out=gsub[:], in_=W['subg']), writes=['gsub'])
        em.op('dve', I('tensor_scalar', out=gsub[:], in0=gsub[:], scalar1=float(1.0 - lam_init), scalar2=None, op0=ALU.mult), reads=['gsub'], writes=['gsub'])
        em.op('dve', I('tensor_tensor', out=dlam[:, 0:64], in0=dlam[:, 0:64], in1=dlam[:, 64:128], op=ALU.mult), reads=['dlam'], writes=['dlam'])
        em.op('dve', I('tensor_tensor', out=dlam[:, 128:192], in0=dlam[:, 128:192], in1=dlam[:, 192:256], op=ALU.mult), reads=['dlam'], writes=['dlam'])
        em.op('dve', I('reduce_sum', out=lamw[:, 0:2], in_=dlam[:].rearrange("p (a b) -> p a b", a=2)[:, :, 0:64], axis=AX.X), reads=['dlam'], writes=['lamw'])
        em.op('act', I('activation', out=lamw[:, 2:4], in_=lamw[:, 0:2], func=AF.Exp), reads=['lamw'], writes=['lamw'])
        em.op('dve', I('tensor_tensor', out=lamt[:], in0=lamw[:, 2:3], in1=lamw[:, 3:4], op=ALU.subtract), reads=['lamw'], writes=['lamt'])
        em.op('dve', I('tensor_scalar', out=lamt[:], in0=lamt[:], scalar1=float(lam_init), scalar2=None, op0=ALU.add), reads=['lamt'], writes=['lamt'])
        em.barrier()
        ctx_kcs = [n_halo_kc + i for i in range(n_ctx // 128)]
        qch = []
        for i in range(nq):
            k0, nl = na_local_chunks(i, nq)
            qch.append((i * 128, [k0 + j for j in range(nl)] + ctx_kcs, variants[i], nl))
        na_attention(c, ar, P, mixT, qT_l, na_kT_asm, na_v_asm, nkc_na, qch, W['bias'], ident, 0, shiftt)
        if ctx_out:
            ar.reset()
            qch = [(n_lat + i * 128, list(range(n_ctx // 128)), None, 0) for i in range(n_ctx // 128)]
            na_attention(c, ar, P, mixT, qT_l, kvT_l3[:, 0:2, n_lat:NT], v_l[n_lat:NT, 0:256], n_ctx // 128, qch, W['bias'], ident, n_lat, shiftt)
        ar.reset()
        pool_mixer(c, ar, P, mixT, pin_asm, rcnt_d, n_lat, pwbd, pscale, 0)
        if ctx_out:
            ar.reset()
            pool_mixer(c, ar, P, mixT, pinc_asm, rcntc_d, n_ctx, pwbd, pscale, n_lat)
        ar.reset()
        lat_kc = n_lat // 128

        def load_all(h, kb, vb, rk, rv):
            for r in range(group):
                em.dma('sp', I('dma_start', out=kb[:, r * n_lat:(r + 1) * n_lat], in_=dk_g[h][r * 128:(r + 1) * 128, :]), reads=['dk_g%d' % h], writes=[rk])
                em.dma('sp', I('dma_start', out=vb[:, r * lat_kc:(r + 1) * lat_kc, 0:128],
                               in_=dv_g[h][r * n_lat:(r + 1) * n_lat, :].rearrange("(c p) e -> p c e", p=128)),
                       reads=['dv_g%d' % h], writes=[rv])
            em.dma('sp', I('dma_start', out=kb[:, group * n_lat:], in_=kvT_l3[:, 4 + h, n_lat:NT]), reads=['kvT_o'], writes=[rk])
            em.dma('sp', I('dma_start', out=vb[:, group * lat_kc:, 0:128],
                           in_=v_l[n_lat:NT, 256 + h * 128:256 + (h + 1) * 128].rearrange("(c p) e -> p c e", p=128)), reads=['v_o'], writes=[rv])

        def load_ctx(h, kb, vb, rk, rv):
            em.dma('sp', I('dma_start', out=kb, in_=kvT_l3[:, 4 + h, n_lat:NT]), reads=['kvT_o'], writes=[rk])
            em.dma('sp', I('dma_start', out=vb[:, :, 0:128],
                           in_=v_l[n_lat:NT, 256 + h * 128:256 + (h + 1) * 128].rearrange("(c p) e -> p c e", p=128)), reads=['v_o'], writes=[rv])
        qtiles = [(t, min(512, n_lat - t)) for t in range(0, n_lat, 512)]
        diff_attention(c, ar, P, mixT, qT_l, qtiles, None, None, n_kc_diff, lamt, gsub, ident, shiftt, cm.epst, loaders=load_all)
        if ctx_out:
            ar.reset()
            diff_attention(c, ar, P, mixT, qT_l, [(n_lat, n_ctx)], None, None, n_ctx // 128, lamt, gsub, ident, shiftt, cm.epst, loaders=load_ctx)
        ar.reset()
        wo = ar.alloc("wo", [128, 8, D], BF16)
        em.dma('pool', I('dma_start', out=wo, in_=W['wout'].rearrange("(k p) n -> p k n", p=128)), writes=['wo'])

        class YB:
            pass
        yb = YB()
        yb.ysb = ar.alloc("ysb", [128, 8, 512], F32)
        yb.sq = ar.alloc("sq", [128, 8, 512], BF16)
        yb.tmp = [ar.alloc("tmpf%d" % i, [128, 512], F32) for i in range(2)]
        yb.rstd = ar.alloc("rstd", [128, 512], F32)
        yb.n_tmp = 0
        ny = 0
        for (t0, w, g) in [s_ for tile in make_tiles(n_lat, n_ctx if ctx_out else 0, 512) for s_ in tile]:
            for k in range(8):
                py = pss['y'][ny % 2]
                ry = psn['y'][ny % 2]
                ny += 1
                mm_group(em, py[:, :w], [(wo[:, kk, k * 128:(k + 1) * 128], mixT[:, kk, t0:t0 + w]) for kk in range(8)],
                         reads=['wo'] + ['mix%d' % i for i in range(t0 // 128, (t0 + w) // 128)], wres=ry)
                y_evac(c, yb, k, 0, w, py[:, :w], ry)
            sandwich_out(c, cm, yb, xT, 1, t0, w, g, 0, pss['ss'], ss_res=psn['ss'])
        ar.reset(to_zero=True)
        fb = FFNBufs(c, tw, alloc=ar.alloc)
        ffn(c, cm, fb, xT, 2, make_tiles(n_lat, n_ctx if ctx_out else 0, tw), W['w1b'], W['w2b'], pss, psnames=psn)
    em.dma('sp', I('dma_start', out=outT_o, in_=xT[:, :, 0:n_lat]), reads=xres(0, n_lat), writes=['outT_o'])
    print("fused instructions:", em.ninst)
    return c.done()


def fused_inputs(x, c, ctx, c_ctx, w_ada, b_ada, norm_g, ffn_w1, ffn_w2, w_in, w_out, na_rpb, pool_w, pool_scale, diff_lambda,
                 diff_subln_g, n_lat):
    f32 = np.float32
    x = np.asarray(x, f32)
    ctx = np.asarray(ctx, f32)
    cvals = np.asarray(c, f32)
    c_ctx = np.asarray(c_ctx, f32)
    seq = x.shape[1]
    n_ctx = ctx.shape[1]
    group = seq // n_lat
    ncores = x.shape[0] * group
    depth = w_ada.shape[0]
    rows_total = seq // GRID_W
    nq = n_lat // 128
    if nq > 4:
        vmap = ((0, 2), (1, 0), (2, 1), (3, nq - 2), (4, nq - 1))
    else:
        vmap = tuple((i + 1, i) for i in range(nq))
    shared = {"pmat": rope_pmat(), "ident": np.eye(128, dtype=f32).astype(NPBF), "rcntc": pool_rcount(np.arange(n_ctx), n_ctx)}
    for l in range(depth):
        shared["w_ada%d" % l] = np.ascontiguousarray(np.asarray(w_ada[l], f32))
        shared["badaT%d" % l] = np.ascontiguousarray(np.asarray(b_ada[l], f32).reshape(72, 128).T)
        shared["normgT%d" % l] = np.ascontiguousarray(np.asarray(norm_g[l], f32).reshape(6, 8, 128).transpose(2, 0, 1))
        shared["w1a%d" % l] = np.ascontiguousarray(np.asarray(ffn_w1[l, 0], f32))
        shared["w2a%d" % l] = np.ascontiguousarray(np.asarray(ffn_w2[l, 0], f32))
        shared["w1b%d" % l] = np.ascontiguousarray(np.asarray(ffn_w1[l, 1], f32))
        shared["w2b%d" % l] = np.ascontiguousarray(np.asarray(ffn_w2[l, 1], f32))
        shared["w_in%d" % l] = np.ascontiguousarray(np.asarray(w_in[l], f32))
        shared["w_out%d" % l] = np.ascontiguousarray(np.asarray(w_out[l], f32))
        shared["pwbd%d" % l] = pool_blockdiag(np.asarray(pool_w[l], f32))
        shared["pscaleT%d" % l] = np.ascontiguousarray(np.asarray(pool_scale[l], f32).reshape(2, 128).T)
        shared["dlam%d" % l] = np.ascontiguousarray(np.broadcast_to(np.asarray(diff_lambda[l], f32).reshape(1, 256), (128, 256)))
        shared["subg%d" % l] = np.ascontiguousarray(np.broadcast_to(np.asarray(diff_subln_g[l], f32)[None, :], (128, 128)))
    in_maps = []
    for i in range(ncores):
        b, j = i // group, i % group
        t_start = j * n_lat
        r0 = t_start // GRID_W
        m = dict(shared)
        m["xT"] = _fm(np.concatenate([x[b, t_start:t_start + n_lat], ctx[b]], axis=0))
        m["cvec"] = np.ascontiguousarray(np.stack([_lay_vec(cvals[b]), _lay_vec(c_ctx)], axis=-1))
        cos, sin = rope_tables(np.arange(t_start, t_start + n_lat))
        m["ropeC"], m["ropeS"] = rope_feature_major(cos, sin, n_ctx)
        m["rcnt"] = pool_rcount(np.arange(t_start, t_start + n_lat), seq)
        ws = np.zeros((128, 2 * group), f32)
        if j > 0:
            ws[:, j - 1] = 1.0
        if j < group - 1:
            ws[:, group + j + 1] = 1.0
        m["wsel"] = ws
        for l in range(depth):
            rpb = np.asarray(na_rpb[l], f32)
            bias = np.full((len(vmap) + (1 if nq <= 4 else 0), 4, 6, 128, 128), NEG, f32)
            for vi, ci in vmap:
                k0, nl = na_local_chunks(ci, nq)
                bias[vi] = na_bias_tiles(rpb, rows_total, r0 + 2 * ci, r0 - 4 + 2 * k0)
            m["na_bias%d" % l] = bias
        in_maps.append(m)
    return in_maps, ncores, group


def kernel_fused(n_lat=N_LAT, **inputs):
    in_maps, ncores, group = fused_inputs(n_lat=n_lat, **inputs)
    n_ctx = inputs["ctx"].shape[1]
    seq = inputs["x"].shape[1]
    tw = 512 if n_lat >= 2048 else 384
    nc = _prog(('F', n_lat, n_ctx), lambda: build_fused(n_lat, n_ctx, tw, depth=inputs["w_ada"].shape[0], group=group))
    res = run_bass_kernel_spmd(nc, in_maps, core_ids=list(range(ncores))).results
    out = np.zeros((inputs["x"].shape[0], seq, D), np.float32)
    for i in range(ncores):
        b, j = i // group, i % group
        out[b, j * n_lat:(j + 1) * n_lat] = _unfm(res[i]["outT"])
    return out


def kernel(x, c, ctx, c_ctx, w_ada, b_ada, norm_g, ffn_w1, ffn_w2, w_in, w_out, na_rpb, pool_w, pool_scale, diff_lambda,
           diff_subln_g):
    return kernel_fused(n_lat=N_LAT, x=x, c=c, ctx=ctx, c_ctx=c_ctx, w_ada=w_ada, b_ada=b_ada, norm_g=norm_g, ffn_w1=ffn_w1,
                        ffn_w2=ffn_w2, w_in=w_in, w_out=w_out, na_rpb=na_rpb, pool_w=pool_w, pool_scale=pool_scale,
                        diff_lambda=diff_lambda, diff_subln_g=diff_subln_g)
```

```python
import math
import numpy as np
import ml_dtypes
from contextlib import ExitStack
import concourse.bass as bass
import concourse.mybir as mybir
from concourse.bass_utils import run_bass_kernel_spmd

F32 = mybir.dt.float32
BF16 = mybir.dt.bfloat16
AF = mybir.ActivationFunctionType
ALU = mybir.AluOpType
AX = mybir.AxisListType
NPBF = ml_dtypes.bfloat16

D = 1024
DFF = 2816
NFC = 22
SEQ = 8192
CTX = 256
GRID_W = 64
EPS = 1e-6
NEG = -30000.0
EXP_SHIFT = -40.0


class Emitter:
    ENGS = ('pe', 'act', 'dve', 'pool', 'sp')

    def __init__(self, nc, stack, n_dma_sems=16):
        self.nc = nc
        self._stack = stack
        self.cccount = 0
        self.prog = {e: [] for e in self.ENGS}
        self.count = {e: 0 for e in self.ENGS}
        self.waited = {e: {} for e in self.ENGS}
        self.dcount = [0] * n_dma_sems
        self.dnext_q = {e: 0 for e in self.ENGS}
        self.lastw = {}
        self.readers = {}
        self.semobj = {}
        for e in self.ENGS:
            self.semobj[('c', e)] = stack.enter_context(nc.semaphore('c_' + e))
        for i in range(n_dma_sems):
            self.semobj[('d', i)] = stack.enter_context(nc.semaphore('d%d' % i))
        self.ninst = 0

    def _deps(self, eng, reads, writes):
        deps = {}
        own = ('c', eng)

        def add(k, v):
            if deps.get(k, 0) < v:
                deps[k] = v
        skip_own = (eng == 'pe')
        for r in reads:
            t = self.lastw.get(r)
            if t is not None and not (skip_own and t[0] == own):
                add(*t)
        for w in writes:
            t = self.lastw.get(w)
            if t is not None and not (skip_own and t[0] == own):
                add(*t)
            for k, v in self.readers.get(w, {}).items():
                if not (skip_own and k == own):
                    add(k, v)
        waits = []
        wd = self.waited[eng]
        for k, v in deps.items():
            if wd.get(k, 0) < v:
                wd[k] = v
                waits.append((k, v))
        return waits

    def _commit(self, tok, reads, writes):
        for w in writes:
            self.lastw[w] = tok
            self.readers[w] = {}
        for r in reads:
            d = self.readers.setdefault(r, {})
            if d.get(tok[0], 0) < tok[1]:
                d[tok[0]] = tok[1]

    def op(self, eng, fn, reads=(), writes=(), inc=True):
        writes = list(writes) + [r for r in reads if r.startswith('ps') and r not in writes]
        waits = self._deps(eng, reads, writes)
        tok = (('c', eng), self.count[eng] + 1)
        if inc:
            self.count[eng] += 1
        self.prog[eng].append((waits, fn, (tok[0], 1) if inc else None))
        self._commit(tok, reads, writes)
        self.ninst += 1
        return tok

    def dma(self, eng, fn, reads=(), writes=()):
        waits = self._deps(eng, reads, writes)
        half = len(self.dcount) // 2
        base = 0 if eng == 'pool' else half
        i = base + self.dnext_q[eng]
        self.dnext_q[eng] = (self.dnext_q[eng] + 1) % half
        k = ('d', i)
        wd = self.waited[eng]
        if wd.get(k, 0) < self.dcount[i]:
            wd[k] = self.dcount[i]
            waits.append((k, self.dcount[i]))
        self.dcount[i] += 16
        tok = (k, self.dcount[i])
        self.prog[eng].append((waits, fn, (k, 16)))
        self._commit(tok, reads, writes)
        self.ninst += 1
        return tok

    def coll(self, fn, reads=(), writes=()):
        eng = 'pool'
        waits = self._deps(eng, reads, writes)
        k = ('cc', self.cccount)
        self.semobj[k] = self._stack.enter_context(self.nc.semaphore('cc_sem%d' % self.cccount))
        self.cccount += 1
        tok = (k, 1)
        self.prog[eng].append((waits, fn, (k, 1)))
        self._commit(tok, reads, writes)
        self.ninst += 1
        return tok

    def finish(self, eng='sp'):
        toks = [(('c', e), self.count[e]) for e in self.ENGS if self.count[e]]
        toks += [(('d', i), c) for i, c in enumerate(self.dcount) if c]
        toks += [(('cc', i), 1) for i in range(self.cccount)]
        waits = []
        wd = self.waited[eng]
        for k, v in toks:
            if wd.get(k, 0) < v:
                wd[k] = v
                waits.append((k, v))
        self.prog[eng].append((waits, None, None))

    def emit(self, block):
        def mk(e):
            def body(engh):
                for waits, fn, inc in self.prog[e]:
                    for k, v in waits:
                        engh.wait_ge(self.semobj[k], v)
                    if fn is not None:
                        if fn[0] == '__call__':
                            ins = fn[1](engh)
                        else:
                            ins = getattr(engh, fn[0])(*fn[1], **fn[2])
                        if inc is not None:
                            ins.then_inc(self.semobj[inc[0]], inc[1])
            return body
        block.tensor(mk('pe'))
        block.scalar(mk('act'))
        block.vector(mk('dve'))
        block.gpsimd(mk('pool'))
        block.sync(mk('sp'))


class Ctx:
    def __init__(self):
        self.nc = bass.Bass("TRN2", target_bir_lowering=False)
        self.st = ExitStack()
        self.em = Emitter(self.nc, self.st)
        self.uid = 0

    def sb(self, name, shape, dt):
        return self.st.enter_context(self.nc.sbuf_tensor("s_" + name, list(shape), dt))

    def ps(self, name, shape, dt=F32):
        return self.st.enter_context(self.nc.psum_tensor("p_" + name, list(shape), dt))

    def din(self, name, shape, dt):
        return self.nc.dram_tensor(name, list(shape), dt, kind="ExternalInput").ap()

    def dout(self, name, shape, dt):
        return self.nc.dram_tensor(name, list(shape), dt, kind="ExternalOutput").ap()

    def done(self):
        self.em.finish('sp')
        with self.nc.Block() as block:
            self.em.emit(block)
        self.st.close()
        return self.nc


def I(name, *a, **kw):
    return (name, a, kw)


def mm_group(em, out_ap, pairs, reads, wres, extra_writes=()):
    n = len(pairs)
    for i, (l, r) in enumerate(pairs):
        em.op('pe', I('matmul', out_ap, lhsT=l, rhs=r, start=(i == 0), stop=(i == n - 1)),
              reads=reads, writes=[wres] + list(extra_writes), inc=(i == n - 1))


class Common:
    def __init__(self, c, need_mods_from_wada=True):
        self.c = c
        em = c.em
        self.ones = c.sb("ones_bf", [128, 128], BF16)
        em.op('dve', I('memset', self.ones[:], 1.0), writes=['ones'])
        self.dummy = c.sb("dummy", [128, 1], F32)
        self.epst = c.sb("epst", [128, 1], F32)
        em.op('dve', I('memset', self.epst[:], EPS), writes=['epst'])
        self.modsT = c.sb("modsT", [128, 72, 2], F32)
        self.normg = c.sb("normgT", [128, 6, 8], F32)
        self.A = c.sb("coefA", [128, 3, 8, 2], F32)
        self.G = c.sb("coefG", [128, 3, 8, 2], F32)

    def load_normg(self, normg_d):
        self.c.em.dma('sp', I('dma_start', out=self.normg[:], in_=normg_d), writes=['normg'])

    def compute_mods(self, cvec_d, wada_d, badaT_d, ps_mods, fb, alloc=None, psname='ps_mods'):
        c, em = self.c, self.c.em
        if alloc is None:
            alloc = lambda name, shape, dt: c.sb(name, shape, dt)[:]
        cv = alloc("cvec", [128, 8, 2], F32)
        scv = alloc("scvec", [128, 8, 2], BF16)
        bad = alloc("badaT", [128, 72], F32)
        em.dma('sp', I('dma_start', out=cv, in_=cvec_d), writes=['cv'])
        em.dma('sp', I('dma_start', out=bad, in_=badaT_d), writes=['bad'])
        em.op('act', I('activation', out=scv, in_=cv, func=AF.Silu), reads=['cv'], writes=['scv'])
        wbuf = [flatview(fb.gT, i * 8 * 512, [128, 8, 512]) for i in range(2)]
        wv = wada_d.rearrange("(k p) n -> p k n", p=128)
        for m in range(18):
            wb = wbuf[m % 2]
            em.dma('pool', I('dma_start', out=wb, in_=wv[:, :, m * 512:(m + 1) * 512]),
                   writes=['wada%d' % (m % 2)])
            for fc in range(4):
                mm_group(em, ps_mods[:, m * 4 + fc, :],
                         [(wb[:, k, fc * 128:(fc + 1) * 128], scv[:, k, :]) for k in range(8)],
                         reads=['wada%d' % (m % 2), 'scv'], wres=psname)
        em.op('dve', I('memset', self.dummy[:], 0.0), writes=['dummy', 'gT', 'wada0', 'wada1'])
        for g in range(2):
            em.op('dve', I('tensor_tensor', out=self.modsT[:, :, g], in0=ps_mods[:, 0:72, g], in1=bad, op=ALU.add),
                  reads=[psname, 'bad'], writes=['modsT'])

    def load_mods(self, modsT_d):
        self.c.em.dma('sp', I('dma_start', out=self.modsT[:], in_=modsT_d), writes=['modsT'])

    def compute_coefs(self):
        em = self.c.em
        for idx, res_w in ((0, 0.5), (1, 1.0), (2, 0.5)):
            for g in range(2):
                em.op('dve', I('scalar_tensor_tensor',
                    out=self.A[:, idx, :, g], in0=self.modsT[:, (3 * idx + 1) * 8:(3 * idx + 2) * 8, g], scalar=1.0,
                    in1=self.normg[:, 2 * idx, :], op0=ALU.add, op1=ALU.mult),
                    reads=['modsT', 'normg'], writes=['coefA'])
                em.op('dve', I('scalar_tensor_tensor',
                    out=self.G[:, idx, :, g], in0=self.modsT[:, (3 * idx + 2) * 8:(3 * idx + 3) * 8, g], scalar=res_w,
                    in1=self.normg[:, 2 * idx + 1, :], op0=ALU.mult, op1=ALU.mult),
                    reads=['modsT', 'normg'], writes=['coefG'])

    def shift(self, idx, k, g):
        return self.modsT[:, 3 * idx * 8 + k, g:g + 1]


class FFNBufs:
    def __init__(self, c, tw, alloc=None, gT=None, nw1=2, nw2=2):
        if alloc is None:
            alloc = lambda name, shape, dt: c.sb(name, shape, dt)[:]
        self.tw = tw
        self.hT = alloc("hT", [128, 8, tw], BF16)
        self.gT = gT if gT is not None else alloc("gT", [128, NFC, tw], BF16)
        assert NFC * tw >= 2 * 8 * 512
        self.w1b = [alloc("w1b%d" % i, [128, 2 * 8 * 256], BF16) for i in range(nw1)]
        self.w2b = [alloc("w2b%d" % i, [128, NFC, 256], BF16) for i in range(nw2)]
        self.ysb = alloc("ysb", [128, 8, tw], F32)
        self.sq = alloc("sq", [128, 8, tw], BF16)
        self.tmp = [alloc("tmpf%d" % i, [128, 512], F32) for i in range(2)]
        self.rstd = alloc("rstd", [128, 512], F32)
        self.sa = [alloc("sa%d" % i, [128, 512], BF16) for i in range(2)]
        self.n_tmp = 0


def rms_rstd(c, cm, src_fn, w, ps_ss, sq, rstd, src_reads, tag):
    em = c.em
    for k in range(8):
        em.op('act', I('activation', out=sq[:, k, :w], in_=src_fn(k), func=AF.Square),
              reads=src_reads, writes=['sq'])
    mm_group(em, ps_ss[:, :w], [(cm.ones[:], sq[:, k, :w]) for k in range(8)], reads=['sq', 'ones'], wres=tag)
    em.op('act', I('activation', out=rstd[:, :w], in_=ps_ss[:, :w], func=AF.Sqrt, scale=1.0 / D, bias=cm.epst[:]),
          reads=[tag, 'epst'], writes=['rstd'])
    em.op('dve', I('reciprocal', out=rstd[:, :w], in_=rstd[:, :w]), reads=['rstd'], writes=['rstd'])


def sandwich_in(c, cm, fb, xT, idx, subs, ps_ss, dst, dst_res, ss_res='ps_ss'):
    em = c.em
    for (t0, w, g, off) in subs:
        rms_rstd(c, cm, lambda k: xT[:, k, t0:t0 + w], w, ps_ss, fb.sq, fb.rstd, xres(t0, w), ss_res)
        for k in range(8):
            tmp = fb.tmp[fb.n_tmp % 2]
            tr = 'tmpf%d' % (fb.n_tmp % 2)
            fb.n_tmp += 1
            em.op('dve', I('tensor_tensor', out=tmp[:, :w], in0=xT[:, k, t0:t0 + w], in1=fb.rstd[:, :w], op=ALU.mult),
                  reads=xres(t0, w) + ['rstd'], writes=[tr])
            em.op('act', I('activation', out=dst[:, k, off:off + w], in_=tmp[:, :w], func=AF.Identity,
                                                               scale=cm.A[:, idx, k, g:g + 1], bias=cm.shift(idx, k, g)),
                  reads=[tr, 'coefA', 'modsT'], writes=[dst_res])


def y_evac(c, fb, k, off, w, yp, yres):
    em = c.em
    em.op('dve', I('tensor_copy', out=fb.ysb[:, k, off:off + w], in_=yp), reads=[yres], writes=['ysb'])
    em.op('dve', I('tensor_tensor', out=fb.sq[:, k, off:off + w], in0=fb.ysb[:, k, off:off + w], in1=fb.ysb[:, k, off:off + w], op=ALU.mult),
          reads=['ysb'], writes=['sq'])


def sandwich_out(c, cm, fb, xT, idx, t0, w, g, off, ps_ss, ss_res='ps_ss'):
    em = c.em
    mm_group(em, ps_ss[:, :w], [(cm.ones[:], fb.sq[:, k, off:off + w]) for k in range(8)], reads=['sq', 'ones'], wres=ss_res)
    em.op('act', I('activation', out=fb.rstd[:, :w], in_=ps_ss[:, :w], func=AF.Sqrt, scale=1.0 / D, bias=cm.epst[:]),
          reads=[ss_res, 'epst'], writes=['rstd'])
    em.op('dve', I('reciprocal', out=fb.rstd[:, :w], in_=fb.rstd[:, :w]), reads=['rstd'], writes=['rstd'])
    for k in range(8):
        tmp = fb.tmp[fb.n_tmp % 2]
        tr = 'tmpf%d' % (fb.n_tmp % 2)
        fb.n_tmp += 1
        em.op('dve', I('scalar_tensor_tensor', out=tmp[:, :w], in0=fb.ysb[:, k, off:off + w], scalar=cm.G[:, idx, k, g:g + 1],
                                                                     in1=fb.rstd[:, :w], op0=ALU.mult, op1=ALU.mult),
              reads=['ysb', 'rstd', 'coefG'], writes=[tr])
        em.op('dve', I('tensor_tensor', out=xT[:, k, t0:t0 + w], in0=xT[:, k, t0:t0 + w], in1=tmp[:, :w], op=ALU.add),
              reads=[tr] + xres(t0, w), writes=xres(t0, w))


def ffn(c, cm, fb, xT, idx, tiles, w1_d, w2_d, pss, psnames=None):
    em = c.em
    ps_ss, ps_a, ps_b, ps_y = pss['ss'], pss['a'], pss['b'], pss['y']
    if psnames is None:
        psnames = {'ss': 'ps_ss', 'a': ['ps_a0', 'ps_a1'], 'b': ['ps_b0', 'ps_b1'], 'y': ['ps_y0', 'ps_y1']}
    w1v = w1_d.rearrange("(k p) (two f) -> p k two f", p=128, two=2)
    w2v = w2_d.rearrange("(f p) d -> p f d", p=128)
    nw1 = 0
    nw2 = 0
    nsa = 0
    ny = 0
    for tile in tiles:
        subs = []
        off = 0
        for (t0, w, g) in tile:
            subs.append((t0, w, g, off))
            off += w
        sandwich_in(c, cm, fb, xT, idx, subs, ps_ss, fb.hT, 'hT', ss_res=psnames['ss'])
        for fp in range(NFC // 2):
            wb = fb.w1b[nw1 % len(fb.w1b)].rearrange("p (a k f) -> p a k f", a=2, k=8)
            wr = 'w1b%d' % (nw1 % len(fb.w1b))
            nw1 += 1
            for two in range(2):
                em.dma('pool', I('dma_start', out=wb[:, two, :, :], in_=w1v[:, :, two, fp * 256:(fp + 1) * 256]), writes=[wr])
            for fi in range(2):
                fc = fp * 2 + fi
                for (t0, w, g, off) in subs:
                    pa = ps_a[nsa % 2]
                    pb = ps_b[nsa % 2]
                    ra, rb = psnames['a'][nsa % 2], psnames['b'][nsa % 2]
                    sa = fb.sa[nsa % 2]
                    rs = 'sa%d' % (nsa % 2)
                    nsa += 1
                    mm_group(em, pa[:, :w], [(wb[:, 0, k, fi * 128:(fi + 1) * 128], fb.hT[:, k, off:off + w]) for k in range(8)],
                             reads=[wr, 'hT'], wres=ra)
                    mm_group(em, pb[:, :w], [(wb[:, 1, k, fi * 128:(fi + 1) * 128], fb.hT[:, k, off:off + w]) for k in range(8)],
                             reads=[wr, 'hT'], wres=rb)
                    em.op('act', I('activation', out=sa[:, :w], in_=pa[:, :w], func=AF.Silu),
                          reads=[ra], writes=[rs])
                    em.op('dve', I('tensor_tensor',
                        out=fb.gT[:, fc, off:off + w], in0=sa[:, :w], in1=pb[:, :w], op=ALU.mult),
                        reads=[rs, rb], writes=['gT'])
        for piece in range(4):
            wb = fb.w2b[nw2 % len(fb.w2b)]
            wr = 'w2b%d' % (nw2 % len(fb.w2b))
            nw2 += 1
            em.dma('pool', I('dma_start', out=wb, in_=w2v[:, :, piece * 256:(piece + 1) * 256]), writes=[wr])
            for (t0, w, g, off) in subs:
                for kk in range(2):
                    k = piece * 2 + kk
                    py = ps_y[ny % 2]
                    ry = psnames['y'][ny % 2]
                    ny += 1
                    mm_group(em, py[:, :w], [(wb[:, f, kk * 128:(kk + 1) * 128], fb.gT[:, f, off:off + w]) for f in range(NFC)],
                             reads=[wr, 'gT'], wres=ry)
                    y_evac(c, fb, k, off, w, py[:, :w], ry)
        for (t0, w, g, off) in subs:
            sandwich_out(c, cm, fb, xT, idx, t0, w, g, off, ps_ss, ss_res=psnames['ss'])


def xres(t0, w):
    return ['x%d' % i for i in range(t0 // 128, (t0 + w + 127) // 128)]


def flatview(ap3, n0, shape):
    flat = ap3.rearrange("p a b -> p (a b)")
    n = 1
    for s in shape[1:]:
        n *= s
    v = flat[:, n0:n0 + n]
    if len(shape) == 2:
        return v
    if len(shape) == 3:
        return v.rearrange("p (a b) -> p a b", a=shape[1])
    return v.rearrange("p (a b c) -> p a b c", a=shape[1], b=shape[2])


def proj_phase(c, cm, fb, xT, tiles, win_d, ropeC_d, ropeS_d, pm_d, pss, qT_o, kvT_o, v_o, alloc=None, psnames=None):
    em = c.em
    ps_ss, ps_a, ps_b, ps_y = pss['ss'], pss['a'], pss['b'], pss['y']
    tw = fb.tw
    winv = win_d.rearrange("(k p) n -> p k n", p=128)
    if alloc is None:
        alloc = lambda name, shape, dt: c.sb(name, shape, dt)[:]
    if psnames is None:
        psnames = {'ss': 'ps_ss', 'a': ['ps_a0', 'ps_a1'], 'b': ['ps_b0', 'ps_b1'], 'y': ['ps_y0', 'ps_y1']}
    pmat = alloc("pmat", [128, 128], BF16)
    em.dma('sp', I('dma_start', out=pmat, in_=pm_d), writes=['pmat'])
    ct = [alloc("ropec%d" % i, [128, 512], F32) for i in range(2)]
    sn = [alloc("ropes%d" % i, [128, 512], F32) for i in range(2)]
    qb = [alloc("qb%d" % i, [128, 512], BF16) for i in range(2)]
    t2 = [alloc("t2_%d" % i, [128, 512], F32) for i in range(2)]
    qst = flatview(fb.gT, 0, [128, 6, tw])
    kvst = flatview(fb.gT, 6 * tw, [128, 8, tw])
    vst = flatview(fb.gT, 14 * tw, [128, tw // 128, 768])
    cnt = {'w': 0, 'p': 0, 'r': 0, 't': 0}

    def next_ps():
        i = cnt['p'] % 4
        cnt['p'] += 1
        return ([ps_a[0], ps_a[1], ps_b[0], ps_b[1]][i], (psnames['a'] + psnames['b'])[i])

    for tile in tiles:
        subs = []
        off = 0
        for (t0, w, g) in tile:
            subs.append((t0, w, g, off))
            off += w
        tww = off
        sandwich_in(c, cm, fb, xT, 1, subs, ps_ss, fb.hT, 'hT', ss_res=psnames['ss'])
        for piece in range(5):
            wb4 = fb.w1b[cnt['w'] % len(fb.w1b)]
            wr = 'w1b%d' % (cnt['w'] % len(fb.w1b))
            cnt['w'] += 1
            wb = wb4.rearrange("p (k n) -> p k n", k=8)
            em.dma('pool', I('dma_start', out=wb, in_=winv[:, :, piece * 512:(piece + 1) * 512]), writes=[wr])
            for (t0, w, g, off) in subs:
                if piece == 0:
                    fm = [(0, 'q', 0, 0.125, False), (1, 'q', 1, 0.125, False), (2, 'kv', 0, 1.0, False), (3, 'kv', 1, 1.0, False)]
                elif piece == 1:
                    fm = [(2, 'kv', 2, 1.0, False), (3, 'kv', 3, 1.0, False)]
                elif piece == 2:
                    fm = [(i, 'q', 2 + i, 0.125, True) for i in range(4)]
                elif piece == 3:
                    fm = [(i, 'kv', 4 + i, 1.0, True) for i in range(4)]
                else:
                    fm = []
                if piece in (2, 3):
                    ci = cnt['r'] % 2
                    cnt['r'] += 1
                    em.dma('sp', I('dma_start', out=ct[ci][:, :w], in_=ropeC_d[:, t0:t0 + w]), writes=['ropec%d' % ci])
                    em.dma('sp', I('dma_start', out=sn[ci][:, :w], in_=ropeS_d[:, t0:t0 + w]), writes=['ropes%d' % ci])
                for (lc, kind, oc, scale, rope) in fm:
                    pp, pr = next_ps()
                    mm_group(em, pp[:, :w], [(wb[:, k, lc * 128:(lc + 1) * 128], fb.hT[:, k, off:off + w]) for k in range(8)],
                             reads=[wr, 'hT'], wres=pr)
                    dst = (qst if kind == 'q' else kvst)[:, oc, off:off + w]
                    if not rope:
                        em.op('act', I('activation', out=dst, in_=pp[:, :w], func=AF.Copy, scale=scale),
                              reads=[pr], writes=['gT'])
                    else:
                        ti = cnt['t'] % 2
                        cnt['t'] += 1
                        em.op('act', I('activation', out=qb[ti][:, :w], in_=pp[:, :w], func=AF.Copy, scale=scale),
                              reads=[pr], writes=['qb%d' % ti])
                        py = ps_y[ti]
                        pyr = psnames['y'][ti]
                        mm_group(em, py[:, :w], [(pmat, qb[ti][:, :w])], reads=['pmat', 'qb%d' % ti], wres=pyr)
                        tmp = fb.tmp[ti]
                        em.op('dve', I('scalar_tensor_tensor',
                            out=tmp[:, :w], in0=pp[:, :w], scalar=scale, in1=ct[ci][:, :w], op0=ALU.mult, op1=ALU.mult),
                            reads=[pr, 'ropec%d' % ci], writes=['tmpf%d' % ti])
                        em.op('dve', I('tensor_tensor', out=t2[ti][:, :w], in0=py[:, :w], in1=sn[ci][:, :w], op=ALU.mult),
                              reads=[pyr, 'ropes%d' % ci], writes=['t2_%d' % ti])
                        em.op('dve', I('tensor_tensor', out=dst, in0=tmp[:, :w], in1=t2[ti][:, :w], op=ALU.add),
                              reads=['tmpf%d' % ti, 't2_%d' % ti], writes=['gT'])
                if piece in (1, 4):
                    c0, ncol, vo = (0, 256, 0) if piece == 1 else (0, 512, 256)
                    for tcn in range(w // 128):
                        pp, pr = next_ps()
                        tk = off + tcn * 128
                        mm_group(em, pp[:, :ncol], [(fb.hT[:, k, tk:tk + 128], wb[:, k, c0:c0 + ncol]) for k in range(8)],
                                 reads=[wr, 'hT'], wres=pr)
                        em.op('act', I('activation', out=vst[:, tk // 128, vo:vo + ncol], in_=pp[:, :ncol], func=AF.Copy),
                              reads=[pr], writes=['gT'])
        tile0 = tile[0][0]
        em.dma('sp', I('dma_start', out=qT_o[:, :, tile0:tile0 + tww], in_=qst[:, :, :tww]), reads=['gT'], writes=['qT_o'])
        em.dma('sp', I('dma_start', out=kvT_o[:, :, tile0:tile0 + tww], in_=kvst[:, :, :tww]), reads=['gT'], writes=['kvT_o'])
        em.dma('sp', I('dma_start',
            out=v_o[tile0:tile0 + tww, :].rearrange("(c p) n -> p c n", p=128), in_=vst[:, :tww // 128, :]), reads=['gT'], writes=['v_o'])


def make_tiles(n_lat, n_ctx, tw):
    subs = []
    t = 0
    while t < n_lat:
        w = min(512, n_lat - t)
        subs.append((t, w, 0))
        t += w
    t = 0
    while t < n_ctx:
        w = min(512, n_ctx - t)
        subs.append((n_lat + t, w, 1))
        t += w
    tiles = []
    cur = []
    room = tw
    for (t0, w, g) in subs:
        while w > 0:
            take = min(w, room)
            cur.append((t0, take, g))
            t0 += take
            w -= take
            room -= take
            if room == 0:
                tiles.append(cur)
                cur = []
                room = tw
    if cur:
        tiles.append(cur)
    return tiles


def build_part_a(n_lat=2048, n_ctx=256, tw=768, upto=3):
    NT = n_lat + n_ctx
    c = Ctx()
    em = c.em
    xT_d = c.din("xT", [128, 8, NT], F32)
    cvec_d = c.din("cvec", [128, 8, 2], F32)
    wada_d = c.din("w_ada", [D, 9 * D], F32)
    bada_d = c.din("badaT", [128, 72], F32)
    normg_d = c.din("normgT", [128, 6, 8], F32)
    w1_d = c.din("w1", [D, 2 * DFF], F32)
    w2_d = c.din("w2", [DFF, D], F32)
    win_d = c.din("w_in", [D, 2560], F32)
    ropeC_d = c.din("ropeC", [128, NT], F32)
    ropeS_d = c.din("ropeS", [128, NT], F32)
    pm_d = c.din("pmat", [128, 128], BF16)
    x1T_o = c.dout("x1T", [128, 8, NT], F32)
    mods_o = c.dout("modsT", [128, 72, 2], F32)
    qT_o = c.dout("qT", [128, 6, NT], BF16)
    kvT_o = c.dout("kvT", [128, 8, NT], BF16)
    v_o = c.dout("vtok", [NT, 768], BF16)

    xT = c.sb("xT_sb", [128, 8, NT], F32)
    cm = Common(c)
    fb = FFNBufs(c, tw)
    pss = {'ss': c.ps("ps_ss", [128, 512]), 'a': [c.ps("ps_a%d" % i, [128, 512]) for i in range(2)],
           'b': [c.ps("ps_b%d" % i, [128, 512]) for i in range(2)], 'y': [c.ps("ps_y%d" % i, [128, 512]) for i in range(2)]}
    ps_mods = c.ps("ps_mods", [128, 256, 2])
    tiles = make_tiles(n_lat, n_ctx, tw)
    for t in range(0, NT, 512):
        w = min(512, NT - t)
        em.dma('sp', I('dma_start', out=xT[:, :, t:t + w], in_=xT_d[:, :, t:t + w]), writes=xres(t, w))
    cm.load_normg(normg_d)
    cm.compute_mods(cvec_d, wada_d, bada_d, ps_mods, fb)
    cm.compute_coefs()
    em.dma('sp', I('dma_start', out=mods_o, in_=cm.modsT[:]), reads=['modsT'], writes=['mods_o'])
    if upto >= 2:
        ffn(c, cm, fb, xT, 0, tiles, w1_d, w2_d, pss)
    em.dma('sp', I('dma_start', out=x1T_o, in_=xT[:]), reads=xres(0, NT), writes=['x1T_o'])
    if upto >= 3:
        proj_phase(c, cm, fb, xT, tiles, win_d, ropeC_d, ropeS_d, pm_d, pss, qT_o, kvT_o, v_o)
    print("part A instructions:", em.ninst)
    return c.done()


def rope_tables(pos):
    pos = np.asarray(pos)
    row = (pos // GRID_W).astype(np.float32)
    col = (pos % GRID_W).astype(np.float32)
    n_freq = 16
    inv_freq = np.power(np.float32(10000.0), -np.arange(n_freq, dtype=np.float32) / np.float32(n_freq)).astype(np.float32)
    ang = np.concatenate([row[:, None] * inv_freq, col[:, None] * inv_freq], axis=-1).astype(np.float32)
    return np.cos(ang).astype(np.float32), np.sin(ang).astype(np.float32)


def rope_feature_major(cos, sin, n_ctx):
    n = cos.shape[0]
    C = np.ones((128, n + n_ctx), np.float32)
    S = np.zeros((128, n + n_ctx), np.float32)
    p = np.arange(128)
    C[:, :n] = cos.T[p % 32]
    sign = np.where((p % 64) < 32, -1.0, 1.0).astype(np.float32)
    S[:, :n] = sin.T[p % 32] * sign[:, None]
    return C, S


def rope_pmat():
    pm = np.zeros((128, 128), np.float32)
    for po in range(128):
        pi = po + 32 if (po % 64) < 32 else po - 32
        pm[pi, po] = 1.0
    return pm.astype(NPBF)


class Arena:
    def __init__(self, c, nelem):
        self.c = c
        self.t = c.sb("arena", [128, nelem], BF16)
        self.n = nelem
        self.off = 0
        self.gen = 0

    def alloc(self, name, shape, dt):
        n = 1
        for s in shape[1:]:
            n *= s
        if dt == F32:
            n *= 2
        self.off = (self.off + 15) // 16 * 16
        assert self.off + n <= self.n, "arena overflow %s: need %d have %d" % (name, self.off + n, self.n)
        v = self.t[:, self.off:self.off + n]
        self.off += n
        if dt == F32:
            v = v.bitcast(F32)
        if len(shape) == 3:
            v = v.rearrange("p (a b) -> p a b", a=shape[1])
        elif len(shape) == 4:
            v = v.rearrange("p (a b c) -> p a b c", a=shape[1], b=shape[2])
        return v

    def reset(self, to_zero=False):
        self.c.em.barrier()
        if to_zero:
            self.base = 0
        self.off = getattr(self, 'base', 0)
        self.gen += 1

    def set_base(self):
        self.base = self.off


def _barrier(self):
    toks = [(('c', e), self.count[e]) for e in self.ENGS if self.count[e]]
    toks += [(('d', i), cc) for i, cc in enumerate(self.dcount) if cc]
    for eng in self.ENGS:
        waits = []
        wd = self.waited[eng]
        for k, v in toks:
            if wd.get(k, 0) < v:
                wd[k] = v
                waits.append((k, v))
        if waits:
            self.prog[eng].append((waits, None, None))


Emitter.barrier = _barrier


def na_attention(c, ar, P, mixT, q_d, kT_src, v_src, n_kc_tot, qchunks, bias_d, ident, col0, shiftt):
    em = c.em
    NK = n_kc_tot * 128
    kT = ar.alloc("na_kT", [128, 2, NK], BF16)
    vv = ar.alloc("na_v", [128, n_kc_tot, 4, 65], BF16)
    nq = len(qchunks)
    qT = ar.alloc("na_q", [128, 2, nq * 128], BF16)
    g = ar.gen
    rk, rv, rq = 'na_kT%d' % g, 'na_v%d' % g, 'na_q%d' % g
    em.dma('sp', I('dma_start', out=kT, in_=kT_src), writes=[rk])
    em.op('dve', I('memset', vv[:, :, :, 64:65], 1.0), writes=[rv])
    for h in range(4):
        em.dma('sp', I('dma_start', out=vv[:, :, h, 0:64], in_=v_src[:, h * 64:(h + 1) * 64].rearrange("(c p) e -> p c e", p=128)), writes=[rv])
    q0 = qchunks[0][0]
    em.dma('sp', I('dma_start', out=qT, in_=q_d[:, 0:2, q0:q0 + nq * 128]), writes=[rq])
    bias = [ar.alloc("na_bias%d" % i, [128, 4, 6, 128], F32) for i in range(2)]
    ssb = [ar.alloc("na_s%d" % i, [128, 6, 128], F32) for i in range(2)]
    pT = [ar.alloc("na_p%d" % i, [128, 8, 128], BF16) for i in range(2)]
    atok = ar.alloc("na_atok", [128, 256], BF16)
    rec = ar.alloc("na_rec", [128, 4], F32)
    cnt = 0
    for qi, (qcol, kcs, bvar, nb) in enumerate(qchunks):
        bi = qi % 2
        if bvar is not None:
            for h in range(4):
                em.dma('sp', I('dma_start', out=bias[bi][:, h, :, :], in_=bias_d[bvar, h].rearrange("j k q -> k j q")), writes=['na_bias%d_%d' % (bi, g)])
        nk = len(kcs)
        for h in range(4):
            hp = (h % 2) * 64
            hc = h // 2
            si = cnt % 2
            cnt += 1
            SX, SY = P['S'][si][:, 0, :].rearrange("p (j q) -> p j q", j=4), P['S'][si][:, 1, :].rearrange("p (j q) -> p j q", j=4)
            rsx, rsy = 'psS%d_0' % si, 'psS%d_1' % si
            qap = qT[hp:hp + 64, hc, qi * 128:(qi + 1) * 128]

            def sdst(jj):
                return (SX[:, jj, :], rsx) if jj < 4 else (SY[:, jj - 4, :], rsy)
            for jj, kc in enumerate(kcs):
                d, r = sdst(jj)
                mm_group(em, d, [(kT[hp:hp + 64, hc, kc * 128:(kc + 1) * 128], qap)], reads=[rk, rq], wres=r)
            sb_ = ssb[si]
            pt = pT[si]
            rs_, rp_ = 'na_s%d_%d' % (si, g), 'na_p%d_%d' % (si, g)
            if nb > 0:
                n1 = min(nb, 4)
                em.op('dve', I('tensor_tensor', out=sb_[:, 0:n1, :], in0=SX[:, 0:n1, :], in1=bias[bi][:, h, 0:n1, :], op=ALU.add),
                      reads=[rsx, 'na_bias%d_%d' % (bi, g)], writes=[rs_])
                if nb > 4:
                    em.op('dve', I('tensor_tensor', out=sb_[:, 4:nb, :], in0=SY[:, 0:nb - 4, :], in1=bias[bi][:, h, 4:nb, :], op=ALU.add),
                          reads=[rsy, 'na_bias%d_%d' % (bi, g)], writes=[rs_])
                em.op('act', I('activation', out=pt[:, 0:nb, :], in_=sb_[:, 0:nb, :], func=AF.Exp, bias=shiftt[:]), reads=[rs_, 'shiftt'], writes=[rp_])
            j = nb
            while j < nk:
                if j < 4:
                    e_ = min(nk, 4)
                    em.op('act', I('activation', out=pt[:, j:e_, :], in_=SX[:, j:e_, :], func=AF.Exp, bias=shiftt[:]), reads=[rsx, 'shiftt'], writes=[rp_])
                else:
                    e_ = nk
                    em.op('act', I('activation', out=pt[:, j:e_, :], in_=SY[:, j - 4:e_ - 4, :], func=AF.Exp, bias=shiftt[:]), reads=[rsy, 'shiftt'], writes=[rp_])
                j = e_
            O = P['O'][:, 0, 0:4 * 65].rearrange("p (h e) -> p h e", h=4)
            mm_group(em, O[:, h, :], [(pt[:, jj, :], vv[:, kc, h, :]) for jj, kc in enumerate(kcs)], reads=[rp_, rv], wres='psU1_0')
        em.op('dve', I('reciprocal', out=rec[:, :], in_=O[:, :, 64]), reads=['psU1_0'], writes=['na_rec%d' % g])
        em.op('dve', I('tensor_tensor', out=atok[:, :].rearrange("p (h e) -> p h e", h=4), in0=O[:, :, 0:64],
                       in1=rec[:, :].unsqueeze(2).to_broadcast([128, 4, 64]), op=ALU.mult),
              reads=['psU1_0', 'na_rec%d' % g], writes=['na_atok%d' % g])
        T = P['T']
        for ch in range(2):
            em.op('pe', I('transpose', out=T[:, ch * 128:(ch + 1) * 128], in_=atok[:, ch * 128:(ch + 1) * 128], identity=ident[:]),
                  reads=['na_atok%d' % g, 'ident'], writes=['psU2_0'])
        col = col0 + qi * 128
        em.op('act', I('activation', out=mixT[:, 0:2, col:col + 128], in_=T[:, 0:256].rearrange("p (c q) -> p c q", c=2), func=AF.Copy),
              reads=['psU2_0'], writes=['mix%d' % (col // 128)])


def pool_mixer(c, ar, P, mixT, pin_src, rcnt_src, n_tok, pwbd, pscale, col0):
    em = c.em
    g = ar.gen
    W = 512
    ubuf = [ar.alloc("pl_u%d" % i, [128, 2, W + 16], BF16) for i in range(2)]
    rc = [ar.alloc("pl_rc%d" % i, [128, 2, W], F32) for i in range(2)]
    s2 = ar.alloc("pl_s2", [128, 2, W + 16], F32)
    s4 = ar.alloc("pl_s4", [128, 2, W + 16], F32)
    s8 = ar.alloc("pl_s8", [128, 2, W + 16], F32)
    s16 = ar.alloc("pl_s16", [128, 2, W + 16], F32)
    pm = ar.alloc("pl_pm", [128, 2, W], F32)
    pb = ar.alloc("pl_pb", [128, 2, W], BF16)
    it = 0
    for t0 in range(0, n_tok, W):
        w = min(W, n_tok - t0)
        u = ubuf[it % 2]
        r = rc[it % 2]
        ru, rr = 'pl_u%d_%d' % (it % 2, g), 'pl_rc%d_%d' % (it % 2, g)
        it += 1
        em.dma('sp', I('dma_start', out=u[:, :, 0:w + 16], in_=pin_src[:, :, t0:t0 + w + 16]), writes=[ru])
        em.dma('sp', I('dma_start', out=r[:, :, 0:w], in_=rcnt_src[:, :, t0:t0 + w]), writes=[rr])
        L = w + 16
        em.op('dve', I('tensor_tensor', out=s2[:, :, 1:L], in0=u[:, :, 0:L - 1], in1=u[:, :, 1:L], op=ALU.add), reads=[ru], writes=['pl_s2_%d' % g])
        em.op('dve', I('tensor_tensor', out=s4[:, :, 2:L - 1], in0=s2[:, :, 1:L - 2], in1=s2[:, :, 3:L], op=ALU.add), reads=['pl_s2_%d' % g], writes=['pl_s4_%d' % g])
        em.op('dve', I('tensor_tensor', out=s8[:, :, 4:L - 3], in0=s4[:, :, 2:L - 5], in1=s4[:, :, 6:L - 1], op=ALU.add), reads=['pl_s4_%d' % g], writes=['pl_s8_%d' % g])
        em.op('dve', I('tensor_tensor', out=s16[:, :, 8:L - 7], in0=s8[:, :, 4:L - 11], in1=s8[:, :, 12:L - 3], op=ALU.add), reads=['pl_s8_%d' % g], writes=['pl_s16_%d' % g])
        for (ch, p0, lvl, lr) in ((0, 0, s2, 'pl_s2_%d' % g), (0, 64, s4, 'pl_s4_%d' % g), (1, 0, s8, 'pl_s8_%d' % g), (1, 64, s16, 'pl_s16_%d' % g)):
            em.op('dve', I('tensor_tensor', out=pm[p0:p0 + 64, ch, 0:w], in0=lvl[p0:p0 + 64, ch, 8:8 + w], in1=r[p0:p0 + 64, ch, 0:w], op=ALU.mult),
                  reads=[lr, rr], writes=['pl_pm_%d' % g])
            em.op('dve', I('tensor_tensor', out=pb[p0:p0 + 64, ch, 0:w], in0=pm[p0:p0 + 64, ch, 0:w], in1=u[p0:p0 + 64, ch, 8:8 + w], op=ALU.subtract),
                  reads=['pl_pm_%d' % g, ru], writes=['pl_pb_%d' % g])
        for ch in range(2):
            pp = P['S'][ch][:, 0, :]
            pr = 'psS%d_0' % ch
            mm_group(em, pp[:, :w], [(pwbd[:, ch, :], pb[:, ch, 0:w])], reads=['pl_pb_%d' % g, 'pwbd'], wres=pr)
            col = col0 + t0
            em.op('act', I('activation', out=mixT[:, 2 + ch, col:col + w], in_=pp[:, :w], func=AF.Copy, scale=pscale[:, ch:ch + 1]),
                  reads=[pr, 'pscale'], writes=['mix%d' % i for i in range(col // 128, (col + w) // 128)])


def diff_attention(c, ar, P, mixT, q_d, qtiles, kT_src, v_src, n_kc, lamt, gsub, ident, shiftt, epst, heads=range(4), loaders=None):
    em = c.em
    g = ar.gen
    NK = n_kc * 128
    kT = [ar.alloc("df_kT%d" % i, [128, NK], BF16) for i in range(2)]
    vv = [ar.alloc("df_v%d" % i, [128, n_kc, 129], BF16) for i in range(2)]
    qmax = max(w for _, w in qtiles)
    qb = [ar.alloc("df_q%d" % i, [128, qmax], BF16) for i in range(2)]
    p12 = [ar.alloc("df_p%d" % i, [128, 2, 512], BF16) for i in range(2)]
    rr = ar.alloc("df_r", [128, 2, 4], F32)
    tt = ar.alloc("df_t", [128, 4, 128], F32)
    oo = ar.alloc("df_o", [128, 4, 128], F32)
    osq = ar.alloc("df_osq", [128, 4, 128], F32)
    ss = ar.alloc("df_ss", [128, 4], F32)
    cb = ar.alloc("df_cb", [128, 4, 128], BF16)
    for i in range(2):
        em.op('dve', I('memset', vv[i][:, :, 128:129], 1.0), writes=['df_v%d_%d' % (i, g)])
    nq = 0
    ns = 0
    for hi, h in enumerate(heads):
        kb, vb = kT[hi % 2], vv[hi % 2]
        rk, rv = 'df_kT%d_%d' % (hi % 2, g), 'df_v%d_%d' % (hi % 2, g)
        if loaders is not None:
            loaders(h, kb, vb, rk, rv)
        else:
            em.dma('sp', I('dma_start', out=kb, in_=kT_src[:, h, :]), writes=[rk])
            step = 16
            for c0 in range(0, n_kc, step):
                c1 = min(n_kc, c0 + step)
                em.dma('sp', I('dma_start', out=vb[:, c0:c1, 0:128],
                               in_=v_src[c0 * 128:c1 * 128, h * 128:(h + 1) * 128].rearrange("(c p) e -> p c e", p=128)), writes=[rv])
        for (t0, w) in qtiles:
            qq = qb[nq % 2]
            rq = 'df_q%d_%d' % (nq % 2, g)
            nq += 1
            em.dma('sp', I('dma_start', out=qq[:, :w], in_=q_d[:, 2 + h, t0:t0 + w]), writes=[rq])
            nqc = w // 128
            U1 = P['U1'][:].rearrange("p a (c e) -> p (a c) e", c=2)
            U2 = P['U2'][:].rearrange("p a (c e) -> p (a c) e", c=2)
            def issue_S(kc, slot):
                S = P['S'][slot]
                rs = ['psS%d_0' % slot, 'psS%d_1' % slot]
                for half in range(2):
                    hp = half * 64
                    mm_group(em, S[:, half, :w], [(kb[hp:hp + 64, kc * 128:(kc + 1) * 128], qq[hp:hp + 64, :w])], reads=[rk, rq], wres=rs[half])

            def issue_exp_pv(kc, slot):
                S = P['S'][slot]
                rs = ['psS%d_0' % slot, 'psS%d_1' % slot]
                pt = p12[slot]
                rp = 'df_p%d_%d' % (slot, g)
                em.op('act', I('activation', out=pt[:, :, :w], in_=S[:, :, :w], func=AF.Exp, bias=shiftt[:]), reads=rs + ['shiftt'], writes=[rp])
                for half, (U, ru) in enumerate(((U1, 'psU1'), (U2, 'psU2'))):
                    for qc in range(nqc):
                        em.op('pe', I('matmul', U[:, qc, 0:129], lhsT=pt[:, half, qc * 128:(qc + 1) * 128], rhs=vb[:, kc, :],
                                      start=(kc == 0 and qc % 2 == 0), stop=(kc == n_kc - 1), skip_group_check=True),
                              reads=[rp, rv], writes=['%s_%d' % (ru, qc // 2)], inc=(qc == nqc - 1))
            issue_S(0, ns % 2)
            for kc in range(n_kc):
                slot = ns % 2
                ns += 1
                if kc + 1 < n_kc:
                    issue_S(kc + 1, ns % 2)
                issue_exp_pv(kc, slot)
            ru1 = ['psU1_0', 'psU1_1'][:(nqc + 1) // 2]
            ru2 = ['psU2_0', 'psU2_1'][:(nqc + 1) // 2]
            rg = 'df_ep%d' % g
            em.op('dve', I('reciprocal', out=rr[:, 0, :nqc], in_=U1[:, :nqc, 128]), reads=ru1, writes=[rg])
            em.op('dve', I('reciprocal', out=rr[:, 1, :nqc], in_=U2[:, :nqc, 128]), reads=ru2, writes=[rg])
            em.op('dve', I('tensor_scalar', out=rr[:, 1, :nqc], in0=rr[:, 1, :nqc], scalar1=lamt[:, 0:1], scalar2=None, op0=ALU.mult), reads=[rg, 'lamt'], writes=[rg])
            em.op('dve', I('tensor_tensor', out=tt[:, :nqc, :], in0=U2[:, :nqc, 0:128], in1=rr[:, 1, :nqc].unsqueeze(2).to_broadcast([128, nqc, 128]), op=ALU.mult),
                  reads=ru2 + [rg], writes=[rg])
            em.op('dve', I('tensor_tensor', out=oo[:, :nqc, :], in0=U1[:, :nqc, 0:128], in1=rr[:, 0, :nqc].unsqueeze(2).to_broadcast([128, nqc, 128]), op=ALU.mult),
                  reads=ru1 + [rg], writes=[rg])
            em.op('dve', I('tensor_tensor', out=oo[:, :nqc, :], in0=oo[:, :nqc, :], in1=tt[:, :nqc, :], op=ALU.subtract), reads=[rg], writes=[rg])
            em.op('dve', I('tensor_tensor', out=osq[:, :nqc, :], in0=oo[:, :nqc, :], in1=oo[:, :nqc, :], op=ALU.mult), reads=[rg], writes=[rg])
            em.op('dve', I('reduce_sum', out=ss[:, :nqc], in_=osq[:, :nqc, :], axis=AX.X), reads=[rg], writes=[rg])
            em.op('act', I('activation', out=ss[:, :nqc], in_=ss[:, :nqc], func=AF.Sqrt, scale=1.0 / 128, bias=epst[:]), reads=[rg, 'epst'], writes=[rg])
            em.op('dve', I('reciprocal', out=ss[:, :nqc], in_=ss[:, :nqc]), reads=[rg], writes=[rg])
            em.op('dve', I('tensor_tensor', out=oo[:, :nqc, :], in0=oo[:, :nqc, :], in1=ss[:, :nqc].unsqueeze(2).to_broadcast([128, nqc, 128]), op=ALU.mult),
                  reads=[rg], writes=[rg])
            em.op('dve', I('tensor_tensor', out=cb[:, :nqc, :], in0=oo[:, :nqc, :], in1=gsub[:, :].unsqueeze(1).to_broadcast([128, nqc, 128]), op=ALU.mult),
                  reads=[rg, 'gsub'], writes=['df_cb%d' % g])
            T = P['T']
            for qc in range(nqc):
                em.op('pe', I('transpose', out=T[:, qc * 128:(qc + 1) * 128], in_=cb[:, qc, :], identity=ident[:]),
                      reads=['df_cb%d' % g, 'ident'], writes=['psU2_0'])
            em.op('act', I('activation', out=mixT[:, 4 + h, t0:t0 + w], in_=T[:, 0:w], func=AF.Copy),
                  reads=['psU2_0'], writes=['mix%d' % i for i in range(t0 // 128, (t0 + w) // 128)])


def na_local_chunks(i, nq):
    if i == 0:
        return 0, 6
    if i == nq - 1:
        return i - 1, 6
    return i, 5


def build_part_b(n_lat=2048, n_ctx=256, tw=768, n_kc_diff=66, na_variants=None, lam_init=0.2, ctx_out=True, n_halo_kc=None):
    NT = n_lat + n_ctx
    nq = n_lat // 128
    if n_halo_kc is None:
        n_halo_kc = nq + 4
    if na_variants is None:
        na_variants = [0] * nq
    c = Ctx()
    em = c.em
    x1T_d = c.din("x1T", [128, 8, NT], F32)
    mods_d = c.din("modsT", [128, 72, 2], F32)
    normg_d = c.din("normgT", [128, 6, 8], F32)
    q_d = c.din("qT", [128, 6, NT], BF16)
    nakT_d = c.din("na_kT", [128, 2, (n_halo_kc + n_ctx // 128) * 128], BF16)
    nav_d = c.din("na_v", [(n_halo_kc + n_ctx // 128) * 128, 256], BF16)
    nakTc_d = c.din("na_kTc", [128, 2, n_ctx], BF16)
    navc_d = c.din("na_vc", [n_ctx, 256], BF16)
    nvar = max(na_variants) + 1
    bias_d = c.din("na_bias", [nvar, 4, 6, 128, 128], F32)
    pin_d = c.din("pinT", [128, 2, n_lat + 16], BF16)
    rcnt_d = c.din("rcnt", [128, 2, n_lat], F32)
    pinc_d = c.din("pinTc", [128, 2, n_ctx + 16], BF16)
    rcntc_d = c.din("rcntc", [128, 2, n_ctx], F32)
    pwbd_d = c.din("pwbd", [128, 2, 128], BF16)
    pscale_d = c.din("pscaleT", [128, 2], F32)
    dkT_d = c.din("dkT", [128, 4, n_kc_diff * 128], BF16)
    dv_d = c.din("dv", [n_kc_diff * 128, 512], BF16)
    dlam_d = c.din("dlam", [128, 256], F32)
    subg_d = c.din("subg", [128, 128], F32)
    ident_d = c.din("ident", [128, 128], BF16)
    wout_d = c.din("w_out", [D, D], F32)
    w1_d = c.din("w1", [D, 2 * DFF], F32)
    w2_d = c.din("w2", [DFF, D], F32)
    x2T_o = c.dout("x2T", [128, 8, NT], F32)

    xT = c.sb("xT_sb", [128, 8, NT], F32)
    mixT_t = c.sb("mixT", [128, 8, NT], BF16)
    mixT = mixT_t[:]
    cm = Common(c)
    ident = c.sb("ident", [128, 128], BF16)
    pwbd = c.sb("pwbd", [128, 2, 128], BF16)
    pscale = c.sb("pscale", [128, 2], F32)
    dlam = c.sb("dlam", [128, 256], F32)
    lamw = c.sb("lamw", [128, 4], F32)
    lamt = c.sb("lamt", [128, 1], F32)
    gsub = c.sb("gsub", [128, 128], F32)
    shiftt = c.sb("shiftt", [128, 1], F32)
    S0 = c.ps("S0", [128, 2, 512])
    S1 = c.ps("S1", [128, 2, 512])
    U1 = c.ps("U1", [128, 2, 512])
    U2 = c.ps("U2", [128, 2, 512])
    P = {'S': [S0, S1], 'U1': U1, 'U2': U2, 'O': U1, 'T': U2[:, 0, :].bitcast(BF16)}
    arena_n = (c.nc.sbuf_bytes_remaining - 2048) // 2 // 16 * 16
    ar = Arena(c, arena_n)
    print("arena elems", arena_n)

    for t in range(0, NT, 512):
        w = min(512, NT - t)
        em.dma('sp', I('dma_start', out=xT[:, :, t:t + w], in_=x1T_d[:, :, t:t + w]), writes=xres(t, w))
    cm.load_normg(normg_d)
    cm.load_mods(mods_d)
    cm.compute_coefs()
    em.dma('sp', I('dma_start', out=ident[:], in_=ident_d), writes=['ident'])
    em.dma('sp', I('dma_start', out=pwbd[:], in_=pwbd_d), writes=['pwbd'])
    em.dma('sp', I('dma_start', out=pscale[:], in_=pscale_d), writes=['pscale'])
    em.dma('sp', I('dma_start', out=dlam[:], in_=dlam_d), writes=['dlam'])
    em.dma('sp', I('dma_start', out=gsub[:], in_=subg_d), writes=['gsub'])
    em.op('dve', I('memset', shiftt[:], EXP_SHIFT), writes=['shiftt'])
    em.op('dve', I('tensor_scalar', out=gsub[:], in0=gsub[:], scalar1=float(1.0 - lam_init), scalar2=None, op0=ALU.mult), reads=['gsub'], writes=['gsub'])
    em.op('dve', I('tensor_tensor', out=dlam[:, 0:64], in0=dlam[:, 0:64], in1=dlam[:, 64:128], op=ALU.mult), reads=['dlam'], writes=['dlam'])
    em.op('dve', I('tensor_tensor', out=dlam[:, 128:192], in0=dlam[:, 128:192], in1=dlam[:, 192:256], op=ALU.mult), reads=['dlam'], writes=['dlam'])
    em.op('dve', I('reduce_sum', out=lamw[:, 0:2], in_=dlam[:].rearrange("p (a b) -> p a b", a=2)[:, :, 0:64], axis=AX.X), reads=['dlam'], writes=['lamw'])
    em.op('act', I('activation', out=lamw[:, 2:4], in_=lamw[:, 0:2], func=AF.Exp), reads=['lamw'], writes=['lamw'])
    em.op('dve', I('tensor_tensor', out=lamt[:], in0=lamw[:, 2:3], in1=lamw[:, 3:4], op=ALU.subtract), reads=['lamw'], writes=['lamt'])
    em.op('dve', I('tensor_scalar', out=lamt[:], in0=lamt[:], scalar1=float(lam_init), scalar2=None, op0=ALU.add), reads=['lamt'], writes=['lamt'])

    nkc_tot = n_halo_kc + n_ctx // 128
    ctx_kcs = [n_halo_kc + i for i in range(n_ctx // 128)]
    qch = []
    for i in range(nq):
        k0, nl = na_local_chunks(i, nq)
        qch.append((i * 128, [k0 + j for j in range(nl)] + ctx_kcs, na_variants[i], nl))
    na_attention(c, ar, P, mixT, q_d, nakT_d, nav_d, nkc_tot, qch, bias_d, ident, 0, shiftt)
    if ctx_out:
        ar.reset()
        qch = [(n_lat + i * 128, list(range(n_ctx // 128)), None, 0) for i in range(n_ctx // 128)]
        na_attention(c, ar, P, mixT, q_d, nakTc_d, navc_d, n_ctx // 128, qch, bias_d, ident, n_lat, shiftt)
    ar.reset()
    pool_mixer(c, ar, P, mixT, pin_d, rcnt_d, n_lat, pwbd, pscale, 0)
    if ctx_out:
        ar.reset()
        pool_mixer(c, ar, P, mixT, pinc_d, rcntc_d, n_ctx, pwbd, pscale, n_lat)
    ar.reset()
    qtiles = [(t, min(512, n_lat - t)) for t in range(0, n_lat, 512)]
    diff_attention(c, ar, P, mixT, q_d, qtiles, dkT_d, dv_d, n_kc_diff, lamt, gsub, ident, shiftt, cm.epst)
    if ctx_out:
        ar.reset()
        ctx0 = n_kc_diff - n_ctx // 128
        qtiles = [(n_lat, n_ctx)]
        diff_attention(c, ar, P, mixT, q_d, qtiles, dkT_d[:, :, ctx0 * 128:], dv_d[ctx0 * 128:, :], n_ctx // 128, lamt, gsub, ident, shiftt, cm.epst)
    ar.reset()
    n_mix = NT if ctx_out else n_lat
    wo = ar.alloc("wo", [128, 8, D], BF16)
    em.dma('pool', I('dma_start', out=wo, in_=wout_d.rearrange("(k p) n -> p k n", p=128)), writes=['wo'])

    class YB:
        pass
    yb = YB()
    yb.ysb = ar.alloc("ysb", [128, 8, 512], F32)
    yb.sq = ar.alloc("sq", [128, 8, 512], BF16)
    yb.tmp = [ar.alloc("tmpf%d" % i, [128, 512], F32) for i in range(2)]
    yb.rstd = ar.alloc("rstd", [128, 512], F32)
    yb.n_tmp = 0
    pss = {'ss': U2[:, 1, :], 'a': [S0[:, 0, :], S0[:, 1, :]], 'b': [S1[:, 0, :], S1[:, 1, :]], 'y': [U1[:, 0, :], U1[:, 1, :]]}
    ny = 0
    for (t0, w, g) in [s_ for tile in make_tiles(n_lat, n_ctx if ctx_out else 0, 512) for s_ in tile]:
        for k in range(8):
            py = pss['y'][ny % 2]
            ry = 'psU1_%d' % (ny % 2)
            ny += 1
            mm_group(em, py[:, :w], [(wo[:, kk, k * 128:(k + 1) * 128], mixT[:, kk, t0:t0 + w]) for kk in range(8)],
                     reads=['wo'] + ['mix%d' % i for i in range(t0 // 128, (t0 + w) // 128)], wres=ry)
            y_evac(c, yb, k, 0, w, py[:, :w], ry)
        sandwich_out(c, cm, yb, xT, 1, t0, w, g, 0, pss['ss'], ss_res='psU2_1')
    ar.reset()
    gT = mixT_t[:].rearrange("p a b -> p (a b)")[:, 0:NFC * tw].rearrange("p (a b) -> p a b", a=NFC) if 8 * NT >= NFC * tw else None
    fb = FFNBufs(c, tw, alloc=ar.alloc, gT=gT)
    tiles = make_tiles(n_lat, n_ctx if ctx_out else 0, tw)
    ffn(c, cm, fb, xT, 2, tiles, w1_d, w2_d, pss, psnames={'ss': 'psU2_1', 'a': ['psS0_0', 'psS0_1'], 'b': ['psS1_0', 'psS1_1'], 'y': ['psU1_0', 'psU1_1']})
    em.dma('sp', I('dma_start', out=x2T_o, in_=xT[:]), reads=xres(0, NT), writes=['x2T_o'])
    print("part B instructions:", em.ninst)
    return c.done()


def na_bias_tiles(rpb, rows_total, q_row0, key_row0, nj=6):
    kr = np.arange(2)[:, None, None, None]
    kc = np.arange(64)[None, :, None, None]
    qr = np.arange(2)[None, None, :, None]
    qc = np.arange(64)[None, None, None, :]
    q_row = q_row0 + qr
    rs = np.clip(q_row - 4, 0, rows_total - 8)
    cs = np.clip(qc - 8, 0, 64 - 16)
    out = np.full((4, nj, 2, 64, 2, 64), NEG, np.float32)
    for j in range(nj):
        key_row = key_row0 + 2 * j + kr
        valid = (key_row >= rs) & (key_row < rs + 8) & (kc >= cs) & (kc < cs + 16) & (key_row >= 0) & (key_row < rows_total)
        valid = np.broadcast_to(valid, (2, 64, 2, 64))
        dr = np.clip(np.broadcast_to(key_row - q_row + 7, (2, 64, 2, 64)), 0, 14)
        dc = np.clip(np.broadcast_to(kc - qc, (2, 64, 2, 64)), -15, 15) + 15
        for h in range(4):
            out[h, j] = np.where(valid, rpb[h][dr, dc], np.float32(NEG))
    return out.reshape(4, nj, 128, 128)


def pool_rcount(t_global, L):
    n = len(t_global)
    out = np.zeros((128, 2, n), np.float32)
    for gi, wdw in enumerate((2, 4, 8, 16)):
        half = wdw // 2
        lo = np.clip(t_global - half, 0, L)
        hi = np.clip(t_global + half, 0, L)
        rc = (1.0 / (hi - lo).astype(np.float32)).astype(np.float32)
        out[(gi % 2) * 64:(gi % 2) * 64 + 64, gi // 2, :] = rc[None, :]
    return out


def halo_cols(arrT, t0, n, halo, L):
    out = np.zeros(arrT.shape[:-1] + (n + 2 * halo,), arrT.dtype)
    a = max(0, t0 - halo)
    b = min(L, t0 + n + halo)
    out[..., a - (t0 - halo):b - (t0 - halo)] = arrT[..., a:b]
    return out


def pool_blockdiag(pool_w):
    out = np.zeros((128, 2, 128), np.float32)
    for gi in range(4):
        p0 = (gi % 2) * 64
        out[p0:p0 + 64, gi // 2, p0:p0 + 64] = pool_w[gi]
    return out.astype(NPBF)


N_LAT = 2048
NT_FULL = N_LAT + CTX
_PROGS = {}


def _fm(a):
    T, F = a.shape
    return np.ascontiguousarray(a.reshape(T, F // 128, 128).transpose(2, 1, 0))


def _unfm(aT):
    return np.ascontiguousarray(aT.transpose(2, 1, 0)).reshape(aT.shape[2], -1)


def _lay_vec(v):
    return np.ascontiguousarray(v.reshape(-1, 128).T)


def _prog(key, fn):
    if key not in _PROGS:
        _PROGS[key] = fn()
    return _PROGS[key]


def kernel_unfused(x, c, ctx, c_ctx, w_ada, b_ada, norm_g, ffn_w1, ffn_w2, w_in, w_out, na_rpb, pool_w, pool_scale, diff_lambda,
           diff_subln_g):
    f32 = np.float32
    x = np.asarray(x, f32)
    ctx = np.asarray(ctx, f32)
    cvals = np.asarray(c, f32)
    c_ctx = np.asarray(c_ctx, f32)
    ncores = 8
    depth = w_ada.shape[0]
    rows_total = SEQ // GRID_W
    nq = N_LAT // 128
    variants = [1, 2] + [0] * (nq - 4) + [3, 4]
    xT = []
    for i in range(ncores):
        b, j = i // 4, i % 4
        xt = np.concatenate([x[b, j * N_LAT:(j + 1) * N_LAT], ctx[b]], axis=0)
        xT.append(_fm(xt))
    ropes = []
    for i in range(ncores):
        j = i % 4
        cos, sin = rope_tables(np.arange(j * N_LAT, (j + 1) * N_LAT))
        ropes.append(rope_feature_major(cos, sin, CTX))
    pmat = rope_pmat()
    ident = np.eye(128, dtype=f32).astype(NPBF)
    TW = 512
    for l in range(depth):
        last = (l == depth - 1)
        lam_init = 0.8 - 0.6 * math.exp(-0.3 * l)
        nca = _prog(('A',), lambda: build_part_a(N_LAT, CTX, TW))
        normgT = np.ascontiguousarray(np.asarray(norm_g[l], f32).reshape(6, 8, 128).transpose(2, 0, 1))
        badaT = np.ascontiguousarray(np.asarray(b_ada[l], f32).reshape(72, 128).T)
        wada_l = np.ascontiguousarray(np.asarray(w_ada[l], f32))
        w1a = np.ascontiguousarray(np.asarray(ffn_w1[l, 0], f32))
        w2a = np.ascontiguousarray(np.asarray(ffn_w2[l, 0], f32))
        win_l = np.ascontiguousarray(np.asarray(w_in[l], f32))
        in_maps = []
        for i in range(ncores):
            b = i // 4
            cvec = np.ascontiguousarray(np.stack([_lay_vec(cvals[b]), _lay_vec(c_ctx)], axis=-1))
            in_maps.append({"xT": xT[i], "cvec": cvec, "w_ada": wada_l, "badaT": badaT, "normgT": normgT, "w1": w1a, "w2": w2a,
                            "w_in": win_l, "ropeC": ropes[i][0], "ropeS": ropes[i][1], "pmat": pmat})
        ra = run_bass_kernel_spmd(nca, in_maps, core_ids=list(range(ncores))).results
        ncb = _prog(('B', l), lambda: build_part_b(N_LAT, CTX, TW, n_kc_diff=(SEQ + CTX) // 128, na_variants=variants,
                                                  lam_init=lam_init, ctx_out=not last))
        w1b_ = np.ascontiguousarray(np.asarray(ffn_w1[l, 1], f32))
        w2b_ = np.ascontiguousarray(np.asarray(ffn_w2[l, 1], f32))
        wout_l = np.ascontiguousarray(np.asarray(w_out[l], f32))
        pwbd = pool_blockdiag(np.asarray(pool_w[l], f32))
        pscaleT = np.ascontiguousarray(np.asarray(pool_scale[l], f32).reshape(2, 128).T)
        dlam = np.ascontiguousarray(np.broadcast_to(np.asarray(diff_lambda[l], f32).reshape(1, 256), (128, 256)))
        subg = np.ascontiguousarray(np.broadcast_to(np.asarray(diff_subln_g[l], f32)[None, :], (128, 128)))
        rpb = np.asarray(na_rpb[l], f32)
        in_maps = []
        for b in range(2):
            cores = [b * 4 + j for j in range(4)]
            kv_lat = np.concatenate([ra[i]["kvT"][:, :, :N_LAT] for i in cores], axis=2)
            kv_ctx = ra[cores[0]]["kvT"][:, :, N_LAT:]
            v_lat = np.concatenate([ra[i]["vtok"][:N_LAT] for i in cores], axis=0)
            v_ctx = ra[cores[0]]["vtok"][N_LAT:]
            dkT = np.ascontiguousarray(np.concatenate([kv_lat[:, 4:8], kv_ctx[:, 4:8]], axis=2))
            dv = np.ascontiguousarray(np.concatenate([v_lat[:, 256:], v_ctx[:, 256:]], axis=0))
            nakTc = np.ascontiguousarray(kv_ctx[:, 0:2])
            navc = np.ascontiguousarray(v_ctx[:, 0:256])
            pinTc = halo_cols(np.ascontiguousarray(kv_ctx[:, 2:4]), 0, CTX, 8, CTX)
            rcntc = pool_rcount(np.arange(CTX), CTX)
            for j in range(4):
                i = cores[j]
                t_start = j * N_LAT
                r0 = t_start // GRID_W
                hk0 = (r0 - 4) * GRID_W
                nhk = (nq + 4) * 128
                na_kT = np.concatenate([halo_cols(kv_lat[:, 0:2], hk0, nhk, 0, SEQ), nakTc], axis=2)
                na_v = np.concatenate([halo_cols(v_lat[:, 0:256].T, hk0, nhk, 0, SEQ).T, navc], axis=0)
                bias = np.full((5, 4, 6, 128, 128), NEG, f32)
                for vi, ci in ((0, 2), (1, 0), (2, 1), (3, nq - 2), (4, nq - 1)):
                    k0, nl = na_local_chunks(ci, nq)
                    bias[vi] = na_bias_tiles(rpb, rows_total, r0 + 2 * ci, r0 - 4 + 2 * k0)
                in_maps.append({
                    "x1T": ra[i]["x1T"], "modsT": ra[i]["modsT"], "normgT": normgT, "qT": ra[i]["qT"],
                    "na_kT": np.ascontiguousarray(na_kT), "na_v": np.ascontiguousarray(na_v), "na_kTc": nakTc, "na_vc": navc,
                    "na_bias": bias,
                    "pinT": halo_cols(kv_lat[:, 2:4], t_start, N_LAT, 8, SEQ), "rcnt": pool_rcount(np.arange(t_start, t_start + N_LAT), SEQ),
                    "pinTc": pinTc, "rcntc": rcntc, "pwbd": pwbd, "pscaleT": pscaleT,
                    "dkT": dkT, "dv": dv, "dlam": dlam, "subg": subg, "ident": ident,
                    "w_out": wout_l, "w1": w1b_, "w2": w2b_,
                })
        rb = run_bass_kernel_spmd(ncb, in_maps, core_ids=list(range(ncores))).results
        xT = [rb[i]["x2T"] for i in range(ncores)]
    out = np.zeros((2, SEQ, D), f32)
    for i in range(ncores):
        b, j = i // 4, i % 4
        out[b, j * N_LAT:(j + 1) * N_LAT] = _unfm(xT[i][:, :, :N_LAT])
    return out


def build_fused(n_lat=2048, n_ctx=256, tw=512, depth=2, group=4, dbg_ctx_out=False):
    NT = n_lat + n_ctx
    nq = n_lat // 128
    n_halo_kc = nq + 4
    nkc_na = n_halo_kc + n_ctx // 128
    n_kc_diff = (group * n_lat + n_ctx) // 128
    variants = [1, 2] + [0] * (nq - 4) + [3, 4] if nq > 4 else list(range(1, nq + 1))
    nvar = max(variants) + 1
    c = Ctx()
    em = c.em
    nc = c.nc
    xT_d = c.din("xT", [128, 8, NT], F32)
    cvec_d = c.din("cvec", [128, 8, 2], F32)
    ropeC_d = c.din("ropeC", [128, NT], F32)
    ropeS_d = c.din("ropeS", [128, NT], F32)
    pm_d = c.din("pmat", [128, 128], BF16)
    ident_d = c.din("ident", [128, 128], BF16)
    rcnt_d = c.din("rcnt", [128, 2, n_lat], F32)
    rcntc_d = c.din("rcntc", [128, 2, n_ctx], F32)
    L = []
    for l in range(depth):
        L.append(dict(
            wada=c.din("w_ada%d" % l, [D, 9 * D], F32), bada=c.din("badaT%d" % l, [128, 72], F32),
            normg=c.din("normgT%d" % l, [128, 6, 8], F32),
            w1a=c.din("w1a%d" % l, [D, 2 * DFF], F32), w2a=c.din("w2a%d" % l, [DFF, D], F32),
            w1b=c.din("w1b%d" % l, [D, 2 * DFF], F32), w2b=c.din("w2b%d" % l, [DFF, D], F32),
            win=c.din("w_in%d" % l, [D, 2560], F32), wout=c.din("w_out%d" % l, [D, D], F32),
            bias=c.din("na_bias%d" % l, [nvar, 4, 6, 128, 128], F32),
            pwbd=c.din("pwbd%d" % l, [128, 2, 128], BF16), pscale=c.din("pscaleT%d" % l, [128, 2], F32),
            dlam=c.din("dlam%d" % l, [128, 256], F32), subg=c.din("subg%d" % l, [128, 128], F32),
        ))
    outT_o = c.dout("outT", [128, 8, n_lat], F32)

    def dram(name, shape, dt):
        return nc.dram_tensor(name, list(shape), dt, kind="Internal").ap()

    xT = c.sb("xT_sb", [128, 8, NT], F32)
    cm = Common(c)
    ident = c.sb("ident", [128, 128], BF16)
    pwbd = c.sb("pwbd", [128, 2, 128], BF16)
    pscale = c.sb("pscale", [128, 2], F32)
    dlam = c.sb("dlam", [128, 256], F32)
    lamw = c.sb("lamw", [128, 4], F32)
    lamt = c.sb("lamt", [128, 1], F32)
    gsub = c.sb("gsub", [128, 128], F32)
    shiftt = c.sb("shiftt", [128, 1], F32)
    zt = c.sb("zeros", [128, 2048], BF16)
    S0 = c.ps("S0", [128, 2, 512])
    S1 = c.ps("S1", [128, 2, 512])
    U1 = c.ps("U1", [128, 2, 512])
    U2 = c.ps("U2", [128, 2, 512])
    P = {'S': [S0, S1], 'U1': U1, 'U2': U2, 'O': U1, 'T': U2[:, 0, :].bitcast(BF16)}
    pss = {'ss': U2[:, 1, :], 'a': [S0[:, 0, :], S0[:, 1, :]], 'b': [S1[:, 0, :], S1[:, 1, :]], 'y': [U1[:, 0, :], U1[:, 1, :]]}
    psn = {'ss': 'psU2_1', 'a': ['psS0_0', 'psS0_1'], 'b': ['psS1_0', 'psS1_1'], 'y': ['psU1_0', 'psU1_1']}
    ps_mods = U2[:, 0, :].rearrange("p (a b) -> p a b", b=2)
    arena_n = (nc.sbuf_bytes_remaining - 2048) // 2 // 16 * 16
    ar = Arena(c, arena_n)
    print("fused arena elems", arena_n)

    for t in range(0, NT, 512):
        w = min(512, NT - t)
        em.dma('sp', I('dma_start', out=xT[:, :, t:t + w], in_=xT_d[:, :, t:t + w]), writes=xres(t, w))
    em.dma('sp', I('dma_start', out=ident[:], in_=ident_d), writes=['ident'])
    em.op('dve', I('memset', shiftt[:], EXP_SHIFT), writes=['shiftt'])
    em.op('dve', I('memset', zt[:], 0.0), writes=['zeros'])
    tiles_all = make_tiles(n_lat, n_ctx, tw)
    wsel_d = c.din("wsel", [128, 2 * group], F32)
    wsel = c.sb("wsel", [128, 2 * group], F32)
    em.dma('sp', I('dma_start', out=wsel[:], in_=wsel_d), writes=['wsel'])

    for l in range(depth):
        W = L[l]
        last = (l == depth - 1)
        ctx_out = (not last) or dbg_ctx_out
        lam_init = 0.8 - 0.6 * math.exp(-0.3 * l)
        ar.reset(to_zero=True)
        em.dma('sp', I('dma_start', out=cm.normg[:], in_=W['normg']), writes=['normg'])
        fb = FFNBufs(c, tw, alloc=ar.alloc, nw1=3, nw2=2)
        cm.compute_mods(cvec_d, W['wada'], W['bada'], ps_mods, fb, alloc=ar.alloc, psname='psU2_0')
        cm.compute_coefs()
        ffn(c, cm, fb, xT, 0, tiles_all, W['w1a'], W['w2a'], pss, psnames=psn)
        qT_l = dram("qT_l%d" % l, [128, 6, NT], BF16)
        kvT_l = dram("kvT_l%d" % l, [128, 8 * NT], BF16)
        v_l = dram("v_l%d" % l, [NT, 768], BF16)
        kvT_l3 = kvT_l.rearrange("p (c t) -> p c t", c=8)
        proj_phase(c, cm, fb, xT, tiles_all, W['win'], ropeC_d, ropeS_d, pm_d, pss, qT_l, kvT_l3, v_l, alloc=ar.alloc, psnames=psn)
        rg = [[g0 * group + j for j in range(group)] for g0 in range(8 // group)]
        kedge_loc = dram("kedge_loc%d" % l, [128, 4 * 512], BF16)
        kedge_loc3 = kedge_loc.rearrange("p (c t) -> p c t", c=4)
        vedge_loc = dram("vedge_loc%d" % l, [512, 256], BF16)
        em.dma('sp', I('dma_start', out=kedge_loc3[:, :, 0:256], in_=kvT_l3[:, 0:4, 0:256]), reads=['kvT_o'], writes=['kedge_loc'])
        em.dma('sp', I('dma_start', out=kedge_loc3[:, :, 256:512], in_=kvT_l3[:, 0:4, n_lat - 256:n_lat]), reads=['kvT_o'], writes=['kedge_loc'])
        em.dma('sp', I('dma_start', out=vedge_loc[0:256, :], in_=v_l[0:256, 0:256]), reads=['v_o'], writes=['vedge_loc'])
        em.dma('sp', I('dma_start', out=vedge_loc[256:512, :], in_=v_l[n_lat - 256:n_lat, 0:256]), reads=['v_o'], writes=['vedge_loc'])
        kedge_g = dram("kedge_g%d" % l, [group * 128, 4 * 512], BF16)
        vedge_g = dram("vedge_g%d" % l, [group * 512, 256], BF16)
        em.coll(I('collective_compute', "AllGather", ALU.bypass, replica_groups=rg, ins=[kedge_loc.opt()], outs=[kedge_g.opt()]),
                reads=['kedge_loc'], writes=['kedge_g'])
        em.coll(I('collective_compute', "AllGather", ALU.bypass, replica_groups=rg, ins=[vedge_loc.opt()], outs=[vedge_g.opt()]),
                reads=['vedge_loc'], writes=['vedge_g'])
        dk_g, dv_g = [], []
        for h in range(4):
            dk_loc = dram("dk_loc%d_%d" % (l, h), [128, n_lat], BF16)
            dv_loc = dram("dv_loc%d_%d" % (l, h), [n_lat, 128], BF16)
            em.dma('sp', I('dma_start', out=dk_loc, in_=kvT_l3[:, 4 + h, 0:n_lat]), reads=['kvT_o'], writes=['dk_loc%d' % h])
            em.dma('sp', I('dma_start', out=dv_loc, in_=v_l[0:n_lat, 256 + h * 128:256 + (h + 1) * 128]), reads=['v_o'], writes=['dv_loc%d' % h])
            dkg = dram("dk_g%d_%d" % (l, h), [group * 128, n_lat], BF16)
            dvg = dram("dv_g%d_%d" % (l, h), [group * n_lat, 128], BF16)
            em.coll(I('collective_compute', "AllGather", ALU.bypass, replica_groups=rg, ins=[dk_loc.opt()], outs=[dkg.opt()]),
                    reads=['dk_loc%d' % h], writes=['dk_g%d' % h])
            em.coll(I('collective_compute', "AllGather", ALU.bypass, replica_groups=rg, ins=[dv_loc.opt()], outs=[dvg.opt()]),
                    reads=['dv_loc%d' % h], writes=['dv_g%d' % h])
            dk_g.append(dkg)
            dv_g.append(dvg)
        ar.reset(to_zero=True)
        ke = ar.alloc("ke", [128, group, 2048], BF16)
        ve = ar.alloc("ve", [128, group, 4, 256], BF16)
        kp = ar.alloc("kp", [128, 2048], BF16)
        kn = ar.alloc("kn", [128, 2048], BF16)
        vp = ar.alloc("vp", [128, 4, 256], BF16)
        vn = ar.alloc("vn", [128, 4, 256], BF16)
        em.dma('sp', I('dma_start', out=ke, in_=kedge_g.rearrange("(r p) n -> p r n", p=128)), reads=['kedge_g'], writes=['ke'])
        for r in range(group):
            em.dma('sp', I('dma_start', out=ve[:, r, :, :], in_=vedge_g[r * 512:(r + 1) * 512, :].rearrange("(a p) n -> p a n", p=128)),
                   reads=['vedge_g'], writes=['ve'])
        for (dst, dres, src, sres, w0) in ((kp, 'kp', lambda r: ke[:, r, :], 'ke', 0), (kn, 'kn', lambda r: ke[:, r, :], 'ke', group),
                                           (vp, 'vp', lambda r: ve[:, r, :, :], 've', 0), (vn, 'vn', lambda r: ve[:, r, :, :], 've', group)):
            em.op('dve', I('tensor_scalar', out=dst, in0=src(0), scalar1=wsel[:, w0:w0 + 1], scalar2=None, op0=ALU.mult),
                  reads=[sres, 'wsel'], writes=[dres])
            for r in range(1, group):
                em.op('dve', I('scalar_tensor_tensor', out=dst, in0=src(r), scalar=wsel[:, w0 + r:w0 + r + 1], in1=dst, op0=ALU.mult, op1=ALU.add),
                      reads=[sres, 'wsel', dres], writes=[dres])
        kp3 = kp.rearrange("p (c t) -> p c t", c=4)
        kn3 = kn.rearrange("p (c t) -> p c t", c=4)
        na_kT_asm = dram("na_kT_asm%d" % l, [128, 2, nkc_na * 128], BF16)
        na_v_asm = dram("na_v_asm%d" % l, [nkc_na * 128, 256], BF16)
        pin_asm = dram("pin_asm%d" % l, [128, 2, n_lat + 16], BF16)
        pinc_asm = dram("pinc_asm%d" % l, [128, 2, n_ctx + 16], BF16)
        em.dma('sp', I('dma_start', out=na_kT_asm[:, :, 0:256], in_=kp3[:, 0:2, 256:512]), reads=['kp'], writes=['na_kT_asm'])
        em.dma('sp', I('dma_start', out=na_kT_asm[:, :, 256 + n_lat:512 + n_lat], in_=kn3[:, 0:2, 0:256]), reads=['kn'], writes=['na_kT_asm'])
        em.dma('sp', I('dma_start', out=na_v_asm[0:256, :].rearrange("(a p) n -> p a n", p=128), in_=vp[:, 2:4, :]), reads=['vp'], writes=['na_v_asm'])
        em.dma('sp', I('dma_start', out=na_v_asm[256 + n_lat:512 + n_lat, :].rearrange("(a p) n -> p a n", p=128), in_=vn[:, 0:2, :]), reads=['vn'], writes=['na_v_asm'])
        em.dma('sp', I('dma_start', out=pin_asm[:, :, 0:8], in_=kp3[:, 2:4, 504:512]), reads=['kp'], writes=['pin_asm'])
        em.dma('sp', I('dma_start', out=pin_asm[:, :, 8 + n_lat:16 + n_lat], in_=kn3[:, 2:4, 0:8]), reads=['kn'], writes=['pin_asm'])
        em.dma('sp', I('dma_start', out=na_kT_asm[:, :, 256:256 + n_lat], in_=kvT_l3[:, 0:2, 0:n_lat]), reads=['kvT_o'], writes=['na_kT_asm'])
        em.dma('sp', I('dma_start', out=na_kT_asm[:, :, 512 + n_lat:], in_=kvT_l3[:, 0:2, n_lat:NT]), reads=['kvT_o'], writes=['na_kT_asm'])
        em.dma('sp', I('dma_start', out=na_v_asm[256:256 + n_lat, :], in_=v_l[0:n_lat, 0:256]), reads=['v_o'], writes=['na_v_asm'])
        em.dma('sp', I('dma_start', out=na_v_asm[512 + n_lat:, :], in_=v_l[n_lat:NT, 0:256]), reads=['v_o'], writes=['na_v_asm'])
        em.dma('sp', I('dma_start', out=pin_asm[:, :, 8:8 + n_lat], in_=kvT_l3[:, 2:4, 0:n_lat]), reads=['kvT_o'], writes=['pin_asm'])
        if ctx_out:
            for (a0, a1) in ((0, 8), (8 + n_ctx, 16 + n_ctx)):
                em.dma('sp', I('dma_start', out=pinc_asm[:, :, a0:a1], in_=zt[:, 0:16].rearrange("p (c t) -> p c t", c=2)), reads=['zeros'], writes=['pinc_asm'])
            em.dma('sp', I('dma_start', out=pinc_asm[:, :, 8:8 + n_ctx], in_=kvT_l3[:, 2:4, n_lat:NT]), reads=['kvT_o'], writes=['pinc_asm'])
        ar.reset(to_zero=True)
        mixT = ar.alloc("mixT", [128, 8, NT], BF16)
        ar.set_base()
        em.dma('sp', I('dma_start', out=pwbd[:], in_=W['pwbd']), writes=['pwbd'])
        em.dma('sp', I('dma_start', out=pscale[:], in_=W['pscale']), writes=['pscale'])
        em.dma('sp', I('dma_start', out=dlam[:], in_=W['dlam']), writes=['dlam'])
        em.dma('sp', I('dma_start', out=gsub[:], in_=W['subg']), writes=['gsub'])
        em.op('dve', I('tensor_scalar', out=gsub[:], in0=gsub[:], scalar1=float(1.0 - lam_init), scalar2=None, op0=ALU.mult), reads=['gsub'], writes=['gsub'])
        em.op('dve', I('tensor_tensor', out=dlam[:, 0:64], in0=dlam[:, 0:64], in1=dlam[:, 64:128], op=ALU.mult), reads=['dlam'], writes=['dlam'])
        em.op('dve', I('tensor_tensor', out=dlam[:, 128:192], in0=dlam[:, 128:192], in1=dlam[:, 192:256], op=ALU.mult), reads=['dlam'], writes=['dlam'])
        em.op('dve', I('reduce_sum', out=lamw[:, 0:2], in_=dlam[:].rearrange("p (a b) -> p a b", a=2)[:, :, 0:64], axis=AX.X), reads=['dlam'], writes=['lamw'])
        em.op('act', I('activation', out=lamw[:, 2:4], in_=lamw[:, 0:2], func=AF.Exp), reads=['lamw'], writes=['lamw'])
        em.op('dve', I('tensor_tensor', out=lamt[:], in0=lamw[:, 2:3], in1=lamw[:, 3:4], op=ALU.subtract), reads=['lamw'], writes=['lamt'])
        em.op('dve', I('tensor_scalar', out=lamt[:], in0=lamt[:], scalar1=float(lam_init), scalar2=None, op0=ALU.add), reads=['lamt'], writes=['lamt'])
        em.barrier()
        ctx_kcs = [n_halo_kc + i for i in range(n_ctx // 128)]
        qch = []
        for i in range(nq):
            k0, nl = na_local_chunks(i, nq)
            qch.append((i * 128, [k0 + j for j in range(nl)] + ctx_kcs, variants[i], nl))
        na_attention(c, ar, P, mixT, qT_l, na_kT_asm, na_v_asm, nkc_na, qch, W['bias'], ident, 0, shiftt)
        if ctx_out:
            ar.reset()
            qch = [(n_lat + i * 128, list(range(n_ctx // 128)), None, 0) for i in range(n_ctx // 128)]
            na_attention(c, ar, P, mixT, qT_l, kvT_l3[:, 0:2, n_lat:NT], v_l[n_lat:NT, 0:256], n_ctx // 128, qch, W['bias'], ident, n_lat, shiftt)
        ar.reset()
        pool_mixer(c, ar, P, mixT, pin_asm, rcnt_d, n_lat, pwbd, pscale, 0)
        if ctx_out:
            ar.reset()
            pool_mixer(c, ar, P, mixT, pinc_asm, rcntc_d, n_ctx, pwbd, pscale, n_lat)
        ar.reset()
        lat_kc = n_lat // 128

        def load_all(h, kb, vb, rk, rv):
            for r in range(group):
                em.dma('sp', I('dma_start', out=kb[:, r * n_lat:(r + 1) * n_lat], in_=dk_g[h][r * 128:(r + 1) * 128, :]), reads=['dk_g%d' % h], writes=[rk])
                em.dma('sp', I('dma_start', out=vb[:, r * lat_kc:(r + 1) * lat_kc, 0:128],
                               in_=dv_g[h][r * n_lat:(r + 1) * n_lat, :].rearrange("(c p) e -> p c e", p=128)),
                       reads=['dv_g%d' % h], writes=[rv])
            em.dma('sp', I('dma_start', out=kb[:, group * n_lat:], in_=kvT_l3[:, 4 + h, n_lat:NT]), reads=['kvT_o'], writes=[rk])
            em.dma('sp', I('dma_start', out=vb[:, group * lat_kc:, 0:128],
                           in_=v_l[n_lat:NT, 256 + h * 128:256 + (h + 1) * 128].rearrange("(c p) e -> p c e", p=128)), reads=['v_o'], writes=[rv])

        def load_ctx(h, kb, vb, rk, rv):
            em.dma('sp', I('dma_start', out=kb, in_=kvT_l3[:, 4 + h, n_lat:NT]), reads=['kvT_o'], writes=[rk])
            em.dma('sp', I('dma_start', out=vb[:, :, 0:128],
                           in_=v_l[n_lat:NT, 256 + h * 128:256 + (h + 1) * 128].rearrange("(c p) e -> p c e", p=128)), reads=['v_o'], writes=[rv])
        qtiles = [(t, min(512, n_lat - t)) for t in range(0, n_lat, 512)]
        diff_attention(c, ar, P, mixT, qT_l, qtiles, None, None, n_kc_diff, lamt, gsub, ident, shiftt, cm.epst, loaders=load_all)
        if ctx_out:
            ar.reset()
            diff_attention(c, ar, P, mixT, qT_l, [(n_lat, n_ctx)], None, None, n_ctx // 128, lamt, gsub, ident, shiftt, cm.epst, loaders=load_ctx)
        ar.reset()
        wo = ar.alloc("wo", [128, 8, D], BF16)
        em.dma('pool', I('dma_start', out=wo, in_=W['wout'].rearrange("(k p) n -> p k n", p=128)), writes=['wo'])

        class YB:
            pass
        yb = YB()
        yb.ysb = ar.alloc("ysb", [128, 8, 512], F32)
        yb.sq = ar.alloc("sq", [128, 8, 512], BF16)
        yb.tmp = [ar.alloc("tmpf%d" % i, [128, 512], F32) for i in range(2)]
        yb.rstd = ar.alloc("rstd", [128, 512], F32)
        yb.n_tmp = 0
        ny = 0
        for (t0, w, g) in [s_ for tile in make_tiles(n_lat, n_ctx if ctx_out else 0, 512) for s_ in tile]:
            for k in range(8):
                py = pss['y'][ny % 2]
                ry = psn['y'][ny % 2]
                ny += 1
                mm_group(em, py[:, :w], [(wo[:, kk, k * 128:(k + 1) * 128], mixT[:, kk, t0:t0 + w]) for kk in range(8)],
                         reads=['wo'] + ['mix%d' % i for i in range(t0 // 128, (t0 + w) // 128)], wres=ry)
                y_evac(c, yb, k, 0, w, py[:, :w], ry)
            sandwich_out(c, cm, yb, xT, 1, t0, w, g, 0, pss['ss'], ss_res=psn['ss'])
        ar.reset(to_zero=True)
        fb = FFNBufs(c, tw, alloc=ar.alloc, nw1=3, nw2=3)
        ffn(c, cm, fb, xT, 2, make_tiles(n_lat, n_ctx if ctx_out else 0, tw), W['w1b'], W['w2b'], pss, psnames=psn)
    em.dma('sp', I('dma_start', out=outT_o, in_=xT[:, :, 0:n_lat]), reads=xres(0, n_lat), writes=['outT_o'])
    print("fused instructions:", em.ninst)
    return c.done()


def fused_inputs(x, c, ctx, c_ctx, w_ada, b_ada, norm_g, ffn_w1, ffn_w2, w_in, w_out, na_rpb, pool_w, pool_scale, diff_lambda,
                 diff_subln_g, n_lat):
    f32 = np.float32
    x = np.asarray(x, f32)
    ctx = np.asarray(ctx, f32)
    cvals = np.asarray(c, f32)
    c_ctx = np.asarray(c_ctx, f32)
    seq = x.shape[1]
    n_ctx = ctx.shape[1]
    group = seq // n_lat
    ncores = x.shape[0] * group
    depth = w_ada.shape[0]
    rows_total = seq // GRID_W
    nq = n_lat // 128
    if nq > 4:
        vmap = ((0, 2), (1, 0), (2, 1), (3, nq - 2), (4, nq - 1))
    else:
        vmap = tuple((i + 1, i) for i in range(nq))
    shared = {"pmat": rope_pmat(), "ident": np.eye(128, dtype=f32).astype(NPBF), "rcntc": pool_rcount(np.arange(n_ctx), n_ctx)}
    for l in range(depth):
        shared["w_ada%d" % l] = np.ascontiguousarray(np.asarray(w_ada[l], f32))
        shared["badaT%d" % l] = np.ascontiguousarray(np.asarray(b_ada[l], f32).reshape(72, 128).T)
        shared["normgT%d" % l] = np.ascontiguousarray(np.asarray(norm_g[l], f32).reshape(6, 8, 128).transpose(2, 0, 1))
        shared["w1a%d" % l] = np.ascontiguousarray(np.asarray(ffn_w1[l, 0], f32))
        shared["w2a%d" % l] = np.ascontiguousarray(np.asarray(ffn_w2[l, 0], f32))
        shared["w1b%d" % l] = np.ascontiguousarray(np.asarray(ffn_w1[l, 1], f32))
        shared["w2b%d" % l] = np.ascontiguousarray(np.asarray(ffn_w2[l, 1], f32))
        shared["w_in%d" % l] = np.ascontiguousarray(np.asarray(w_in[l], f32))
        shared["w_out%d" % l] = np.ascontiguousarray(np.asarray(w_out[l], f32))
        shared["pwbd%d" % l] = pool_blockdiag(np.asarray(pool_w[l], f32))
        shared["pscaleT%d" % l] = np.ascontiguousarray(np.asarray(pool_scale[l], f32).reshape(2, 128).T)
        shared["dlam%d" % l] = np.ascontiguousarray(np.broadcast_to(np.asarray(diff_lambda[l], f32).reshape(1, 256), (128, 256)))
        shared["subg%d" % l] = np.ascontiguousarray(np.broadcast_to(np.asarray(diff_subln_g[l], f32)[None, :], (128, 128)))
    in_maps = []
    for i in range(ncores):
        b, j = i // group, i % group
        t_start = j * n_lat
        r0 = t_start // GRID_W
        m = dict(shared)
        m["xT"] = _fm(np.concatenate([x[b, t_start:t_start + n_lat], ctx[b]], axis=0))
        m["cvec"] = np.ascontiguousarray(np.stack([_lay_vec(cvals[b]), _lay_vec(c_ctx)], axis=-1))
        cos, sin = rope_tables(np.arange(t_start, t_start + n_lat))
        m["ropeC"], m["ropeS"] = rope_feature_major(cos, sin, n_ctx)
        m["rcnt"] = pool_rcount(np.arange(t_start, t_start + n_lat), seq)
        ws = np.zeros((128, 2 * group), f32)
        if j > 0:
            ws[:, j - 1] = 1.0
        if j < group - 1:
            ws[:, group + j + 1] = 1.0
        m["wsel"] = ws
        for l in range(depth):
            rpb = np.asarray(na_rpb[l], f32)
            bias = np.full((len(vmap) + (1 if nq <= 4 else 0), 4, 6, 128, 128), NEG, f32)
            for vi, ci in vmap:
                k0, nl = na_local_chunks(ci, nq)
                bias[vi] = na_bias_tiles(rpb, rows_total, r0 + 2 * ci, r0 - 4 + 2 * k0)
            m["na_bias%d" % l] = bias
        in_maps.append(m)
    return in_maps, ncores, group


def kernel_fused(n_lat=N_LAT, **inputs):
    in_maps, ncores, group = fused_inputs(n_lat=n_lat, **inputs)
    n_ctx = inputs["ctx"].shape[1]
    seq = inputs["x"].shape[1]
    tw = 512 if n_lat >= 2048 else 384
    nc = _prog(('F', n_lat, n_ctx), lambda: build_fused(n_lat, n_ctx, tw, depth=inputs["w_ada"].shape[0], group=group))
    res = run_bass_kernel_spmd(nc, in_maps, core_ids=list(range(ncores))).results
    out = np.zeros((inputs["x"].shape[0], seq, D), np.float32)
    for i in range(ncores):
        b, j = i // group, i % group
        out[b, j * n_lat:(j + 1) * n_lat] = _unfm(res[i]["outT"])
    return out


def kernel(x, c, ctx, c_ctx, w_ada, b_ada, norm_g, ffn_w1, ffn_w2, w_in, w_out, na_rpb, pool_w, pool_scale, diff_lambda,
           diff_subln_g):
    return kernel_fused(n_lat=N_LAT, x=x, c=c, ctx=ctx, c_ctx=c_ctx, w_ada=w_ada, b_ada=b_ada, norm_g=norm_g, ffn_w1=ffn_w1,
                        ffn_w2=ffn_w2, w_in=w_in, w_out=w_out, na_rpb=na_rpb, pool_w=pool_w, pool_scale=pool_scale,
                        diff_lambda=diff_lambda, diff_subln_g=diff_subln_g)
```

```python
import math
import numpy as np
import ml_dtypes
from contextlib import ExitStack
import concourse.bass as bass
import concourse.mybir as mybir
from concourse.bass_utils import run_bass_kernel_spmd

F32 = mybir.dt.float32
BF16 = mybir.dt.bfloat16
AF = mybir.ActivationFunctionType
ALU = mybir.AluOpType
AX = mybir.AxisListType
NPBF = ml_dtypes.bfloat16

D = 1024
DFF = 2816
NFC = 22
SEQ = 8192
CTX = 256
GRID_W = 64
EPS = 1e-6
NEG = -30000.0
EXP_SHIFT = -40.0


class Emitter:
    ENGS = ('pe', 'act', 'dve', 'pool', 'sp')

    def __init__(self, nc, stack, n_dma_sems=16):
        self.nc = nc
        self._stack = stack
        self.cccount = 0
        self.prog = {e: [] for e in self.ENGS}
        self.count = {e: 0 for e in self.ENGS}
        self.waited = {e: {} for e in self.ENGS}
        self.dcount = [0] * n_dma_sems
        self.dnext_q = {e: 0 for e in self.ENGS}
        self.lastw = {}
        self.readers = {}
        self.semobj = {}
        for e in self.ENGS:
            self.semobj[('c', e)] = stack.enter_context(nc.semaphore('c_' + e))
        for i in range(n_dma_sems):
            self.semobj[('d', i)] = stack.enter_context(nc.semaphore('d%d' % i))
        self.ninst = 0

    def _deps(self, eng, reads, writes):
        deps = {}
        own = ('c', eng)

        def add(k, v):
            if deps.get(k, 0) < v:
                deps[k] = v
        skip_own = (eng == 'pe')
        for r in reads:
            t = self.lastw.get(r)
            if t is not None and not (skip_own and t[0] == own):
                add(*t)
        for w in writes:
            t = self.lastw.get(w)
            if t is not None and not (skip_own and t[0] == own):
                add(*t)
            for k, v in self.readers.get(w, {}).items():
                if not (skip_own and k == own):
                    add(k, v)
        waits = []
        wd = self.waited[eng]
        for k, v in deps.items():
            if wd.get(k, 0) < v:
                wd[k] = v
                waits.append((k, v))
        return waits

    def _commit(self, tok, reads, writes):
        for w in writes:
            self.lastw[w] = tok
            self.readers[w] = {}
        for r in reads:
            d = self.readers.setdefault(r, {})
            if d.get(tok[0], 0) < tok[1]:
                d[tok[0]] = tok[1]

    def op(self, eng, fn, reads=(), writes=(), inc=True):
        writes = list(writes) + [r for r in reads if r.startswith('ps') and r not in writes]
        waits = self._deps(eng, reads, writes)
        tok = (('c', eng), self.count[eng] + 1)
        if inc:
            self.count[eng] += 1
        self.prog[eng].append((waits, fn, (tok[0], 1) if inc else None))
        self._commit(tok, reads, writes)
        self.ninst += 1
        return tok

    def dma(self, eng, fn, reads=(), writes=()):
        waits = self._deps(eng, reads, writes)
        half = len(self.dcount) // 2
        base = 0 if eng == 'pool' else half
        i = base + self.dnext_q[eng]
        self.dnext_q[eng] = (self.dnext_q[eng] + 1) % half
        k = ('d', i)
        wd = self.waited[eng]
        if wd.get(k, 0) < self.dcount[i]:
            wd[k] = self.dcount[i]
            waits.append((k, self.dcount[i]))
        self.dcount[i] += 16
        tok = (k, self.dcount[i])
        self.prog[eng].append((waits, fn, (k, 16)))
        self._commit(tok, reads, writes)
        self.ninst += 1
        return tok

    def coll(self, fn, reads=(), writes=()):
        eng = 'pool'
        waits = self._deps(eng, reads, writes)
        k = ('cc', self.cccount)
        self.semobj[k] = self._stack.enter_context(self.nc.semaphore('cc_sem%d' % self.cccount))
        self.cccount += 1
        tok = (k, 1)
        self.prog[eng].append((waits, fn, (k, 1)))
        self._commit(tok, reads, writes)
        self.ninst += 1
        return tok

    def finish(self, eng='sp'):
        toks = [(('c', e), self.count[e]) for e in self.ENGS if self.count[e]]
        toks += [(('d', i), c) for i, c in enumerate(self.dcount) if c]
        toks += [(('cc', i), 1) for i in range(self.cccount)]
        waits = []
        wd = self.waited[eng]
        for k, v in toks:
            if wd.get(k, 0) < v:
                wd[k] = v
                waits.append((k, v))
        self.prog[eng].append((waits, None, None))

    def emit(self, block):
        def mk(e):
            def body(engh):
                for waits, fn, inc in self.prog[e]:
                    for k, v in waits:
                        engh.wait_ge(self.semobj[k], v)
                    if fn is not None:
                        if fn[0] == '__call__':
                            ins = fn[1](engh)
                        else:
                            ins = getattr(engh, fn[0])(*fn[1], **fn[2])
                        if inc is not None:
                            ins.then_inc(self.semobj[inc[0]], inc[1])
            return body
        block.tensor(mk('pe'))
        block.scalar(mk('act'))
        block.vector(mk('dve'))
        block.gpsimd(mk('pool'))
        block.sync(mk('sp'))


class Ctx:
    def __init__(self):
        self.nc = bass.Bass("TRN2", target_bir_lowering=False)
        self.st = ExitStack()
        self.em = Emitter(self.nc, self.st)
        self.uid = 0

    def sb(self, name, shape, dt):
        return self.st.enter_context(self.nc.sbuf_tensor("s_" + name, list(shape), dt))

    def ps(self, name, shape, dt=F32):
        return self.st.enter_context(self.nc.psum_tensor("p_" + name, list(shape), dt))

    def din(self, name, shape, dt):
        return self.nc.dram_tensor(name, list(shape), dt, kind="ExternalInput").ap()

    def dout(self, name, shape, dt):
        return self.nc.dram_tensor(name, list(shape), dt, kind="ExternalOutput").ap()

    def done(self):
        self.em.finish('sp')
        with self.nc.Block() as block:
            self.em.emit(block)
        self.st.close()
        return self.nc


def I(name, *a, **kw):
    return (name, a, kw)


def mm_group(em, out_ap, pairs, reads, wres, extra_writes=()):
    n = len(pairs)
    for i, (l, r) in enumerate(pairs):
        em.op('pe', I('matmul', out_ap, lhsT=l, rhs=r, start=(i == 0), stop=(i == n - 1)),
              reads=reads, writes=[wres] + list(extra_writes), inc=(i == n - 1))


class Common:
    def __init__(self, c, need_mods_from_wada=True):
        self.c = c
        em = c.em
        self.ones = c.sb("ones_bf", [128, 128], BF16)
        em.op('dve', I('memset', self.ones[:], 1.0), writes=['ones'])
        self.dummy = c.sb("dummy", [128, 1], F32)
        self.epst = c.sb("epst", [128, 1], F32)
        em.op('dve', I('memset', self.epst[:], EPS), writes=['epst'])
        self.modsT = c.sb("modsT", [128, 72, 2], F32)
        self.normg = c.sb("normgT", [128, 6, 8], F32)
        self.A = c.sb("coefA", [128, 3, 8, 2], F32)
        self.G = c.sb("coefG", [128, 3, 8, 2], F32)

    def load_normg(self, normg_d):
        self.c.em.dma('sp', I('dma_start', out=self.normg[:], in_=normg_d), writes=['normg'])

    def compute_mods(self, cvec_d, wada_d, badaT_d, ps_mods, fb, alloc=None, psname='ps_mods'):
        c, em = self.c, self.c.em
        if alloc is None:
            alloc = lambda name, shape, dt: c.sb(name, shape, dt)[:]
        cv = alloc("cvec", [128, 8, 2], F32)
        scv = alloc("scvec", [128, 8, 2], BF16)
        bad = alloc("badaT", [128, 72], F32)
        em.dma('sp', I('dma_start', out=cv, in_=cvec_d), writes=['cv'])
        em.dma('sp', I('dma_start', out=bad, in_=badaT_d), writes=['bad'])
        em.op('act', I('activation', out=scv, in_=cv, func=AF.Silu), reads=['cv'], writes=['scv'])
        wbuf = [flatview(fb.gT, i * 8 * 512, [128, 8, 512]) for i in range(2)]
        wv = wada_d.rearrange("(k p) n -> p k n", p=128)
        for m in range(18):
            wb = wbuf[m % 2]
            em.dma('pool', I('dma_start', out=wb, in_=wv[:, :, m * 512:(m + 1) * 512]),
                   writes=['wada%d' % (m % 2)])
            for fc in range(4):
                mm_group(em, ps_mods[:, m * 4 + fc, :],
                         [(wb[:, k, fc * 128:(fc + 1) * 128], scv[:, k, :]) for k in range(8)],
                         reads=['wada%d' % (m % 2), 'scv'], wres=psname)
        em.op('dve', I('memset', self.dummy[:], 0.0), writes=['dummy', 'gT', 'wada0', 'wada1'])
        for g in range(2):
            em.op('dve', I('tensor_tensor', out=self.modsT[:, :, g], in0=ps_mods[:, 0:72, g], in1=bad, op=ALU.add),
                  reads=[psname, 'bad'], writes=['modsT'])

    def load_mods(self, modsT_d):
        self.c.em.dma('sp', I('dma_start', out=self.modsT[:], in_=modsT_d), writes=['modsT'])

    def compute_coefs(self):
        em = self.c.em
        for idx, res_w in ((0, 0.5), (1, 1.0), (2, 0.5)):
            for g in range(2):
                em.op('dve', I('scalar_tensor_tensor',
                    out=self.A[:, idx, :, g], in0=self.modsT[:, (3 * idx + 1) * 8:(3 * idx + 2) * 8, g], scalar=1.0,
                    in1=self.normg[:, 2 * idx, :], op0=ALU.add, op1=ALU.mult),
                    reads=['modsT', 'normg'], writes=['coefA'])
                em.op('dve', I('scalar_tensor_tensor',
                    out=self.G[:, idx, :, g], in0=self.modsT[:, (3 * idx + 2) * 8:(3 * idx + 3) * 8, g], scalar=res_w,
                    in1=self.normg[:, 2 * idx + 1, :], op0=ALU.mult, op1=ALU.mult),
                    reads=['modsT', 'normg'], writes=['coefG'])

    def shift(self, idx, k, g):
        return self.modsT[:, 3 * idx * 8 + k, g:g + 1]


class FFNBufs:
    def __init__(self, c, tw, alloc=None, gT=None, nw1=2, nw2=2):
        if alloc is None:
            alloc = lambda name, shape, dt: c.sb(name, shape, dt)[:]
        self.tw = tw
        self.hT = alloc("hT", [128, 8, tw], BF16)
        self.gT = gT if gT is not None else alloc("gT", [128, NFC, tw], BF16)
        assert NFC * tw >= 2 * 8 * 512
        self.w1b = [alloc("w1b%d" % i, [128, 2 * 8 * 256], BF16) for i in range(nw1)]
        self.w2b = [alloc("w2b%d" % i, [128, NFC, 256], BF16) for i in range(nw2)]
        self.ysb = alloc("ysb", [128, 8, tw], F32)
        self.sq = alloc("sq", [128, 8, tw], BF16)
        self.tmp = [alloc("tmpf%d" % i, [128, 512], F32) for i in range(2)]
        self.rstd = alloc("rstd", [128, 512], F32)
        self.sa = [alloc("sa%d" % i, [128, 512], BF16) for i in range(2)]
        self.n_tmp = 0


def rms_rstd(c, cm, src_fn, w, ps_ss, sq, rstd, src_reads, tag):
    em = c.em
    for k in range(8):
        em.op('act', I('activation', out=sq[:, k, :w], in_=src_fn(k), func=AF.Square),
              reads=src_reads, writes=['sq'])
    mm_group(em, ps_ss[:, :w], [(cm.ones[:], sq[:, k, :w]) for k in range(8)], reads=['sq', 'ones'], wres=tag)
    em.op('act', I('activation', out=rstd[:, :w], in_=ps_ss[:, :w], func=AF.Sqrt, scale=1.0 / D, bias=cm.epst[:]),
          reads=[tag, 'epst'], writes=['rstd'])
    em.op('dve', I('reciprocal', out=rstd[:, :w], in_=rstd[:, :w]), reads=['rstd'], writes=['rstd'])


def sandwich_in(c, cm, fb, xT, idx, subs, ps_ss, dst, dst_res, ss_res='ps_ss'):
    em = c.em
    for (t0, w, g, off) in subs:
        rms_rstd(c, cm, lambda k: xT[:, k, t0:t0 + w], w, ps_ss, fb.sq, fb.rstd, xres(t0, w), ss_res)
        for k in range(8):
            tmp = fb.tmp[fb.n_tmp % 2]
            tr = 'tmpf%d' % (fb.n_tmp % 2)
            fb.n_tmp += 1
            em.op('dve', I('tensor_tensor', out=tmp[:, :w], in0=xT[:, k, t0:t0 + w], in1=fb.rstd[:, :w], op=ALU.mult),
                  reads=xres(t0, w) + ['rstd'], writes=[tr])
            em.op('act', I('activation', out=dst[:, k, off:off + w], in_=tmp[:, :w], func=AF.Identity,
                                                               scale=cm.A[:, idx, k, g:g + 1], bias=cm.shift(idx, k, g)),
                  reads=[tr, 'coefA', 'modsT'], writes=[dst_res])


def y_evac(c, fb, k, off, w, yp, yres):
    em = c.em
    em.op('dve', I('tensor_copy', out=fb.ysb[:, k, off:off + w], in_=yp), reads=[yres], writes=['ysb'])
    em.op('dve', I('tensor_tensor', out=fb.sq[:, k, off:off + w], in0=fb.ysb[:, k, off:off + w], in1=fb.ysb[:, k, off:off + w], op=ALU.mult),
          reads=['ysb'], writes=['sq'])


def sandwich_out(c, cm, fb, xT, idx, t0, w, g, off, ps_ss, ss_res='ps_ss'):
    em = c.em
    mm_group(em, ps_ss[:, :w], [(cm.ones[:], fb.sq[:, k, off:off + w]) for k in range(8)], reads=['sq', 'ones'], wres=ss_res)
    em.op('act', I('activation', out=fb.rstd[:, :w], in_=ps_ss[:, :w], func=AF.Sqrt, scale=1.0 / D, bias=cm.epst[:]),
          reads=[ss_res, 'epst'], writes=['rstd'])
    em.op('dve', I('reciprocal', out=fb.rstd[:, :w], in_=fb.rstd[:, :w]), reads=['rstd'], writes=['rstd'])
    for k in range(8):
        tmp = fb.tmp[fb.n_tmp % 2]
        tr = 'tmpf%d' % (fb.n_tmp % 2)
        fb.n_tmp += 1
        em.op('dve', I('scalar_tensor_tensor', out=tmp[:, :w], in0=fb.ysb[:, k, off:off + w], scalar=cm.G[:, idx, k, g:g + 1],
                                                                     in1=fb.rstd[:, :w], op0=ALU.mult, op1=ALU.mult),
              reads=['ysb', 'rstd', 'coefG'], writes=[tr])
        em.op('dve', I('tensor_tensor', out=xT[:, k, t0:t0 + w], in0=xT[:, k, t0:t0 + w], in1=tmp[:, :w], op=ALU.add),
              reads=[tr] + xres(t0, w), writes=xres(t0, w))


def ffn(c, cm, fb, xT, idx, tiles, w1_d, w2_d, pss, psnames=None):
    em = c.em
    ps_ss, ps_a, ps_b, ps_y = pss['ss'], pss['a'], pss['b'], pss['y']
    if psnames is None:
        psnames = {'ss': 'ps_ss', 'a': ['ps_a0', 'ps_a1'], 'b': ['ps_b0', 'ps_b1'], 'y': ['ps_y0', 'ps_y1']}
    w1v = w1_d.rearrange("(k p) (two f) -> p k two f", p=128, two=2)
    w2v = w2_d.rearrange("(f p) d -> p f d", p=128)
    nw1 = 0
    nw2 = 0
    nsa = 0
    ny = 0
    for tile in tiles:
        subs = []
        off = 0
        for (t0, w, g) in tile:
            subs.append((t0, w, g, off))
            off += w
        sandwich_in(c, cm, fb, xT, idx, subs, ps_ss, fb.hT, 'hT', ss_res=psnames['ss'])
        for fp in range(NFC // 2):
            wb = fb.w1b[nw1 % len(fb.w1b)].rearrange("p (a k f) -> p a k f", a=2, k=8)
            wr = 'w1b%d' % (nw1 % len(fb.w1b))
            nw1 += 1
            for two in range(2):
                em.dma('pool', I('dma_start', out=wb[:, two, :, :], in_=w1v[:, :, two, fp * 256:(fp + 1) * 256]), writes=[wr])
            for fi in range(2):
                fc = fp * 2 + fi
                for (t0, w, g, off) in subs:
                    pa = ps_a[nsa % 2]
                    pb = ps_b[nsa % 2]
                    ra, rb = psnames['a'][nsa % 2], psnames['b'][nsa % 2]
                    sa = fb.sa[nsa % 2]
                    rs = 'sa%d' % (nsa % 2)
                    nsa += 1
                    mm_group(em, pa[:, :w], [(wb[:, 0, k, fi * 128:(fi + 1) * 128], fb.hT[:, k, off:off + w]) for k in range(8)],
                             reads=[wr, 'hT'], wres=ra)
                    mm_group(em, pb[:, :w], [(wb[:, 1, k, fi * 128:(fi + 1) * 128], fb.hT[:, k, off:off + w]) for k in range(8)],
                             reads=[wr, 'hT'], wres=rb)
                    em.op('act', I('activation', out=sa[:, :w], in_=pa[:, :w], func=AF.Silu),
                          reads=[ra], writes=[rs])
                    em.op('dve', I('tensor_tensor',
                        out=fb.gT[:, fc, off:off + w], in0=sa[:, :w], in1=pb[:, :w], op=ALU.mult),
                        reads=[rs, rb], writes=['gT'])
        for piece in range(4):
            wb = fb.w2b[nw2 % len(fb.w2b)]
            wr = 'w2b%d' % (nw2 % len(fb.w2b))
            nw2 += 1
            em.dma('pool', I('dma_start', out=wb, in_=w2v[:, :, piece * 256:(piece + 1) * 256]), writes=[wr])
            for (t0, w, g, off) in subs:
                for kk in range(2):
                    k = piece * 2 + kk
                    py = ps_y[ny % 2]
                    ry = psnames['y'][ny % 2]
                    ny += 1
                    mm_group(em, py[:, :w], [(wb[:, f, kk * 128:(kk + 1) * 128], fb.gT[:, f, off:off + w]) for f in range(NFC)],
                             reads=[wr, 'gT'], wres=ry)
                    y_evac(c, fb, k, off, w, py[:, :w], ry)
        for (t0, w, g, off) in subs:
            sandwich_out(c, cm, fb, xT, idx, t0, w, g, off, ps_ss, ss_res=psnames['ss'])


def xres(t0, w):
    return ['x%d' % i for i in range(t0 // 128, (t0 + w + 127) // 128)]


def flatview(ap3, n0, shape):
    flat = ap3.rearrange("p a b -> p (a b)")
    n = 1
    for s in shape[1:]:
        n *= s
    v = flat[:, n0:n0 + n]
    if len(shape) == 2:
        return v
    if len(shape) == 3:
        return v.rearrange("p (a b) -> p a b", a=shape[1])
    return v.rearrange("p (a b c) -> p a b c", a=shape[1], b=shape[2])


def proj_phase(c, cm, fb, xT, tiles, win_d, ropeC_d, ropeS_d, pm_d, pss, qT_o, kvT_o, v_o, alloc=None, psnames=None):
    em = c.em
    ps_ss, ps_a, ps_b, ps_y = pss['ss'], pss['a'], pss['b'], pss['y']
    tw = fb.tw
    winv = win_d.rearrange("(k p) n -> p k n", p=128)
    if alloc is None:
        alloc = lambda name, shape, dt: c.sb(name, shape, dt)[:]
    if psnames is None:
        psnames = {'ss': 'ps_ss', 'a': ['ps_a0', 'ps_a1'], 'b': ['ps_b0', 'ps_b1'], 'y': ['ps_y0', 'ps_y1']}
    pmat = alloc("pmat", [128, 128], BF16)
    em.dma('sp', I('dma_start', out=pmat, in_=pm_d), writes=['pmat'])
    ct = [alloc("ropec%d" % i, [128, 512], F32) for i in range(2)]
    sn = [alloc("ropes%d" % i, [128, 512], F32) for i in range(2)]
    qb = [alloc("qb%d" % i, [128, 512], BF16) for i in range(2)]
    t2 = [alloc("t2_%d" % i, [128, 512], F32) for i in range(2)]
    qst = flatview(fb.gT, 0, [128, 6, tw])
    kvst = flatview(fb.gT, 6 * tw, [128, 8, tw])
    vst = flatview(fb.gT, 14 * tw, [128, tw // 128, 768])
    cnt = {'w': 0, 'p': 0, 'r': 0, 't': 0}

    def next_ps():
        i = cnt['p'] % 4
        cnt['p'] += 1
        return ([ps_a[0], ps_a[1], ps_b[0], ps_b[1]][i], (psnames['a'] + psnames['b'])[i])

    for tile in tiles:
        subs = []
        off = 0
        for (t0, w, g) in tile:
            subs.append((t0, w, g, off))
            off += w
        tww = off
        sandwich_in(c, cm, fb, xT, 1, subs, ps_ss, fb.hT, 'hT', ss_res=psnames['ss'])
        for piece in range(5):
            wb4 = fb.w1b[cnt['w'] % len(fb.w1b)]
            wr = 'w1b%d' % (cnt['w'] % len(fb.w1b))
            cnt['w'] += 1
            wb = wb4.rearrange("p (k n) -> p k n", k=8)
            em.dma('pool', I('dma_start', out=wb, in_=winv[:, :, piece * 512:(piece + 1) * 512]), writes=[wr])
            for (t0, w, g, off) in subs:
                if piece == 0:
                    fm = [(0, 'q', 0, 0.125, False), (1, 'q', 1, 0.125, False), (2, 'kv', 0, 1.0, False), (3, 'kv', 1, 1.0, False)]
                elif piece == 1:
                    fm = [(2, 'kv', 2, 1.0, False), (3, 'kv', 3, 1.0, False)]
                elif piece == 2:
                    fm = [(i, 'q', 2 + i, 0.125, True) for i in range(4)]
                elif piece == 3:
                    fm = [(i, 'kv', 4 + i, 1.0, True) for i in range(4)]
                else:
                    fm = []
                if piece in (2, 3):
                    ci = cnt['r'] % 2
                    cnt['r'] += 1
                    em.dma('sp', I('dma_start', out=ct[ci][:, :w], in_=ropeC_d[:, t0:t0 + w]), writes=['ropec%d' % ci])
                    em.dma('sp', I('dma_start', out=sn[ci][:, :w], in_=ropeS_d[:, t0:t0 + w]), writes=['ropes%d' % ci])
                for (lc, kind, oc, scale, rope) in fm:
                    pp, pr = next_ps()
                    mm_group(em, pp[:, :w], [(wb[:, k, lc * 128:(lc + 1) * 128], fb.hT[:, k, off:off + w]) for k in range(8)],
                             reads=[wr, 'hT'], wres=pr)
                    dst = (qst if kind == 'q' else kvst)[:, oc, off:off + w]
                    if not rope:
                        em.op('act', I('activation', out=dst, in_=pp[:, :w], func=AF.Copy, scale=scale),
                              reads=[pr], writes=['gT'])
                    else:
                        ti = cnt['t'] % 2
                        cnt['t'] += 1
                        em.op('act', I('activation', out=qb[ti][:, :w], in_=pp[:, :w], func=AF.Copy, scale=scale),
                              reads=[pr], writes=['qb%d' % ti])
                        py = ps_y[ti]
                        pyr = psnames['y'][ti]
                        mm_group(em, py[:, :w], [(pmat, qb[ti][:, :w])], reads=['pmat', 'qb%d' % ti], wres=pyr)
                        tmp = fb.tmp[ti]
                        em.op('dve', I('scalar_tensor_tensor',
                            out=tmp[:, :w], in0=pp[:, :w], scalar=scale, in1=ct[ci][:, :w], op0=ALU.mult, op1=ALU.mult),
                            reads=[pr, 'ropec%d' % ci], writes=['tmpf%d' % ti])
                        em.op('dve', I('tensor_tensor', out=t2[ti][:, :w], in0=py[:, :w], in1=sn[ci][:, :w], op=ALU.mult),
                              reads=[pyr, 'ropes%d' % ci], writes=['t2_%d' % ti])
                        em.op('dve', I('tensor_tensor', out=dst, in0=tmp[:, :w], in1=t2[ti][:, :w], op=ALU.add),
                              reads=['tmpf%d' % ti, 't2_%d' % ti], writes=['gT'])
                if piece in (1, 4):
                    c0, ncol, vo = (0, 256, 0) if piece == 1 else (0, 512, 256)
                    for tcn in range(w // 128):
                        pp, pr = next_ps()
                        tk = off + tcn * 128
                        mm_group(em, pp[:, :ncol], [(fb.hT[:, k, tk:tk + 128], wb[:, k, c0:c0 + ncol]) for k in range(8)],
                                 reads=[wr, 'hT'], wres=pr)
                        em.op('act', I('activation', out=vst[:, tk // 128, vo:vo + ncol], in_=pp[:, :ncol], func=AF.Copy),
                              reads=[pr], writes=['gT'])
        tile0 = tile[0][0]
        em.dma('sp', I('dma_start', out=qT_o[:, :, tile0:tile0 + tww], in_=qst[:, :, :tww]), reads=['gT'], writes=['qT_o'])
        em.dma('sp', I('dma_start', out=kvT_o[:, :, tile0:tile0 + tww], in_=kvst[:, :, :tww]), reads=['gT'], writes=['kvT_o'])
        em.dma('sp', I('dma_start',
            out=v_o[tile0:tile0 + tww, :].rearrange("(c p) n -> p c n", p=128), in_=vst[:, :tww // 128, :]), reads=['gT'], writes=['v_o'])


def make_tiles(n_lat, n_ctx, tw):
    subs = []
    t = 0
    while t < n_lat:
        w = min(512, n_lat - t)
        subs.append((t, w, 0))
        t += w
    t = 0
    while t < n_ctx:
        w = min(512, n_ctx - t)
        subs.append((n_lat + t, w, 1))
        t += w
    tiles = []
    cur = []
    room = tw
    for (t0, w, g) in subs:
        while w > 0:
            take = min(w, room)
            cur.append((t0, take, g))
            t0 += take
            w -= take
            room -= take
            if room == 0:
                tiles.append(cur)
                cur = []
                room = tw
    if cur:
        tiles.append(cur)
    return tiles


def build_part_a(n_lat=2048, n_ctx=256, tw=768, upto=3):
    NT = n_lat + n_ctx
    c = Ctx()
    em = c.em
    xT_d = c.din("xT", [128, 8, NT], F32)
    cvec_d = c.din("cvec", [128, 8, 2], F32)
    wada_d = c.din("w_ada", [D, 9 * D], F32)
    bada_d = c.din("badaT", [128, 72], F32)
    normg_d = c.din("normgT", [128, 6, 8], F32)
    w1_d = c.din("w1", [D, 2 * DFF], F32)
    w2_d = c.din("w2", [DFF, D], F32)
    win_d = c.din("w_in", [D, 2560], F32)
    ropeC_d = c.din("ropeC", [128, NT], F32)
    ropeS_d = c.din("ropeS", [128, NT], F32)
    pm_d = c.din("pmat", [128, 128], BF16)
    x1T_o = c.dout("x1T", [128, 8, NT], F32)
    mods_o = c.dout("modsT", [128, 72, 2], F32)
    qT_o = c.dout("qT", [128, 6, NT], BF16)
    kvT_o = c.dout("kvT", [128, 8, NT], BF16)
    v_o = c.dout("vtok", [NT, 768], BF16)

    xT = c.sb("xT_sb", [128, 8, NT], F32)
    cm = Common(c)
    fb = FFNBufs(c, tw)
    pss = {'ss': c.ps("ps_ss", [128, 512]), 'a': [c.ps("ps_a%d" % i, [128, 512]) for i in range(2)],
           'b': [c.ps("ps_b%d" % i, [128, 512]) for i in range(2)], 'y': [c.ps("ps_y%d" % i, [128, 512]) for i in range(2)]}
    ps_mods = c.ps("ps_mods", [128, 256, 2])
    tiles = make_tiles(n_lat, n_ctx, tw)
    for t in range(0, NT, 512):
        w = min(512, NT - t)
        em.dma('sp', I('dma_start', out=xT[:, :, t:t + w], in_=xT_d[:, :, t:t + w]), writes=xres(t, w))
    cm.load_normg(normg_d)
    cm.compute_mods(cvec_d, wada_d, bada_d, ps_mods, fb)
    cm.compute_coefs()
    em.dma('sp', I('dma_start', out=mods_o, in_=cm.modsT[:]), reads=['modsT'], writes=['mods_o'])
    if upto >= 2:
        ffn(c, cm, fb, xT, 0, tiles, w1_d, w2_d, pss)
    em.dma('sp', I('dma_start', out=x1T_o, in_=xT[:]), reads=xres(0, NT), writes=['x1T_o'])
    if upto >= 3:
        proj_phase(c, cm, fb, xT, tiles, win_d, ropeC_d, ropeS_d, pm_d, pss, qT_o, kvT_o, v_o)
    print("part A instructions:", em.ninst)
    return c.done()


def rope_tables(pos):
    pos = np.asarray(pos)
    row = (pos // GRID_W).astype(np.float32)
    col = (pos % GRID_W).astype(np.float32)
    n_freq = 16
    inv_freq = np.power(np.float32(10000.0), -np.arange(n_freq, dtype=np.float32) / np.float32(n_freq)).astype(np.float32)
    ang = np.concatenate([row[:, None] * inv_freq, col[:, None] * inv_freq], axis=-1).astype(np.float32)
    return np.cos(ang).astype(np.float32), np.sin(ang).astype(np.float32)


def rope_feature_major(cos, sin, n_ctx):
    n = cos.shape[0]
    C = np.ones((128, n + n_ctx), np.float32)
    S = np.zeros((128, n + n_ctx), np.float32)
    p = np.arange(128)
    C[:, :n] = cos.T[p % 32]
    sign = np.where((p % 64) < 32, -1.0, 1.0).astype(np.float32)
    S[:, :n] = sin.T[p % 32] * sign[:, None]
    return C, S


def rope_pmat():
    pm = np.zeros((128, 128), np.float32)
    for po in range(128):
        pi = po + 32 if (po % 64) < 32 else po - 32
        pm[pi, po] = 1.0
    return pm.astype(NPBF)


class Arena:
    def __init__(self, c, nelem):
        self.c = c
        self.t = c.sb("arena", [128, nelem], BF16)
        self.n = nelem
        self.off = 0
        self.gen = 0

    def alloc(self, name, shape, dt):
        n = 1
        for s in shape[1:]:
            n *= s
        if dt == F32:
            n *= 2
        self.off = (self.off + 15) // 16 * 16
        assert self.off + n <= self.n, "arena overflow %s: need %d have %d" % (name, self.off + n, self.n)
        v = self.t[:, self.off:self.off + n]
        self.off += n
        if dt == F32:
            v = v.bitcast(F32)
        if len(shape) == 3:
            v = v.rearrange("p (a b) -> p a b", a=shape[1])
        elif len(shape) == 4:
            v = v.rearrange("p (a b c) -> p a b c", a=shape[1], b=shape[2])
        return v

    def reset(self, to_zero=False):
        self.c.em.barrier()
        if to_zero:
            self.base = 0
        self.off = getattr(self, 'base', 0)
        self.gen += 1

    def set_base(self):
        self.base = self.off


def _barrier(self):
    toks = [(('c', e), self.count[e]) for e in self.ENGS if self.count[e]]
    toks += [(('d', i), cc) for i, cc in enumerate(self.dcount) if cc]
    for eng in self.ENGS:
        waits = []
        wd = self.waited[eng]
        for k, v in toks:
            if wd.get(k, 0) < v:
                wd[k] = v
                waits.append((k, v))
        if waits:
            self.prog[eng].append((waits, None, None))


Emitter.barrier = _barrier


def na_attention(c, ar, P, mixT, q_d, kT_src, v_src, n_kc_tot, qchunks, bias_d, ident, col0, shiftt):
    em = c.em
    NK = n_kc_tot * 128
    kT = ar.alloc("na_kT", [128, 2, NK], BF16)
    vv = ar.alloc("na_v", [128, n_kc_tot, 4, 65], BF16)
    nq = len(qchunks)
    qT = ar.alloc("na_q", [128, 2, nq * 128], BF16)
    g = ar.gen
    rk, rv, rq = 'na_kT%d' % g, 'na_v%d' % g, 'na_q%d' % g
    em.dma('sp', I('dma_start', out=kT, in_=kT_src), writes=[rk])
    em.op('dve', I('memset', vv[:, :, :, 64:65], 1.0), writes=[rv])
    for h in range(4):
        em.dma('sp', I('dma_start', out=vv[:, :, h, 0:64], in_=v_src[:, h * 64:(h + 1) * 64].rearrange("(c p) e -> p c e", p=128)), writes=[rv])
    q0 = qchunks[0][0]
    em.dma('sp', I('dma_start', out=qT, in_=q_d[:, 0:2, q0:q0 + nq * 128]), writes=[rq])
    bias = [ar.alloc("na_bias%d" % i, [128, 4, 6, 128], F32) for i in range(2)]
    ssb = [ar.alloc("na_s%d" % i, [128, 6, 128], F32) for i in range(2)]
    pT = [ar.alloc("na_p%d" % i, [128, 8, 128], BF16) for i in range(2)]
    atok = ar.alloc("na_atok", [128, 256], BF16)
    rec = ar.alloc("na_rec", [128, 4], F32)
    cnt = 0
    for qi, (qcol, kcs, bvar, nb) in enumerate(qchunks):
        bi = qi % 2
        if bvar is not None:
            for h in range(4):
                em.dma('sp', I('dma_start', out=bias[bi][:, h, :, :], in_=bias_d[bvar, h].rearrange("j k q -> k j q")), writes=['na_bias%d_%d' % (bi, g)])
        nk = len(kcs)
        O = P['O'][:, 0, 0:4 * 65].rearrange("p (h e) -> p h e", h=4)

        def na_views(h, si):
            hp = (h % 2) * 64
            hc = h // 2
            SX = P['S'][si][:, 0, :].rearrange("p (j q) -> p j q", j=4)
            SY = P['S'][si][:, 1, :].rearrange("p (j q) -> p j q", j=4)
            return hp, hc, SX, SY, 'psS%d_0' % si, 'psS%d_1' % si

        def issue_S(h, si):
            hp, hc, SX, SY, rsx, rsy = na_views(h, si)
            qap = qT[hp:hp + 64, hc, qi * 128:(qi + 1) * 128]
            for jj, kc in enumerate(kcs):
                d, r = (SX[:, jj, :], rsx) if jj < 4 else (SY[:, jj - 4, :], rsy)
                mm_group(em, d, [(kT[hp:hp + 64, hc, kc * 128:(kc + 1) * 128], qap)], reads=[rk, rq], wres=r)

        def issue_post(h, si):
            hp, hc, SX, SY, rsx, rsy = na_views(h, si)
            sb_ = ssb[si]
            pt = pT[si]
            rs_, rp_ = 'na_s%d_%d' % (si, g), 'na_p%d_%d' % (si, g)
            if nb > 0:
                n1 = min(nb, 4)
                em.op('dve', I('tensor_tensor', out=sb_[:, 0:n1, :], in0=SX[:, 0:n1, :], in1=bias[bi][:, h, 0:n1, :], op=ALU.add),
                      reads=[rsx, 'na_bias%d_%d' % (bi, g)], writes=[rs_])
                if nb > 4:
                    em.op('dve', I('tensor_tensor', out=sb_[:, 4:nb, :], in0=SY[:, 0:nb - 4, :], in1=bias[bi][:, h, 4:nb, :], op=ALU.add),
                          reads=[rsy, 'na_bias%d_%d' % (bi, g)], writes=[rs_])
                em.op('act', I('activation', out=pt[:, 0:nb, :], in_=sb_[:, 0:nb, :], func=AF.Exp, bias=shiftt[:]), reads=[rs_, 'shiftt'], writes=[rp_])
            j = nb
            while j < nk:
                if j < 4:
                    e_ = min(nk, 4)
                    em.op('act', I('activation', out=pt[:, j:e_, :], in_=SX[:, j:e_, :], func=AF.Exp, bias=shiftt[:]), reads=[rsx, 'shiftt'], writes=[rp_])
                else:
                    e_ = nk
                    em.op('act', I('activation', out=pt[:, j:e_, :], in_=SY[:, j - 4:e_ - 4, :], func=AF.Exp, bias=shiftt[:]), reads=[rsy, 'shiftt'], writes=[rp_])
                j = e_
            mm_group(em, O[:, h, :], [(pt[:, jj, :], vv[:, kc, h, :]) for jj, kc in enumerate(kcs)], reads=[rp_, rv], wres='psU1_0')
        issue_S(0, cnt % 2)
        for h in range(4):
            si = cnt % 2
            cnt += 1
            if h + 1 < 4:
                issue_S(h + 1, cnt % 2)
            issue_post(h, si)
        em.op('dve', I('reciprocal', out=rec[:, :], in_=O[:, :, 64]), reads=['psU1_0'], writes=['na_rec%d' % g])
        em.op('dve', I('tensor_tensor', out=atok[:, :].rearrange("p (h e) -> p h e", h=4), in0=O[:, :, 0:64],
                       in1=rec[:, :].unsqueeze(2).to_broadcast([128, 4, 64]), op=ALU.mult),
              reads=['psU1_0', 'na_rec%d' % g], writes=['na_atok%d' % g])
        T = P['T']
        for ch in range(2):
            em.op('pe', I('transpose', out=T[:, ch * 128:(ch + 1) * 128], in_=atok[:, ch * 128:(ch + 1) * 128], identity=ident[:]),
                  reads=['na_atok%d' % g, 'ident'], writes=['psU2_0'])
        col = col0 + qi * 128
        em.op('act', I('activation', out=mixT[:, 0:2, col:col + 128], in_=T[:, 0:256].rearrange("p (c q) -> p c q", c=2), func=AF.Copy),
              reads=['psU2_0'], writes=['mix%d' % (col // 128)])


def pool_mixer(c, ar, P, mixT, pin_src, rcnt_src, n_tok, pwbd, pscale, col0):
    em = c.em
    g = ar.gen
    W = 512
    ubuf = [ar.alloc("pl_u%d" % i, [128, 2, W + 16], BF16) for i in range(2)]
    rc = [ar.alloc("pl_rc%d" % i, [128, 2, W], F32) for i in range(2)]
    s2 = ar.alloc("pl_s2", [128, 2, W + 16], F32)
    s4 = ar.alloc("pl_s4", [128, 2, W + 16], F32)
    s8 = ar.alloc("pl_s8", [128, 2, W + 16], F32)
    s16 = ar.alloc("pl_s16", [128, 2, W + 16], F32)
    pm = ar.alloc("pl_pm", [128, 2, W], F32)
    pb = ar.alloc("pl_pb", [128, 2, W], BF16)
    it = 0
    for t0 in range(0, n_tok, W):
        w = min(W, n_tok - t0)
        u = ubuf[it % 2]
        r = rc[it % 2]
        ru, rr = 'pl_u%d_%d' % (it % 2, g), 'pl_rc%d_%d' % (it % 2, g)
        it += 1
        em.dma('sp', I('dma_start', out=u[:, :, 0:w + 16], in_=pin_src[:, :, t0:t0 + w + 16]), writes=[ru])
        em.dma('sp', I('dma_start', out=r[:, :, 0:w], in_=rcnt_src[:, :, t0:t0 + w]), writes=[rr])
        L = w + 16
        em.op('dve', I('tensor_tensor', out=s2[:, :, 1:L], in0=u[:, :, 0:L - 1], in1=u[:, :, 1:L], op=ALU.add), reads=[ru], writes=['pl_s2_%d' % g])
        em.op('dve', I('tensor_tensor', out=s4[:, :, 2:L - 1], in0=s2[:, :, 1:L - 2], in1=s2[:, :, 3:L], op=ALU.add), reads=['pl_s2_%d' % g], writes=['pl_s4_%d' % g])
        em.op('dve', I('tensor_tensor', out=s8[:, :, 4:L - 3], in0=s4[:, :, 2:L - 5], in1=s4[:, :, 6:L - 1], op=ALU.add), reads=['pl_s4_%d' % g], writes=['pl_s8_%d' % g])
        em.op('dve', I('tensor_tensor', out=s16[:, :, 8:L - 7], in0=s8[:, :, 4:L - 11], in1=s8[:, :, 12:L - 3], op=ALU.add), reads=['pl_s8_%d' % g], writes=['pl_s16_%d' % g])
        for (ch, p0, lvl, lr) in ((0, 0, s2, 'pl_s2_%d' % g), (0, 64, s4, 'pl_s4_%d' % g), (1, 0, s8, 'pl_s8_%d' % g), (1, 64, s16, 'pl_s16_%d' % g)):
            em.op('dve', I('tensor_tensor', out=pm[p0:p0 + 64, ch, 0:w], in0=lvl[p0:p0 + 64, ch, 8:8 + w], in1=r[p0:p0 + 64, ch, 0:w], op=ALU.mult),
                  reads=[lr, rr], writes=['pl_pm_%d' % g])
            em.op('dve', I('tensor_tensor', out=pb[p0:p0 + 64, ch, 0:w], in0=pm[p0:p0 + 64, ch, 0:w], in1=u[p0:p0 + 64, ch, 8:8 + w], op=ALU.subtract),
                  reads=['pl_pm_%d' % g, ru], writes=['pl_pb_%d' % g])
        for ch in range(2):
            pp = P['S'][ch][:, 0, :]
            pr = 'psS%d_0' % ch
            mm_group(em, pp[:, :w], [(pwbd[:, ch, :], pb[:, ch, 0:w])], reads=['pl_pb_%d' % g, 'pwbd'], wres=pr)
            col = col0 + t0
            em.op('act', I('activation', out=mixT[:, 2 + ch, col:col + w], in_=pp[:, :w], func=AF.Copy, scale=pscale[:, ch:ch + 1]),
                  reads=[pr, 'pscale'], writes=['mix%d' % i for i in range(col // 128, (col + w) // 128)])


def diff_attention(c, ar, P, mixT, q_d, qtiles, kT_src, v_src, n_kc, lamt, gsub, ident, shiftt, epst, heads=range(4), loaders=None):
    em = c.em
    g = ar.gen
    NK = n_kc * 128
    kT = [ar.alloc("df_kT%d" % i, [128, NK], BF16) for i in range(2)]
    vv = [ar.alloc("df_v%d" % i, [128, n_kc, 129], BF16) for i in range(2)]
    qmax = max(w for _, w in qtiles)
    qb = [ar.alloc("df_q%d" % i, [128, qmax], BF16) for i in range(2)]
    p12 = [ar.alloc("df_p%d" % i, [128, 2, 512], BF16) for i in range(2)]
    rr = ar.alloc("df_r", [128, 2, 4], F32)
    tt = ar.alloc("df_t", [128, 4, 128], F32)
    oo = ar.alloc("df_o", [128, 4, 128], F32)
    osq = ar.alloc("df_osq", [128, 4, 128], F32)
    ss = ar.alloc("df_ss", [128, 4], F32)
    cb = ar.alloc("df_cb", [128, 4, 128], BF16)
    for i in range(2):
        em.op('dve', I('memset', vv[i][:, :, 128:129], 1.0), writes=['df_v%d_%d' % (i, g)])
    nq = 0
    ns = 0
    for hi, h in enumerate(heads):
        kb, vb = kT[hi % 2], vv[hi % 2]
        rk, rv = 'df_kT%d_%d' % (hi % 2, g), 'df_v%d_%d' % (hi % 2, g)
        if loaders is not None:
            loaders(h, kb, vb, rk, rv)
        else:
            em.dma('sp', I('dma_start', out=kb, in_=kT_src[:, h, :]), writes=[rk])
            step = 16
            for c0 in range(0, n_kc, step):
                c1 = min(n_kc, c0 + step)
                em.dma('sp', I('dma_start', out=vb[:, c0:c1, 0:128],
                               in_=v_src[c0 * 128:c1 * 128, h * 128:(h + 1) * 128].rearrange("(c p) e -> p c e", p=128)), writes=[rv])
        for (t0, w) in qtiles:
            qq = qb[nq % 2]
            rq = 'df_q%d_%d' % (nq % 2, g)
            nq += 1
            em.dma('sp', I('dma_start', out=qq[:, :w], in_=q_d[:, 2 + h, t0:t0 + w]), writes=[rq])
            nqc = w // 128
            U1 = P['U1'][:].rearrange("p a (c e) -> p (a c) e", c=2)
            U2 = P['U2'][:].rearrange("p a (c e) -> p (a c) e", c=2)
            def issue_S(kc, slot):
                S = P['S'][slot]
                rs = ['psS%d_0' % slot, 'psS%d_1' % slot]
                for half in range(2):
                    hp = half * 64
                    mm_group(em, S[:, half, :w], [(kb[hp:hp + 64, kc * 128:(kc + 1) * 128], qq[hp:hp + 64, :w])], reads=[rk, rq], wres=rs[half])

            def issue_exp_pv(kc, slot):
                S = P['S'][slot]
                rs = ['psS%d_0' % slot, 'psS%d_1' % slot]
                pt = p12[slot]
                rp = 'df_p%d_%d' % (slot, g)
                em.op('act', I('activation', out=pt[:, :, :w], in_=S[:, :, :w], func=AF.Exp, bias=shiftt[:]), reads=rs + ['shiftt'], writes=[rp])
                for half, (U, ru) in enumerate(((U1, 'psU1'), (U2, 'psU2'))):
                    for qc in range(nqc):
                        em.op('pe', I('matmul', U[:, qc, 0:129], lhsT=pt[:, half, qc * 128:(qc + 1) * 128], rhs=vb[:, kc, :],
                                      start=(kc == 0 and qc % 2 == 0), stop=(kc == n_kc - 1), skip_group_check=True),
                              reads=[rp, rv], writes=['%s_%d' % (ru, qc // 2)], inc=(qc == nqc - 1))
            issue_S(0, ns % 2)
            for kc in range(n_kc):
                slot = ns % 2
                ns += 1
                if kc + 1 < n_kc:
                    issue_S(kc + 1, ns % 2)
                issue_exp_pv(kc, slot)
            ru1 = ['psU1_0', 'psU1_1'][:(nqc + 1) // 2]
            ru2 = ['psU2_0', 'psU2_1'][:(nqc + 1) // 2]
            rg = 'df_ep%d' % g
            em.op('dve', I('reciprocal', out=rr[:, 0, :nqc], in_=U1[:, :nqc, 128]), reads=ru1, writes=[rg])
            em.op('dve', I('reciprocal', out=rr[:, 1, :nqc], in_=U2[:, :nqc, 128]), reads=ru2, writes=[rg])
            em.op('dve', I('tensor_scalar', out=rr[:, 1, :nqc], in0=rr[:, 1, :nqc], scalar1=lamt[:, 0:1], scalar2=None, op0=ALU.mult), reads=[rg, 'lamt'], writes=[rg])
            em.op('dve', I('tensor_tensor', out=tt[:, :nqc, :], in0=U2[:, :nqc, 0:128], in1=rr[:, 1, :nqc].unsqueeze(2).to_broadcast([128, nqc, 128]), op=ALU.mult),
                  reads=ru2 + [rg], writes=[rg])
            em.op('dve', I('tensor_tensor', out=oo[:, :nqc, :], in0=U1[:, :nqc, 0:128], in1=rr[:, 0, :nqc].unsqueeze(2).to_broadcast([128, nqc, 128]), op=ALU.mult),
                  reads=ru1 + [rg], writes=[rg])
            em.op('dve', I('tensor_tensor', out=oo[:, :nqc, :], in0=oo[:, :nqc, :], in1=tt[:, :nqc, :], op=ALU.subtract), reads=[rg], writes=[rg])
            em.op('dve', I('tensor_tensor', out=osq[:, :nqc, :], in0=oo[:, :nqc, :], in1=oo[:, :nqc, :], op=ALU.mult), reads=[rg], writes=[rg])
            em.op('dve', I('reduce_sum', out=ss[:, :nqc], in_=osq[:, :nqc, :], axis=AX.X), reads=[rg], writes=[rg])
            em.op('act', I('activation', out=ss[:, :nqc], in_=ss[:, :nqc], func=AF.Sqrt, scale=1.0 / 128, bias=epst[:]), reads=[rg, 'epst'], writes=[rg])
            em.op('dve', I('reciprocal', out=ss[:, :nqc], in_=ss[:, :nqc]), reads=[rg], writes=[rg])
            em.op('dve', I('tensor_tensor', out=oo[:, :nqc, :], in0=oo[:, :nqc, :], in1=ss[:, :nqc].unsqueeze(2).to_broadcast([128, nqc, 128]), op=ALU.mult),
                  reads=[rg], writes=[rg])
            em.op('dve', I('tensor_tensor', out=cb[:, :nqc, :], in0=oo[:, :nqc, :], in1=gsub[:, :].unsqueeze(1).to_broadcast([128, nqc, 128]), op=ALU.mult),
                  reads=[rg, 'gsub'], writes=['df_cb%d' % g])
            T = P['T']
            for qc in range(nqc):
                em.op('pe', I('transpose', out=T[:, qc * 128:(qc + 1) * 128], in_=cb[:, qc, :], identity=ident[:]),
                      reads=['df_cb%d' % g, 'ident'], writes=['psU2_0'])
            em.op('act', I('activation', out=mixT[:, 4 + h, t0:t0 + w], in_=T[:, 0:w], func=AF.Copy),
                  reads=['psU2_0'], writes=['mix%d' % i for i in range(t0 // 128, (t0 + w) // 128)])


def na_local_chunks(i, nq):
    if i == 0:
        return 0, 6
    if i == nq - 1:
        return i - 1, 6
    return i, 5


def build_part_b(n_lat=2048, n_ctx=256, tw=768, n_kc_diff=66, na_variants=None, lam_init=0.2, ctx_out=True, n_halo_kc=None):
    NT = n_lat + n_ctx
    nq = n_lat // 128
    if n_halo_kc is None:
        n_halo_kc = nq + 4
    if na_variants is None:
        na_variants = [0] * nq
    c = Ctx()
    em = c.em
    x1T_d = c.din("x1T", [128, 8, NT], F32)
    mods_d = c.din("modsT", [128, 72, 2], F32)
    normg_d = c.din("normgT", [128, 6, 8], F32)
    q_d = c.din("qT", [128, 6, NT], BF16)
    nakT_d = c.din("na_kT", [128, 2, (n_halo_kc + n_ctx // 128) * 128], BF16)
    nav_d = c.din("na_v", [(n_halo_kc + n_ctx // 128) * 128, 256], BF16)
    nakTc_d = c.din("na_kTc", [128, 2, n_ctx], BF16)
    navc_d = c.din("na_vc", [n_ctx, 256], BF16)
    nvar = max(na_variants) + 1
    bias_d = c.din("na_bias", [nvar, 4, 6, 128, 128], F32)
    pin_d = c.din("pinT", [128, 2, n_lat + 16], BF16)
    rcnt_d = c.din("rcnt", [128, 2, n_lat], F32)
    pinc_d = c.din("pinTc", [128, 2, n_ctx + 16], BF16)
    rcntc_d = c.din("rcntc", [128, 2, n_ctx], F32)
    pwbd_d = c.din("pwbd", [128, 2, 128], BF16)
    pscale_d = c.din("pscaleT", [128, 2], F32)
    dkT_d = c.din("dkT", [128, 4, n_kc_diff * 128], BF16)
    dv_d = c.din("dv", [n_kc_diff * 128, 512], BF16)
    dlam_d = c.din("dlam", [128, 256], F32)
    subg_d = c.din("subg", [128, 128], F32)
    ident_d = c.din("ident", [128, 128], BF16)
    wout_d = c.din("w_out", [D, D], F32)
    w1_d = c.din("w1", [D, 2 * DFF], F32)
    w2_d = c.din("w2", [DFF, D], F32)
    x2T_o = c.dout("x2T", [128, 8, NT], F32)

    xT = c.sb("xT_sb", [128, 8, NT], F32)
    mixT_t = c.sb("mixT", [128, 8, NT], BF16)
    mixT = mixT_t[:]
    cm = Common(c)
    ident = c.sb("ident", [128, 128], BF16)
    pwbd = c.sb("pwbd", [128, 2, 128], BF16)
    pscale = c.sb("pscale", [128, 2], F32)
    dlam = c.sb("dlam", [128, 256], F32)
    lamw = c.sb("lamw", [128, 4], F32)
    lamt = c.sb("lamt", [128, 1], F32)
    gsub = c.sb("gsub", [128, 128], F32)
    shiftt = c.sb("shiftt", [128, 1], F32)
    S0 = c.ps("S0", [128, 2, 512])
    S1 = c.ps("S1", [128, 2, 512])
    U1 = c.ps("U1", [128, 2, 512])
    U2 = c.ps("U2", [128, 2, 512])
    P = {'S': [S0, S1], 'U1': U1, 'U2': U2, 'O': U1, 'T': U2[:, 0, :].bitcast(BF16)}
    arena_n = (c.nc.sbuf_bytes_remaining - 2048) // 2 // 16 * 16
    ar = Arena(c, arena_n)
    print("arena elems", arena_n)

    for t in range(0, NT, 512):
        w = min(512, NT - t)
        em.dma('sp', I('dma_start', out=xT[:, :, t:t + w], in_=x1T_d[:, :, t:t + w]), writes=xres(t, w))
    cm.load_normg(normg_d)
    cm.load_mods(mods_d)
    cm.compute_coefs()
    em.dma('sp', I('dma_start', out=ident[:], in_=ident_d), writes=['ident'])
    em.dma('sp', I('dma_start', out=pwbd[:], in_=pwbd_d), writes=['pwbd'])
    em.dma('sp', I('dma_start', out=pscale[:], in_=pscale_d), writes=['pscale'])
    em.dma('sp', I('dma_start', out=dlam[:], in_=dlam_d), writes=['dlam'])
    em.dma('sp', I('dma_start', out=gsub[:], in_=subg_d), writes=['gsub'])
    em.op('dve', I('memset', shiftt[:], EXP_SHIFT), writes=['shiftt'])
    em.op('dve', I('tensor_scalar', out=gsub[:], in0=gsub[:], scalar1=float(1.0 - lam_init), scalar2=None, op0=ALU.mult), reads=['gsub'], writes=['gsub'])
    em.op('dve', I('tensor_tensor', out=dlam[:, 0:64], in0=dlam[:, 0:64], in1=dlam[:, 64:128], op=ALU.mult), reads=['dlam'], writes=['dlam'])
    em.op('dve', I('tensor_tensor', out=dlam[:, 128:192], in0=dlam[:, 128:192], in1=dlam[:, 192:256], op=ALU.mult), reads=['dlam'], writes=['dlam'])
    em.op('dve', I('reduce_sum', out=lamw[:, 0:2], in_=dlam[:].rearrange("p (a b) -> p a b", a=2)[:, :, 0:64], axis=AX.X), reads=['dlam'], writes=['lamw'])
    em.op('act', I('activation', out=lamw[:, 2:4], in_=lamw[:, 0:2], func=AF.Exp), reads=['lamw'], writes=['lamw'])
    em.op('dve', I('tensor_tensor', out=lamt[:], in0=lamw[:, 2:3], in1=lamw[:, 3:4], op=ALU.subtract), reads=['lamw'], writes=['lamt'])
    em.op('dve', I('tensor_scalar', out=lamt[:], in0=lamt[:], scalar1=float(lam_init), scalar2=None, op0=ALU.add), reads=['lamt'], writes=['lamt'])

    nkc_tot = n_halo_kc + n_ctx // 128
    ctx_kcs = [n_halo_kc + i for i in range(n_ctx // 128)]
    qch = []
    for i in range(nq):
        k0, nl = na_local_chunks(i, nq)
        qch.append((i * 128, [k0 + j for j in range(nl)] + ctx_kcs, na_variants[i], nl))
    na_attention(c, ar, P, mixT, q_d, nakT_d, nav_d, nkc_tot, qch, bias_d, ident, 0, shiftt)
    if ctx_out:
        ar.reset()
        qch = [(n_lat + i * 128, list(range(n_ctx // 128)), None, 0) for i in range(n_ctx // 128)]
        na_attention(c, ar, P, mixT, q_d, nakTc_d, navc_d, n_ctx // 128, qch, bias_d, ident, n_lat, shiftt)
    ar.reset()
    pool_mixer(c, ar, P, mixT, pin_d, rcnt_d, n_lat, pwbd, pscale, 0)
    if ctx_out:
        ar.reset()
        pool_mixer(c, ar, P, mixT, pinc_d, rcntc_d, n_ctx, pwbd, pscale, n_lat)
    ar.reset()
    qtiles = [(t, min(512, n_lat - t)) for t in range(0, n_lat, 512)]
    diff_attention(c, ar, P, mixT, q_d, qtiles, dkT_d, dv_d, n_kc_diff, lamt, gsub, ident, shiftt, cm.epst)
    if ctx_out:
        ar.reset()
        ctx0 = n_kc_diff - n_ctx // 128
        qtiles = [(n_lat, n_ctx)]
        diff_attention(c, ar, P, mixT, q_d, qtiles, dkT_d[:, :, ctx0 * 128:], dv_d[ctx0 * 128:, :], n_ctx // 128, lamt, gsub, ident, shiftt, cm.epst)
    ar.reset()
    n_mix = NT if ctx_out else n_lat
    wo = ar.alloc("wo", [128, 8, D], BF16)
    em.dma('pool', I('dma_start', out=wo, in_=wout_d.rearrange("(k p) n -> p k n", p=128)), writes=['wo'])

    class YB:
        pass
    yb = YB()
    yb.ysb = ar.alloc("ysb", [128, 8, 512], F32)
    yb.sq = ar.alloc("sq", [128, 8, 512], BF16)
    yb.tmp = [ar.alloc("tmpf%d" % i, [128, 512], F32) for i in range(2)]
    yb.rstd = ar.alloc("rstd", [128, 512], F32)
    yb.n_tmp = 0
    pss = {'ss': U2[:, 1, :], 'a': [S0[:, 0, :], S0[:, 1, :]], 'b': [S1[:, 0, :], S1[:, 1, :]], 'y': [U1[:, 0, :], U1[:, 1, :]]}
    ny = 0
    for (t0, w, g) in [s_ for tile in make_tiles(n_lat, n_ctx if ctx_out else 0, 512) for s_ in tile]:
        for k in range(8):
            py = pss['y'][ny % 2]
            ry = 'psU1_%d' % (ny % 2)
            ny += 1
            mm_group(em, py[:, :w], [(wo[:, kk, k * 128:(k + 1) * 128], mixT[:, kk, t0:t0 + w]) for kk in range(8)],
                     reads=['wo'] + ['mix%d' % i for i in range(t0 // 128, (t0 + w) // 128)], wres=ry)
            y_evac(c, yb, k, 0, w, py[:, :w], ry)
        sandwich_out(c, cm, yb, xT, 1, t0, w, g, 0, pss['ss'], ss_res='psU2_1')
    ar.reset()
    gT = mixT_t[:].rearrange("p a b -> p (a b)")[:, 0:NFC * tw].rearrange("p (a b) -> p a b", a=NFC) if 8 * NT >= NFC * tw else None
    fb = FFNBufs(c, tw, alloc=ar.alloc, gT=gT)
    tiles = make_tiles(n_lat, n_ctx if ctx_out else 0, tw)
    ffn(c, cm, fb, xT, 2, tiles, w1_d, w2_d, pss, psnames={'ss': 'psU2_1', 'a': ['psS0_0', 'psS0_1'], 'b': ['psS1_0', 'psS1_1'], 'y': ['psU1_0', 'psU1_1']})
    em.dma('sp', I('dma_start', out=x2T_o, in_=xT[:]), reads=xres(0, NT), writes=['x2T_o'])
    print("part B instructions:", em.ninst)
    return c.done()


def na_bias_tiles(rpb, rows_total, q_row0, key_row0, nj=6):
    kr = np.arange(2)[:, None, None, None]
    kc = np.arange(64)[None, :, None, None]
    qr = np.arange(2)[None, None, :, None]
    qc = np.arange(64)[None, None, None, :]
    q_row = q_row0 + qr
    rs = np.clip(q_row - 4, 0, rows_total - 8)
    cs = np.clip(qc - 8, 0, 64 - 16)
    out = np.full((4, nj, 2, 64, 2, 64), NEG, np.float32)
    for j in range(nj):
        key_row = key_row0 + 2 * j + kr
        valid = (key_row >= rs) & (key_row < rs + 8) & (kc >= cs) & (kc < cs + 16) & (key_row >= 0) & (key_row < rows_total)
        valid = np.broadcast_to(valid, (2, 64, 2, 64))
        dr = np.clip(np.broadcast_to(key_row - q_row + 7, (2, 64, 2, 64)), 0, 14)
        dc = np.clip(np.broadcast_to(kc - qc, (2, 64, 2, 64)), -15, 15) + 15
        for h in range(4):
            out[h, j] = np.where(valid, rpb[h][dr, dc], np.float32(NEG))
    return out.reshape(4, nj, 128, 128)


def pool_rcount(t_global, L):
    n = len(t_global)
    out = np.zeros((128, 2, n), np.float32)
    for gi, wdw in enumerate((2, 4, 8, 16)):
        half = wdw // 2
        lo = np.clip(t_global - half, 0, L)
        hi = np.clip(t_global + half, 0, L)
        rc = (1.0 / (hi - lo).astype(np.float32)).astype(np.float32)
        out[(gi % 2) * 64:(gi % 2) * 64 + 64, gi // 2, :] = rc[None, :]
    return out


def halo_cols(arrT, t0, n, halo, L):
    out = np.zeros(arrT.shape[:-1] + (n + 2 * halo,), arrT.dtype)
    a = max(0, t0 - halo)
    b = min(L, t0 + n + halo)
    out[..., a - (t0 - halo):b - (t0 - halo)] = arrT[..., a:b]
    return out


def pool_blockdiag(pool_w):
    out = np.zeros((128, 2, 128), np.float32)
    for gi in range(4):
        p0 = (gi % 2) * 64
        out[p0:p0 + 64, gi // 2, p0:p0 + 64] = pool_w[gi]
    return out.astype(NPBF)


N_LAT = 2048
NT_FULL = N_LAT + CTX
_PROGS = {}


def _fm(a):
    T, F = a.shape
    return np.ascontiguousarray(a.reshape(T, F // 128, 128).transpose(2, 1, 0))


def _unfm(aT):
    return np.ascontiguousarray(aT.transpose(2, 1, 0)).reshape(aT.shape[2], -1)


def _lay_vec(v):
    return np.ascontiguousarray(v.reshape(-1, 128).T)


def _prog(key, fn):
    if key not in _PROGS:
        _PROGS[key] = fn()
    return _PROGS[key]


def kernel_unfused(x, c, ctx, c_ctx, w_ada, b_ada, norm_g, ffn_w1, ffn_w2, w_in, w_out, na_rpb, pool_w, pool_scale, diff_lambda,
           diff_subln_g):
    f32 = np.float32
    x = np.asarray(x, f32)
    ctx = np.asarray(ctx, f32)
    cvals = np.asarray(c, f32)
    c_ctx = np.asarray(c_ctx, f32)
    ncores = 8
    depth = w_ada.shape[0]
    rows_total = SEQ // GRID_W
    nq = N_LAT // 128
    variants = [1, 2] + [0] * (nq - 4) + [3, 4]
    xT = []
    for i in range(ncores):
        b, j = i // 4, i % 4
        xt = np.concatenate([x[b, j * N_LAT:(j + 1) * N_LAT], ctx[b]], axis=0)
        xT.append(_fm(xt))
    ropes = []
    for i in range(ncores):
        j = i % 4
        cos, sin = rope_tables(np.arange(j * N_LAT, (j + 1) * N_LAT))
        ropes.append(rope_feature_major(cos, sin, CTX))
    pmat = rope_pmat()
    ident = np.eye(128, dtype=f32).astype(NPBF)
    TW = 512
    for l in range(depth):
        last = (l == depth - 1)
        lam_init = 0.8 - 0.6 * math.exp(-0.3 * l)
        nca = _prog(('A',), lambda: build_part_a(N_LAT, CTX, TW))
        normgT = np.ascontiguousarray(np.asarray(norm_g[l], f32).reshape(6, 8, 128).transpose(2, 0, 1))
        badaT = np.ascontiguousarray(np.asarray(b_ada[l], f32).reshape(72, 128).T)
        wada_l = np.ascontiguousarray(np.asarray(w_ada[l], f32))
        w1a = np.ascontiguousarray(np.asarray(ffn_w1[l, 0], f32))
        w2a = np.ascontiguousarray(np.asarray(ffn_w2[l, 0], f32))
        win_l = np.ascontiguousarray(np.asarray(w_in[l], f32))
        in_maps = []
        for i in range(ncores):
            b = i // 4
            cvec = np.ascontiguousarray(np.stack([_lay_vec(cvals[b]), _lay_vec(c_ctx)], axis=-1))
            in_maps.append({"xT": xT[i], "cvec": cvec, "w_ada": wada_l, "badaT": badaT, "normgT": normgT, "w1": w1a, "w2": w2a,
                            "w_in": win_l, "ropeC": ropes[i][0], "ropeS": ropes[i][1], "pmat": pmat})
        ra = run_bass_kernel_spmd(nca, in_maps, core_ids=list(range(ncores))).results
        ncb = _prog(('B', l), lambda: build_part_b(N_LAT, CTX, TW, n_kc_diff=(SEQ + CTX) // 128, na_variants=variants,
                                                  lam_init=lam_init, ctx_out=not last))
        w1b_ = np.ascontiguousarray(np.asarray(ffn_w1[l, 1], f32))
        w2b_ = np.ascontiguousarray(np.asarray(ffn_w2[l, 1], f32))
        wout_l = np.ascontiguousarray(np.asarray(w_out[l], f32))
        pwbd = pool_blockdiag(np.asarray(pool_w[l], f32))
        pscaleT = np.ascontiguousarray(np.asarray(pool_scale[l], f32).reshape(2, 128).T)
        dlam = np.ascontiguousarray(np.broadcast_to(np.asarray(diff_lambda[l], f32).reshape(1, 256), (128, 256)))
        subg = np.ascontiguousarray(np.broadcast_to(np.asarray(diff_subln_g[l], f32)[None, :], (128, 128)))
        rpb = np.asarray(na_rpb[l], f32)
        in_maps = []
        for b in range(2):
            cores = [b * 4 + j for j in range(4)]
            kv_lat = np.concatenate([ra[i]["kvT"][:, :, :N_LAT] for i in cores], axis=2)
            kv_ctx = ra[cores[0]]["kvT"][:, :, N_LAT:]
            v_lat = np.concatenate([ra[i]["vtok"][:N_LAT] for i in cores], axis=0)
            v_ctx = ra[cores[0]]["vtok"][N_LAT:]
            dkT = np.ascontiguousarray(np.concatenate([kv_lat[:, 4:8], kv_ctx[:, 4:8]], axis=2))
            dv = np.ascontiguousarray(np.concatenate([v_lat[:, 256:], v_ctx[:, 256:]], axis=0))
            nakTc = np.ascontiguousarray(kv_ctx[:, 0:2])
            navc = np.ascontiguousarray(v_ctx[:, 0:256])
            pinTc = halo_cols(np.ascontiguousarray(kv_ctx[:, 2:4]), 0, CTX, 8, CTX)
            rcntc = pool_rcount(np.arange(CTX), CTX)
            for j in range(4):
                i = cores[j]
                t_start = j * N_LAT
                r0 = t_start // GRID_W
                hk0 = (r0 - 4) * GRID_W
                nhk = (nq + 4) * 128
                na_kT = np.concatenate([halo_cols(kv_lat[:, 0:2], hk0, nhk, 0, SEQ), nakTc], axis=2)
                na_v = np.concatenate([halo_cols(v_lat[:, 0:256].T, hk0, nhk, 0, SEQ).T, navc], axis=0)
                bias = np.full((5, 4, 6, 128, 128), NEG, f32)
                for vi, ci in ((0, 2), (1, 0), (2, 1), (3, nq - 2), (4, nq - 1)):
                    k0, nl = na_local_chunks(ci, nq)
                    bias[vi] = na_bias_tiles(rpb, rows_total, r0 + 2 * ci, r0 - 4 + 2 * k0)
                in_maps.append({
                    "x1T": ra[i]["x1T"], "modsT": ra[i]["modsT"], "normgT": normgT, "qT": ra[i]["qT"],
                    "na_kT": np.ascontiguousarray(na_kT), "na_v": np.ascontiguousarray(na_v), "na_kTc": nakTc, "na_vc": navc,
                    "na_bias": bias,
                    "pinT": halo_cols(kv_lat[:, 2:4], t_start, N_LAT, 8, SEQ), "rcnt": pool_rcount(np.arange(t_start, t_start + N_LAT), SEQ),
                    "pinTc": pinTc, "rcntc": rcntc, "pwbd": pwbd, "pscaleT": pscaleT,
                    "dkT": dkT, "dv": dv, "dlam": dlam, "subg": subg, "ident": ident,
                    "w_out": wout_l, "w1": w1b_, "w2": w2b_,
                })
        rb = run_bass_kernel_spmd(ncb, in_maps, core_ids=list(range(ncores))).results
        xT = [rb[i]["x2T"] for i in range(ncores)]
    out = np.zeros((2, SEQ, D), f32)
    for i in range(ncores):
        b, j = i // 4, i % 4
        out[b, j * N_LAT:(j + 1) * N_LAT] = _unfm(xT[i][:, :, :N_LAT])
    return out


def build_fused(n_lat=2048, n_ctx=256, tw=512, depth=2, group=4, dbg_ctx_out=False):
    NT = n_lat + n_ctx
    nq = n_lat // 128
    n_halo_kc = nq + 4
    nkc_na = n_halo_kc + n_ctx // 128
    n_kc_diff = (group * n_lat + n_ctx) // 128
    variants = [1, 2] + [0] * (nq - 4) + [3, 4] if nq > 4 else list(range(1, nq + 1))
    nvar = max(variants) + 1
    c = Ctx()
    em = c.em
    nc = c.nc
    xT_d = c.din("xT", [128, 8, NT], F32)
    cvec_d = c.din("cvec", [128, 8, 2], F32)
    ropeC_d = c.din("ropeC", [128, NT], F32)
    ropeS_d = c.din("ropeS", [128, NT], F32)
    pm_d = c.din("pmat", [128, 128], BF16)
    ident_d = c.din("ident", [128, 128], BF16)
    rcnt_d = c.din("rcnt", [128, 2, n_lat], F32)
    rcntc_d = c.din("rcntc", [128, 2, n_ctx], F32)
    L = []
    for l in range(depth):
        L.append(dict(
            wada=c.din("w_ada%d" % l, [D, 9 * D], F32), bada=c.din("badaT%d" % l, [128, 72], F32),
            normg=c.din("normgT%d" % l, [128, 6, 8], F32),
            w1a=c.din("w1a%d" % l, [D, 2 * DFF], F32), w2a=c.din("w2a%d" % l, [DFF, D], F32),
            w1b=c.din("w1b%d" % l, [D, 2 * DFF], F32), w2b=c.din("w2b%d" % l, [DFF, D], F32),
            win=c.din("w_in%d" % l, [D, 2560], F32), wout=c.din("w_out%d" % l, [D, D], F32),
            bias=c.din("na_bias%d" % l, [nvar, 4, 6, 128, 128], F32),
            pwbd=c.din("pwbd%d" % l, [128, 2, 128], BF16), pscale=c.din("pscaleT%d" % l, [128, 2], F32),
            dlam=c.din("dlam%d" % l, [128, 256], F32), subg=c.din("subg%d" % l, [128, 128], F32),
        ))
    outT_o = c.dout("outT", [128, 8, n_lat], F32)

    def dram(name, shape, dt):
        return nc.dram_tensor(name, list(shape), dt, kind="Internal").ap()

    xT = c.sb("xT_sb", [128, 8, NT], F32)
    cm = Common(c)
    ident = c.sb("ident", [128, 128], BF16)
    pwbd = c.sb("pwbd", [128, 2, 128], BF16)
    pscale = c.sb("pscale", [128, 2], F32)
    dlam = c.sb("dlam", [128, 256], F32)
    lamw = c.sb("lamw", [128, 4], F32)
    lamt = c.sb("lamt", [128, 1], F32)
    gsub = c.sb("gsub", [128, 128], F32)
    shiftt = c.sb("shiftt", [128, 1], F32)
    zt = c.sb("zeros", [128, 2048], BF16)
    S0 = c.ps("S0", [128, 2, 512])
    S1 = c.ps("S1", [128, 2, 512])
    U1 = c.ps("U1", [128, 2, 512])
    U2 = c.ps("U2", [128, 2, 512])
    P = {'S': [S0, S1], 'U1': U1, 'U2': U2, 'O': U1, 'T': U2[:, 0, :].bitcast(BF16)}
    pss = {'ss': U2[:, 1, :], 'a': [S0[:, 0, :], S0[:, 1, :]], 'b': [S1[:, 0, :], S1[:, 1, :]], 'y': [U1[:, 0, :], U1[:, 1, :]]}
    psn = {'ss': 'psU2_1', 'a': ['psS0_0', 'psS0_1'], 'b': ['psS1_0', 'psS1_1'], 'y': ['psU1_0', 'psU1_1']}
    ps_mods = U2[:, 0, :].rearrange("p (a b) -> p a b", b=2)
    arena_n = (nc.sbuf_bytes_remaining - 2048) // 2 // 16 * 16
    ar = Arena(c, arena_n)
    print("fused arena elems", arena_n)

    for t in range(0, NT, 512):
        w = min(512, NT - t)
        em.dma('sp', I('dma_start', out=xT[:, :, t:t + w], in_=xT_d[:, :, t:t + w]), writes=xres(t, w))
    em.dma('sp', I('dma_start', out=ident[:], in_=ident_d), writes=['ident'])
    em.op('dve', I('memset', shiftt[:], EXP_SHIFT), writes=['shiftt'])
    em.op('dve', I('memset', zt[:], 0.0), writes=['zeros'])
    tiles_all = make_tiles(n_lat, n_ctx, tw)
    wsel_d = c.din("wsel", [128, 2 * group], F32)
    wsel = c.sb("wsel", [128, 2 * group], F32)
    em.dma('sp', I('dma_start', out=wsel[:], in_=wsel_d), writes=['wsel'])

    for l in range(depth):
        W = L[l]
        last = (l == depth - 1)
        ctx_out = (not last) or dbg_ctx_out
        lam_init = 0.8 - 0.6 * math.exp(-0.3 * l)
        ar.reset(to_zero=True)
        em.dma('sp', I('dma_start', out=cm.normg[:], in_=W['normg']), writes=['normg'])
        fb = FFNBufs(c, tw, alloc=ar.alloc, nw1=3, nw2=2)
        cm.compute_mods(cvec_d, W['wada'], W['bada'], ps_mods, fb, alloc=ar.alloc, psname='psU2_0')
        cm.compute_coefs()
        ffn(c, cm, fb, xT, 0, tiles_all, W['w1a'], W['w2a'], pss, psnames=psn)
        qT_l = dram("qT_l%d" % l, [128, 6, NT], BF16)
        kvT_l = dram("kvT_l%d" % l, [128, 8 * NT], BF16)
        v_l = dram("v_l%d" % l, [NT, 768], BF16)
        kvT_l3 = kvT_l.rearrange("p (c t) -> p c t", c=8)
        proj_phase(c, cm, fb, xT, tiles_all, W['win'], ropeC_d, ropeS_d, pm_d, pss, qT_l, kvT_l3, v_l, alloc=ar.alloc, psnames=psn)
        rg = [[g0 * group + j for j in range(group)] for g0 in range(8 // group)]
        kedge_loc = dram("kedge_loc%d" % l, [128, 4 * 512], BF16)
        kedge_loc3 = kedge_loc.rearrange("p (c t) -> p c t", c=4)
        vedge_loc = dram("vedge_loc%d" % l, [512, 256], BF16)
        em.dma('sp', I('dma_start', out=kedge_loc3[:, :, 0:256], in_=kvT_l3[:, 0:4, 0:256]), reads=['kvT_o'], writes=['kedge_loc'])
        em.dma('sp', I('dma_start', out=kedge_loc3[:, :, 256:512], in_=kvT_l3[:, 0:4, n_lat - 256:n_lat]), reads=['kvT_o'], writes=['kedge_loc'])
        em.dma('sp', I('dma_start', out=vedge_loc[0:256, :], in_=v_l[0:256, 0:256]), reads=['v_o'], writes=['vedge_loc'])
        em.dma('sp', I('dma_start', out=vedge_loc[256:512, :], in_=v_l[n_lat - 256:n_lat, 0:256]), reads=['v_o'], writes=['vedge_loc'])
        kedge_g = dram("kedge_g%d" % l, [group * 128, 4 * 512], BF16)
        vedge_g = dram("vedge_g%d" % l, [group * 512, 256], BF16)
        em.coll(I('collective_compute', "AllGather", ALU.bypass, replica_groups=rg, ins=[kedge_loc.opt()], outs=[kedge_g.opt()]),
                reads=['kedge_loc'], writes=['kedge_g'])
        em.coll(I('collective_compute', "AllGather", ALU.bypass, replica_groups=rg, ins=[vedge_loc.opt()], outs=[vedge_g.opt()]),
                reads=['vedge_loc'], writes=['vedge_g'])
        dk_g, dv_g = [], []
        for h in range(4):
            dk_loc = dram("dk_loc%d_%d" % (l, h), [128, n_lat], BF16)
            dv_loc = dram("dv_loc%d_%d" % (l, h), [n_lat, 128], BF16)
            em.dma('sp', I('dma_start', out=dk_loc, in_=kvT_l3[:, 4 + h, 0:n_lat]), reads=['kvT_o'], writes=['dk_loc%d' % h])
            em.dma('sp', I('dma_start', out=dv_loc, in_=v_l[0:n_lat, 256 + h * 128:256 + (h + 1) * 128]), reads=['v_o'], writes=['dv_loc%d' % h])
            dkg = dram("dk_g%d_%d" % (l, h), [group * 128, n_lat], BF16)
            dvg = dram("dv_g%d_%d" % (l, h), [group * n_lat, 128], BF16)
            em.coll(I('collective_compute', "AllGather", ALU.bypass, replica_groups=rg, ins=[dk_loc.opt()], outs=[dkg.opt()]),
                    reads=['dk_loc%d' % h], writes=['dk_g%d' % h])
            em.coll(I('collective_compute', "AllGather", ALU.bypass, replica_groups=rg, ins=[dv_loc.opt()], outs=[dvg.opt()]),
                    reads=['dv_loc%d' % h], writes=['dv_g%d' % h])
            dk_g.append(dkg)
            dv_g.append(dvg)
        ar.reset(to_zero=True)
        ke = ar.alloc("ke", [128, group, 2048], BF16)
        ve = ar.alloc("ve", [128, group, 4, 256], BF16)
        kp = ar.alloc("kp", [128, 2048], BF16)
        kn = ar.alloc("kn", [128, 2048], BF16)
        vp = ar.alloc("vp", [128, 4, 256], BF16)
        vn = ar.alloc("vn", [128, 4, 256], BF16)
        em.dma('sp', I('dma_start', out=ke, in_=kedge_g.rearrange("(r p) n -> p r n", p=128)), reads=['kedge_g'], writes=['ke'])
        for r in range(group):
            em.dma('sp', I('dma_start', out=ve[:, r, :, :], in_=vedge_g[r * 512:(r + 1) * 512, :].rearrange("(a p) n -> p a n", p=128)),
                   reads=['vedge_g'], writes=['ve'])
        for (dst, dres, src, sres, w0) in ((kp, 'kp', lambda r: ke[:, r, :], 'ke', 0), (kn, 'kn', lambda r: ke[:, r, :], 'ke', group),
                                           (vp, 'vp', lambda r: ve[:, r, :, :], 've', 0), (vn, 'vn', lambda r: ve[:, r, :, :], 've', group)):
            em.op('dve', I('tensor_scalar', out=dst, in0=src(0), scalar1=wsel[:, w0:w0 + 1], scalar2=None, op0=ALU.mult),
                  reads=[sres, 'wsel'], writes=[dres])
            for r in range(1, group):
                em.op('dve', I('scalar_tensor_tensor', out=dst, in0=src(r), scalar=wsel[:, w0 + r:w0 + r + 1], in1=dst, op0=ALU.mult, op1=ALU.add),
                      reads=[sres, 'wsel', dres], writes=[dres])
        kp3 = kp.rearrange("p (c t) -> p c t", c=4)
        kn3 = kn.rearrange("p (c t) -> p c t", c=4)
        na_kT_asm = dram("na_kT_asm%d" % l, [128, 2, nkc_na * 128], BF16)
        na_v_asm = dram("na_v_asm%d" % l, [nkc_na * 128, 256], BF16)
        pin_asm = dram("pin_asm%d" % l, [128, 2, n_lat + 16], BF16)
        pinc_asm = dram("pinc_asm%d" % l, [128, 2, n_ctx + 16], BF16)
        em.dma('sp', I('dma_start', out=na_kT_asm[:, :, 0:256], in_=kp3[:, 0:2, 256:512]), reads=['kp'], writes=['na_kT_asm'])
        em.dma('sp', I('dma_start', out=na_kT_asm[:, :, 256 + n_lat:512 + n_lat], in_=kn3[:, 0:2, 0:256]), reads=['kn'], writes=['na_kT_asm'])
        em.dma('sp', I('dma_start', out=na_v_asm[0:256, :].rearrange("(a p) n -> p a n", p=128), in_=vp[:, 2:4, :]), reads=['vp'], writes=['na_v_asm'])
        em.dma('sp', I('dma_start', out=na_v_asm[256 + n_lat:512 + n_lat, :].rearrange("(a p) n -> p a n", p=128), in_=vn[:, 0:2, :]), reads=['vn'], writes=['na_v_asm'])
        em.dma('sp', I('dma_start', out=pin_asm[:, :, 0:8], in_=kp3[:, 2:4, 504:512]), reads=['kp'], writes=['pin_asm'])
        em.dma('sp', I('dma_start', out=pin_asm[:, :, 8 + n_lat:16 + n_lat], in_=kn3[:, 2:4, 0:8]), reads=['kn'], writes=['pin_asm'])
        em.dma('sp', I('dma_start', out=na_kT_asm[:, :, 256:256 + n_lat], in_=kvT_l3[:, 0:2, 0:n_lat]), reads=['kvT_o'], writes=['na_kT_asm'])
        em.dma('sp', I('dma_start', out=na_kT_asm[:, :, 512 + n_lat:], in_=kvT_l3[:, 0:2, n_lat:NT]), reads=['kvT_o'], writes=['na_kT_asm'])
        em.dma('sp', I('dma_start', out=na_v_asm[256:256 + n_lat, :], in_=v_l[0:n_lat, 0:256]), reads=['v_o'], writes=['na_v_asm'])
        em.dma('sp', I('dma_start', out=na_v_asm[512 + n_lat:, :], in_=v_l[n_lat:NT, 0:256]), reads=['v_o'], writes=['na_v_asm'])
        em.dma('sp', I('dma_start', out=pin_asm[:, :, 8:8 + n_lat], in_=kvT_l3[:, 2:4, 0:n_lat]), reads=['kvT_o'], writes=['pin_asm'])
        if ctx_out:
            for (a0, a1) in ((0, 8), (8 + n_ctx, 16 + n_ctx)):
                em.dma('sp', I('dma_start', out=pinc_asm[:, :, a0:a1], in_=zt[:, 0:16].rearrange("p (c t) -> p c t", c=2)), reads=['zeros'], writes=['pinc_asm'])
            em.dma('sp', I('dma_start', out=pinc_asm[:, :, 8:8 + n_ctx], in_=kvT_l3[:, 2:4, n_lat:NT]), reads=['kvT_o'], writes=['pinc_asm'])
        ar.reset(to_zero=True)
        mixT = ar.alloc("mixT", [128, 8, NT], BF16)
        ar.set_base()
        em.dma('sp', I('dma_start', out=pwbd[:], in_=W['pwbd']), writes=['pwbd'])
        em.dma('sp', I('dma_start', out=pscale[:], in_=W['pscale']), writes=['pscale'])
        em.dma('sp', I('dma_start', out=dlam[:], in_=W['dlam']), writes=['dlam'])
        em.dma('sp', I('dma_start', out=gsub[:], in_=W['subg']), writes=['gsub'])
        em.op('dve', I('tensor_scalar', out=gsub[:], in0=gsub[:], scalar1=float(1.0 - lam_init), scalar2=None, op0=ALU.mult), reads=['gsub'], writes=['gsub'])
        em.op('dve', I('tensor_tensor', out=dlam[:, 0:64], in0=dlam[:, 0:64], in1=dlam[:, 64:128], op=ALU.mult), reads=['dlam'], writes=['dlam'])
        em.op('dve', I('tensor_tensor', out=dlam[:, 128:192], in0=dlam[:, 128:192], in1=dlam[:, 192:256], op=ALU.mult), reads=['dlam'], writes=['dlam'])
        em.op('dve', I('reduce_sum', out=lamw[:, 0:2], in_=dlam[:].rearrange("p (a b) -> p a b", a=2)[:, :, 0:64], axis=AX.X), reads=['dlam'], writes=['lamw'])
        em.op('act', I('activation', out=lamw[:, 2:4], in_=lamw[:, 0:2], func=AF.Exp), reads=['lamw'], writes=['lamw'])
        em.op('dve', I('tensor_tensor', out=lamt[:], in0=lamw[:, 2:3], in1=lamw[:, 3:4], op=ALU.subtract), reads=['lamw'], writes=['lamt'])
        em.op('dve', I('tensor_scalar', out=lamt[:], in0=lamt[:], scalar1=float(lam_init), scalar2=None, op0=ALU.add), reads=['lamt'], writes=['lamt'])
        em.barrier()
        ctx_kcs = [n_halo_kc + i for i in range(n_ctx // 128)]
        qch = []
        for i in range(nq):
            k0, nl = na_local_chunks(i, nq)
            qch.append((i * 128, [k0 + j for j in range(nl)] + ctx_kcs, variants[i], nl))
        na_attention(c, ar, P, mixT, qT_l, na_kT_asm, na_v_asm, nkc_na, qch, W['bias'], ident, 0, shiftt)
        if ctx_out:
            ar.reset()
            qch = [(n_lat + i * 128, list(range(n_ctx // 128)), None, 0) for i in range(n_ctx // 128)]
            na_attention(c, ar, P, mixT, qT_l, kvT_l3[:, 0:2, n_lat:NT], v_l[n_lat:NT, 0:256], n_ctx // 128, qch, W['bias'], ident, n_lat, shiftt)
        ar.reset()
        pool_mixer(c, ar, P, mixT, pin_asm, rcnt_d, n_lat, pwbd, pscale, 0)
        if ctx_out:
            ar.reset()
            pool_mixer(c, ar, P, mixT, pinc_asm, rcntc_d, n_ctx, pwbd, pscale, n_lat)
        ar.reset()
        lat_kc = n_lat // 128

        def load_all(h, kb, vb, rk, rv):
            for r in range(group):
                em.dma('sp', I('dma_start', out=kb[:, r * n_lat:(r + 1) * n_lat], in_=dk_g[h][r * 128:(r + 1) * 128, :]), reads=['dk_g%d' % h], writes=[rk])
                em.dma('sp', I('dma_start', out=vb[:, r * lat_kc:(r + 1) * lat_kc, 0:128],
                               in_=dv_g[h][r * n_lat:(r + 1) * n_lat, :].rearrange("(c p) e -> p c e", p=128)),
                       reads=['dv_g%d' % h], writes=[rv])
            em.dma('sp', I('dma_start', out=kb[:, group * n_lat:], in_=kvT_l3[:, 4 + h, n_lat:NT]), reads=['kvT_o'], writes=[rk])
            em.dma('sp', I('dma_start', out=vb[:, group * lat_kc:, 0:128],
                           in_=v_l[n_lat:NT, 256 + h * 128:256 + (h + 1) * 128].rearrange("(c p) e -> p c e", p=128)), reads=['v_o'], writes=[rv])

        def load_ctx(h, kb, vb, rk, rv):
            em.dma('sp', I('dma_start', out=kb, in_=kvT_l3[:, 4 + h, n_lat:NT]), reads=['kvT_o'], writes=[rk])
            em.dma('sp', I('dma_start', out=vb[:, :, 0:128],
                           in_=v_l[n_lat:NT, 256 + h * 128:256 + (h + 1) * 128].rearrange("(c p) e -> p c e", p=128)), reads=['v_o'], writes=[rv])
        qtiles = [(t, min(512, n_lat - t)) for t in range(0, n_lat, 512)]
        diff_attention(c, ar, P, mixT, qT_l, qtiles, None, None, n_kc_diff, lamt, gsub, ident, shiftt, cm.epst, loaders=load_all)
        if ctx_out:
            ar.reset()
            diff_attention(c, ar, P, mixT, qT_l, [(n_lat, n_ctx)], None, None, n_ctx // 128, lamt, gsub, ident, shiftt, cm.epst, loaders=load_ctx)
        ar.reset()
        wo = ar.alloc("wo", [128, 8, D], BF16)
        em.dma('pool', I('dma_start', out=wo, in_=W['wout'].rearrange("(k p) n -> p k n", p=128)), writes=['wo'])

        class YB:
            pass
        yb = YB()
        yb.ysb = ar.alloc("ysb", [128, 8, 512], F32)
        yb.sq = ar.alloc("sq", [128, 8, 512], BF16)
        yb.tmp = [ar.alloc("tmpf%d" % i, [128, 512], F32) for i in range(2)]
        yb.rstd = ar.alloc("rstd", [128, 512], F32)
        yb.n_tmp = 0
        ny = 0
        for (t0, w, g) in [s_ for tile in make_tiles(n_lat, n_ctx if ctx_out else 0, 512) for s_ in tile]:
            for k in range(8):
                py = pss['y'][ny % 2]
                ry = psn['y'][ny % 2]
                ny += 1
                mm_group(em, py[:, :w], [(wo[:, kk, k * 128:(k + 1) * 128], mixT[:, kk, t0:t0 + w]) for kk in range(8)],
                         reads=['wo'] + ['mix%d' % i for i in range(t0 // 128, (t0 + w) // 128)], wres=ry)
                y_evac(c, yb, k, 0, w, py[:, :w], ry)
            sandwich_out(c, cm, yb, xT, 1, t0, w, g, 0, pss['ss'], ss_res=psn['ss'])
        ar.reset(to_zero=True)
        fb = FFNBufs(c, tw, alloc=ar.alloc, nw1=3, nw2=3)
        ffn(c, cm, fb, xT, 2, make_tiles(n_lat, n_ctx if ctx_out else 0, tw), W['w1b'], W['w2b'], pss, psnames=psn)
    em.dma('sp', I('dma_start', out=outT_o, in_=xT[:, :, 0:n_lat]), reads=xres(0, n_lat), writes=['outT_o'])
    print("fused instructions:", em.ninst)
    return c.done()


def fused_inputs(x, c, ctx, c_ctx, w_ada, b_ada, norm_g, ffn_w1, ffn_w2, w_in, w_out, na_rpb, pool_w, pool_scale, diff_lambda,
                 diff_subln_g, n_lat):
    f32 = np.float32
    x = np.asarray(x, f32)
    ctx = np.asarray(ctx, f32)
    cvals = np.asarray(c, f32)
    c_ctx = np.asarray(c_ctx, f32)
    seq = x.shape[1]
    n_ctx = ctx.shape[1]
    group = seq // n_lat
    ncores = x.shape[0] * group
    depth = w_ada.shape[0]
    rows_total = seq // GRID_W
    nq = n_lat // 128
    if nq > 4:
        vmap = ((0, 2), (1, 0), (2, 1), (3, nq - 2), (4, nq - 1))
    else:
        vmap = tuple((i + 1, i) for i in range(nq))
    shared = {"pmat": rope_pmat(), "ident": np.eye(128, dtype=f32).astype(NPBF), "rcntc": pool_rcount(np.arange(n_ctx), n_ctx)}
    for l in range(depth):
        shared["w_ada%d" % l] = np.ascontiguousarray(np.asarray(w_ada[l], f32))
        shared["badaT%d" % l] = np.ascontiguousarray(np.asarray(b_ada[l], f32).reshape(72, 128).T)
        shared["normgT%d" % l] = np.ascontiguousarray(np.asarray(norm_g[l], f32).reshape(6, 8, 128).transpose(2, 0, 1))
        shared["w1a%d" % l] = np.ascontiguousarray(np.asarray(ffn_w1[l, 0], f32))
        shared["w2a%d" % l] = np.ascontiguousarray(np.asarray(ffn_w2[l, 0], f32))
        shared["w1b%d" % l] = np.ascontiguousarray(np.asarray(ffn_w1[l, 1], f32))
        shared["w2b%d" % l] = np.ascontiguousarray(np.asarray(ffn_w2[l, 1], f32))
        shared["w_in%d" % l] = np.ascontiguousarray(np.asarray(w_in[l], f32))
        shared["w_out%d" % l] = np.ascontiguousarray(np.asarray(w_out[l], f32))
        shared["pwbd%d" % l] = pool_blockdiag(np.asarray(pool_w[l], f32))
        shared["pscaleT%d" % l] = np.ascontiguousarray(np.asarray(pool_scale[l], f32).reshape(2, 128).T)
        shared["dlam%d" % l] = np.ascontiguousarray(np.broadcast_to(np.asarray(diff_lambda[l], f32).reshape(1, 256), (128, 256)))
        shared["subg%d" % l] = np.ascontiguousarray(np.broadcast_to(np.asarray(diff_subln_g[l], f32)[None, :], (128, 128)))
    in_maps = []
    for i in range(ncores):
        b, j = i // group, i % group
        t_start = j * n_lat
        r0 = t_start // GRID_W
        m = dict(shared)
        m["xT"] = _fm(np.concatenate([x[b, t_start:t_start + n_lat], ctx[b]], axis=0))
        m["cvec"] = np.ascontiguousarray(np.stack([_lay_vec(cvals[b]), _lay_vec(c_ctx)], axis=-1))
        cos, sin = rope_tables(np.arange(t_start, t_start + n_lat))
        m["ropeC"], m["ropeS"] = rope_feature_major(cos, sin, n_ctx)
        m["rcnt"] = pool_rcount(np.arange(t_start, t_start + n_lat), seq)
        ws = np.zeros((128, 2 * group), f32)
        if j > 0:
            ws[:, j - 1] = 1.0
        if j < group - 1:
            ws[:, group + j + 1] = 1.0
        m["wsel"] = ws
        for l in range(depth):
            rpb = np.asarray(na_rpb[l], f32)
            bias = np.full((len(vmap) + (1 if nq <= 4 else 0), 4, 6, 128, 128), NEG, f32)
            for vi, ci in vmap:
                k0, nl = na_local_chunks(ci, nq)
                bias[vi] = na_bias_tiles(rpb, rows_total, r0 + 2 * ci, r0 - 4 + 2 * k0)
            m["na_bias%d" % l] = bias
        in_maps.append(m)
    return in_maps, ncores, group


def kernel_fused(n_lat=N_LAT, **inputs):
    in_maps, ncores, group = fused_inputs(n_lat=n_lat, **inputs)
    n_ctx = inputs["ctx"].shape[1]
    seq = inputs["x"].shape[1]
    tw = 512 if n_lat >= 2048 else 384
    nc = _prog(('F', n_lat, n_ctx), lambda: build_fused(n_lat, n_ctx, tw, depth=inputs["w_ada"].shape[0], group=group))
    res = run_bass_kernel_spmd(nc, in_maps, core_ids=list(range(ncores))).results
    out = np.zeros((inputs["x"].shape[0], seq, D), np.float32)
    for i in range(ncores):
        b, j = i // group, i % group
        out[b, j * n_lat:(j + 1) * n_lat] = _unfm(res[i]["outT"])
    return out


def kernel(x, c, ctx, c_ctx, w_ada, b_ada, norm_g, ffn_w1, ffn_w2, w_in, w_out, na_rpb, pool_w, pool_scale, diff_lambda,
           diff_subln_g):
    return kernel_fused(n_lat=N_LAT, x=x, c=c, ctx=ctx, c_ctx=c_ctx, w_ada=w_ada, b_ada=b_ada, norm_g=norm_g, ffn_w1=ffn_w1,
                        ffn_w2=ffn_w2, w_in=w_in, w_out=w_out, na_rpb=na_rpb, pool_w=pool_w, pool_scale=pool_scale,
                        diff_lambda=diff_lambda, diff_subln_g=diff_subln_g)
```

```python
import math
import numpy as np
import ml_dtypes
from contextlib import ExitStack
import concourse.bass as bass
import concourse.mybir as mybir
from concourse.bass_utils import run_bass_kernel_spmd

F32 = mybir.dt.float32
BF16 = mybir.dt.bfloat16
AF = mybir.ActivationFunctionType
ALU = mybir.AluOpType
AX = mybir.AxisListType
NPBF = ml_dtypes.bfloat16

D = 1024
DFF = 2816
NFC = 22
SEQ = 8192
CTX = 256
GRID_W = 64
EPS = 1e-6
NEG = -30000.0
EXP_SHIFT = -40.0


class Emitter:
    ENGS = ('pe', 'act', 'dve', 'pool', 'sp')

    def __init__(self, nc, stack, n_dma_sems=16):
        self.nc = nc
        self._stack = stack
        self.cccount = 0
        self.prog = {e: [] for e in self.ENGS}
        self.count = {e: 0 for e in self.ENGS}
        self.waited = {e: {} for e in self.ENGS}
        self.dcount = [0] * n_dma_sems
        self.dnext_q = {e: 0 for e in self.ENGS}
        self.lastw = {}
        self.readers = {}
        self.semobj = {}
        for e in self.ENGS:
            self.semobj[('c', e)] = stack.enter_context(nc.semaphore('c_' + e))
        for i in range(n_dma_sems):
            self.semobj[('d', i)] = stack.enter_context(nc.semaphore('d%d' % i))
        self.ninst = 0

    def _deps(self, eng, reads, writes):
        deps = {}
        own = ('c', eng)

        def add(k, v):
            if deps.get(k, 0) < v:
                deps[k] = v
        skip_own = (eng == 'pe')
        for r in reads:
            t = self.lastw.get(r)
            if t is not None and not (skip_own and t[0] == own):
                add(*t)
        for w in writes:
            t = self.lastw.get(w)
            if t is not None and not (skip_own and t[0] == own):
                add(*t)
            for k, v in self.readers.get(w, {}).items():
                if not (skip_own and k == own):
                    add(k, v)
        waits = []
        wd = self.waited[eng]
        for k, v in deps.items():
            if wd.get(k, 0) < v:
                wd[k] = v
                waits.append((k, v))
        return waits

    def _commit(self, tok, reads, writes):
        for w in writes:
            self.lastw[w] = tok
            self.readers[w] = {}
        for r in reads:
            d = self.readers.setdefault(r, {})
            if d.get(tok[0], 0) < tok[1]:
                d[tok[0]] = tok[1]

    def op(self, eng, fn, reads=(), writes=(), inc=True):
        writes = list(writes) + [r for r in reads if r.startswith('ps') and r not in writes]
        waits = self._deps(eng, reads, writes)
        tok = (('c', eng), self.count[eng] + 1)
        if inc:
            self.count[eng] += 1
        self.prog[eng].append((waits, fn, (tok[0], 1) if inc else None))
        self._commit(tok, reads, writes)
        self.ninst += 1
        return tok

    def dma(self, eng, fn, reads=(), writes=(), bg=False):
        waits = self._deps(eng, reads, writes)
        if bg:
            return self._dma_bg(eng, fn, reads, writes, waits)
        half = len(self.dcount) // 2
        base = 0 if eng == 'pool' else half
        i = base + self.dnext_q[eng]
        self.dnext_q[eng] = (self.dnext_q[eng] + 1) % half
        k = ('d', i)
        wd = self.waited[eng]
        if wd.get(k, 0) < self.dcount[i]:
            wd[k] = self.dcount[i]
            waits.append((k, self.dcount[i]))
        self.dcount[i] += 16
        tok = (k, self.dcount[i])
        self.prog[eng].append((waits, fn, (k, 16)))
        self._commit(tok, reads, writes)
        self.ninst += 1
        return tok

    def _dma_bg(self, eng, fn, reads, writes, waits):
        nb = 8
        if not hasattr(self, 'bcount'):
            self.bcount = [0] * nb
            self.bnext = 0
            for i in range(nb):
                self.semobj[('b', i)] = self._stack.enter_context(self.nc.semaphore('bg%d' % i))
        i = self.bnext
        self.bnext = (self.bnext + 1) % nb
        k = ('b', i)
        wd = self.waited[eng]
        if wd.get(k, 0) < self.bcount[i]:
            wd[k] = self.bcount[i]
            waits.append((k, self.bcount[i]))
        self.bcount[i] += 16
        tok = (k, self.bcount[i])
        self.prog[eng].append((waits, fn, (k, 16)))
        self._commit(tok, reads, writes)
        self.ninst += 1
        return tok

    def coll(self, fn, reads=(), writes=()):
        eng = 'pool'
        waits = self._deps(eng, reads, writes)
        k = ('cc', self.cccount)
        self.semobj[k] = self._stack.enter_context(self.nc.semaphore('cc_sem%d' % self.cccount))
        self.cccount += 1
        tok = (k, 1)
        self.prog[eng].append((waits, fn, (k, 1)))
        self._commit(tok, reads, writes)
        self.ninst += 1
        return tok

    def finish(self, eng='sp'):
        toks = [(('c', e), self.count[e]) for e in self.ENGS if self.count[e]]
        toks += [(('d', i), c) for i, c in enumerate(self.dcount) if c]
        toks += [(('cc', i), 1) for i in range(self.cccount)]
        toks += [(('b', i), v) for i, v in enumerate(getattr(self, 'bcount', [])) if v]
        waits = []
        wd = self.waited[eng]
        for k, v in toks:
            if wd.get(k, 0) < v:
                wd[k] = v
                waits.append((k, v))
        self.prog[eng].append((waits, None, None))

    def emit(self, block):
        def mk(e):
            def body(engh):
                for waits, fn, inc in self.prog[e]:
                    for k, v in waits:
                        engh.wait_ge(self.semobj[k], v)
                    if fn is not None:
                        if fn[0] == '__call__':
                            ins = fn[1](engh)
                        else:
                            ins = getattr(engh, fn[0])(*fn[1], **fn[2])
                        if inc is not None:
                            ins.then_inc(self.semobj[inc[0]], inc[1])
            return body
        block.tensor(mk('pe'))
        block.scalar(mk('act'))
        block.vector(mk('dve'))
        block.gpsimd(mk('pool'))
        block.sync(mk('sp'))


class Ctx:
    def __init__(self):
        self.nc = bass.Bass("TRN2", target_bir_lowering=False)
        self.st = ExitStack()
        self.em = Emitter(self.nc, self.st)
        self.uid = 0

    def sb(self, name, shape, dt):
        return self.st.enter_context(self.nc.sbuf_tensor("s_" + name, list(shape), dt))

    def ps(self, name, shape, dt=F32):
        return self.st.enter_context(self.nc.psum_tensor("p_" + name, list(shape), dt))

    def din(self, name, shape, dt):
        return self.nc.dram_tensor(name, list(shape), dt, kind="ExternalInput").ap()

    def dout(self, name, shape, dt):
        return self.nc.dram_tensor(name, list(shape), dt, kind="ExternalOutput").ap()

    def done(self):
        self.em.finish('sp')
        with self.nc.Block() as block:
            self.em.emit(block)
        self.st.close()
        return self.nc


def I(name, *a, **kw):
    return (name, a, kw)


def mm_group(em, out_ap, pairs, reads, wres, extra_writes=()):
    n = len(pairs)
    for i, (l, r) in enumerate(pairs):
        em.op('pe', I('matmul', out_ap, lhsT=l, rhs=r, start=(i == 0), stop=(i == n - 1)),
              reads=reads, writes=[wres] + list(extra_writes), inc=(i == n - 1))


class Common:
    def __init__(self, c, need_mods_from_wada=True):
        self.c = c
        em = c.em
        self.ones = c.sb("ones_bf", [128, 128], BF16)
        em.op('dve', I('memset', self.ones[:], 1.0), writes=['ones'])
        self.dummy = c.sb("dummy", [128, 1], F32)
        self.epst = c.sb("epst", [128, 1], F32)
        em.op('dve', I('memset', self.epst[:], EPS), writes=['epst'])
        self.modsT = c.sb("modsT", [128, 72, 2], F32)
        self.normg = c.sb("normgT", [128, 6, 8], F32)
        self.A = c.sb("coefA", [128, 3, 8, 2], F32)
        self.G = c.sb("coefG", [128, 3, 8, 2], F32)

    def load_normg(self, normg_d):
        self.c.em.dma('sp', I('dma_start', out=self.normg[:], in_=normg_d), writes=['normg'])

    def compute_mods(self, cvec_d, wada_d, badaT_d, ps_mods, fb, alloc=None, psname='ps_mods'):
        c, em = self.c, self.c.em
        if alloc is None:
            alloc = lambda name, shape, dt: c.sb(name, shape, dt)[:]
        cv = alloc("cvec", [128, 8, 2], F32)
        scv = alloc("scvec", [128, 8, 2], BF16)
        bad = alloc("badaT", [128, 72], F32)
        em.dma('sp', I('dma_start', out=cv, in_=cvec_d), writes=['cv'])
        em.dma('sp', I('dma_start', out=bad, in_=badaT_d), writes=['bad'])
        em.op('act', I('activation', out=scv, in_=cv, func=AF.Silu), reads=['cv'], writes=['scv'])
        wbuf = [flatview(fb.gT, i * 8 * 512, [128, 8, 512]) for i in range(2)]
        wv = wada_d.rearrange("(k p) n -> p k n", p=128)
        for m in range(18):
            wb = wbuf[m % 2]
            em.dma('pool', I('dma_start', out=wb, in_=wv[:, :, m * 512:(m + 1) * 512]),
                   writes=['wada%d' % (m % 2)])
            for fc in range(4):
                mm_group(em, ps_mods[:, m * 4 + fc, :],
                         [(wb[:, k, fc * 128:(fc + 1) * 128], scv[:, k, :]) for k in range(8)],
                         reads=['wada%d' % (m % 2), 'scv'], wres=psname)
        em.op('dve', I('memset', self.dummy[:], 0.0), writes=['dummy', 'gT', 'wada0', 'wada1'])
        for g in range(2):
            em.op('dve', I('tensor_tensor', out=self.modsT[:, :, g], in0=ps_mods[:, 0:72, g], in1=bad, op=ALU.add),
                  reads=[psname, 'bad'], writes=['modsT'])

    def load_mods(self, modsT_d):
        self.c.em.dma('sp', I('dma_start', out=self.modsT[:], in_=modsT_d), writes=['modsT'])

    def compute_coefs(self):
        em = self.c.em
        for idx, res_w in ((0, 0.5), (1, 1.0), (2, 0.5)):
            for g in range(2):
                em.op('dve', I('scalar_tensor_tensor',
                    out=self.A[:, idx, :, g], in0=self.modsT[:, (3 * idx + 1) * 8:(3 * idx + 2) * 8, g], scalar=1.0,
                    in1=self.normg[:, 2 * idx, :], op0=ALU.add, op1=ALU.mult),
                    reads=['modsT', 'normg'], writes=['coefA'])
                em.op('dve', I('scalar_tensor_tensor',
                    out=self.G[:, idx, :, g], in0=self.modsT[:, (3 * idx + 2) * 8:(3 * idx + 3) * 8, g], scalar=res_w,
                    in1=self.normg[:, 2 * idx + 1, :], op0=ALU.mult, op1=ALU.mult),
                    reads=['modsT', 'normg'], writes=['coefG'])

    def shift(self, idx, k, g):
        return self.modsT[:, 3 * idx * 8 + k, g:g + 1]


class FFNBufs:
    def __init__(self, c, tw, alloc=None, gT=None, nw1=2, nw2=2):
        if alloc is None:
            alloc = lambda name, shape, dt: c.sb(name, shape, dt)[:]
        self.tw = tw
        self.hT = alloc("hT", [128, 8, tw], BF16)
        self.gT = gT if gT is not None else alloc("gT", [128, NFC, tw], BF16)
        assert NFC * tw >= 2 * 8 * 512
        self.w1b = [alloc("w1b%d" % i, [128, 2 * 8 * 256], BF16) for i in range(nw1)]
        self.w2b = [alloc("w2b%d" % i, [128, NFC, 256], BF16) for i in range(nw2)]
        self.ysb = alloc("ysb", [128, 8, tw], F32)
        self.sq = alloc("sq", [128, 8, tw], BF16)
        self.tmp = [alloc("tmpf%d" % i, [128, 512], F32) for i in range(2)]
        self.rstd = alloc("rstd", [128, 512], F32)
        self.sa = [alloc("sa%d" % i, [128, 512], BF16) for i in range(2)]
        self.n_tmp = 0


def rms_rstd(c, cm, src_fn, w, ps_ss, sq, rstd, src_reads, tag):
    em = c.em
    for k in range(8):
        em.op('act', I('activation', out=sq[:, k, :w], in_=src_fn(k), func=AF.Square),
              reads=src_reads, writes=['sq'])
    mm_group(em, ps_ss[:, :w], [(cm.ones[:], sq[:, k, :w]) for k in range(8)], reads=['sq', 'ones'], wres=tag)
    em.op('act', I('activation', out=rstd[:, :w], in_=ps_ss[:, :w], func=AF.Sqrt, scale=1.0 / D, bias=cm.epst[:]),
          reads=[tag, 'epst'], writes=['rstd'])
    em.op('dve', I('reciprocal', out=rstd[:, :w], in_=rstd[:, :w]), reads=['rstd'], writes=['rstd'])


def sandwich_in(c, cm, fb, xT, idx, subs, ps_ss, dst, dst_res, ss_res='ps_ss'):
    em = c.em
    for (t0, w, g, off) in subs:
        rms_rstd(c, cm, lambda k: xT[:, k, t0:t0 + w], w, ps_ss, fb.sq, fb.rstd, xres(t0, w), ss_res)
        for k in range(8):
            tmp = fb.tmp[fb.n_tmp % 2]
            tr = 'tmpf%d' % (fb.n_tmp % 2)
            fb.n_tmp += 1
            em.op('dve', I('tensor_tensor', out=tmp[:, :w], in0=xT[:, k, t0:t0 + w], in1=fb.rstd[:, :w], op=ALU.mult),
                  reads=xres(t0, w) + ['rstd'], writes=[tr])
            em.op('act', I('activation', out=dst[:, k, off:off + w], in_=tmp[:, :w], func=AF.Identity,
                                                               scale=cm.A[:, idx, k, g:g + 1], bias=cm.shift(idx, k, g)),
                  reads=[tr, 'coefA', 'modsT'], writes=[dst_res])


def y_evac(c, fb, k, off, w, yp, yres):
    em = c.em
    em.op('dve', I('tensor_copy', out=fb.ysb[:, k, off:off + w], in_=yp), reads=[yres], writes=['ysb'])
    em.op('dve', I('tensor_tensor', out=fb.sq[:, k, off:off + w], in0=fb.ysb[:, k, off:off + w], in1=fb.ysb[:, k, off:off + w], op=ALU.mult),
          reads=['ysb'], writes=['sq'])


def sandwich_out(c, cm, fb, xT, idx, t0, w, g, off, ps_ss, ss_res='ps_ss'):
    em = c.em
    mm_group(em, ps_ss[:, :w], [(cm.ones[:], fb.sq[:, k, off:off + w]) for k in range(8)], reads=['sq', 'ones'], wres=ss_res)
    em.op('act', I('activation', out=fb.rstd[:, :w], in_=ps_ss[:, :w], func=AF.Sqrt, scale=1.0 / D, bias=cm.epst[:]),
          reads=[ss_res, 'epst'], writes=['rstd'])
    em.op('dve', I('reciprocal', out=fb.rstd[:, :w], in_=fb.rstd[:, :w]), reads=['rstd'], writes=['rstd'])
    for k in range(8):
        tmp = fb.tmp[fb.n_tmp % 2]
        tr = 'tmpf%d' % (fb.n_tmp % 2)
        fb.n_tmp += 1
        em.op('dve', I('scalar_tensor_tensor', out=tmp[:, :w], in0=fb.ysb[:, k, off:off + w], scalar=cm.G[:, idx, k, g:g + 1],
                                                                     in1=fb.rstd[:, :w], op0=ALU.mult, op1=ALU.mult),
              reads=['ysb', 'rstd', 'coefG'], writes=[tr])
        em.op('dve', I('tensor_tensor', out=xT[:, k, t0:t0 + w], in0=xT[:, k, t0:t0 + w], in1=tmp[:, :w], op=ALU.add),
              reads=[tr] + xres(t0, w), writes=xres(t0, w))


def ffn(c, cm, fb, xT, idx, tiles, w1_d, w2_d, pss, psnames=None):
    em = c.em
    ps_ss, ps_a, ps_b, ps_y = pss['ss'], pss['a'], pss['b'], pss['y']
    if psnames is None:
        psnames = {'ss': 'ps_ss', 'a': ['ps_a0', 'ps_a1'], 'b': ['ps_b0', 'ps_b1'], 'y': ['ps_y0', 'ps_y1']}
    w1v = w1_d.rearrange("(k p) (two f) -> p k two f", p=128, two=2)
    w2v = w2_d.rearrange("(f p) d -> p f d", p=128)
    nw1 = 0
    nw2 = 0
    nsa = 0
    ny = 0
    subs_list = []
    for tile in tiles:
        subs = []
        off = 0
        for (t0, w, g) in tile:
            subs.append((t0, w, g, off))
            off += w
        subs_list.append(subs)
    sandwich_in(c, cm, fb, xT, idx, subs_list[0], ps_ss, fb.hT, 'hT', ss_res=psnames['ss'])
    for ti, subs in enumerate(subs_list):
        for fp in range(NFC // 2):
            wb = fb.w1b[nw1 % len(fb.w1b)].rearrange("p (a k f) -> p a k f", a=2, k=8)
            wr = 'w1b%d' % (nw1 % len(fb.w1b))
            nw1 += 1
            for two in range(2):
                em.dma('pool', I('dma_start', out=wb[:, two, :, :], in_=w1v[:, :, two, fp * 256:(fp + 1) * 256]), writes=[wr])
            for fi in range(2):
                fc = fp * 2 + fi
                for (t0, w, g, off) in subs:
                    pa = ps_a[nsa % 2]
                    pb = ps_b[nsa % 2]
                    ra, rb = psnames['a'][nsa % 2], psnames['b'][nsa % 2]
                    sa = fb.sa[nsa % 2]
                    rs = 'sa%d' % (nsa % 2)
                    nsa += 1
                    mm_group(em, pa[:, :w], [(wb[:, 0, k, fi * 128:(fi + 1) * 128], fb.hT[:, k, off:off + w]) for k in range(8)],
                             reads=[wr, 'hT'], wres=ra)
                    mm_group(em, pb[:, :w], [(wb[:, 1, k, fi * 128:(fi + 1) * 128], fb.hT[:, k, off:off + w]) for k in range(8)],
                             reads=[wr, 'hT'], wres=rb)
                    em.op('act', I('activation', out=sa[:, :w], in_=pa[:, :w], func=AF.Silu),
                          reads=[ra], writes=[rs])
                    em.op('dve', I('tensor_tensor',
                        out=fb.gT[:, fc, off:off + w], in0=sa[:, :w], in1=pb[:, :w], op=ALU.mult),
                        reads=[rs, rb], writes=['gT'])
        if ti + 1 < len(subs_list):
            sandwich_in(c, cm, fb, xT, idx, subs_list[ti + 1], ps_ss, fb.hT, 'hT', ss_res=psnames['ss'])
        for piece in range(4):
            wb = fb.w2b[nw2 % len(fb.w2b)]
            wr = 'w2b%d' % (nw2 % len(fb.w2b))
            nw2 += 1
            em.dma('pool', I('dma_start', out=wb, in_=w2v[:, :, piece * 256:(piece + 1) * 256]), writes=[wr])
            for (t0, w, g, off) in subs:
                for kk in range(2):
                    k = piece * 2 + kk
                    py = ps_y[ny % 2]
                    ry = psnames['y'][ny % 2]
                    ny += 1
                    mm_group(em, py[:, :w], [(wb[:, f, kk * 128:(kk + 1) * 128], fb.gT[:, f, off:off + w]) for f in range(NFC)],
                             reads=[wr, 'gT'], wres=ry)
                    y_evac(c, fb, k, off, w, py[:, :w], ry)
        for (t0, w, g, off) in subs:
            sandwich_out(c, cm, fb, xT, idx, t0, w, g, off, ps_ss, ss_res=psnames['ss'])


def xres(t0, w):
    return ['x%d' % i for i in range(t0 // 128, (t0 + w + 127) // 128)]


def flatview(ap3, n0, shape):
    flat = ap3.rearrange("p a b -> p (a b)")
    n = 1
    for s in shape[1:]:
        n *= s
    v = flat[:, n0:n0 + n]
    if len(shape) == 2:
        return v
    if len(shape) == 3:
        return v.rearrange("p (a b) -> p a b", a=shape[1])
    return v.rearrange("p (a b c) -> p a b c", a=shape[1], b=shape[2])


def proj_phase(c, cm, fb, xT, tiles, win_d, ropeC_d, ropeS_d, pm_d, pss, qT_o, kvT_o, v_o, alloc=None, psnames=None):
    em = c.em
    ps_ss, ps_a, ps_b, ps_y = pss['ss'], pss['a'], pss['b'], pss['y']
    tw = fb.tw
    winv = win_d.rearrange("(k p) n -> p k n", p=128)
    if alloc is None:
        alloc = lambda name, shape, dt: c.sb(name, shape, dt)[:]
    if psnames is None:
        psnames = {'ss': 'ps_ss', 'a': ['ps_a0', 'ps_a1'], 'b': ['ps_b0', 'ps_b1'], 'y': ['ps_y0', 'ps_y1']}
    pmat = alloc("pmat", [128, 128], BF16)
    em.dma('sp', I('dma_start', out=pmat, in_=pm_d), writes=['pmat'])
    ct = [alloc("ropec%d" % i, [128, 512], F32) for i in range(2)]
    sn = [alloc("ropes%d" % i, [128, 512], F32) for i in range(2)]
    qb = [alloc("qb%d" % i, [128, 512], BF16) for i in range(2)]
    t2 = [alloc("t2_%d" % i, [128, 512], F32) for i in range(2)]
    qst = flatview(fb.gT, 0, [128, 6, tw])
    kvst = flatview(fb.gT, 6 * tw, [128, 8, tw])
    vst = flatview(fb.gT, 14 * tw, [128, tw // 128, 768])
    cnt = {'w': 0, 'p': 0, 'r': 0, 't': 0}

    def next_ps():
        i = cnt['p'] % 4
        cnt['p'] += 1
        return ([ps_a[0], ps_a[1], ps_b[0], ps_b[1]][i], (psnames['a'] + psnames['b'])[i])

    for tile in tiles:
        subs = []
        off = 0
        for (t0, w, g) in tile:
            subs.append((t0, w, g, off))
            off += w
        tww = off
        sandwich_in(c, cm, fb, xT, 1, subs, ps_ss, fb.hT, 'hT', ss_res=psnames['ss'])
        for piece in range(5):
            wb4 = fb.w1b[cnt['w'] % len(fb.w1b)]
            wr = 'w1b%d' % (cnt['w'] % len(fb.w1b))
            cnt['w'] += 1
            wb = wb4.rearrange("p (k n) -> p k n", k=8)
            em.dma('pool', I('dma_start', out=wb, in_=winv[:, :, piece * 512:(piece + 1) * 512]), writes=[wr])
            for (t0, w, g, off) in subs:
                if piece == 0:
                    fm = [(0, 'q', 0, 0.125, False), (1, 'q', 1, 0.125, False), (2, 'kv', 0, 1.0, False), (3, 'kv', 1, 1.0, False)]
                elif piece == 1:
                    fm = [(2, 'kv', 2, 1.0, False), (3, 'kv', 3, 1.0, False)]
                elif piece == 2:
                    fm = [(i, 'q', 2 + i, 0.125, True) for i in range(4)]
                elif piece == 3:
                    fm = [(i, 'kv', 4 + i, 1.0, True) for i in range(4)]
                else:
                    fm = []
                if piece in (2, 3):
                    ci = cnt['r'] % 2
                    cnt['r'] += 1
                    em.dma('sp', I('dma_start', out=ct[ci][:, :w], in_=ropeC_d[:, t0:t0 + w]), writes=['ropec%d' % ci])
                    em.dma('sp', I('dma_start', out=sn[ci][:, :w], in_=ropeS_d[:, t0:t0 + w]), writes=['ropes%d' % ci])
                for (lc, kind, oc, scale, rope) in fm:
                    pp, pr = next_ps()
                    mm_group(em, pp[:, :w], [(wb[:, k, lc * 128:(lc + 1) * 128], fb.hT[:, k, off:off + w]) for k in range(8)],
                             reads=[wr, 'hT'], wres=pr)
                    dst = (qst if kind == 'q' else kvst)[:, oc, off:off + w]
                    if not rope:
                        em.op('act', I('activation', out=dst, in_=pp[:, :w], func=AF.Copy, scale=scale),
                              reads=[pr], writes=['gT'])
                    else:
                        ti = cnt['t'] % 2
                        cnt['t'] += 1
                        em.op('act', I('activation', out=qb[ti][:, :w], in_=pp[:, :w], func=AF.Copy, scale=scale),
                              reads=[pr], writes=['qb%d' % ti])
                        py = ps_y[ti]
                        pyr = psnames['y'][ti]
                        mm_group(em, py[:, :w], [(pmat, qb[ti][:, :w])], reads=['pmat', 'qb%d' % ti], wres=pyr)
                        tmp = fb.tmp[ti]
                        em.op('dve', I('scalar_tensor_tensor',
                            out=tmp[:, :w], in0=pp[:, :w], scalar=scale, in1=ct[ci][:, :w], op0=ALU.mult, op1=ALU.mult),
                            reads=[pr, 'ropec%d' % ci], writes=['tmpf%d' % ti])
                        em.op('dve', I('tensor_tensor', out=t2[ti][:, :w], in0=py[:, :w], in1=sn[ci][:, :w], op=ALU.mult),
                              reads=[pyr, 'ropes%d' % ci], writes=['t2_%d' % ti])
                        em.op('dve', I('tensor_tensor', out=dst, in0=tmp[:, :w], in1=t2[ti][:, :w], op=ALU.add),
                              reads=['tmpf%d' % ti, 't2_%d' % ti], writes=['gT'])
                if piece in (1, 4):
                    c0, ncol, vo = (0, 256, 0) if piece == 1 else (0, 512, 256)
                    for tcn in range(w // 128):
                        pp, pr = next_ps()
                        tk = off + tcn * 128
                        mm_group(em, pp[:, :ncol], [(fb.hT[:, k, tk:tk + 128], wb[:, k, c0:c0 + ncol]) for k in range(8)],
                                 reads=[wr, 'hT'], wres=pr)
                        em.op('act', I('activation', out=vst[:, tk // 128, vo:vo + ncol], in_=pp[:, :ncol], func=AF.Copy),
                              reads=[pr], writes=['gT'])
        tile0 = tile[0][0]
        em.dma('sp', I('dma_start', out=qT_o[:, :, tile0:tile0 + tww], in_=qst[:, :, :tww]), reads=['gT'], writes=['qT_o'])
        em.dma('sp', I('dma_start', out=kvT_o[:, :, tile0:tile0 + tww], in_=kvst[:, :, :tww]), reads=['gT'], writes=['kvT_o'])
        em.dma('sp', I('dma_start',
            out=v_o[tile0:tile0 + tww, :].rearrange("(c p) n -> p c n", p=128), in_=vst[:, :tww // 128, :]), reads=['gT'], writes=['v_o'])


def make_tiles(n_lat, n_ctx, tw):
    subs = []
    t = 0
    while t < n_lat:
        w = min(512, n_lat - t)
        subs.append((t, w, 0))
        t += w
    t = 0
    while t < n_ctx:
        w = min(512, n_ctx - t)
        subs.append((n_lat + t, w, 1))
        t += w
    tiles = []
    cur = []
    room = tw
    for (t0, w, g) in subs:
        while w > 0:
            take = min(w, room)
            cur.append((t0, take, g))
            t0 += take
            w -= take
            room -= take
            if room == 0:
                tiles.append(cur)
                cur = []
                room = tw
    if cur:
        tiles.append(cur)
    return tiles


def build_part_a(n_lat=2048, n_ctx=256, tw=768, upto=3):
    NT = n_lat + n_ctx
    c = Ctx()
    em = c.em
    xT_d = c.din("xT", [128, 8, NT], F32)
    cvec_d = c.din("cvec", [128, 8, 2], F32)
    wada_d = c.din("w_ada", [D, 9 * D], F32)
    bada_d = c.din("badaT", [128, 72], F32)
    normg_d = c.din("normgT", [128, 6, 8], F32)
    w1_d = c.din("w1", [D, 2 * DFF], F32)
    w2_d = c.din("w2", [DFF, D], F32)
    win_d = c.din("w_in", [D, 2560], F32)
    ropeC_d = c.din("ropeC", [128, NT], F32)
    ropeS_d = c.din("ropeS", [128, NT], F32)
    pm_d = c.din("pmat", [128, 128], BF16)
    x1T_o = c.dout("x1T", [128, 8, NT], F32)
    mods_o = c.dout("modsT", [128, 72, 2], F32)
    qT_o = c.dout("qT", [128, 6, NT], BF16)
    kvT_o = c.dout("kvT", [128, 8, NT], BF16)
    v_o = c.dout("vtok", [NT, 768], BF16)

    xT = c.sb("xT_sb", [128, 8, NT], F32)
    cm = Common(c)
    fb = FFNBufs(c, tw)
    pss = {'ss': c.ps("ps_ss", [128, 512]), 'a': [c.ps("ps_a%d" % i, [128, 512]) for i in range(2)],
           'b': [c.ps("ps_b%d" % i, [128, 512]) for i in range(2)], 'y': [c.ps("ps_y%d" % i, [128, 512]) for i in range(2)]}
    ps_mods = c.ps("ps_mods", [128, 256, 2])
    tiles = make_tiles(n_lat, n_ctx, tw)
    for t in range(0, NT, 512):
        w = min(512, NT - t)
        em.dma('sp', I('dma_start', out=xT[:, :, t:t + w], in_=xT_d[:, :, t:t + w]), writes=xres(t, w))
    cm.load_normg(normg_d)
    cm.compute_mods(cvec_d, wada_d, bada_d, ps_mods, fb)
    cm.compute_coefs()
    em.dma('sp', I('dma_start', out=mods_o, in_=cm.modsT[:]), reads=['modsT'], writes=['mods_o'])
    if upto >= 2:
        ffn(c, cm, fb, xT, 0, tiles, w1_d, w2_d, pss)
    em.dma('sp', I('dma_start', out=x1T_o, in_=xT[:]), reads=xres(0, NT), writes=['x1T_o'])
    if upto >= 3:
        proj_phase(c, cm, fb, xT, tiles, win_d, ropeC_d, ropeS_d, pm_d, pss, qT_o, kvT_o, v_o)
    print("part A instructions:", em.ninst)
    return c.done()


def rope_tables(pos):
    pos = np.asarray(pos)
    row = (pos // GRID_W).astype(np.float32)
    col = (pos % GRID_W).astype(np.float32)
    n_freq = 16
    inv_freq = np.power(np.float32(10000.0), -np.arange(n_freq, dtype=np.float32) / np.float32(n_freq)).astype(np.float32)
    ang = np.concatenate([row[:, None] * inv_freq, col[:, None] * inv_freq], axis=-1).astype(np.float32)
    return np.cos(ang).astype(np.float32), np.sin(ang).astype(np.float32)


def rope_feature_major(cos, sin, n_ctx):
    n = cos.shape[0]
    C = np.ones((128, n + n_ctx), np.float32)
    S = np.zeros((128, n + n_ctx), np.float32)
    p = np.arange(128)
    C[:, :n] = cos.T[p % 32]
    sign = np.where((p % 64) < 32, -1.0, 1.0).astype(np.float32)
    S[:, :n] = sin.T[p % 32] * sign[:, None]
    return C, S


def rope_pmat():
    pm = np.zeros((128, 128), np.float32)
    for po in range(128):
        pi = po + 32 if (po % 64) < 32 else po - 32
        pm[pi, po] = 1.0
    return pm.astype(NPBF)


class Arena:
    def __init__(self, c, nelem):
        self.c = c
        self.t = c.sb("arena", [128, nelem], BF16)
        self.n = nelem
        self.off = 0
        self.gen = 0

    def alloc(self, name, shape, dt):
        n = 1
        for s in shape[1:]:
            n *= s
        if dt == F32:
            n *= 2
        self.off = (self.off + 15) // 16 * 16
        assert self.off + n <= self.n, "arena overflow %s: need %d have %d" % (name, self.off + n, self.n)
        v = self.t[:, self.off:self.off + n]
        self.off += n
        if dt == F32:
            v = v.bitcast(F32)
        if len(shape) == 3:
            v = v.rearrange("p (a b) -> p a b", a=shape[1])
        elif len(shape) == 4:
            v = v.rearrange("p (a b c) -> p a b c", a=shape[1], b=shape[2])
        return v

    def reset(self, to_zero=False):
        self.c.em.barrier()
        if to_zero:
            self.base = 0
        self.off = getattr(self, 'base', 0)
        self.gen += 1

    def set_base(self):
        self.base = self.off


def _barrier(self):
    toks = [(('c', e), self.count[e]) for e in self.ENGS if self.count[e]]
    toks += [(('d', i), cc) for i, cc in enumerate(self.dcount) if cc]
    for eng in self.ENGS:
        waits = []
        wd = self.waited[eng]
        for k, v in toks:
            if wd.get(k, 0) < v:
                wd[k] = v
                waits.append((k, v))
        if waits:
            self.prog[eng].append((waits, None, None))


Emitter.barrier = _barrier


def na_attention(c, ar, P, mixT, q_d, kT_src, v_src, n_kc_tot, qchunks, bias_d, ident, col0, shiftt):
    em = c.em
    NK = n_kc_tot * 128
    kT = ar.alloc("na_kT", [128, 2, NK], BF16)
    vv = ar.alloc("na_v", [128, n_kc_tot, 4, 65], BF16)
    nq = len(qchunks)
    qT = ar.alloc("na_q", [128, 2, nq * 128], BF16)
    g = ar.gen
    rk, rv, rq = 'na_kT%d' % g, 'na_v%d' % g, 'na_q%d' % g
    em.dma('sp', I('dma_start', out=kT, in_=kT_src), writes=[rk])
    em.op('dve', I('memset', vv[:, :, :, 64:65], 1.0), writes=[rv])
    for h in range(4):
        em.dma('sp', I('dma_start', out=vv[:, :, h, 0:64], in_=v_src[:, h * 64:(h + 1) * 64].rearrange("(c p) e -> p c e", p=128)), writes=[rv])
    q0 = qchunks[0][0]
    em.dma('sp', I('dma_start', out=qT, in_=q_d[:, 0:2, q0:q0 + nq * 128]), writes=[rq])
    bias = [ar.alloc("na_bias%d" % i, [128, 4, 6, 128], F32) for i in range(2)]
    ssb = [ar.alloc("na_s%d" % i, [128, 6, 128], F32) for i in range(2)]
    pT = [ar.alloc("na_p%d" % i, [128, 8, 128], BF16) for i in range(2)]
    atok = ar.alloc("na_atok", [128, 256], BF16)
    rec = ar.alloc("na_rec", [128, 4], F32)
    cnt = 0
    for qi, (qcol, kcs, bvar, nb) in enumerate(qchunks):
        bi = qi % 2
        if bvar is not None:
            for h in range(4):
                em.dma('sp', I('dma_start', out=bias[bi][:, h, :, :], in_=bias_d[bvar, h].rearrange("j k q -> k j q")), writes=['na_bias%d_%d' % (bi, g)])
        nk = len(kcs)
        O = P['O'][:, 0, 0:4 * 65].rearrange("p (h e) -> p h e", h=4)

        def na_views(h, si):
            hp = (h % 2) * 64
            hc = h // 2
            SX = P['S'][si][:, 0, :].rearrange("p (j q) -> p j q", j=4)
            SY = P['S'][si][:, 1, :].rearrange("p (j q) -> p j q", j=4)
            return hp, hc, SX, SY, 'psS%d_0' % si, 'psS%d_1' % si

        def issue_S(h, si):
            hp, hc, SX, SY, rsx, rsy = na_views(h, si)
            qap = qT[hp:hp + 64, hc, qi * 128:(qi + 1) * 128]
            for jj, kc in enumerate(kcs):
                d, r = (SX[:, jj, :], rsx) if jj < 4 else (SY[:, jj - 4, :], rsy)
                mm_group(em, d, [(kT[hp:hp + 64, hc, kc * 128:(kc + 1) * 128], qap)], reads=[rk, rq], wres=r)

        def issue_post(h, si):
            hp, hc, SX, SY, rsx, rsy = na_views(h, si)
            sb_ = ssb[si]
            pt = pT[si]
            rs_, rp_ = 'na_s%d_%d' % (si, g), 'na_p%d_%d' % (si, g)
            if nb > 0:
                n1 = min(nb, 4)
                em.op('dve', I('tensor_tensor', out=sb_[:, 0:n1, :], in0=SX[:, 0:n1, :], in1=bias[bi][:, h, 0:n1, :], op=ALU.add),
                      reads=[rsx, 'na_bias%d_%d' % (bi, g)], writes=[rs_])
                if nb > 4:
                    em.op('dve', I('tensor_tensor', out=sb_[:, 4:nb, :], in0=SY[:, 0:nb - 4, :], in1=bias[bi][:, h, 4:nb, :], op=ALU.add),
                          reads=[rsy, 'na_bias%d_%d' % (bi, g)], writes=[rs_])
                em.op('act', I('activation', out=pt[:, 0:nb, :], in_=sb_[:, 0:nb, :], func=AF.Exp, bias=shiftt[:]), reads=[rs_, 'shiftt'], writes=[rp_])
            j = nb
            while j < nk:
                if j < 4:
                    e_ = min(nk, 4)
                    em.op('act', I('activation', out=pt[:, j:e_, :], in_=SX[:, j:e_, :], func=AF.Exp, bias=shiftt[:]), reads=[rsx, 'shiftt'], writes=[rp_])
                else:
                    e_ = nk
                    em.op('act', I('activation', out=pt[:, j:e_, :], in_=SY[:, j - 4:e_ - 4, :], func=AF.Exp, bias=shiftt[:]), reads=[rsy, 'shiftt'], writes=[rp_])
                j = e_
            mm_group(em, O[:, h, :], [(pt[:, jj, :], vv[:, kc, h, :]) for jj, kc in enumerate(kcs)], reads=[rp_, rv], wres='psU1_0')
        issue_S(0, cnt % 2)
        for h in range(4):
            si = cnt % 2
            cnt += 1
            if h + 1 < 4:
                issue_S(h + 1, cnt % 2)
            issue_post(h, si)
        em.op('dve', I('reciprocal', out=rec[:, :], in_=O[:, :, 64]), reads=['psU1_0'], writes=['na_rec%d' % g])
        em.op('dve', I('tensor_tensor', out=atok[:, :].rearrange("p (h e) -> p h e", h=4), in0=O[:, :, 0:64],
                       in1=rec[:, :].unsqueeze(2).to_broadcast([128, 4, 64]), op=ALU.mult),
              reads=['psU1_0', 'na_rec%d' % g], writes=['na_atok%d' % g])
        T = P['T']
        for ch in range(2):
            em.op('pe', I('transpose', out=T[:, ch * 128:(ch + 1) * 128], in_=atok[:, ch * 128:(ch + 1) * 128], identity=ident[:]),
                  reads=['na_atok%d' % g, 'ident'], writes=['psU2_0'])
        col = col0 + qi * 128
        em.op('act', I('activation', out=mixT[:, 0:2, col:col + 128], in_=T[:, 0:256].rearrange("p (c q) -> p c q", c=2), func=AF.Copy),
              reads=['psU2_0'], writes=['mix%d' % (col // 128)])


def pool_mixer(c, ar, P, mixT, pin_src, rcnt_src, n_tok, pwbd, pscale, col0):
    em = c.em
    g = ar.gen
    W = 512
    ubuf = [ar.alloc("pl_u%d" % i, [128, 2, W + 16], BF16) for i in range(2)]
    rc = [ar.alloc("pl_rc%d" % i, [128, 2, W], F32) for i in range(2)]
    s2 = ar.alloc("pl_s2", [128, 2, W + 16], F32)
    s4 = ar.alloc("pl_s4", [128, 2, W + 16], F32)
    s8 = ar.alloc("pl_s8", [128, 2, W + 16], F32)
    s16 = ar.alloc("pl_s16", [128, 2, W + 16], F32)
    pm = ar.alloc("pl_pm", [128, 2, W], F32)
    pb = ar.alloc("pl_pb", [128, 2, W], BF16)
    it = 0
    for t0 in range(0, n_tok, W):
        w = min(W, n_tok - t0)
        u = ubuf[it % 2]
        r = rc[it % 2]
        ru, rr = 'pl_u%d_%d' % (it % 2, g), 'pl_rc%d_%d' % (it % 2, g)
        it += 1
        em.dma('sp', I('dma_start', out=u[:, :, 0:w + 16], in_=pin_src[:, :, t0:t0 + w + 16]), writes=[ru])
        em.dma('sp', I('dma_start', out=r[:, :, 0:w], in_=rcnt_src[:, :, t0:t0 + w]), writes=[rr])
        L = w + 16
        em.op('dve', I('tensor_tensor', out=s2[:, :, 1:L], in0=u[:, :, 0:L - 1], in1=u[:, :, 1:L], op=ALU.add), reads=[ru], writes=['pl_s2_%d' % g])
        em.op('dve', I('tensor_tensor', out=s4[:, :, 2:L - 1], in0=s2[:, :, 1:L - 2], in1=s2[:, :, 3:L], op=ALU.add), reads=['pl_s2_%d' % g], writes=['pl_s4_%d' % g])
        em.op('dve', I('tensor_tensor', out=s8[:, :, 4:L - 3], in0=s4[:, :, 2:L - 5], in1=s4[:, :, 6:L - 1], op=ALU.add), reads=['pl_s4_%d' % g], writes=['pl_s8_%d' % g])
        em.op('dve', I('tensor_tensor', out=s16[:, :, 8:L - 7], in0=s8[:, :, 4:L - 11], in1=s8[:, :, 12:L - 3], op=ALU.add), reads=['pl_s8_%d' % g], writes=['pl_s16_%d' % g])
        for (ch, p0, lvl, lr) in ((0, 0, s2, 'pl_s2_%d' % g), (0, 64, s4, 'pl_s4_%d' % g), (1, 0, s8, 'pl_s8_%d' % g), (1, 64, s16, 'pl_s16_%d' % g)):
            em.op('dve', I('tensor_tensor', out=pm[p0:p0 + 64, ch, 0:w], in0=lvl[p0:p0 + 64, ch, 8:8 + w], in1=r[p0:p0 + 64, ch, 0:w], op=ALU.mult),
                  reads=[lr, rr], writes=['pl_pm_%d' % g])
            em.op('dve', I('tensor_tensor', out=pb[p0:p0 + 64, ch, 0:w], in0=pm[p0:p0 + 64, ch, 0:w], in1=u[p0:p0 + 64, ch, 8:8 + w], op=ALU.subtract),
                  reads=['pl_pm_%d' % g, ru], writes=['pl_pb_%d' % g])
        for ch in range(2):
            pp = P['S'][ch][:, 0, :]
            pr = 'psS%d_0' % ch
            mm_group(em, pp[:, :w], [(pwbd[:, ch, :], pb[:, ch, 0:w])], reads=['pl_pb_%d' % g, 'pwbd'], wres=pr)
            col = col0 + t0
            em.op('act', I('activation', out=mixT[:, 2 + ch, col:col + w], in_=pp[:, :w], func=AF.Copy, scale=pscale[:, ch:ch + 1]),
                  reads=[pr, 'pscale'], writes=['mix%d' % i for i in range(col // 128, (col + w) // 128)])


def diff_attention(c, ar, P, mixT, q_d, qtiles, kT_src, v_src, n_kc, lamt, gsub, ident, shiftt, epst, heads=range(4), loaders=None):
    em = c.em
    g = ar.gen
    NK = n_kc * 128
    kT = [ar.alloc("df_kT%d" % i, [128, NK], BF16) for i in range(2)]
    vv = [ar.alloc("df_v%d" % i, [128, n_kc, 129], BF16) for i in range(2)]
    qmax = max(w for _, w in qtiles)
    qb = [ar.alloc("df_q%d" % i, [128, qmax], BF16) for i in range(2)]
    p12 = [ar.alloc("df_p%d" % i, [128, 2, 512], BF16) for i in range(2)]
    rr = ar.alloc("df_r", [128, 2, 4], F32)
    tt = ar.alloc("df_t", [128, 4, 128], F32)
    oo = ar.alloc("df_o", [128, 4, 128], F32)
    osq = ar.alloc("df_osq", [128, 4, 128], F32)
    ss = ar.alloc("df_ss", [128, 4], F32)
    cb = ar.alloc("df_cb", [128, 4, 128], BF16)
    for i in range(2):
        em.op('dve', I('memset', vv[i][:, :, 128:129], 1.0), writes=['df_v%d_%d' % (i, g)])
    nq = 0
    ns = 0
    for hi, h in enumerate(heads):
        kb, vb = kT[hi % 2], vv[hi % 2]
        rk, rv = 'df_kT%d_%d' % (hi % 2, g), 'df_v%d_%d' % (hi % 2, g)
        if loaders is not None:
            loaders(h, kb, vb, rk, rv)
        else:
            em.dma('sp', I('dma_start', out=kb, in_=kT_src[:, h, :]), writes=[rk])
            step = 16
            for c0 in range(0, n_kc, step):
                c1 = min(n_kc, c0 + step)
                em.dma('sp', I('dma_start', out=vb[:, c0:c1, 0:128],
                               in_=v_src[c0 * 128:c1 * 128, h * 128:(h + 1) * 128].rearrange("(c p) e -> p c e", p=128)), writes=[rv])
        for (t0, w) in qtiles:
            qq = qb[nq % 2]
            rq = 'df_q%d_%d' % (nq % 2, g)
            nq += 1
            em.dma('sp', I('dma_start', out=qq[:, :w], in_=q_d[:, 2 + h, t0:t0 + w]), writes=[rq])
            nqc = w // 128
            U1 = P['U1'][:].rearrange("p a (c e) -> p (a c) e", c=2)
            U2 = P['U2'][:].rearrange("p a (c e) -> p (a c) e", c=2)
            def issue_S(kc, slot):
                S = P['S'][slot]
                rs = ['psS%d_0' % slot, 'psS%d_1' % slot]
                for half in range(2):
                    hp = half * 64
                    mm_group(em, S[:, half, :w], [(kb[hp:hp + 64, kc * 128:(kc + 1) * 128], qq[hp:hp + 64, :w])], reads=[rk, rq], wres=rs[half])

            def issue_exp_pv(kc, slot):
                S = P['S'][slot]
                rs = ['psS%d_0' % slot, 'psS%d_1' % slot]
                pt = p12[slot]
                rp = 'df_p%d_%d' % (slot, g)
                em.op('act', I('activation', out=pt[:, :, :w], in_=S[:, :, :w], func=AF.Exp, bias=shiftt[:]), reads=rs + ['shiftt'], writes=[rp])
                for half, (U, ru) in enumerate(((U1, 'psU1'), (U2, 'psU2'))):
                    for qc in range(nqc):
                        em.op('pe', I('matmul', U[:, qc, 0:129], lhsT=pt[:, half, qc * 128:(qc + 1) * 128], rhs=vb[:, kc, :],
                                      start=(kc == 0 and qc % 2 == 0), stop=(kc == n_kc - 1), skip_group_check=True),
                              reads=[rp, rv], writes=['%s_%d' % (ru, qc // 2)], inc=(qc == nqc - 1))
            issue_S(0, ns % 2)
            for kc in range(n_kc):
                slot = ns % 2
                ns += 1
                if kc + 1 < n_kc:
                    issue_S(kc + 1, ns % 2)
                issue_exp_pv(kc, slot)
            ru1 = ['psU1_0', 'psU1_1'][:(nqc + 1) // 2]
            ru2 = ['psU2_0', 'psU2_1'][:(nqc + 1) // 2]
            rg = 'df_ep%d' % g
            em.op('dve', I('reciprocal', out=rr[:, 0, :nqc], in_=U1[:, :nqc, 128]), reads=ru1, writes=[rg])
            em.op('dve', I('reciprocal', out=rr[:, 1, :nqc], in_=U2[:, :nqc, 128]), reads=ru2, writes=[rg])
            em.op('dve', I('tensor_scalar', out=rr[:, 1, :nqc], in0=rr[:, 1, :nqc], scalar1=lamt[:, 0:1], scalar2=None, op0=ALU.mult), reads=[rg, 'lamt'], writes=[rg])
            em.op('dve', I('tensor_tensor', out=tt[:, :nqc, :], in0=U2[:, :nqc, 0:128], in1=rr[:, 1, :nqc].unsqueeze(2).to_broadcast([128, nqc, 128]), op=ALU.mult),
                  reads=ru2 + [rg], writes=[rg])
            em.op('dve', I('tensor_tensor', out=oo[:, :nqc, :], in0=U1[:, :nqc, 0:128], in1=rr[:, 0, :nqc].unsqueeze(2).to_broadcast([128, nqc, 128]), op=ALU.mult),
                  reads=ru1 + [rg], writes=[rg])
            em.op('dve', I('tensor_tensor', out=oo[:, :nqc, :], in0=oo[:, :nqc, :], in1=tt[:, :nqc, :], op=ALU.subtract), reads=[rg], writes=[rg])
            em.op('dve', I('tensor_tensor', out=osq[:, :nqc, :], in0=oo[:, :nqc, :], in1=oo[:, :nqc, :], op=ALU.mult), reads=[rg], writes=[rg])
            em.op('dve', I('reduce_sum', out=ss[:, :nqc], in_=osq[:, :nqc, :], axis=AX.X), reads=[rg], writes=[rg])
            em.op('act', I('activation', out=ss[:, :nqc], in_=ss[:, :nqc], func=AF.Sqrt, scale=1.0 / 128, bias=epst[:]), reads=[rg, 'epst'], writes=[rg])
            em.op('dve', I('reciprocal', out=ss[:, :nqc], in_=ss[:, :nqc]), reads=[rg], writes=[rg])
            em.op('dve', I('tensor_tensor', out=oo[:, :nqc, :], in0=oo[:, :nqc, :], in1=ss[:, :nqc].unsqueeze(2).to_broadcast([128, nqc, 128]), op=ALU.mult),
                  reads=[rg], writes=[rg])
            em.op('dve', I('tensor_tensor', out=cb[:, :nqc, :], in0=oo[:, :nqc, :], in1=gsub[:, :].unsqueeze(1).to_broadcast([128, nqc, 128]), op=ALU.mult),
                  reads=[rg, 'gsub'], writes=['df_cb%d' % g])
            T = P['T']
            for qc in range(nqc):
                em.op('pe', I('transpose', out=T[:, qc * 128:(qc + 1) * 128], in_=cb[:, qc, :], identity=ident[:]),
                      reads=['df_cb%d' % g, 'ident'], writes=['psU2_0'])
            em.op('act', I('activation', out=mixT[:, 4 + h, t0:t0 + w], in_=T[:, 0:w], func=AF.Copy),
                  reads=['psU2_0'], writes=['mix%d' % i for i in range(t0 // 128, (t0 + w) // 128)])


def na_local_chunks(i, nq):
    if i == 0:
        return 0, 6
    if i == nq - 1:
        return i - 1, 6
    return i, 5


def build_part_b(n_lat=2048, n_ctx=256, tw=768, n_kc_diff=66, na_variants=None, lam_init=0.2, ctx_out=True, n_halo_kc=None):
    NT = n_lat + n_ctx
    nq = n_lat // 128
    if n_halo_kc is None:
        n_halo_kc = nq + 4
    if na_variants is None:
        na_variants = [0] * nq
    c = Ctx()
    em = c.em
    x1T_d = c.din("x1T", [128, 8, NT], F32)
    mods_d = c.din("modsT", [128, 72, 2], F32)
    normg_d = c.din("normgT", [128, 6, 8], F32)
    q_d = c.din("qT", [128, 6, NT], BF16)
    nakT_d = c.din("na_kT", [128, 2, (n_halo_kc + n_ctx // 128) * 128], BF16)
    nav_d = c.din("na_v", [(n_halo_kc + n_ctx // 128) * 128, 256], BF16)
    nakTc_d = c.din("na_kTc", [128, 2, n_ctx], BF16)
    navc_d = c.din("na_vc", [n_ctx, 256], BF16)
    nvar = max(na_variants) + 1
    bias_d = c.din("na_bias", [nvar, 4, 6, 128, 128], F32)
    pin_d = c.din("pinT", [128, 2, n_lat + 16], BF16)
    rcnt_d = c.din("rcnt", [128, 2, n_lat], F32)
    pinc_d = c.din("pinTc", [128, 2, n_ctx + 16], BF16)
    rcntc_d = c.din("rcntc", [128, 2, n_ctx], F32)
    pwbd_d = c.din("pwbd", [128, 2, 128], BF16)
    pscale_d = c.din("pscaleT", [128, 2], F32)
    dkT_d = c.din("dkT", [128, 4, n_kc_diff * 128], BF16)
    dv_d = c.din("dv", [n_kc_diff * 128, 512], BF16)
    dlam_d = c.din("dlam", [128, 256], F32)
    subg_d = c.din("subg", [128, 128], F32)
    ident_d = c.din("ident", [128, 128], BF16)
    wout_d = c.din("w_out", [D, D], F32)
    w1_d = c.din("w1", [D, 2 * DFF], F32)
    w2_d = c.din("w2", [DFF, D], F32)
    x2T_o = c.dout("x2T", [128, 8, NT], F32)

    xT = c.sb("xT_sb", [128, 8, NT], F32)
    mixT_t = c.sb("mixT", [128, 8, NT], BF16)
    mixT = mixT_t[:]
    cm = Common(c)
    ident = c.sb("ident", [128, 128], BF16)
    pwbd = c.sb("pwbd", [128, 2, 128], BF16)
    pscale = c.sb("pscale", [128, 2], F32)
    dlam = c.sb("dlam", [128, 256], F32)
    lamw = c.sb("lamw", [128, 4], F32)
    lamt = c.sb("lamt", [128, 1], F32)
    gsub = c.sb("gsub", [128, 128], F32)
    shiftt = c.sb("shiftt", [128, 1], F32)
    S0 = c.ps("S0", [128, 2, 512])
    S1 = c.ps("S1", [128, 2, 512])
    U1 = c.ps("U1", [128, 2, 512])
    U2 = c.ps("U2", [128, 2, 512])
    P = {'S': [S0, S1], 'U1': U1, 'U2': U2, 'O': U1, 'T': U2[:, 0, :].bitcast(BF16)}
    arena_n = (c.nc.sbuf_bytes_remaining - 2048) // 2 // 16 * 16
    ar = Arena(c, arena_n)
    print("arena elems", arena_n)

    for t in range(0, NT, 512):
        w = min(512, NT - t)
        em.dma('sp', I('dma_start', out=xT[:, :, t:t + w], in_=x1T_d[:, :, t:t + w]), writes=xres(t, w))
    cm.load_normg(normg_d)
    cm.load_mods(mods_d)
    cm.compute_coefs()
    em.dma('sp', I('dma_start', out=ident[:], in_=ident_d), writes=['ident'])
    em.dma('sp', I('dma_start', out=pwbd[:], in_=pwbd_d), writes=['pwbd'])
    em.dma('sp', I('dma_start', out=pscale[:], in_=pscale_d), writes=['pscale'])
    em.dma('sp', I('dma_start', out=dlam[:], in_=dlam_d), writes=['dlam'])
    em.dma('sp', I('dma_start', out=gsub[:], in_=subg_d), writes=['gsub'])
    em.op('dve', I('memset', shiftt[:], EXP_SHIFT), writes=['shiftt'])
    em.op('dve', I('tensor_scalar', out=gsub[:], in0=gsub[:], scalar1=float(1.0 - lam_init), scalar2=None, op0=ALU.mult), reads=['gsub'], writes=['gsub'])
    em.op('dve', I('tensor_tensor', out=dlam[:, 0:64], in0=dlam[:, 0:64], in1=dlam[:, 64:128], op=ALU.mult), reads=['dlam'], writes=['dlam'])
    em.op('dve', I('tensor_tensor', out=dlam[:, 128:192], in0=dlam[:, 128:192], in1=dlam[:, 192:256], op=ALU.mult), reads=['dlam'], writes=['dlam'])
    em.op('dve', I('reduce_sum', out=lamw[:, 0:2], in_=dlam[:].rearrange("p (a b) -> p a b", a=2)[:, :, 0:64], axis=AX.X), reads=['dlam'], writes=['lamw'])
    em.op('act', I('activation', out=lamw[:, 2:4], in_=lamw[:, 0:2], func=AF.Exp), reads=['lamw'], writes=['lamw'])
    em.op('dve', I('tensor_tensor', out=lamt[:], in0=lamw[:, 2:3], in1=lamw[:, 3:4], op=ALU.subtract), reads=['lamw'], writes=['lamt'])
    em.op('dve', I('tensor_scalar', out=lamt[:], in0=lamt[:], scalar1=float(lam_init), scalar2=None, op0=ALU.add), reads=['lamt'], writes=['lamt'])

    nkc_tot = n_halo_kc + n_ctx // 128
    ctx_kcs = [n_halo_kc + i for i in range(n_ctx // 128)]
    qch = []
    for i in range(nq):
        k0, nl = na_local_chunks(i, nq)
        qch.append((i * 128, [k0 + j for j in range(nl)] + ctx_kcs, na_variants[i], nl))
    na_attention(c, ar, P, mixT, q_d, nakT_d, nav_d, nkc_tot, qch, bias_d, ident, 0, shiftt)
    if ctx_out:
        ar.reset()
        qch = [(n_lat + i * 128, list(range(n_ctx // 128)), None, 0) for i in range(n_ctx // 128)]
        na_attention(c, ar, P, mixT, q_d, nakTc_d, navc_d, n_ctx // 128, qch, bias_d, ident, n_lat, shiftt)
    ar.reset()
    pool_mixer(c, ar, P, mixT, pin_d, rcnt_d, n_lat, pwbd, pscale, 0)
    if ctx_out:
        ar.reset()
        pool_mixer(c, ar, P, mixT, pinc_d, rcntc_d, n_ctx, pwbd, pscale, n_lat)
    ar.reset()
    qtiles = [(t, min(512, n_lat - t)) for t in range(0, n_lat, 512)]
    diff_attention(c, ar, P, mixT, q_d, qtiles, dkT_d, dv_d, n_kc_diff, lamt, gsub, ident, shiftt, cm.epst)
    if ctx_out:
        ar.reset()
        ctx0 = n_kc_diff - n_ctx // 128
        qtiles = [(n_lat, n_ctx)]
        diff_attention(c, ar, P, mixT, q_d, qtiles, dkT_d[:, :, ctx0 * 128:], dv_d[ctx0 * 128:, :], n_ctx // 128, lamt, gsub, ident, shiftt, cm.epst)
    ar.reset()
    n_mix = NT if ctx_out else n_lat
    wo = ar.alloc("wo", [128, 8, D], BF16)
    em.dma('pool', I('dma_start', out=wo, in_=wout_d.rearrange("(k p) n -> p k n", p=128)), writes=['wo'])

    class YB:
        pass
    yb = YB()
    yb.ysb = ar.alloc("ysb", [128, 8, 512], F32)
    yb.sq = ar.alloc("sq", [128, 8, 512], BF16)
    yb.tmp = [ar.alloc("tmpf%d" % i, [128, 512], F32) for i in range(2)]
    yb.rstd = ar.alloc("rstd", [128, 512], F32)
    yb.n_tmp = 0
    pss = {'ss': U2[:, 1, :], 'a': [S0[:, 0, :], S0[:, 1, :]], 'b': [S1[:, 0, :], S1[:, 1, :]], 'y': [U1[:, 0, :], U1[:, 1, :]]}
    ny = 0
    for (t0, w, g) in [s_ for tile in make_tiles(n_lat, n_ctx if ctx_out else 0, 512) for s_ in tile]:
        for k in range(8):
            py = pss['y'][ny % 2]
            ry = 'psU1_%d' % (ny % 2)
            ny += 1
            mm_group(em, py[:, :w], [(wo[:, kk, k * 128:(k + 1) * 128], mixT[:, kk, t0:t0 + w]) for kk in range(8)],
                     reads=['wo'] + ['mix%d' % i for i in range(t0 // 128, (t0 + w) // 128)], wres=ry)
            y_evac(c, yb, k, 0, w, py[:, :w], ry)
        sandwich_out(c, cm, yb, xT, 1, t0, w, g, 0, pss['ss'], ss_res='psU2_1')
    ar.reset()
    gT = mixT_t[:].rearrange("p a b -> p (a b)")[:, 0:NFC * tw].rearrange("p (a b) -> p a b", a=NFC) if 8 * NT >= NFC * tw else None
    fb = FFNBufs(c, tw, alloc=ar.alloc, gT=gT)
    tiles = make_tiles(n_lat, n_ctx if ctx_out else 0, tw)
    ffn(c, cm, fb, xT, 2, tiles, w1_d, w2_d, pss, psnames={'ss': 'psU2_1', 'a': ['psS0_0', 'psS0_1'], 'b': ['psS1_0', 'psS1_1'], 'y': ['psU1_0', 'psU1_1']})
    em.dma('sp', I('dma_start', out=x2T_o, in_=xT[:]), reads=xres(0, NT), writes=['x2T_o'])
    print("part B instructions:", em.ninst)
    return c.done()


def na_bias_tiles(rpb, rows_total, q_row0, key_row0, nj=6):
    kr = np.arange(2)[:, None, None, None]
    kc = np.arange(64)[None, :, None, None]
    qr = np.arange(2)[None, None, :, None]
    qc = np.arange(64)[None, None, None, :]
    q_row = q_row0 + qr
    rs = np.clip(q_row - 4, 0, rows_total - 8)
    cs = np.clip(qc - 8, 0, 64 - 16)
    out = np.full((4, nj, 2, 64, 2, 64), NEG, np.float32)
    for j in range(nj):
        key_row = key_row0 + 2 * j + kr
        valid = (key_row >= rs) & (key_row < rs + 8) & (kc >= cs) & (kc < cs + 16) & (key_row >= 0) & (key_row < rows_total)
        valid = np.broadcast_to(valid, (2, 64, 2, 64))
        dr = np.clip(np.broadcast_to(key_row - q_row + 7, (2, 64, 2, 64)), 0, 14)
        dc = np.clip(np.broadcast_to(kc - qc, (2, 64, 2, 64)), -15, 15) + 15
        for h in range(4):
            out[h, j] = np.where(valid, rpb[h][dr, dc], np.float32(NEG))
    return out.reshape(4, nj, 128, 128)


def pool_rcount(t_global, L):
    n = len(t_global)
    out = np.zeros((128, 2, n), np.float32)
    for gi, wdw in enumerate((2, 4, 8, 16)):
        half = wdw // 2
        lo = np.clip(t_global - half, 0, L)
        hi = np.clip(t_global + half, 0, L)
        rc = (1.0 / (hi - lo).astype(np.float32)).astype(np.float32)
        out[(gi % 2) * 64:(gi % 2) * 64 + 64, gi // 2, :] = rc[None, :]
    return out


def halo_cols(arrT, t0, n, halo, L):
    out = np.zeros(arrT.shape[:-1] + (n + 2 * halo,), arrT.dtype)
    a = max(0, t0 - halo)
    b = min(L, t0 + n + halo)
    out[..., a - (t0 - halo):b - (t0 - halo)] = arrT[..., a:b]
    return out


def pool_blockdiag(pool_w):
    out = np.zeros((128, 2, 128), np.float32)
    for gi in range(4):
        p0 = (gi % 2) * 64
        out[p0:p0 + 64, gi // 2, p0:p0 + 64] = pool_w[gi]
    return out.astype(NPBF)


N_LAT = 2048
NT_FULL = N_LAT + CTX
_PROGS = {}


def _fm(a):
    T, F = a.shape
    return np.ascontiguousarray(a.reshape(T, F // 128, 128).transpose(2, 1, 0))


def _unfm(aT):
    return np.ascontiguousarray(aT.transpose(2, 1, 0)).reshape(aT.shape[2], -1)


def _lay_vec(v):
    return np.ascontiguousarray(v.reshape(-1, 128).T)


def _prog(key, fn):
    if key not in _PROGS:
        _PROGS[key] = fn()
    return _PROGS[key]


def kernel_unfused(x, c, ctx, c_ctx, w_ada, b_ada, norm_g, ffn_w1, ffn_w2, w_in, w_out, na_rpb, pool_w, pool_scale, diff_lambda,
           diff_subln_g):
    f32 = np.float32
    x = np.asarray(x, f32)
    ctx = np.asarray(ctx, f32)
    cvals = np.asarray(c, f32)
    c_ctx = np.asarray(c_ctx, f32)
    ncores = 8
    depth = w_ada.shape[0]
    rows_total = SEQ // GRID_W
    nq = N_LAT // 128
    variants = [1, 2] + [0] * (nq - 4) + [3, 4]
    xT = []
    for i in range(ncores):
        b, j = i // 4, i % 4
        xt = np.concatenate([x[b, j * N_LAT:(j + 1) * N_LAT], ctx[b]], axis=0)
        xT.append(_fm(xt))
    ropes = []
    for i in range(ncores):
        j = i % 4
        cos, sin = rope_tables(np.arange(j * N_LAT, (j + 1) * N_LAT))
        ropes.append(rope_feature_major(cos, sin, CTX))
    pmat = rope_pmat()
    ident = np.eye(128, dtype=f32).astype(NPBF)
    TW = 512
    for l in range(depth):
        last = (l == depth - 1)
        lam_init = 0.8 - 0.6 * math.exp(-0.3 * l)
        nca = _prog(('A',), lambda: build_part_a(N_LAT, CTX, TW))
        normgT = np.ascontiguousarray(np.asarray(norm_g[l], f32).reshape(6, 8, 128).transpose(2, 0, 1))
        badaT = np.ascontiguousarray(np.asarray(b_ada[l], f32).reshape(72, 128).T)
        wada_l = np.ascontiguousarray(np.asarray(w_ada[l], f32))
        w1a = np.ascontiguousarray(np.asarray(ffn_w1[l, 0], f32))
        w2a = np.ascontiguousarray(np.asarray(ffn_w2[l, 0], f32))
        win_l = np.ascontiguousarray(np.asarray(w_in[l], f32))
        in_maps = []
        for i in range(ncores):
            b = i // 4
            cvec = np.ascontiguousarray(np.stack([_lay_vec(cvals[b]), _lay_vec(c_ctx)], axis=-1))
            in_maps.append({"xT": xT[i], "cvec": cvec, "w_ada": wada_l, "badaT": badaT, "normgT": normgT, "w1": w1a, "w2": w2a,
                            "w_in": win_l, "ropeC": ropes[i][0], "ropeS": ropes[i][1], "pmat": pmat})
        ra = run_bass_kernel_spmd(nca, in_maps, core_ids=list(range(ncores))).results
        ncb = _prog(('B', l), lambda: build_part_b(N_LAT, CTX, TW, n_kc_diff=(SEQ + CTX) // 128, na_variants=variants,
                                                  lam_init=lam_init, ctx_out=not last))
        w1b_ = np.ascontiguousarray(np.asarray(ffn_w1[l, 1], f32))
        w2b_ = np.ascontiguousarray(np.asarray(ffn_w2[l, 1], f32))
        wout_l = np.ascontiguousarray(np.asarray(w_out[l], f32))
        pwbd = pool_blockdiag(np.asarray(pool_w[l], f32))
        pscaleT = np.ascontiguousarray(np.asarray(pool_scale[l], f32).reshape(2, 128).T)
        dlam = np.ascontiguousarray(np.broadcast_to(np.asarray(diff_lambda[l], f32).reshape(1, 256), (128, 256)))
        subg = np.ascontiguousarray(np.broadcast_to(np.asarray(diff_subln_g[l], f32)[None, :], (128, 128)))
        rpb = np.asarray(na_rpb[l], f32)
        in_maps = []
        for b in range(2):
            cores = [b * 4 + j for j in range(4)]
            kv_lat = np.concatenate([ra[i]["kvT"][:, :, :N_LAT] for i in cores], axis=2)
            kv_ctx = ra[cores[0]]["kvT"][:, :, N_LAT:]
            v_lat = np.concatenate([ra[i]["vtok"][:N_LAT] for i in cores], axis=0)
            v_ctx = ra[cores[0]]["vtok"][N_LAT:]
            dkT = np.ascontiguousarray(np.concatenate([kv_lat[:, 4:8], kv_ctx[:, 4:8]], axis=2))
            dv = np.ascontiguousarray(np.concatenate([v_lat[:, 256:], v_ctx[:, 256:]], axis=0))
            nakTc = np.ascontiguousarray(kv_ctx[:, 0:2])
            navc = np.ascontiguousarray(v_ctx[:, 0:256])
            pinTc = halo_cols(np.ascontiguousarray(kv_ctx[:, 2:4]), 0, CTX, 8, CTX)
            rcntc = pool_rcount(np.arange(CTX), CTX)
            for j in range(4):
                i = cores[j]
                t_start = j * N_LAT
                r0 = t_start // GRID_W
                hk0 = (r0 - 4) * GRID_W
                nhk = (nq + 4) * 128
                na_kT = np.concatenate([halo_cols(kv_lat[:, 0:2], hk0, nhk, 0, SEQ), nakTc], axis=2)
                na_v = np.concatenate([halo_cols(v_lat[:, 0:256].T, hk0, nhk, 0, SEQ).T, navc], axis=0)
                bias = np.full((5, 4, 6, 128, 128), NEG, f32)
                for vi, ci in ((0, 2), (1, 0), (2, 1), (3, nq - 2), (4, nq - 1)):
                    k0, nl = na_local_chunks(ci, nq)
                    bias[vi] = na_bias_tiles(rpb, rows_total, r0 + 2 * ci, r0 - 4 + 2 * k0)
                in_maps.append({
                    "x1T": ra[i]["x1T"], "modsT": ra[i]["modsT"], "normgT": normgT, "qT": ra[i]["qT"],
                    "na_kT": np.ascontiguousarray(na_kT), "na_v": np.ascontiguousarray(na_v), "na_kTc": nakTc, "na_vc": navc,
                    "na_bias": bias,
                    "pinT": halo_cols(kv_lat[:, 2:4], t_start, N_LAT, 8, SEQ), "rcnt": pool_rcount(np.arange(t_start, t_start + N_LAT), SEQ),
                    "pinTc": pinTc, "rcntc": rcntc, "pwbd": pwbd, "pscaleT": pscaleT,
                    "dkT": dkT, "dv": dv, "dlam": dlam, "subg": subg, "ident": ident,
                    "w_out": wout_l, "w1": w1b_, "w2": w2b_,
                })
        rb = run_bass_kernel_spmd(ncb, in_maps, core_ids=list(range(ncores))).results
        xT = [rb[i]["x2T"] for i in range(ncores)]
    out = np.zeros((2, SEQ, D), f32)
    for i in range(ncores):
        b, j = i // 4, i % 4
        out[b, j * N_LAT:(j + 1) * N_LAT] = _unfm(xT[i][:, :, :N_LAT])
    return out


def build_fused(n_lat=2048, n_ctx=256, tw=512, depth=2, group=4, dbg_ctx_out=False):
    NT = n_lat + n_ctx
    nq = n_lat // 128
    n_halo_kc = nq + 4
    nkc_na = n_halo_kc + n_ctx // 128
    n_kc_diff = (group * n_lat + n_ctx) // 128
    variants = [1, 2] + [0] * (nq - 4) + [3, 4] if nq > 4 else list(range(1, nq + 1))
    nvar = max(variants) + 1
    c = Ctx()
    em = c.em
    nc = c.nc
    xT_d = c.din("xT", [128, 8, NT], F32)
    cvec_d = c.din("cvec", [128, 8, 2], F32)
    ropeC_d = c.din("ropeC", [128, NT], F32)
    ropeS_d = c.din("ropeS", [128, NT], F32)
    pm_d = c.din("pmat", [128, 128], BF16)
    ident_d = c.din("ident", [128, 128], BF16)
    rcnt_d = c.din("rcnt", [128, 2, n_lat], F32)
    rcntc_d = c.din("rcntc", [128, 2, n_ctx], F32)
    L = []
    for l in range(depth):
        L.append(dict(
            wada=c.din("w_ada%d" % l, [D, 9 * D], F32), bada=c.din("badaT%d" % l, [128, 72], F32),
            normg=c.din("normgT%d" % l, [128, 6, 8], F32),
            w1a=c.din("w1a%d" % l, [D, 2 * DFF], F32), w2a=c.din("w2a%d" % l, [DFF, D], F32),
            w1b=c.din("w1b%d" % l, [D, 2 * DFF], F32), w2b=c.din("w2b%d" % l, [DFF, D], F32),
            win=c.din("w_in%d" % l, [D, 2560], F32), wout=c.din("w_out%d" % l, [D, D], F32),
            bias=c.din("na_bias%d" % l, [nvar, 4, 6, 128, 128], F32),
            pwbd=c.din("pwbd%d" % l, [128, 2, 128], BF16), pscale=c.din("pscaleT%d" % l, [128, 2], F32),
            dlam=c.din("dlam%d" % l, [128, 256], F32), subg=c.din("subg%d" % l, [128, 128], F32),
        ))
    outT_o = c.dout("outT", [128, 8, n_lat], F32)

    def dram(name, shape, dt):
        return nc.dram_tensor(name, list(shape), dt, kind="Internal").ap()

    xT = c.sb("xT_sb", [128, 8, NT], F32)
    cm = Common(c)
    ident = c.sb("ident", [128, 128], BF16)
    pwbd = c.sb("pwbd", [128, 2, 128], BF16)
    pscale = c.sb("pscale", [128, 2], F32)
    dlam = c.sb("dlam", [128, 256], F32)
    lamw = c.sb("lamw", [128, 4], F32)
    lamt = c.sb("lamt", [128, 1], F32)
    gsub = c.sb("gsub", [128, 128], F32)
    shiftt = c.sb("shiftt", [128, 1], F32)
    zt = c.sb("zeros", [128, 2048], BF16)
    S0 = c.ps("S0", [128, 2, 512])
    S1 = c.ps("S1", [128, 2, 512])
    U1 = c.ps("U1", [128, 2, 512])
    U2 = c.ps("U2", [128, 2, 512])
    P = {'S': [S0, S1], 'U1': U1, 'U2': U2, 'O': U1, 'T': U2[:, 0, :].bitcast(BF16)}
    pss = {'ss': U2[:, 1, :], 'a': [S0[:, 0, :], S0[:, 1, :]], 'b': [S1[:, 0, :], S1[:, 1, :]], 'y': [U1[:, 0, :], U1[:, 1, :]]}
    psn = {'ss': 'psU2_1', 'a': ['psS0_0', 'psS0_1'], 'b': ['psS1_0', 'psS1_1'], 'y': ['psU1_0', 'psU1_1']}
    ps_mods = U2[:, 0, :].rearrange("p (a b) -> p a b", b=2)
    arena_n = (nc.sbuf_bytes_remaining - 2048) // 2 // 16 * 16
    ar = Arena(c, arena_n)
    print("fused arena elems", arena_n)

    for t in range(0, NT, 512):
        w = min(512, NT - t)
        em.dma('sp', I('dma_start', out=xT[:, :, t:t + w], in_=xT_d[:, :, t:t + w]), writes=xres(t, w))
    em.dma('sp', I('dma_start', out=ident[:], in_=ident_d), writes=['ident'])
    em.op('dve', I('memset', shiftt[:], EXP_SHIFT), writes=['shiftt'])
    em.op('dve', I('memset', zt[:], 0.0), writes=['zeros'])
    tiles_all = make_tiles(n_lat, n_ctx, tw)
    wsel_d = c.din("wsel", [128, 2 * group], F32)
    wsel = c.sb("wsel", [128, 2 * group], F32)
    em.dma('sp', I('dma_start', out=wsel[:], in_=wsel_d), writes=['wsel'])

    for l in range(depth):
        W = L[l]
        last = (l == depth - 1)
        ctx_out = (not last) or dbg_ctx_out
        lam_init = 0.8 - 0.6 * math.exp(-0.3 * l)
        ar.reset(to_zero=True)
        em.dma('sp', I('dma_start', out=cm.normg[:], in_=W['normg']), writes=['normg'])
        fb = FFNBufs(c, tw, alloc=ar.alloc, nw1=3, nw2=2)
        cm.compute_mods(cvec_d, W['wada'], W['bada'], ps_mods, fb, alloc=ar.alloc, psname='psU2_0')
        cm.compute_coefs()
        ffn(c, cm, fb, xT, 0, tiles_all, W['w1a'], W['w2a'], pss, psnames=psn)
        qT_l = dram("qT_l%d" % l, [128, 6, NT], BF16)
        kvT_l = dram("kvT_l%d" % l, [128, 8 * NT], BF16)
        v_l = dram("v_l%d" % l, [NT, 768], BF16)
        kvT_l3 = kvT_l.rearrange("p (c t) -> p c t", c=8)
        proj_phase(c, cm, fb, xT, tiles_all, W['win'], ropeC_d, ropeS_d, pm_d, pss, qT_l, kvT_l3, v_l, alloc=ar.alloc, psnames=psn)
        rg = [[g0 * group + j for j in range(group)] for g0 in range(8 // group)]
        kedge_loc = dram("kedge_loc%d" % l, [128, 4 * 512], BF16)
        kedge_loc3 = kedge_loc.rearrange("p (c t) -> p c t", c=4)
        vedge_loc = dram("vedge_loc%d" % l, [512, 256], BF16)
        em.dma('sp', I('dma_start', out=kedge_loc3[:, :, 0:256], in_=kvT_l3[:, 0:4, 0:256]), reads=['kvT_o'], writes=['kedge_loc'])
        em.dma('sp', I('dma_start', out=kedge_loc3[:, :, 256:512], in_=kvT_l3[:, 0:4, n_lat - 256:n_lat]), reads=['kvT_o'], writes=['kedge_loc'])
        em.dma('sp', I('dma_start', out=vedge_loc[0:256, :], in_=v_l[0:256, 0:256]), reads=['v_o'], writes=['vedge_loc'])
        em.dma('sp', I('dma_start', out=vedge_loc[256:512, :], in_=v_l[n_lat - 256:n_lat, 0:256]), reads=['v_o'], writes=['vedge_loc'])
        kedge_g = dram("kedge_g%d" % l, [group * 128, 4 * 512], BF16)
        vedge_g = dram("vedge_g%d" % l, [group * 512, 256], BF16)
        em.coll(I('collective_compute', "AllGather", ALU.bypass, replica_groups=rg, ins=[kedge_loc.opt()], outs=[kedge_g.opt()]),
                reads=['kedge_loc'], writes=['kedge_g'])
        em.coll(I('collective_compute', "AllGather", ALU.bypass, replica_groups=rg, ins=[vedge_loc.opt()], outs=[vedge_g.opt()]),
                reads=['vedge_loc'], writes=['vedge_g'])
        dk_g, dv_g = [], []
        for h in range(4):
            dk_loc = dram("dk_loc%d_%d" % (l, h), [128, n_lat], BF16)
            dv_loc = dram("dv_loc%d_%d" % (l, h), [n_lat, 128], BF16)
            em.dma('sp', I('dma_start', out=dk_loc, in_=kvT_l3[:, 4 + h, 0:n_lat]), reads=['kvT_o'], writes=['dk_loc%d' % h], bg=True)
            em.dma('sp', I('dma_start', out=dv_loc, in_=v_l[0:n_lat, 256 + h * 128:256 + (h + 1) * 128]), reads=['v_o'], writes=['dv_loc%d' % h], bg=True)
            dkg = dram("dk_g%d_%d" % (l, h), [group * 128, n_lat], BF16)
            dvg = dram("dv_g%d_%d" % (l, h), [group * n_lat, 128], BF16)
            em.coll(I('collective_compute', "AllGather", ALU.bypass, replica_groups=rg, ins=[dk_loc.opt()], outs=[dkg.opt()]),
                    reads=['dk_loc%d' % h], writes=['dk_g%d' % h])
            em.coll(I('collective_compute', "AllGather", ALU.bypass, replica_groups=rg, ins=[dv_loc.opt()], outs=[dvg.opt()]),
                    reads=['dv_loc%d' % h], writes=['dv_g%d' % h])
            dk_g.append(dkg)
            dv_g.append(dvg)
        ar.reset(to_zero=True)
        ke = ar.alloc("ke", [128, group, 2048], BF16)
        ve = ar.alloc("ve", [128, group, 4, 256], BF16)
        kp = ar.alloc("kp", [128, 2048], BF16)
        kn = ar.alloc("kn", [128, 2048], BF16)
        vp = ar.alloc("vp", [128, 4, 256], BF16)
        vn = ar.alloc("vn", [128, 4, 256], BF16)
        em.dma('sp', I('dma_start', out=ke, in_=kedge_g.rearrange("(r p) n -> p r n", p=128)), reads=['kedge_g'], writes=['ke'])
        for r in range(group):
            em.dma('sp', I('dma_start', out=ve[:, r, :, :], in_=vedge_g[r * 512:(r + 1) * 512, :].rearrange("(a p) n -> p a n", p=128)),
                   reads=['vedge_g'], writes=['ve'])
        for (dst, dres, src, sres, w0) in ((kp, 'kp', lambda r: ke[:, r, :], 'ke', 0), (kn, 'kn', lambda r: ke[:, r, :], 'ke', group),
                                           (vp, 'vp', lambda r: ve[:, r, :, :], 've', 0), (vn, 'vn', lambda r: ve[:, r, :, :], 've', group)):
            em.op('dve', I('tensor_scalar', out=dst, in0=src(0), scalar1=wsel[:, w0:w0 + 1], scalar2=None, op0=ALU.mult),
                  reads=[sres, 'wsel'], writes=[dres])
            for r in range(1, group):
                em.op('dve', I('scalar_tensor_tensor', out=dst, in0=src(r), scalar=wsel[:, w0 + r:w0 + r + 1], in1=dst, op0=ALU.mult, op1=ALU.add),
                      reads=[sres, 'wsel', dres], writes=[dres])
        kp3 = kp.rearrange("p (c t) -> p c t", c=4)
        kn3 = kn.rearrange("p (c t) -> p c t", c=4)
        na_kT_asm = dram("na_kT_asm%d" % l, [128, 2, nkc_na * 128], BF16)
        na_v_asm = dram("na_v_asm%d" % l, [nkc_na * 128, 256], BF16)
        pin_asm = dram("pin_asm%d" % l, [128, 2, n_lat + 16], BF16)
        pinc_asm = dram("pinc_asm%d" % l, [128, 2, n_ctx + 16], BF16)
        em.dma('sp', I('dma_start', out=na_kT_asm[:, :, 0:256], in_=kp3[:, 0:2, 256:512]), reads=['kp'], writes=['na_kT_asm'])
        em.dma('sp', I('dma_start', out=na_kT_asm[:, :, 256 + n_lat:512 + n_lat], in_=kn3[:, 0:2, 0:256]), reads=['kn'], writes=['na_kT_asm'])
        em.dma('sp', I('dma_start', out=na_v_asm[0:256, :].rearrange("(a p) n -> p a n", p=128), in_=vp[:, 2:4, :]), reads=['vp'], writes=['na_v_asm'])
        em.dma('sp', I('dma_start', out=na_v_asm[256 + n_lat:512 + n_lat, :].rearrange("(a p) n -> p a n", p=128), in_=vn[:, 0:2, :]), reads=['vn'], writes=['na_v_asm'])
        em.dma('sp', I('dma_start', out=pin_asm[:, :, 0:8], in_=kp3[:, 2:4, 504:512]), reads=['kp'], writes=['pin_asm'])
        em.dma('sp', I('dma_start', out=pin_asm[:, :, 8 + n_lat:16 + n_lat], in_=kn3[:, 2:4, 0:8]), reads=['kn'], writes=['pin_asm'])
        em.dma('sp', I('dma_start', out=na_kT_asm[:, :, 256:256 + n_lat], in_=kvT_l3[:, 0:2, 0:n_lat]), reads=['kvT_o'], writes=['na_kT_asm'])
        em.dma('sp', I('dma_start', out=na_kT_asm[:, :, 512 + n_lat:], in_=kvT_l3[:, 0:2, n_lat:NT]), reads=['kvT_o'], writes=['na_kT_asm'])
        em.dma('sp', I('dma_start', out=na_v_asm[256:256 + n_lat, :], in_=v_l[0:n_lat, 0:256]), reads=['v_o'], writes=['na_v_asm'])
        em.dma('sp', I('dma_start', out=na_v_asm[512 + n_lat:, :], in_=v_l[n_lat:NT, 0:256]), reads=['v_o'], writes=['na_v_asm'])
        em.dma('sp', I('dma_start', out=pin_asm[:, :, 8:8 + n_lat], in_=kvT_l3[:, 2:4, 0:n_lat]), reads=['kvT_o'], writes=['pin_asm'])
        if ctx_out:
            for (a0, a1) in ((0, 8), (8 + n_ctx, 16 + n_ctx)):
                em.dma('sp', I('dma_start', out=pinc_asm[:, :, a0:a1], in_=zt[:, 0:16].rearrange("p (c t) -> p c t", c=2)), reads=['zeros'], writes=['pinc_asm'])
            em.dma('sp', I('dma_start', out=pinc_asm[:, :, 8:8 + n_ctx], in_=kvT_l3[:, 2:4, n_lat:NT]), reads=['kvT_o'], writes=['pinc_asm'])
        ar.reset(to_zero=True)
        mixT = ar.alloc("mixT", [128, 8, NT], BF16)
        ar.set_base()
        em.dma('sp', I('dma_start', out=pwbd[:], in_=W['pwbd']), writes=['pwbd'])
        em.dma('sp', I('dma_start', out=pscale[:], in_=W['pscale']), writes=['pscale'])
        em.dma('sp', I('dma_start', out=dlam[:], in_=W['dlam']), writes=['dlam'])
        em.dma('sp', I('dma_start', out=gsub[:], in_=W['subg']), writes=['gsub'])
        em.op('dve', I('tensor_scalar', out=gsub[:], in0=gsub[:], scalar1=float(1.0 - lam_init), scalar2=None, op0=ALU.mult), reads=['gsub'], writes=['gsub'])
        em.op('dve', I('tensor_tensor', out=dlam[:, 0:64], in0=dlam[:, 0:64], in1=dlam[:, 64:128], op=ALU.mult), reads=['dlam'], writes=['dlam'])
        em.op('dve', I('tensor_tensor', out=dlam[:, 128:192], in0=dlam[:, 128:192], in1=dlam[:, 192:256], op=ALU.mult), reads=['dlam'], writes=['dlam'])
        em.op('dve', I('reduce_sum', out=lamw[:, 0:2], in_=dlam[:].rearrange("p (a b) -> p a b", a=2)[:, :, 0:64], axis=AX.X), reads=['dlam'], writes=['lamw'])
        em.op('act', I('activation', out=lamw[:, 2:4], in_=lamw[:, 0:2], func=AF.Exp), reads=['lamw'], writes=['lamw'])
        em.op('dve', I('tensor_tensor', out=lamt[:], in0=lamw[:, 2:3], in1=lamw[:, 3:4], op=ALU.subtract), reads=['lamw'], writes=['lamt'])
        em.op('dve', I('tensor_scalar', out=lamt[:], in0=lamt[:], scalar1=float(lam_init), scalar2=None, op0=ALU.add), reads=['lamt'], writes=['lamt'])
        em.barrier()
        ctx_kcs = [n_halo_kc + i for i in range(n_ctx // 128)]
        qch = []
        for i in range(nq):
            k0, nl = na_local_chunks(i, nq)
            qch.append((i * 128, [k0 + j for j in range(nl)] + ctx_kcs, variants[i], nl))
        na_attention(c, ar, P, mixT, qT_l, na_kT_asm, na_v_asm, nkc_na, qch, W['bias'], ident, 0, shiftt)
        if ctx_out:
            ar.reset()
            qch = [(n_lat + i * 128, list(range(n_ctx // 128)), None, 0) for i in range(n_ctx // 128)]
            na_attention(c, ar, P, mixT, qT_l, kvT_l3[:, 0:2, n_lat:NT], v_l[n_lat:NT, 0:256], n_ctx // 128, qch, W['bias'], ident, n_lat, shiftt)
        ar.reset()
        pool_mixer(c, ar, P, mixT, pin_asm, rcnt_d, n_lat, pwbd, pscale, 0)
        if ctx_out:
            ar.reset()
            pool_mixer(c, ar, P, mixT, pinc_asm, rcntc_d, n_ctx, pwbd, pscale, n_lat)
        ar.reset()
        lat_kc = n_lat // 128

        def load_all(h, kb, vb, rk, rv):
            for r in range(group):
                em.dma('sp', I('dma_start', out=kb[:, r * n_lat:(r + 1) * n_lat], in_=dk_g[h][r * 128:(r + 1) * 128, :]), reads=['dk_g%d' % h], writes=[rk])
                em.dma('sp', I('dma_start', out=vb[:, r * lat_kc:(r + 1) * lat_kc, 0:128],
                               in_=dv_g[h][r * n_lat:(r + 1) * n_lat, :].rearrange("(c p) e -> p c e", p=128)),
                       reads=['dv_g%d' % h], writes=[rv])
            em.dma('sp', I('dma_start', out=kb[:, group * n_lat:], in_=kvT_l3[:, 4 + h, n_lat:NT]), reads=['kvT_o'], writes=[rk])
            em.dma('sp', I('dma_start', out=vb[:, group * lat_kc:, 0:128],
                           in_=v_l[n_lat:NT, 256 + h * 128:256 + (h + 1) * 128].rearrange("(c p) e -> p c e", p=128)), reads=['v_o'], writes=[rv])

        def load_ctx(h, kb, vb, rk, rv):
            em.dma('sp', I('dma_start', out=kb, in_=kvT_l3[:, 4 + h, n_lat:NT]), reads=['kvT_o'], writes=[rk])
            em.dma('sp', I('dma_start', out=vb[:, :, 0:128],
                           in_=v_l[n_lat:NT, 256 + h * 128:256 + (h + 1) * 128].rearrange("(c p) e -> p c e", p=128)), reads=['v_o'], writes=[rv])
        qtiles = [(t, min(512, n_lat - t)) for t in range(0, n_lat, 512)]
        diff_attention(c, ar, P, mixT, qT_l, qtiles, None, None, n_kc_diff, lamt, gsub, ident, shiftt, cm.epst, loaders=load_all)
        if ctx_out:
            ar.reset()
            diff_attention(c, ar, P, mixT, qT_l, [(n_lat, n_ctx)], None, None, n_ctx // 128, lamt, gsub, ident, shiftt, cm.epst, loaders=load_ctx)
        ar.reset()
        wo = ar.alloc("wo", [128, 8, D], BF16)
        em.dma('pool', I('dma_start', out=wo, in_=W['wout'].rearrange("(k p) n -> p k n", p=128)), writes=['wo'])

        class YB:
            pass
        yb = YB()
        yb.ysb = ar.alloc("ysb", [128, 8, 512], F32)
        yb.sq = ar.alloc("sq", [128, 8, 512], BF16)
        yb.tmp = [ar.alloc("tmpf%d" % i, [128, 512], F32) for i in range(2)]
        yb.rstd = ar.alloc("rstd", [128, 512], F32)
        yb.n_tmp = 0
        ny = 0
        for (t0, w, g) in [s_ for tile in make_tiles(n_lat, n_ctx if ctx_out else 0, 512) for s_ in tile]:
            for k in range(8):
                py = pss['y'][ny % 2]
                ry = psn['y'][ny % 2]
                ny += 1
                mm_group(em, py[:, :w], [(wo[:, kk, k * 128:(k + 1) * 128], mixT[:, kk, t0:t0 + w]) for kk in range(8)],
                         reads=['wo'] + ['mix%d' % i for i in range(t0 // 128, (t0 + w) // 128)], wres=ry)
                y_evac(c, yb, k, 0, w, py[:, :w], ry)
            sandwich_out(c, cm, yb, xT, 1, t0, w, g, 0, pss['ss'], ss_res=psn['ss'])
        ar.reset(to_zero=True)
        fb = FFNBufs(c, tw, alloc=ar.alloc, nw1=3, nw2=3)
        ffn(c, cm, fb, xT, 2, make_tiles(n_lat, n_ctx if ctx_out else 0, tw), W['w1b'], W['w2b'], pss, psnames=psn)
    em.dma('sp', I('dma_start', out=outT_o, in_=xT[:, :, 0:n_lat]), reads=xres(0, n_lat), writes=['outT_o'])
    print("fused instructions:", em.ninst)
    return c.done()


def fused_inputs(x, c, ctx, c_ctx, w_ada, b_ada, norm_g, ffn_w1, ffn_w2, w_in, w_out, na_rpb, pool_w, pool_scale, diff_lambda,
                 diff_subln_g, n_lat):
    f32 = np.float32
    x = np.asarray(x, f32)
    ctx = np.asarray(ctx, f32)
    cvals = np.asarray(c, f32)
    c_ctx = np.asarray(c_ctx, f32)
    seq = x.shape[1]
    n_ctx = ctx.shape[1]
    group = seq // n_lat
    ncores = x.shape[0] * group
    depth = w_ada.shape[0]
    rows_total = seq // GRID_W
    nq = n_lat // 128
    if nq > 4:
        vmap = ((0, 2), (1, 0), (2, 1), (3, nq - 2), (4, nq - 1))
    else:
        vmap = tuple((i + 1, i) for i in range(nq))
    shared = {"pmat": rope_pmat(), "ident": np.eye(128, dtype=f32).astype(NPBF), "rcntc": pool_rcount(np.arange(n_ctx), n_ctx)}
    for l in range(depth):
        shared["w_ada%d" % l] = np.ascontiguousarray(np.asarray(w_ada[l], f32))
        shared["badaT%d" % l] = np.ascontiguousarray(np.asarray(b_ada[l], f32).reshape(72, 128).T)
        shared["normgT%d" % l] = np.ascontiguousarray(np.asarray(norm_g[l], f32).reshape(6, 8, 128).transpose(2, 0, 1))
        shared["w1a%d" % l] = np.ascontiguousarray(np.asarray(ffn_w1[l, 0], f32))
        shared["w2a%d" % l] = np.ascontiguousarray(np.asarray(ffn_w2[l, 0], f32))
        shared["w1b%d" % l] = np.ascontiguousarray(np.asarray(ffn_w1[l, 1], f32))
        shared["w2b%d" % l] = np.ascontiguousarray(np.asarray(ffn_w2[l, 1], f32))
        shared["w_in%d" % l] = np.ascontiguousarray(np.asarray(w_in[l], f32))
        shared["w_out%d" % l] = np.ascontiguousarray(np.asarray(w_out[l], f32))
        shared["pwbd%d" % l] = pool_blockdiag(np.asarray(pool_w[l], f32))
        shared["pscaleT%d" % l] = np.ascontiguousarray(np.asarray(pool_scale[l], f32).reshape(2, 128).T)
        shared["dlam%d" % l] = np.ascontiguousarray(np.broadcast_to(np.asarray(diff_lambda[l], f32).reshape(1, 256), (128, 256)))
        shared["subg%d" % l] = np.ascontiguousarray(np.broadcast_to(np.asarray(diff_subln_g[l], f32)[None, :], (128, 128)))
    in_maps = []
    for i in range(ncores):
        b, j = i // group, i % group
        t_start = j * n_lat
        r0 = t_start // GRID_W
        m = dict(shared)
        m["xT"] = _fm(np.concatenate([x[b, t_start:t_start + n_lat], ctx[b]], axis=0))
        m["cvec"] = np.ascontiguousarray(np.stack([_lay_vec(cvals[b]), _lay_vec(c_ctx)], axis=-1))
        cos, sin = rope_tables(np.arange(t_start, t_start + n_lat))
        m["ropeC"], m["ropeS"] = rope_feature_major(cos, sin, n_ctx)
        m["rcnt"] = pool_rcount(np.arange(t_start, t_start + n_lat), seq)
        ws = np.zeros((128, 2 * group), f32)
        if j > 0:
            ws[:, j - 1] = 1.0
        if j < group - 1:
            ws[:, group + j + 1] = 1.0
        m["wsel"] = ws
        for l in range(depth):
            rpb = np.asarray(na_rpb[l], f32)
            bias = np.full((len(vmap) + (1 if nq <= 4 else 0), 4, 6, 128, 128), NEG, f32)
            for vi, ci in vmap:
                k0, nl = na_local_chunks(ci, nq)
                bias[vi] = na_bias_tiles(rpb, rows_total, r0 + 2 * ci, r0 - 4 + 2 * k0)
            m["na_bias%d" % l] = bias
        in_maps.append(m)
    return in_maps, ncores, group


def kernel_fused(n_lat=N_LAT, **inputs):
    in_maps, ncores, group = fused_inputs(n_lat=n_lat, **inputs)
    n_ctx = inputs["ctx"].shape[1]
    seq = inputs["x"].shape[1]
    tw = 512 if n_lat >= 2048 else 384
    nc = _prog(('F', n_lat, n_ctx), lambda: build_fused(n_lat, n_ctx, tw, depth=inputs["w_ada"].shape[0], group=group))
    res = run_bass_kernel_spmd(nc, in_maps, core_ids=list(range(ncores))).results
    out = np.zeros((inputs["x"].shape[0], seq, D), np.float32)
    for i in range(ncores):
        b, j = i // group, i % group
        out[b, j * n_lat:(j + 1) * n_lat] = _unfm(res[i]["outT"])
    return out


def kernel(x, c, ctx, c_ctx, w_ada, b_ada, norm_g, ffn_w1, ffn_w2, w_in, w_out, na_rpb, pool_w, pool_scale, diff_lambda,
           diff_subln_g):
    return kernel_fused(n_lat=N_LAT, x=x, c=c, ctx=ctx, c_ctx=c_ctx, w_ada=w_ada, b_ada=b_ada, norm_g=norm_g, ffn_w1=ffn_w1,
                        ffn_w2=ffn_w2, w_in=w_in, w_out=w_out, na_rpb=na_rpb, pool_w=pool_w, pool_scale=pool_scale,
                        diff_lambda=diff_lambda, diff_subln_g=diff_subln_g)
```

```python
import math
import numpy as np
import ml_dtypes
from contextlib import ExitStack
import concourse.bass as bass
import concourse.mybir as mybir
from concourse.bass_utils import run_bass_kernel_spmd

F32 = mybir.dt.float32
BF16 = mybir.dt.bfloat16
AF = mybir.ActivationFunctionType
ALU = mybir.AluOpType
AX = mybir.AxisListType
NPBF = ml_dtypes.bfloat16

D = 1024
DFF = 2816
NFC = 22
SEQ = 8192
CTX = 256
GRID_W = 64
EPS = 1e-6
NEG = -30000.0
EXP_SHIFT = -40.0


class Emitter:
    ENGS = ('pe', 'act', 'dve', 'pool', 'sp')

    def __init__(self, nc, stack, n_dma_sems=32):
        self.nc = nc
        self._stack = stack
        self.cccount = 0
        self.prog = {e: [] for e in self.ENGS}
        self.count = {e: 0 for e in self.ENGS}
        self.waited = {e: {} for e in self.ENGS}
        self.dcount = [0] * n_dma_sems
        self.dnext_q = {e: 0 for e in self.ENGS}
        self.lastw = {}
        self.readers = {}
        self.semobj = {}
        for e in self.ENGS:
            self.semobj[('c', e)] = stack.enter_context(nc.semaphore('c_' + e))
        for i in range(n_dma_sems):
            self.semobj[('d', i)] = stack.enter_context(nc.semaphore('d%d' % i))
        self.ninst = 0

    def _deps(self, eng, reads, writes):
        deps = {}
        own = ('c', eng)

        def add(k, v):
            if deps.get(k, 0) < v:
                deps[k] = v
        skip_own = (eng == 'pe')
        for r in reads:
            t = self.lastw.get(r)
            if t is not None and not (skip_own and t[0] == own):
                add(*t)
        for w in writes:
            t = self.lastw.get(w)
            if t is not None and not (skip_own and t[0] == own):
                add(*t)
            for k, v in self.readers.get(w, {}).items():
                if not (skip_own and k == own):
                    add(k, v)
        waits = []
        wd = self.waited[eng]
        for k, v in deps.items():
            if wd.get(k, 0) < v:
                wd[k] = v
                waits.append((k, v))
        return waits

    def _commit(self, tok, reads, writes):
        for w in writes:
            self.lastw[w] = tok
            self.readers[w] = {}
        for r in reads:
            d = self.readers.setdefault(r, {})
            if d.get(tok[0], 0) < tok[1]:
                d[tok[0]] = tok[1]

    def op(self, eng, fn, reads=(), writes=(), inc=True):
        writes = list(writes) + [r for r in reads if r.startswith('ps') and r not in writes]
        waits = self._deps(eng, reads, writes)
        tok = (('c', eng), self.count[eng] + 1)
        if inc:
            self.count[eng] += 1
        self.prog[eng].append((waits, fn, (tok[0], 1) if inc else None))
        self._commit(tok, reads, writes)
        self.ninst += 1
        return tok

    def dma(self, eng, fn, reads=(), writes=(), bg=False):
        waits = self._deps(eng, reads, writes)
        if bg:
            return self._dma_bg(eng, fn, reads, writes, waits)
        half = len(self.dcount) // 2
        base = 0 if eng == 'pool' else half
        i = base + self.dnext_q[eng]
        self.dnext_q[eng] = (self.dnext_q[eng] + 1) % half
        k = ('d', i)
        wd = self.waited[eng]
        if wd.get(k, 0) < self.dcount[i]:
            wd[k] = self.dcount[i]
            waits.append((k, self.dcount[i]))
        self.dcount[i] += 16
        tok = (k, self.dcount[i])
        self.prog[eng].append((waits, fn, (k, 16)))
        self._commit(tok, reads, writes)
        self.ninst += 1
        return tok

    def _dma_bg(self, eng, fn, reads, writes, waits):
        nb = 8
        if not hasattr(self, 'bcount'):
            self.bcount = [0] * nb
            self.bnext = 0
            for i in range(nb):
                self.semobj[('b', i)] = self._stack.enter_context(self.nc.semaphore('bg%d' % i))
        i = self.bnext
        self.bnext = (self.bnext + 1) % nb
        k = ('b', i)
        wd = self.waited[eng]
        if wd.get(k, 0) < self.bcount[i]:
            wd[k] = self.bcount[i]
            waits.append((k, self.bcount[i]))
        self.bcount[i] += 16
        tok = (k, self.bcount[i])
        self.prog[eng].append((waits, fn, (k, 16)))
        self._commit(tok, reads, writes)
        self.ninst += 1
        return tok

    def coll(self, fn, reads=(), writes=()):
        eng = 'pool'
        waits = self._deps(eng, reads, writes)
        k = ('cc', self.cccount)
        self.semobj[k] = self._stack.enter_context(self.nc.semaphore('cc_sem%d' % self.cccount))
        self.cccount += 1
        tok = (k, 1)
        self.prog[eng].append((waits, fn, (k, 1)))
        self._commit(tok, reads, writes)
        self.ninst += 1
        return tok

    def finish(self, eng='sp'):
        toks = [(('c', e), self.count[e]) for e in self.ENGS if self.count[e]]
        toks += [(('d', i), c) for i, c in enumerate(self.dcount) if c]
        toks += [(('cc', i), 1) for i in range(self.cccount)]
        toks += [(('b', i), v) for i, v in enumerate(getattr(self, 'bcount', [])) if v]
        waits = []
        wd = self.waited[eng]
        for k, v in toks:
            if wd.get(k, 0) < v:
                wd[k] = v
                waits.append((k, v))
        self.prog[eng].append((waits, None, None))

    def emit(self, block):
        def mk(e):
            def body(engh):
                for waits, fn, inc in self.prog[e]:
                    for k, v in waits:
                        engh.wait_ge(self.semobj[k], v)
                    if fn is not None:
                        if fn[0] == '__call__':
                            ins = fn[1](engh)
                        else:
                            ins = getattr(engh, fn[0])(*fn[1], **fn[2])
                        if inc is not None:
                            ins.then_inc(self.semobj[inc[0]], inc[1])
            return body
        block.tensor(mk('pe'))
        block.scalar(mk('act'))
        block.vector(mk('dve'))
        block.gpsimd(mk('pool'))
        block.sync(mk('sp'))


class Ctx:
    def __init__(self):
        self.nc = bass.Bass("TRN2", target_bir_lowering=False)
        self.st = ExitStack()
        self.em = Emitter(self.nc, self.st)
        self.uid = 0

    def sb(self, name, shape, dt):
        return self.st.enter_context(self.nc.sbuf_tensor("s_" + name, list(shape), dt))

    def ps(self, name, shape, dt=F32):
        return self.st.enter_context(self.nc.psum_tensor("p_" + name, list(shape), dt))

    def din(self, name, shape, dt):
        return self.nc.dram_tensor(name, list(shape), dt, kind="ExternalInput").ap()

    def dout(self, name, shape, dt):
        return self.nc.dram_tensor(name, list(shape), dt, kind="ExternalOutput").ap()

    def done(self):
        self.em.finish('sp')
        with self.nc.Block() as block:
            self.em.emit(block)
        self.st.close()
        return self.nc


def I(name, *a, **kw):
    return (name, a, kw)


def mm_group(em, out_ap, pairs, reads, wres, extra_writes=()):
    n = len(pairs)
    for i, (l, r) in enumerate(pairs):
        em.op('pe', I('matmul', out_ap, lhsT=l, rhs=r, start=(i == 0), stop=(i == n - 1)),
              reads=reads, writes=[wres] + list(extra_writes), inc=(i == n - 1))


class Common:
    def __init__(self, c, need_mods_from_wada=True):
        self.c = c
        em = c.em
        self.ones = c.sb("ones_bf", [128, 128], BF16)
        em.op('dve', I('memset', self.ones[:], 1.0), writes=['ones'])
        self.dummy = c.sb("dummy", [128, 1], F32)
        self.epst = c.sb("epst", [128, 1], F32)
        em.op('dve', I('memset', self.epst[:], EPS), writes=['epst'])
        self.modsT = c.sb("modsT", [128, 72, 2], F32)
        self.normg = c.sb("normgT", [128, 6, 8], F32)
        self.A = c.sb("coefA", [128, 3, 8, 2], F32)
        self.G = c.sb("coefG", [128, 3, 8, 2], F32)

    def load_normg(self, normg_d):
        self.c.em.dma('sp', I('dma_start', out=self.normg[:], in_=normg_d), writes=['normg'])

    def compute_mods(self, cvec_d, wada_d, badaT_d, ps_mods, fb, alloc=None, psname='ps_mods'):
        c, em = self.c, self.c.em
        if alloc is None:
            alloc = lambda name, shape, dt: c.sb(name, shape, dt)[:]
        cv = alloc("cvec", [128, 8, 2], F32)
        scv = alloc("scvec", [128, 8, 2], BF16)
        bad = alloc("badaT", [128, 72], F32)
        em.dma('sp', I('dma_start', out=cv, in_=cvec_d), writes=['cv'])
        em.dma('sp', I('dma_start', out=bad, in_=badaT_d), writes=['bad'])
        em.op('act', I('activation', out=scv, in_=cv, func=AF.Silu), reads=['cv'], writes=['scv'])
        wbuf = [flatview(fb.gT, i * 8 * 512, [128, 8, 512]) for i in range(2)]
        wv = wada_d.rearrange("(k p) n -> p k n", p=128)
        for m in range(18):
            wb = wbuf[m % 2]
            em.dma('pool', I('dma_start', out=wb, in_=wv[:, :, m * 512:(m + 1) * 512]),
                   writes=['wada%d' % (m % 2)])
            for fc in range(4):
                mm_group(em, ps_mods[:, m * 4 + fc, :],
                         [(wb[:, k, fc * 128:(fc + 1) * 128], scv[:, k, :]) for k in range(8)],
                         reads=['wada%d' % (m % 2), 'scv'], wres=psname)
        em.op('dve', I('memset', self.dummy[:], 0.0), writes=['dummy', 'gT', 'wada0', 'wada1'])
        for g in range(2):
            em.op('dve', I('tensor_tensor', out=self.modsT[:, :, g], in0=ps_mods[:, 0:72, g], in1=bad, op=ALU.add),
                  reads=[psname, 'bad'], writes=['modsT'])

    def load_mods(self, modsT_d):
        self.c.em.dma('sp', I('dma_start', out=self.modsT[:], in_=modsT_d), writes=['modsT'])

    def compute_coefs(self):
        em = self.c.em
        for idx, res_w in ((0, 0.5), (1, 1.0), (2, 0.5)):
            for g in range(2):
                em.op('dve', I('scalar_tensor_tensor',
                    out=self.A[:, idx, :, g], in0=self.modsT[:, (3 * idx + 1) * 8:(3 * idx + 2) * 8, g], scalar=1.0,
                    in1=self.normg[:, 2 * idx, :], op0=ALU.add, op1=ALU.mult),
                    reads=['modsT', 'normg'], writes=['coefA'])
                em.op('dve', I('scalar_tensor_tensor',
                    out=self.G[:, idx, :, g], in0=self.modsT[:, (3 * idx + 2) * 8:(3 * idx + 3) * 8, g], scalar=res_w,
                    in1=self.normg[:, 2 * idx + 1, :], op0=ALU.mult, op1=ALU.mult),
                    reads=['modsT', 'normg'], writes=['coefG'])

    def shift(self, idx, k, g):
        return self.modsT[:, 3 * idx * 8 + k, g:g + 1]


class FFNBufs:
    def __init__(self, c, tw, alloc=None, gT=None, nw1=2, nw2=2):
        if alloc is None:
            alloc = lambda name, shape, dt: c.sb(name, shape, dt)[:]
        self.tw = tw
        self.hT = alloc("hT", [128, 8, tw], BF16)
        self.gT = gT if gT is not None else alloc("gT", [128, NFC, tw], BF16)
        assert NFC * tw >= 2 * 8 * 512
        self.w1b = [alloc("w1b%d" % i, [128, 2 * 8 * 256], BF16) for i in range(nw1)]
        self.w2b = [alloc("w2b%d" % i, [128, NFC, 256], BF16) for i in range(nw2)]
        self.ysb = alloc("ysb", [128, 8, tw], F32)
        self.sq = alloc("sq", [128, 8, tw], BF16)
        self.tmp = [alloc("tmpf%d" % i, [128, 512], F32) for i in range(2)]
        self.rstd = alloc("rstd", [128, 512], F32)
        self.sa = [alloc("sa%d" % i, [128, 512], BF16) for i in range(2)]
        self.n_tmp = 0


def rms_rstd(c, cm, src_fn, w, ps_ss, sq, rstd, src_reads, tag):
    em = c.em
    for k in range(8):
        em.op('act', I('activation', out=sq[:, k, :w], in_=src_fn(k), func=AF.Square),
              reads=src_reads, writes=['sq'])
    mm_group(em, ps_ss[:, :w], [(cm.ones[:], sq[:, k, :w]) for k in range(8)], reads=['sq', 'ones'], wres=tag)
    em.op('act', I('activation', out=rstd[:, :w], in_=ps_ss[:, :w], func=AF.Sqrt, scale=1.0 / D, bias=cm.epst[:]),
          reads=[tag, 'epst'], writes=['rstd'])
    em.op('dve', I('reciprocal', out=rstd[:, :w], in_=rstd[:, :w]), reads=['rstd'], writes=['rstd'])


def sandwich_in(c, cm, fb, xT, idx, subs, ps_ss, dst, dst_res, ss_res='ps_ss'):
    em = c.em
    for (t0, w, g, off) in subs:
        rms_rstd(c, cm, lambda k: xT[:, k, t0:t0 + w], w, ps_ss, fb.sq, fb.rstd, xres(t0, w), ss_res)
        for k in range(8):
            tmp = fb.tmp[fb.n_tmp % 2]
            tr = 'tmpf%d' % (fb.n_tmp % 2)
            fb.n_tmp += 1
            em.op('dve', I('tensor_tensor', out=tmp[:, :w], in0=xT[:, k, t0:t0 + w], in1=fb.rstd[:, :w], op=ALU.mult),
                  reads=xres(t0, w) + ['rstd'], writes=[tr])
            em.op('act', I('activation', out=dst[:, k, off:off + w], in_=tmp[:, :w], func=AF.Identity,
                                                               scale=cm.A[:, idx, k, g:g + 1], bias=cm.shift(idx, k, g)),
                  reads=[tr, 'coefA', 'modsT'], writes=[dst_res])


def y_evac(c, fb, k, off, w, yp, yres):
    em = c.em
    em.op('dve', I('tensor_copy', out=fb.ysb[:, k, off:off + w], in_=yp), reads=[yres], writes=['ysb'])
    em.op('dve', I('tensor_tensor', out=fb.sq[:, k, off:off + w], in0=fb.ysb[:, k, off:off + w], in1=fb.ysb[:, k, off:off + w], op=ALU.mult),
          reads=['ysb'], writes=['sq'])


def sandwich_out(c, cm, fb, xT, idx, t0, w, g, off, ps_ss, ss_res='ps_ss'):
    em = c.em
    mm_group(em, ps_ss[:, :w], [(cm.ones[:], fb.sq[:, k, off:off + w]) for k in range(8)], reads=['sq', 'ones'], wres=ss_res)
    em.op('act', I('activation', out=fb.rstd[:, :w], in_=ps_ss[:, :w], func=AF.Sqrt, scale=1.0 / D, bias=cm.epst[:]),
          reads=[ss_res, 'epst'], writes=['rstd'])
    em.op('dve', I('reciprocal', out=fb.rstd[:, :w], in_=fb.rstd[:, :w]), reads=['rstd'], writes=['rstd'])
    for k in range(8):
        tmp = fb.tmp[fb.n_tmp % 2]
        tr = 'tmpf%d' % (fb.n_tmp % 2)
        fb.n_tmp += 1
        em.op('dve', I('scalar_tensor_tensor', out=tmp[:, :w], in0=fb.ysb[:, k, off:off + w], scalar=cm.G[:, idx, k, g:g + 1],
                                                                     in1=fb.rstd[:, :w], op0=ALU.mult, op1=ALU.mult),
              reads=['ysb', 'rstd', 'coefG'], writes=[tr])
        em.op('dve', I('tensor_tensor', out=xT[:, k, t0:t0 + w], in0=xT[:, k, t0:t0 + w], in1=tmp[:, :w], op=ALU.add),
              reads=[tr] + xres(t0, w), writes=xres(t0, w))


def ffn(c, cm, fb, xT, idx, tiles, w1_d, w2_d, pss, psnames=None):
    em = c.em
    ps_ss, ps_a, ps_b, ps_y = pss['ss'], pss['a'], pss['b'], pss['y']
    if psnames is None:
        psnames = {'ss': 'ps_ss', 'a': ['ps_a0', 'ps_a1'], 'b': ['ps_b0', 'ps_b1'], 'y': ['ps_y0', 'ps_y1']}
    w1v = w1_d.rearrange("(k p) (two f) -> p k two f", p=128, two=2)
    w2v = w2_d.rearrange("(f p) d -> p f d", p=128)
    nw1 = 0
    nw2 = 0
    nsa = 0
    ny = 0
    subs_list = []
    for tile in tiles:
        subs = []
        off = 0
        for (t0, w, g) in tile:
            subs.append((t0, w, g, off))
            off += w
        subs_list.append(subs)
    sandwich_in(c, cm, fb, xT, idx, subs_list[0], ps_ss, fb.hT, 'hT', ss_res=psnames['ss'])
    for ti, subs in enumerate(subs_list):
        for fp in range(NFC // 2):
            wb = fb.w1b[nw1 % len(fb.w1b)].rearrange("p (a k f) -> p a k f", a=2, k=8)
            wr = 'w1b%d' % (nw1 % len(fb.w1b))
            nw1 += 1
            for two in range(2):
                em.dma('pool', I('dma_start', out=wb[:, two, :, :], in_=w1v[:, :, two, fp * 256:(fp + 1) * 256]), writes=[wr])
            for fi in range(2):
                fc = fp * 2 + fi
                for (t0, w, g, off) in subs:
                    pa = ps_a[nsa % 2]
                    pb = ps_b[nsa % 2]
                    ra, rb = psnames['a'][nsa % 2], psnames['b'][nsa % 2]
                    sa = fb.sa[nsa % 2]
                    rs = 'sa%d' % (nsa % 2)
                    nsa += 1
                    mm_group(em, pa[:, :w], [(wb[:, 0, k, fi * 128:(fi + 1) * 128], fb.hT[:, k, off:off + w]) for k in range(8)],
                             reads=[wr, 'hT'], wres=ra)
                    mm_group(em, pb[:, :w], [(wb[:, 1, k, fi * 128:(fi + 1) * 128], fb.hT[:, k, off:off + w]) for k in range(8)],
                             reads=[wr, 'hT'], wres=rb)
                    em.op('act', I('activation', out=sa[:, :w], in_=pa[:, :w], func=AF.Silu),
                          reads=[ra], writes=[rs])
                    em.op('dve', I('tensor_tensor',
                        out=fb.gT[:, fc, off:off + w], in0=sa[:, :w], in1=pb[:, :w], op=ALU.mult),
                        reads=[rs, rb], writes=['gT'])
        if ti + 1 < len(subs_list):
            sandwich_in(c, cm, fb, xT, idx, subs_list[ti + 1], ps_ss, fb.hT, 'hT', ss_res=psnames['ss'])
        for piece in range(4):
            wb = fb.w2b[nw2 % len(fb.w2b)]
            wr = 'w2b%d' % (nw2 % len(fb.w2b))
            nw2 += 1
            em.dma('pool', I('dma_start', out=wb, in_=w2v[:, :, piece * 256:(piece + 1) * 256]), writes=[wr])
            for (t0, w, g, off) in subs:
                for kk in range(2):
                    k = piece * 2 + kk
                    py = ps_y[ny % 2]
                    ry = psnames['y'][ny % 2]
                    ny += 1
                    mm_group(em, py[:, :w], [(wb[:, f, kk * 128:(kk + 1) * 128], fb.gT[:, f, off:off + w]) for f in range(NFC)],
                             reads=[wr, 'gT'], wres=ry)
                    y_evac(c, fb, k, off, w, py[:, :w], ry)
        for (t0, w, g, off) in subs:
            sandwich_out(c, cm, fb, xT, idx, t0, w, g, off, ps_ss, ss_res=psnames['ss'])


def xres(t0, w):
    return ['x%d' % i for i in range(t0 // 128, (t0 + w + 127) // 128)]


def flatview(ap3, n0, shape):
    flat = ap3.rearrange("p a b -> p (a b)")
    n = 1
    for s in shape[1:]:
        n *= s
    v = flat[:, n0:n0 + n]
    if len(shape) == 2:
        return v
    if len(shape) == 3:
        return v.rearrange("p (a b) -> p a b", a=shape[1])
    return v.rearrange("p (a b c) -> p a b c", a=shape[1], b=shape[2])


def proj_phase(c, cm, fb, xT, tiles, win_d, ropeC_d, ropeS_d, pm_d, pss, qT_o, kvT_o, v_o, alloc=None, psnames=None):
    em = c.em
    ps_ss, ps_a, ps_b, ps_y = pss['ss'], pss['a'], pss['b'], pss['y']
    tw = fb.tw
    winv = win_d.rearrange("(k p) n -> p k n", p=128)
    if alloc is None:
        alloc = lambda name, shape, dt: c.sb(name, shape, dt)[:]
    if psnames is None:
        psnames = {'ss': 'ps_ss', 'a': ['ps_a0', 'ps_a1'], 'b': ['ps_b0', 'ps_b1'], 'y': ['ps_y0', 'ps_y1']}
    pmat = alloc("pmat", [128, 128], BF16)
    em.dma('sp', I('dma_start', out=pmat, in_=pm_d), writes=['pmat'])
    ct = [alloc("ropec%d" % i, [128, 512], F32) for i in range(2)]
    sn = [alloc("ropes%d" % i, [128, 512], F32) for i in range(2)]
    qb = [alloc("qb%d" % i, [128, 512], BF16) for i in range(2)]
    t2 = [alloc("t2_%d" % i, [128, 512], F32) for i in range(2)]
    qst = flatview(fb.gT, 0, [128, 6, tw])
    kvst = flatview(fb.gT, 6 * tw, [128, 8, tw])
    vst = flatview(fb.gT, 14 * tw, [128, tw // 128, 768])
    cnt = {'w': 0, 'p': 0, 'r': 0, 't': 0}

    def next_ps():
        i = cnt['p'] % 4
        cnt['p'] += 1
        return ([ps_a[0], ps_a[1], ps_b[0], ps_b[1]][i], (psnames['a'] + psnames['b'])[i])

    for tile in tiles:
        subs = []
        off = 0
        for (t0, w, g) in tile:
            subs.append((t0, w, g, off))
            off += w
        tww = off
        sandwich_in(c, cm, fb, xT, 1, subs, ps_ss, fb.hT, 'hT', ss_res=psnames['ss'])
        for piece in range(5):
            wb4 = fb.w1b[cnt['w'] % len(fb.w1b)]
            wr = 'w1b%d' % (cnt['w'] % len(fb.w1b))
            cnt['w'] += 1
            wb = wb4.rearrange("p (k n) -> p k n", k=8)
            em.dma('pool', I('dma_start', out=wb, in_=winv[:, :, piece * 512:(piece + 1) * 512]), writes=[wr])
            for (t0, w, g, off) in subs:
                if piece == 0:
                    fm = [(0, 'q', 0, 0.125, False), (1, 'q', 1, 0.125, False), (2, 'kv', 0, 1.0, False), (3, 'kv', 1, 1.0, False)]
                elif piece == 1:
                    fm = [(2, 'kv', 2, 1.0, False), (3, 'kv', 3, 1.0, False)]
                elif piece == 2:
                    fm = [(i, 'q', 2 + i, 0.125, True) for i in range(4)]
                elif piece == 3:
                    fm = [(i, 'kv', 4 + i, 1.0, True) for i in range(4)]
                else:
                    fm = []
                if piece in (2, 3):
                    ci = cnt['r'] % 2
                    cnt['r'] += 1
                    em.dma('sp', I('dma_start', out=ct[ci][:, :w], in_=ropeC_d[:, t0:t0 + w]), writes=['ropec%d' % ci])
                    em.dma('sp', I('dma_start', out=sn[ci][:, :w], in_=ropeS_d[:, t0:t0 + w]), writes=['ropes%d' % ci])
                for (lc, kind, oc, scale, rope) in fm:
                    pp, pr = next_ps()
                    mm_group(em, pp[:, :w], [(wb[:, k, lc * 128:(lc + 1) * 128], fb.hT[:, k, off:off + w]) for k in range(8)],
                             reads=[wr, 'hT'], wres=pr)
                    dst = (qst if kind == 'q' else kvst)[:, oc, off:off + w]
                    if not rope:
                        em.op('act', I('activation', out=dst, in_=pp[:, :w], func=AF.Copy, scale=scale),
                              reads=[pr], writes=['gT'])
                    else:
                        ti = cnt['t'] % 2
                        cnt['t'] += 1
                        em.op('act', I('activation', out=qb[ti][:, :w], in_=pp[:, :w], func=AF.Copy, scale=scale),
                              reads=[pr], writes=['qb%d' % ti])
                        py = ps_y[ti]
                        pyr = psnames['y'][ti]
                        mm_group(em, py[:, :w], [(pmat, qb[ti][:, :w])], reads=['pmat', 'qb%d' % ti], wres=pyr)
                        tmp = fb.tmp[ti]
                        em.op('dve', I('scalar_tensor_tensor',
                            out=tmp[:, :w], in0=pp[:, :w], scalar=scale, in1=ct[ci][:, :w], op0=ALU.mult, op1=ALU.mult),
                            reads=[pr, 'ropec%d' % ci], writes=['tmpf%d' % ti])
                        em.op('dve', I('tensor_tensor', out=t2[ti][:, :w], in0=py[:, :w], in1=sn[ci][:, :w], op=ALU.mult),
                              reads=[pyr, 'ropes%d' % ci], writes=['t2_%d' % ti])
                        em.op('dve', I('tensor_tensor', out=dst, in0=tmp[:, :w], in1=t2[ti][:, :w], op=ALU.add),
                              reads=['tmpf%d' % ti, 't2_%d' % ti], writes=['gT'])
                if piece in (1, 4):
                    c0, ncol, vo = (0, 256, 0) if piece == 1 else (0, 512, 256)
                    for tcn in range(w // 128):
                        pp, pr = next_ps()
                        tk = off + tcn * 128
                        mm_group(em, pp[:, :ncol], [(fb.hT[:, k, tk:tk + 128], wb[:, k, c0:c0 + ncol]) for k in range(8)],
                                 reads=[wr, 'hT'], wres=pr)
                        em.op('act', I('activation', out=vst[:, tk // 128, vo:vo + ncol], in_=pp[:, :ncol], func=AF.Copy),
                              reads=[pr], writes=['gT'])
        tile0 = tile[0][0]
        em.dma('sp', I('dma_start', out=qT_o[:, :, tile0:tile0 + tww], in_=qst[:, :, :tww]), reads=['gT'], writes=['qT_o'])
        em.dma('sp', I('dma_start', out=kvT_o[:, :, tile0:tile0 + tww], in_=kvst[:, :, :tww]), reads=['gT'], writes=['kvT_o'])
        em.dma('sp', I('dma_start',
            out=v_o[tile0:tile0 + tww, :].rearrange("(c p) n -> p c n", p=128), in_=vst[:, :tww // 128, :]), reads=['gT'], writes=['v_o'])


def make_tiles(n_lat, n_ctx, tw):
    subs = []
    t = 0
    while t < n_lat:
        w = min(512, n_lat - t)
        subs.append((t, w, 0))
        t += w
    t = 0
    while t < n_ctx:
        w = min(512, n_ctx - t)
        subs.append((n_lat + t, w, 1))
        t += w
    tiles = []
    cur = []
    room = tw
    for (t0, w, g) in subs:
        while w > 0:
            take = min(w, room)
            cur.append((t0, take, g))
            t0 += take
            w -= take
            room -= take
            if room == 0:
                tiles.append(cur)
                cur = []
                room = tw
    if cur:
        tiles.append(cur)
    return tiles


def build_part_a(n_lat=2048, n_ctx=256, tw=768, upto=3):
    NT = n_lat + n_ctx
    c = Ctx()
    em = c.em
    xT_d = c.din("xT", [128, 8, NT], F32)
    cvec_d = c.din("cvec", [128, 8, 2], F32)
    wada_d = c.din("w_ada", [D, 9 * D], F32)
    bada_d = c.din("badaT", [128, 72], F32)
    normg_d = c.din("normgT", [128, 6, 8], F32)
    w1_d = c.din("w1", [D, 2 * DFF], F32)
    w2_d = c.din("w2", [DFF, D], F32)
    win_d = c.din("w_in", [D, 2560], F32)
    ropeC_d = c.din("ropeC", [128, NT], F32)
    ropeS_d = c.din("ropeS", [128, NT], F32)
    pm_d = c.din("pmat", [128, 128], BF16)
    x1T_o = c.dout("x1T", [128, 8, NT], F32)
    mods_o = c.dout("modsT", [128, 72, 2], F32)
    qT_o = c.dout("qT", [128, 6, NT], BF16)
    kvT_o = c.dout("kvT", [128, 8, NT], BF16)
    v_o = c.dout("vtok", [NT, 768], BF16)

    xT = c.sb("xT_sb", [128, 8, NT], F32)
    cm = Common(c)
    fb = FFNBufs(c, tw)
    pss = {'ss': c.ps("ps_ss", [128, 512]), 'a': [c.ps("ps_a%d" % i, [128, 512]) for i in range(2)],
           'b': [c.ps("ps_b%d" % i, [128, 512]) for i in range(2)], 'y': [c.ps("ps_y%d" % i, [128, 512]) for i in range(2)]}
    ps_mods = c.ps("ps_mods", [128, 256, 2])
    tiles = make_tiles(n_lat, n_ctx, tw)
    for t in range(0, NT, 512):
        w = min(512, NT - t)
        em.dma('sp', I('dma_start', out=xT[:, :, t:t + w], in_=xT_d[:, :, t:t + w]), writes=xres(t, w))
    cm.load_normg(normg_d)
    cm.compute_mods(cvec_d, wada_d, bada_d, ps_mods, fb)
    cm.compute_coefs()
    em.dma('sp', I('dma_start', out=mods_o, in_=cm.modsT[:]), reads=['modsT'], writes=['mods_o'])
    if upto >= 2:
        ffn(c, cm, fb, xT, 0, tiles, w1_d, w2_d, pss)
    em.dma('sp', I('dma_start', out=x1T_o, in_=xT[:]), reads=xres(0, NT), writes=['x1T_o'])
    if upto >= 3:
        proj_phase(c, cm, fb, xT, tiles, win_d, ropeC_d, ropeS_d, pm_d, pss, qT_o, kvT_o, v_o)
    print("part A instructions:", em.ninst)
    return c.done()


def rope_tables(pos):
    pos = np.asarray(pos)
    row = (pos // GRID_W).astype(np.float32)
    col = (pos % GRID_W).astype(np.float32)
    n_freq = 16
    inv_freq = np.power(np.float32(10000.0), -np.arange(n_freq, dtype=np.float32) / np.float32(n_freq)).astype(np.float32)
    ang = np.concatenate([row[:, None] * inv_freq, col[:, None] * inv_freq], axis=-1).astype(np.float32)
    return np.cos(ang).astype(np.float32), np.sin(ang).astype(np.float32)


def rope_feature_major(cos, sin, n_ctx):
    n = cos.shape[0]
    C = np.ones((128, n + n_ctx), np.float32)
    S = np.zeros((128, n + n_ctx), np.float32)
    p = np.arange(128)
    C[:, :n] = cos.T[p % 32]
    sign = np.where((p % 64) < 32, -1.0, 1.0).astype(np.float32)
    S[:, :n] = sin.T[p % 32] * sign[:, None]
    return C, S


def rope_pmat():
    pm = np.zeros((128, 128), np.float32)
    for po in range(128):
        pi = po + 32 if (po % 64) < 32 else po - 32
        pm[pi, po] = 1.0
    return pm.astype(NPBF)


class Arena:
    def __init__(self, c, nelem):
        self.c = c
        self.t = c.sb("arena", [128, nelem], BF16)
        self.n = nelem
        self.off = 0
        self.gen = 0

    def alloc(self, name, shape, dt):
        n = 1
        for s in shape[1:]:
            n *= s
        if dt == F32:
            n *= 2
        self.off = (self.off + 15) // 16 * 16
        assert self.off + n <= self.n, "arena overflow %s: need %d have %d" % (name, self.off + n, self.n)
        v = self.t[:, self.off:self.off + n]
        self.off += n
        if dt == F32:
            v = v.bitcast(F32)
        if len(shape) == 3:
            v = v.rearrange("p (a b) -> p a b", a=shape[1])
        elif len(shape) == 4:
            v = v.rearrange("p (a b c) -> p a b c", a=shape[1], b=shape[2])
        return v

    def reset(self, to_zero=False):
        self.c.em.barrier()
        if to_zero:
            self.base = 0
        self.off = getattr(self, 'base', 0)
        self.gen += 1

    def set_base(self):
        self.base = self.off


def _barrier(self):
    toks = [(('c', e), self.count[e]) for e in self.ENGS if self.count[e]]
    toks += [(('d', i), cc) for i, cc in enumerate(self.dcount) if cc]
    for eng in self.ENGS:
        waits = []
        wd = self.waited[eng]
        for k, v in toks:
            if wd.get(k, 0) < v:
                wd[k] = v
                waits.append((k, v))
        if waits:
            self.prog[eng].append((waits, None, None))


Emitter.barrier = _barrier


def na_attention(c, ar, P, mixT, q_d, kT_src, v_src, n_kc_tot, qchunks, bias_d, ident, col0, shiftt):
    em = c.em
    NK = n_kc_tot * 128
    kT = ar.alloc("na_kT", [128, 2, NK], BF16)
    vv = ar.alloc("na_v", [128, n_kc_tot, 4, 65], BF16)
    nq = len(qchunks)
    qT = ar.alloc("na_q", [128, 2, nq * 128], BF16)
    g = ar.gen
    rk, rv, rq = 'na_kT%d' % g, 'na_v%d' % g, 'na_q%d' % g
    em.dma('sp', I('dma_start', out=kT, in_=kT_src), writes=[rk])
    em.op('dve', I('memset', vv[:, :, :, 64:65], 1.0), writes=[rv])
    for h in range(4):
        em.dma('sp', I('dma_start', out=vv[:, :, h, 0:64], in_=v_src[:, h * 64:(h + 1) * 64].rearrange("(c p) e -> p c e", p=128)), writes=[rv])
    q0 = qchunks[0][0]
    em.dma('sp', I('dma_start', out=qT, in_=q_d[:, 0:2, q0:q0 + nq * 128]), writes=[rq])
    bias = [ar.alloc("na_bias%d" % i, [128, 4, 6, 128], F32) for i in range(2)]
    ssb = [ar.alloc("na_s%d" % i, [128, 6, 128], F32) for i in range(2)]
    pT = [ar.alloc("na_p%d" % i, [128, 8, 128], BF16) for i in range(2)]
    atok = ar.alloc("na_atok", [128, 256], BF16)
    rec = ar.alloc("na_rec", [128, 4], F32)
    cnt = 0
    for qi, (qcol, kcs, bvar, nb) in enumerate(qchunks):
        bi = qi % 2
        if bvar is not None:
            for h in range(4):
                em.dma('sp', I('dma_start', out=bias[bi][:, h, :, :], in_=bias_d[bvar, h].rearrange("j k q -> k j q")), writes=['na_bias%d_%d' % (bi, g)])
        nk = len(kcs)
        O = P['O'][:, 0, 0:4 * 65].rearrange("p (h e) -> p h e", h=4)

        def na_views(h, si):
            hp = (h % 2) * 64
            hc = h // 2
            SX = P['S'][si][:, 0, :].rearrange("p (j q) -> p j q", j=4)
            SY = P['S'][si][:, 1, :].rearrange("p (j q) -> p j q", j=4)
            return hp, hc, SX, SY, 'psS%d_0' % si, 'psS%d_1' % si

        def issue_S(h, si):
            hp, hc, SX, SY, rsx, rsy = na_views(h, si)
            qap = qT[hp:hp + 64, hc, qi * 128:(qi + 1) * 128]
            for jj, kc in enumerate(kcs):
                d, r = (SX[:, jj, :], rsx) if jj < 4 else (SY[:, jj - 4, :], rsy)
                mm_group(em, d, [(kT[hp:hp + 64, hc, kc * 128:(kc + 1) * 128], qap)], reads=[rk, rq], wres=r)

        def issue_post(h, si):
            hp, hc, SX, SY, rsx, rsy = na_views(h, si)
            sb_ = ssb[si]
            pt = pT[si]
            rs_, rp_ = 'na_s%d_%d' % (si, g), 'na_p%d_%d' % (si, g)
            if nb > 0:
                n1 = min(nb, 4)
                em.op('dve', I('tensor_tensor', out=sb_[:, 0:n1, :], in0=SX[:, 0:n1, :], in1=bias[bi][:, h, 0:n1, :], op=ALU.add),
                      reads=[rsx, 'na_bias%d_%d' % (bi, g)], writes=[rs_])
                if nb > 4:
                    em.op('dve', I('tensor_tensor', out=sb_[:, 4:nb, :], in0=SY[:, 0:nb - 4, :], in1=bias[bi][:, h, 4:nb, :], op=ALU.add),
                          reads=[rsy, 'na_bias%d_%d' % (bi, g)], writes=[rs_])
                em.op('act', I('activation', out=pt[:, 0:nb, :], in_=sb_[:, 0:nb, :], func=AF.Exp, bias=shiftt[:]), reads=[rs_, 'shiftt'], writes=[rp_])
            j = nb
            while j < nk:
                if j < 4:
                    e_ = min(nk, 4)
                    em.op('act', I('activation', out=pt[:, j:e_, :], in_=SX[:, j:e_, :], func=AF.Exp, bias=shiftt[:]), reads=[rsx, 'shiftt'], writes=[rp_])
                else:
                    e_ = nk
                    em.op('act', I('activation', out=pt[:, j:e_, :], in_=SY[:, j - 4:e_ - 4, :], func=AF.Exp, bias=shiftt[:]), reads=[rsy, 'shiftt'], writes=[rp_])
                j = e_
            mm_group(em, O[:, h, :], [(pt[:, jj, :], vv[:, kc, h, :]) for jj, kc in enumerate(kcs)], reads=[rp_, rv], wres='psU1_0')
        issue_S(0, cnt % 2)
        for h in range(4):
            si = cnt % 2
            cnt += 1
            if h + 1 < 4:
                issue_S(h + 1, cnt % 2)
            issue_post(h, si)
        em.op('dve', I('reciprocal', out=rec[:, :], in_=O[:, :, 64]), reads=['psU1_0'], writes=['na_rec%d' % g])
        em.op('dve', I('tensor_tensor', out=atok[:, :].rearrange("p (h e) -> p h e", h=4), in0=O[:, :, 0:64],
                       in1=rec[:, :].unsqueeze(2).to_broadcast([128, 4, 64]), op=ALU.mult),
              reads=['psU1_0', 'na_rec%d' % g], writes=['na_atok%d' % g])
        T = P['T']
        for ch in range(2):
            em.op('pe', I('transpose', out=T[:, ch * 128:(ch + 1) * 128], in_=atok[:, ch * 128:(ch + 1) * 128], identity=ident[:]),
                  reads=['na_atok%d' % g, 'ident'], writes=['psU2_0'])
        col = col0 + qi * 128
        em.op('act', I('activation', out=mixT[:, 0:2, col:col + 128], in_=T[:, 0:256].rearrange("p (c q) -> p c q", c=2), func=AF.Copy),
              reads=['psU2_0'], writes=['mix%d' % (col // 128)])


def pool_mixer(c, ar, P, mixT, pin_src, rcnt_src, n_tok, pwbd, pscale, col0):
    em = c.em
    g = ar.gen
    W = 512
    ubuf = [ar.alloc("pl_u%d" % i, [128, 2, W + 16], BF16) for i in range(2)]
    rc = [ar.alloc("pl_rc%d" % i, [128, 2, W], F32) for i in range(2)]
    s2 = ar.alloc("pl_s2", [128, 2, W + 16], F32)
    s4 = ar.alloc("pl_s4", [128, 2, W + 16], F32)
    s8 = ar.alloc("pl_s8", [128, 2, W + 16], F32)
    s16 = ar.alloc("pl_s16", [128, 2, W + 16], F32)
    pm = ar.alloc("pl_pm", [128, 2, W], F32)
    pb = ar.alloc("pl_pb", [128, 2, W], BF16)
    it = 0
    for t0 in range(0, n_tok, W):
        w = min(W, n_tok - t0)
        u = ubuf[it % 2]
        r = rc[it % 2]
        ru, rr = 'pl_u%d_%d' % (it % 2, g), 'pl_rc%d_%d' % (it % 2, g)
        it += 1
        em.dma('sp', I('dma_start', out=u[:, :, 0:w + 16], in_=pin_src[:, :, t0:t0 + w + 16]), writes=[ru])
        em.dma('sp', I('dma_start', out=r[:, :, 0:w], in_=rcnt_src[:, :, t0:t0 + w]), writes=[rr])
        L = w + 16
        em.op('dve', I('tensor_tensor', out=s2[:, :, 1:L], in0=u[:, :, 0:L - 1], in1=u[:, :, 1:L], op=ALU.add), reads=[ru], writes=['pl_s2_%d' % g])
        em.op('dve', I('tensor_tensor', out=s4[:, :, 2:L - 1], in0=s2[:, :, 1:L - 2], in1=s2[:, :, 3:L], op=ALU.add), reads=['pl_s2_%d' % g], writes=['pl_s4_%d' % g])
        em.op('dve', I('tensor_tensor', out=s8[:, :, 4:L - 3], in0=s4[:, :, 2:L - 5], in1=s4[:, :, 6:L - 1], op=ALU.add), reads=['pl_s4_%d' % g], writes=['pl_s8_%d' % g])
        em.op('dve', I('tensor_tensor', out=s16[:, :, 8:L - 7], in0=s8[:, :, 4:L - 11], in1=s8[:, :, 12:L - 3], op=ALU.add), reads=['pl_s8_%d' % g], writes=['pl_s16_%d' % g])
        for (ch, p0, lvl, lr) in ((0, 0, s2, 'pl_s2_%d' % g), (0, 64, s4, 'pl_s4_%d' % g), (1, 0, s8, 'pl_s8_%d' % g), (1, 64, s16, 'pl_s16_%d' % g)):
            em.op('dve', I('tensor_tensor', out=pm[p0:p0 + 64, ch, 0:w], in0=lvl[p0:p0 + 64, ch, 8:8 + w], in1=r[p0:p0 + 64, ch, 0:w], op=ALU.mult),
                  reads=[lr, rr], writes=['pl_pm_%d' % g])
            em.op('dve', I('tensor_tensor', out=pb[p0:p0 + 64, ch, 0:w], in0=pm[p0:p0 + 64, ch, 0:w], in1=u[p0:p0 + 64, ch, 8:8 + w], op=ALU.subtract),
                  reads=['pl_pm_%d' % g, ru], writes=['pl_pb_%d' % g])
        for ch in range(2):
            pp = P['S'][ch][:, 0, :]
            pr = 'psS%d_0' % ch
            mm_group(em, pp[:, :w], [(pwbd[:, ch, :], pb[:, ch, 0:w])], reads=['pl_pb_%d' % g, 'pwbd'], wres=pr)
            col = col0 + t0
            em.op('act', I('activation', out=mixT[:, 2 + ch, col:col + w], in_=pp[:, :w], func=AF.Copy, scale=pscale[:, ch:ch + 1]),
                  reads=[pr, 'pscale'], writes=['mix%d' % i for i in range(col // 128, (col + w) // 128)])


def diff_attention(c, ar, P, mixT, q_d, qtiles, kT_src, v_src, n_kc, lamt, gsub, ident, shiftt, epst, heads=range(4), loaders=None):
    em = c.em
    g = ar.gen
    NK = n_kc * 128
    kT = [ar.alloc("df_kT%d" % i, [128, NK], BF16) for i in range(2)]
    vv = [ar.alloc("df_v%d" % i, [128, n_kc, 129], BF16) for i in range(2)]
    qmax = max(w for _, w in qtiles)
    qb = [ar.alloc("df_q%d" % i, [128, qmax], BF16) for i in range(2)]
    p12 = [ar.alloc("df_p%d" % i, [128, 2, 512], BF16) for i in range(2)]
    rr = ar.alloc("df_r", [128, 2, 4], F32)
    tt = ar.alloc("df_t", [128, 4, 128], F32)
    oo = ar.alloc("df_o", [128, 4, 128], F32)
    osq = ar.alloc("df_osq", [128, 4, 128], F32)
    ss = ar.alloc("df_ss", [128, 4], F32)
    cb = ar.alloc("df_cb", [128, 4, 128], BF16)
    for i in range(2):
        em.op('dve', I('memset', vv[i][:, :, 128:129], 1.0), writes=['df_v%d_%d' % (i, g)])
    nq = 0
    ns = 0
    for hi, h in enumerate(heads):
        kb, vb = kT[hi % 2], vv[hi % 2]
        rk, rv = 'df_kT%d_%d' % (hi % 2, g), 'df_v%d_%d' % (hi % 2, g)
        if loaders is not None:
            loaders(h, kb, vb, rk, rv)
        else:
            em.dma('sp', I('dma_start', out=kb, in_=kT_src[:, h, :]), writes=[rk])
            step = 16
            for c0 in range(0, n_kc, step):
                c1 = min(n_kc, c0 + step)
                em.dma('sp', I('dma_start', out=vb[:, c0:c1, 0:128],
                               in_=v_src[c0 * 128:c1 * 128, h * 128:(h + 1) * 128].rearrange("(c p) e -> p c e", p=128)), writes=[rv])
        for (t0, w) in qtiles:
            qq = qb[nq % 2]
            rq = 'df_q%d_%d' % (nq % 2, g)
            nq += 1
            em.dma('sp', I('dma_start', out=qq[:, :w], in_=q_d[:, 2 + h, t0:t0 + w]), writes=[rq])
            nqc = w // 128
            U1 = P['U1'][:].rearrange("p a (c e) -> p (a c) e", c=2)
            U2 = P['U2'][:].rearrange("p a (c e) -> p (a c) e", c=2)
            def issue_S(kc, slot):
                S = P['S'][slot]
                rs = ['psS%d_0' % slot, 'psS%d_1' % slot]
                for half in range(2):
                    hp = half * 64
                    mm_group(em, S[:, half, :w], [(kb[hp:hp + 64, kc * 128:(kc + 1) * 128], qq[hp:hp + 64, :w])], reads=[rk, rq], wres=rs[half])

            def issue_exp_pv(kc, slot):
                S = P['S'][slot]
                rs = ['psS%d_0' % slot, 'psS%d_1' % slot]
                pt = p12[slot]
                rp = 'df_p%d_%d' % (slot, g)
                em.op('act', I('activation', out=pt[:, :, :w], in_=S[:, :, :w], func=AF.Exp, bias=shiftt[:]), reads=rs + ['shiftt'], writes=[rp])
                for half, (U, ru) in enumerate(((U1, 'psU1'), (U2, 'psU2'))):
                    for qc in range(nqc):
                        em.op('pe', I('matmul', U[:, qc, 0:129], lhsT=pt[:, half, qc * 128:(qc + 1) * 128], rhs=vb[:, kc, :],
                                      start=(kc == 0 and qc % 2 == 0), stop=(kc == n_kc - 1), skip_group_check=True),
                              reads=[rp, rv], writes=['%s_%d' % (ru, qc // 2)], inc=(qc == nqc - 1))
            issue_S(0, ns % 2)
            for kc in range(n_kc):
                slot = ns % 2
                ns += 1
                if kc + 1 < n_kc:
                    issue_S(kc + 1, ns % 2)
                issue_exp_pv(kc, slot)
            ru1 = ['psU1_0', 'psU1_1'][:(nqc + 1) // 2]
            ru2 = ['psU2_0', 'psU2_1'][:(nqc + 1) // 2]
            rg = 'df_ep%d' % g
            em.op('dve', I('reciprocal', out=rr[:, 0, :nqc], in_=U1[:, :nqc, 128]), reads=ru1, writes=[rg])
            em.op('dve', I('reciprocal', out=rr[:, 1, :nqc], in_=U2[:, :nqc, 128]), reads=ru2, writes=[rg])
            em.op('dve', I('tensor_scalar', out=rr[:, 1, :nqc], in0=rr[:, 1, :nqc], scalar1=lamt[:, 0:1], scalar2=None, op0=ALU.mult), reads=[rg, 'lamt'], writes=[rg])
            em.op('dve', I('tensor_tensor', out=tt[:, :nqc, :], in0=U2[:, :nqc, 0:128], in1=rr[:, 1, :nqc].unsqueeze(2).to_broadcast([128, nqc, 128]), op=ALU.mult),
                  reads=ru2 + [rg], writes=[rg])
            em.op('dve', I('tensor_tensor', out=oo[:, :nqc, :], in0=U1[:, :nqc, 0:128], in1=rr[:, 0, :nqc].unsqueeze(2).to_broadcast([128, nqc, 128]), op=ALU.mult),
                  reads=ru1 + [rg], writes=[rg])
            em.op('dve', I('tensor_tensor', out=oo[:, :nqc, :], in0=oo[:, :nqc, :], in1=tt[:, :nqc, :], op=ALU.subtract), reads=[rg], writes=[rg])
            em.op('dve', I('tensor_tensor', out=osq[:, :nqc, :], in0=oo[:, :nqc, :], in1=oo[:, :nqc, :], op=ALU.mult), reads=[rg], writes=[rg])
            em.op('dve', I('reduce_sum', out=ss[:, :nqc], in_=osq[:, :nqc, :], axis=AX.X), reads=[rg], writes=[rg])
            em.op('act', I('activation', out=ss[:, :nqc], in_=ss[:, :nqc], func=AF.Sqrt, scale=1.0 / 128, bias=epst[:]), reads=[rg, 'epst'], writes=[rg])
            em.op('dve', I('reciprocal', out=ss[:, :nqc], in_=ss[:, :nqc]), reads=[rg], writes=[rg])
            em.op('dve', I('tensor_tensor', out=oo[:, :nqc, :], in0=oo[:, :nqc, :], in1=ss[:, :nqc].unsqueeze(2).to_broadcast([128, nqc, 128]), op=ALU.mult),
                  reads=[rg], writes=[rg])
            em.op('dve', I('tensor_tensor', out=cb[:, :nqc, :], in0=oo[:, :nqc, :], in1=gsub[:, :].unsqueeze(1).to_broadcast([128, nqc, 128]), op=ALU.mult),
                  reads=[rg, 'gsub'], writes=['df_cb%d' % g])
            T = P['T']
            for qc in range(nqc):
                em.op('pe', I('transpose', out=T[:, qc * 128:(qc + 1) * 128], in_=cb[:, qc, :], identity=ident[:]),
                      reads=['df_cb%d' % g, 'ident'], writes=['psU2_0'])
            em.op('act', I('activation', out=mixT[:, 4 + h, t0:t0 + w], in_=T[:, 0:w], func=AF.Copy),
                  reads=['psU2_0'], writes=['mix%d' % i for i in range(t0 // 128, (t0 + w) // 128)])


def na_local_chunks(i, nq):
    if i == 0:
        return 0, 6
    if i == nq - 1:
        return i - 1, 6
    return i, 5


def build_part_b(n_lat=2048, n_ctx=256, tw=768, n_kc_diff=66, na_variants=None, lam_init=0.2, ctx_out=True, n_halo_kc=None):
    NT = n_lat + n_ctx
    nq = n_lat // 128
    if n_halo_kc is None:
        n_halo_kc = nq + 4
    if na_variants is None:
        na_variants = [0] * nq
    c = Ctx()
    em = c.em
    x1T_d = c.din("x1T", [128, 8, NT], F32)
    mods_d = c.din("modsT", [128, 72, 2], F32)
    normg_d = c.din("normgT", [128, 6, 8], F32)
    q_d = c.din("qT", [128, 6, NT], BF16)
    nakT_d = c.din("na_kT", [128, 2, (n_halo_kc + n_ctx // 128) * 128], BF16)
    nav_d = c.din("na_v", [(n_halo_kc + n_ctx // 128) * 128, 256], BF16)
    nakTc_d = c.din("na_kTc", [128, 2, n_ctx], BF16)
    navc_d = c.din("na_vc", [n_ctx, 256], BF16)
    nvar = max(na_variants) + 1
    bias_d = c.din("na_bias", [nvar, 4, 6, 128, 128], F32)
    pin_d = c.din("pinT", [128, 2, n_lat + 16], BF16)
    rcnt_d = c.din("rcnt", [128, 2, n_lat], F32)
    pinc_d = c.din("pinTc", [128, 2, n_ctx + 16], BF16)
    rcntc_d = c.din("rcntc", [128, 2, n_ctx], F32)
    pwbd_d = c.din("pwbd", [128, 2, 128], BF16)
    pscale_d = c.din("pscaleT", [128, 2], F32)
    dkT_d = c.din("dkT", [128, 4, n_kc_diff * 128], BF16)
    dv_d = c.din("dv", [n_kc_diff * 128, 512], BF16)
    dlam_d = c.din("dlam", [128, 256], F32)
    subg_d = c.din("subg", [128, 128], F32)
    ident_d = c.din("ident", [128, 128], BF16)
    wout_d = c.din("w_out", [D, D], F32)
    w1_d = c.din("w1", [D, 2 * DFF], F32)
    w2_d = c.din("w2", [DFF, D], F32)
    x2T_o = c.dout("x2T", [128, 8, NT], F32)

    xT = c.sb("xT_sb", [128, 8, NT], F32)
    mixT_t = c.sb("mixT", [128, 8, NT], BF16)
    mixT = mixT_t[:]
    cm = Common(c)
    ident = c.sb("ident", [128, 128], BF16)
    pwbd = c.sb("pwbd", [128, 2, 128], BF16)
    pscale = c.sb("pscale", [128, 2], F32)
    dlam = c.sb("dlam", [128, 256], F32)
    lamw = c.sb("lamw", [128, 4], F32)
    lamt = c.sb("lamt", [128, 1], F32)
    gsub = c.sb("gsub", [128, 128], F32)
    shiftt = c.sb("shiftt", [128, 1], F32)
    S0 = c.ps("S0", [128, 2, 512])
    S1 = c.ps("S1", [128, 2, 512])
    U1 = c.ps("U1", [128, 2, 512])
    U2 = c.ps("U2", [128, 2, 512])
    P = {'S': [S0, S1], 'U1': U1, 'U2': U2, 'O': U1, 'T': U2[:, 0, :].bitcast(BF16)}
    arena_n = (c.nc.sbuf_bytes_remaining - 2048) // 2 // 16 * 16
    ar = Arena(c, arena_n)
    print("arena elems", arena_n)

    for t in range(0, NT, 512):
        w = min(512, NT - t)
        em.dma('sp', I('dma_start', out=xT[:, :, t:t + w], in_=x1T_d[:, :, t:t + w]), writes=xres(t, w))
    cm.load_normg(normg_d)
    cm.load_mods(mods_d)
    cm.compute_coefs()
    em.dma('sp', I('dma_start', out=ident[:], in_=ident_d), writes=['ident'])
    em.dma('sp', I('dma_start', out=pwbd[:], in_=pwbd_d), writes=['pwbd'])
    em.dma('sp', I('dma_start', out=pscale[:], in_=pscale_d), writes=['pscale'])
    em.dma('sp', I('dma_start', out=dlam[:], in_=dlam_d), writes=['dlam'])
    em.dma('sp', I('dma_start', out=gsub[:], in_=subg_d), writes=['gsub'])
    em.op('dve', I('memset', shiftt[:], EXP_SHIFT), writes=['shiftt'])
    em.op('dve', I('tensor_scalar', out=gsub[:], in0=gsub[:], scalar1=float(1.0 - lam_init), scalar2=None, op0=ALU.mult), reads=['gsub'], writes=['gsub'])
    em.op('dve', I('tensor_tensor', out=dlam[:, 0:64], in0=dlam[:, 0:64], in1=dlam[:, 64:128], op=ALU.mult), reads=['dlam'], writes=['dlam'])
    em.op('dve', I('tensor_tensor', out=dlam[:, 128:192], in0=dlam[:, 128:192], in1=dlam[:, 192:256], op=ALU.mult), reads=['dlam'], writes=['dlam'])
    em.op('dve', I('reduce_sum', out=lamw[:, 0:2], in_=dlam[:].rearrange("p (a b) -> p a b", a=2)[:, :, 0:64], axis=AX.X), reads=['dlam'], writes=['lamw'])
    em.op('act', I('activation', out=lamw[:, 2:4], in_=lamw[:, 0:2], func=AF.Exp), reads=['lamw'], writes=['lamw'])
    em.op('dve', I('tensor_tensor', out=lamt[:], in0=lamw[:, 2:3], in1=lamw[:, 3:4], op=ALU.subtract), reads=['lamw'], writes=['lamt'])
    em.op('dve', I('tensor_scalar', out=lamt[:], in0=lamt[:], scalar1=float(lam_init), scalar2=None, op0=ALU.add), reads=['lamt'], writes=['lamt'])

    nkc_tot = n_halo_kc + n_ctx // 128
    ctx_kcs = [n_halo_kc + i for i in range(n_ctx // 128)]
    qch = []
    for i in range(nq):
        k0, nl = na_local_chunks(i, nq)
        qch.append((i * 128, [k0 + j for j in range(nl)] + ctx_kcs, na_variants[i], nl))
    na_attention(c, ar, P, mixT, q_d, nakT_d, nav_d, nkc_tot, qch, bias_d, ident, 0, shiftt)
    if ctx_out:
        ar.reset()
        qch = [(n_lat + i * 128, list(range(n_ctx // 128)), None, 0) for i in range(n_ctx // 128)]
        na_attention(c, ar, P, mixT, q_d, nakTc_d, navc_d, n_ctx // 128, qch, bias_d, ident, n_lat, shiftt)
    ar.reset()
    pool_mixer(c, ar, P, mixT, pin_d, rcnt_d, n_lat, pwbd, pscale, 0)
    if ctx_out:
        ar.reset()
        pool_mixer(c, ar, P, mixT, pinc_d, rcntc_d, n_ctx, pwbd, pscale, n_lat)
    ar.reset()
    qtiles = [(t, min(512, n_lat - t)) for t in range(0, n_lat, 512)]
    diff_attention(c, ar, P, mixT, q_d, qtiles, dkT_d, dv_d, n_kc_diff, lamt, gsub, ident, shiftt, cm.epst)
    if ctx_out:
        ar.reset()
        ctx0 = n_kc_diff - n_ctx // 128
        qtiles = [(n_lat, n_ctx)]
        diff_attention(c, ar, P, mixT, q_d, qtiles, dkT_d[:, :, ctx0 * 128:], dv_d[ctx0 * 128:, :], n_ctx // 128, lamt, gsub, ident, shiftt, cm.epst)
    ar.reset()
    n_mix = NT if ctx_out else n_lat
    wo = ar.alloc("wo", [128, 8, D], BF16)
    em.dma('pool', I('dma_start', out=wo, in_=wout_d.rearrange("(k p) n -> p k n", p=128)), writes=['wo'])

    class YB:
        pass
    yb = YB()
    yb.ysb = ar.alloc("ysb", [128, 8, 512], F32)
    yb.sq = ar.alloc("sq", [128, 8, 512], BF16)
    yb.tmp = [ar.alloc("tmpf%d" % i, [128, 512], F32) for i in range(2)]
    yb.rstd = ar.alloc("rstd", [128, 512], F32)
    yb.n_tmp = 0
    pss = {'ss': U2[:, 1, :], 'a': [S0[:, 0, :], S0[:, 1, :]], 'b': [S1[:, 0, :], S1[:, 1, :]], 'y': [U1[:, 0, :], U1[:, 1, :]]}
    ny = 0
    for (t0, w, g) in [s_ for tile in make_tiles(n_lat, n_ctx if ctx_out else 0, 512) for s_ in tile]:
        for k in range(8):
            py = pss['y'][ny % 2]
            ry = 'psU1_%d' % (ny % 2)
            ny += 1
            mm_group(em, py[:, :w], [(wo[:, kk, k * 128:(k + 1) * 128], mixT[:, kk, t0:t0 + w]) for kk in range(8)],
                     reads=['wo'] + ['mix%d' % i for i in range(t0 // 128, (t0 + w) // 128)], wres=ry)
            y_evac(c, yb, k, 0, w, py[:, :w], ry)
        sandwich_out(c, cm, yb, xT, 1, t0, w, g, 0, pss['ss'], ss_res='psU2_1')
    ar.reset()
    gT = mixT_t[:].rearrange("p a b -> p (a b)")[:, 0:NFC * tw].rearrange("p (a b) -> p a b", a=NFC) if 8 * NT >= NFC * tw else None
    fb = FFNBufs(c, tw, alloc=ar.alloc, gT=gT)
    tiles = make_tiles(n_lat, n_ctx if ctx_out else 0, tw)
    ffn(c, cm, fb, xT, 2, tiles, w1_d, w2_d, pss, psnames={'ss': 'psU2_1', 'a': ['psS0_0', 'psS0_1'], 'b': ['psS1_0', 'psS1_1'], 'y': ['psU1_0', 'psU1_1']})
    em.dma('sp', I('dma_start', out=x2T_o, in_=xT[:]), reads=xres(0, NT), writes=['x2T_o'])
    print("part B instructions:", em.ninst)
    return c.done()


def na_bias_tiles(rpb, rows_total, q_row0, key_row0, nj=6):
    kr = np.arange(2)[:, None, None, None]
    kc = np.arange(64)[None, :, None, None]
    qr = np.arange(2)[None, None, :, None]
    qc = np.arange(64)[None, None, None, :]
    q_row = q_row0 + qr
    rs = np.clip(q_row - 4, 0, rows_total - 8)
    cs = np.clip(qc - 8, 0, 64 - 16)
    out = np.full((4, nj, 2, 64, 2, 64), NEG, np.float32)
    for j in range(nj):
        key_row = key_row0 + 2 * j + kr
        valid = (key_row >= rs) & (key_row < rs + 8) & (kc >= cs) & (kc < cs + 16) & (key_row >= 0) & (key_row < rows_total)
        valid = np.broadcast_to(valid, (2, 64, 2, 64))
        dr = np.clip(np.broadcast_to(key_row - q_row + 7, (2, 64, 2, 64)), 0, 14)
        dc = np.clip(np.broadcast_to(kc - qc, (2, 64, 2, 64)), -15, 15) + 15
        for h in range(4):
            out[h, j] = np.where(valid, rpb[h][dr, dc], np.float32(NEG))
    return out.reshape(4, nj, 128, 128)


def pool_rcount(t_global, L):
    n = len(t_global)
    out = np.zeros((128, 2, n), np.float32)
    for gi, wdw in enumerate((2, 4, 8, 16)):
        half = wdw // 2
        lo = np.clip(t_global - half, 0, L)
        hi = np.clip(t_global + half, 0, L)
        rc = (1.0 / (hi - lo).astype(np.float32)).astype(np.float32)
        out[(gi % 2) * 64:(gi % 2) * 64 + 64, gi // 2, :] = rc[None, :]
    return out


def halo_cols(arrT, t0, n, halo, L):
    out = np.zeros(arrT.shape[:-1] + (n + 2 * halo,), arrT.dtype)
    a = max(0, t0 - halo)
    b = min(L, t0 + n + halo)
    out[..., a - (t0 - halo):b - (t0 - halo)] = arrT[..., a:b]
    return out


def pool_blockdiag(pool_w):
    out = np.zeros((128, 2, 128), np.float32)
    for gi in range(4):
        p0 = (gi % 2) * 64
        out[p0:p0 + 64, gi // 2, p0:p0 + 64] = pool_w[gi]
    return out.astype(NPBF)


N_LAT = 2048
NT_FULL = N_LAT + CTX
_PROGS = {}


def _fm(a):
    T, F = a.shape
    return np.ascontiguousarray(a.reshape(T, F // 128, 128).transpose(2, 1, 0))


def _unfm(aT):
    return np.ascontiguousarray(aT.transpose(2, 1, 0)).reshape(aT.shape[2], -1)


def _lay_vec(v):
    return np.ascontiguousarray(v.reshape(-1, 128).T)


def _prog(key, fn):
    if key not in _PROGS:
        _PROGS[key] = fn()
    return _PROGS[key]


def kernel_unfused(x, c, ctx, c_ctx, w_ada, b_ada, norm_g, ffn_w1, ffn_w2, w_in, w_out, na_rpb, pool_w, pool_scale, diff_lambda,
           diff_subln_g):
    f32 = np.float32
    x = np.asarray(x, f32)
    ctx = np.asarray(ctx, f32)
    cvals = np.asarray(c, f32)
    c_ctx = np.asarray(c_ctx, f32)
    ncores = 8
    depth = w_ada.shape[0]
    rows_total = SEQ // GRID_W
    nq = N_LAT // 128
    variants = [1, 2] + [0] * (nq - 4) + [3, 4]
    xT = []
    for i in range(ncores):
        b, j = i // 4, i % 4
        xt = np.concatenate([x[b, j * N_LAT:(j + 1) * N_LAT], ctx[b]], axis=0)
        xT.append(_fm(xt))
    ropes = []
    for i in range(ncores):
        j = i % 4
        cos, sin = rope_tables(np.arange(j * N_LAT, (j + 1) * N_LAT))
        ropes.append(rope_feature_major(cos, sin, CTX))
    pmat = rope_pmat()
    ident = np.eye(128, dtype=f32).astype(NPBF)
    TW = 512
    for l in range(depth):
        last = (l == depth - 1)
        lam_init = 0.8 - 0.6 * math.exp(-0.3 * l)
        nca = _prog(('A',), lambda: build_part_a(N_LAT, CTX, TW))
        normgT = np.ascontiguousarray(np.asarray(norm_g[l], f32).reshape(6, 8, 128).transpose(2, 0, 1))
        badaT = np.ascontiguousarray(np.asarray(b_ada[l], f32).reshape(72, 128).T)
        wada_l = np.ascontiguousarray(np.asarray(w_ada[l], f32))
        w1a = np.ascontiguousarray(np.asarray(ffn_w1[l, 0], f32))
        w2a = np.ascontiguousarray(np.asarray(ffn_w2[l, 0], f32))
        win_l = np.ascontiguousarray(np.asarray(w_in[l], f32))
        in_maps = []
        for i in range(ncores):
            b = i // 4
            cvec = np.ascontiguousarray(np.stack([_lay_vec(cvals[b]), _lay_vec(c_ctx)], axis=-1))
            in_maps.append({"xT": xT[i], "cvec": cvec, "w_ada": wada_l, "badaT": badaT, "normgT": normgT, "w1": w1a, "w2": w2a,
                            "w_in": win_l, "ropeC": ropes[i][0], "ropeS": ropes[i][1], "pmat": pmat})
        ra = run_bass_kernel_spmd(nca, in_maps, core_ids=list(range(ncores))).results
        ncb = _prog(('B', l), lambda: build_part_b(N_LAT, CTX, TW, n_kc_diff=(SEQ + CTX) // 128, na_variants=variants,
                                                  lam_init=lam_init, ctx_out=not last))
        w1b_ = np.ascontiguousarray(np.asarray(ffn_w1[l, 1], f32))
        w2b_ = np.ascontiguousarray(np.asarray(ffn_w2[l, 1], f32))
        wout_l = np.ascontiguousarray(np.asarray(w_out[l], f32))
        pwbd = pool_blockdiag(np.asarray(pool_w[l], f32))
        pscaleT = np.ascontiguousarray(np.asarray(pool_scale[l], f32).reshape(2, 128).T)
        dlam = np.ascontiguousarray(np.broadcast_to(np.asarray(diff_lambda[l], f32).reshape(1, 256), (128, 256)))
        subg = np.ascontiguousarray(np.broadcast_to(np.asarray(diff_subln_g[l], f32)[None, :], (128, 128)))
        rpb = np.asarray(na_rpb[l], f32)
        in_maps = []
        for b in range(2):
            cores = [b * 4 + j for j in range(4)]
            kv_lat = np.concatenate([ra[i]["kvT"][:, :, :N_LAT] for i in cores], axis=2)
            kv_ctx = ra[cores[0]]["kvT"][:, :, N_LAT:]
            v_lat = np.concatenate([ra[i]["vtok"][:N_LAT] for i in cores], axis=0)
            v_ctx = ra[cores[0]]["vtok"][N_LAT:]
            dkT = np.ascontiguousarray(np.concatenate([kv_lat[:, 4:8], kv_ctx[:, 4:8]], axis=2))
            dv = np.ascontiguousarray(np.concatenate([v_lat[:, 256:], v_ctx[:, 256:]], axis=0))
            nakTc = np.ascontiguousarray(kv_ctx[:, 0:2])
            navc = np.ascontiguousarray(v_ctx[:, 0:256])
            pinTc = halo_cols(np.ascontiguousarray(kv_ctx[:, 2:4]), 0, CTX, 8, CTX)
            rcntc = pool_rcount(np.arange(CTX), CTX)
            for j in range(4):
                i = cores[j]
                t_start = j * N_LAT
                r0 = t_start // GRID_W
                hk0 = (r0 - 4) * GRID_W
                nhk = (nq + 4) * 128
                na_kT = np.concatenate([halo_cols(kv_lat[:, 0:2], hk0, nhk, 0, SEQ), nakTc], axis=2)
                na_v = np.concatenate([halo_cols(v_lat[:, 0:256].T, hk0, nhk, 0, SEQ).T, navc], axis=0)
                bias = np.full((5, 4, 6, 128, 128), NEG, f32)
                for vi, ci in ((0, 2), (1, 0), (2, 1), (3, nq - 2), (4, nq - 1)):
                    k0, nl = na_local_chunks(ci, nq)
                    bias[vi] = na_bias_tiles(rpb, rows_total, r0 + 2 * ci, r0 - 4 + 2 * k0)
                in_maps.append({
                    "x1T": ra[i]["x1T"], "modsT": ra[i]["modsT"], "normgT": normgT, "qT": ra[i]["qT"],
                    "na_kT": np.ascontiguousarray(na_kT), "na_v": np.ascontiguousarray(na_v), "na_kTc": nakTc, "na_vc": navc,
                    "na_bias": bias,
                    "pinT": halo_cols(kv_lat[:, 2:4], t_start, N_LAT, 8, SEQ), "rcnt": pool_rcount(np.arange(t_start, t_start + N_LAT), SEQ),
                    "pinTc": pinTc, "rcntc": rcntc, "pwbd": pwbd, "pscaleT": pscaleT,
                    "dkT": dkT, "dv": dv, "dlam": dlam, "subg": subg, "ident": ident,
                    "w_out": wout_l, "w1": w1b_, "w2": w2b_,
                })
        rb = run_bass_kernel_spmd(ncb, in_maps, core_ids=list(range(ncores))).results
        xT = [rb[i]["x2T"] for i in range(ncores)]
    out = np.zeros((2, SEQ, D), f32)
    for i in range(ncores):
        b, j = i // 4, i % 4
        out[b, j * N_LAT:(j + 1) * N_LAT] = _unfm(xT[i][:, :, :N_LAT])
    return out


def build_fused(n_lat=2048, n_ctx=256, tw=512, depth=2, group=4, dbg_ctx_out=False):
    NT = n_lat + n_ctx
    nq = n_lat // 128
    n_halo_kc = nq + 4
    nkc_na = n_halo_kc + n_ctx // 128
    n_kc_diff = (group * n_lat + n_ctx) // 128
    variants = [1, 2] + [0] * (nq - 4) + [3, 4] if nq > 4 else list(range(1, nq + 1))
    nvar = max(variants) + 1
    c = Ctx()
    em = c.em
    nc = c.nc
    xT_d = c.din("xT", [128, 8, NT], F32)
    cvec_d = c.din("cvec", [128, 8, 2], F32)
    ropeC_d = c.din("ropeC", [128, NT], F32)
    ropeS_d = c.din("ropeS", [128, NT], F32)
    pm_d = c.din("pmat", [128, 128], BF16)
    ident_d = c.din("ident", [128, 128], BF16)
    rcnt_d = c.din("rcnt", [128, 2, n_lat], F32)
    rcntc_d = c.din("rcntc", [128, 2, n_ctx], F32)
    L = []
    for l in range(depth):
        L.append(dict(
            wada=c.din("w_ada%d" % l, [D, 9 * D], F32), bada=c.din("badaT%d" % l, [128, 72], F32),
            normg=c.din("normgT%d" % l, [128, 6, 8], F32),
            w1a=c.din("w1a%d" % l, [D, 2 * DFF], F32), w2a=c.din("w2a%d" % l, [DFF, D], F32),
            w1b=c.din("w1b%d" % l, [D, 2 * DFF], F32), w2b=c.din("w2b%d" % l, [DFF, D], F32),
            win=c.din("w_in%d" % l, [D, 2560], F32), wout=c.din("w_out%d" % l, [D, D], F32),
            bias=c.din("na_bias%d" % l, [nvar, 4, 6, 128, 128], F32),
            pwbd=c.din("pwbd%d" % l, [128, 2, 128], BF16), pscale=c.din("pscaleT%d" % l, [128, 2], F32),
            dlam=c.din("dlam%d" % l, [128, 256], F32), subg=c.din("subg%d" % l, [128, 128], F32),
        ))
    outT_o = c.dout("outT", [128, 8, n_lat], F32)

    def dram(name, shape, dt):
        return nc.dram_tensor(name, list(shape), dt, kind="Internal").ap()

    xT = c.sb("xT_sb", [128, 8, NT], F32)
    cm = Common(c)
    ident = c.sb("ident", [128, 128], BF16)
    pwbd = c.sb("pwbd", [128, 2, 128], BF16)
    pscale = c.sb("pscale", [128, 2], F32)
    dlam = c.sb("dlam", [128, 256], F32)
    lamw = c.sb("lamw", [128, 4], F32)
    lamt = c.sb("lamt", [128, 1], F32)
    gsub = c.sb("gsub", [128, 128], F32)
    shiftt = c.sb("shiftt", [128, 1], F32)
    zt = c.sb("zeros", [128, 2048], BF16)
    S0 = c.ps("S0", [128, 2, 512])
    S1 = c.ps("S1", [128, 2, 512])
    U1 = c.ps("U1", [128, 2, 512])
    U2 = c.ps("U2", [128, 2, 512])
    P = {'S': [S0, S1], 'U1': U1, 'U2': U2, 'O': U1, 'T': U2[:, 0, :].bitcast(BF16)}
    pss = {'ss': U2[:, 1, :], 'a': [S0[:, 0, :], S0[:, 1, :]], 'b': [S1[:, 0, :], S1[:, 1, :]], 'y': [U1[:, 0, :], U1[:, 1, :]]}
    psn = {'ss': 'psU2_1', 'a': ['psS0_0', 'psS0_1'], 'b': ['psS1_0', 'psS1_1'], 'y': ['psU1_0', 'psU1_1']}
    ps_mods = U2[:, 0, :].rearrange("p (a b) -> p a b", b=2)
    arena_n = (nc.sbuf_bytes_remaining - 2048) // 2 // 16 * 16
    ar = Arena(c, arena_n)
    print("fused arena elems", arena_n)

    for t in range(0, NT, 512):
        w = min(512, NT - t)
        em.dma('sp', I('dma_start', out=xT[:, :, t:t + w], in_=xT_d[:, :, t:t + w]), writes=xres(t, w))
    em.dma('sp', I('dma_start', out=ident[:], in_=ident_d), writes=['ident'])
    em.op('dve', I('memset', shiftt[:], EXP_SHIFT), writes=['shiftt'])
    em.op('dve', I('memset', zt[:], 0.0), writes=['zeros'])
    tiles_all = make_tiles(n_lat, n_ctx, tw)
    wsel_d = c.din("wsel", [128, 2 * group], F32)
    wsel = c.sb("wsel", [128, 2 * group], F32)
    em.dma('sp', I('dma_start', out=wsel[:], in_=wsel_d), writes=['wsel'])

    for l in range(depth):
        W = L[l]
        last = (l == depth - 1)
        ctx_out = (not last) or dbg_ctx_out
        lam_init = 0.8 - 0.6 * math.exp(-0.3 * l)
        ar.reset(to_zero=True)
        em.dma('sp', I('dma_start', out=cm.normg[:], in_=W['normg']), writes=['normg'])
        fb = FFNBufs(c, tw, alloc=ar.alloc, nw1=3, nw2=2)
        cm.compute_mods(cvec_d, W['wada'], W['bada'], ps_mods, fb, alloc=ar.alloc, psname='psU2_0')
        cm.compute_coefs()
        ffn(c, cm, fb, xT, 0, tiles_all, W['w1a'], W['w2a'], pss, psnames=psn)
        qT_l = dram("qT_l%d" % l, [128, 6, NT], BF16)
        kvT_l = dram("kvT_l%d" % l, [128, 8 * NT], BF16)
        v_l = dram("v_l%d" % l, [NT, 768], BF16)
        kvT_l3 = kvT_l.rearrange("p (c t) -> p c t", c=8)
        proj_phase(c, cm, fb, xT, tiles_all, W['win'], ropeC_d, ropeS_d, pm_d, pss, qT_l, kvT_l3, v_l, alloc=ar.alloc, psnames=psn)
        rg = [[g0 * group + j for j in range(group)] for g0 in range(8 // group)]
        kedge_loc = dram("kedge_loc%d" % l, [128, 4 * 512], BF16)
        kedge_loc3 = kedge_loc.rearrange("p (c t) -> p c t", c=4)
        vedge_loc = dram("vedge_loc%d" % l, [512, 256], BF16)
        em.dma('sp', I('dma_start', out=kedge_loc3[:, :, 0:256], in_=kvT_l3[:, 0:4, 0:256]), reads=['kvT_o'], writes=['kedge_loc'])
        em.dma('sp', I('dma_start', out=kedge_loc3[:, :, 256:512], in_=kvT_l3[:, 0:4, n_lat - 256:n_lat]), reads=['kvT_o'], writes=['kedge_loc'])
        em.dma('sp', I('dma_start', out=vedge_loc[0:256, :], in_=v_l[0:256, 0:256]), reads=['v_o'], writes=['vedge_loc'])
        em.dma('sp', I('dma_start', out=vedge_loc[256:512, :], in_=v_l[n_lat - 256:n_lat, 0:256]), reads=['v_o'], writes=['vedge_loc'])
        kedge_g = dram("kedge_g%d" % l, [group * 128, 4 * 512], BF16)
        vedge_g = dram("vedge_g%d" % l, [group * 512, 256], BF16)
        em.coll(I('collective_compute', "AllGather", ALU.bypass, replica_groups=rg, ins=[kedge_loc.opt()], outs=[kedge_g.opt()]),
                reads=['kedge_loc'], writes=['kedge_g'])
        em.coll(I('collective_compute', "AllGather", ALU.bypass, replica_groups=rg, ins=[vedge_loc.opt()], outs=[vedge_g.opt()]),
                reads=['vedge_loc'], writes=['vedge_g'])
        dk_g, dv_g = [], []
        for h in range(4):
            dk_loc = dram("dk_loc%d_%d" % (l, h), [128, n_lat], BF16)
            dv_loc = dram("dv_loc%d_%d" % (l, h), [n_lat, 128], BF16)
            em.dma('sp', I('dma_start', out=dk_loc, in_=kvT_l3[:, 4 + h, 0:n_lat]), reads=['kvT_o'], writes=['dk_loc%d' % h], bg=True)
            em.dma('sp', I('dma_start', out=dv_loc, in_=v_l[0:n_lat, 256 + h * 128:256 + (h + 1) * 128]), reads=['v_o'], writes=['dv_loc%d' % h], bg=True)
            dkg = dram("dk_g%d_%d" % (l, h), [group * 128, n_lat], BF16)
            dvg = dram("dv_g%d_%d" % (l, h), [group * n_lat, 128], BF16)
            em.coll(I('collective_compute', "AllGather", ALU.bypass, replica_groups=rg, ins=[dk_loc.opt()], outs=[dkg.opt()]),
                    reads=['dk_loc%d' % h], writes=['dk_g%d' % h])
            em.coll(I('collective_compute', "AllGather", ALU.bypass, replica_groups=rg, ins=[dv_loc.opt()], outs=[dvg.opt()]),
                    reads=['dv_loc%d' % h], writes=['dv_g%d' % h])
            dk_g.append(dkg)
            dv_g.append(dvg)
        ar.reset(to_zero=True)
        ke = ar.alloc("ke", [128, group, 2048], BF16)
        ve = ar.alloc("ve", [128, group, 4, 256], BF16)
        kp = ar.alloc("kp", [128, 2048], BF16)
        kn = ar.alloc("kn", [128, 2048], BF16)
        vp = ar.alloc("vp", [128, 4, 256], BF16)
        vn = ar.alloc("vn", [128, 4, 256], BF16)
        em.dma('sp', I('dma_start', out=ke, in_=kedge_g.rearrange("(r p) n -> p r n", p=128)), reads=['kedge_g'], writes=['ke'])
        for r in range(group):
            em.dma('sp', I('dma_start', out=ve[:, r, :, :], in_=vedge_g[r * 512:(r + 1) * 512, :].rearrange("(a p) n -> p a n", p=128)),
                   reads=['vedge_g'], writes=['ve'])
        for (dst, dres, src, sres, w0) in ((kp, 'kp', lambda r: ke[:, r, :], 'ke', 0), (kn, 'kn', lambda r: ke[:, r, :], 'ke', group),
                                           (vp, 'vp', lambda r: ve[:, r, :, :], 've', 0), (vn, 'vn', lambda r: ve[:, r, :, :], 've', group)):
            em.op('dve', I('tensor_scalar', out=dst, in0=src(0), scalar1=wsel[:, w0:w0 + 1], scalar2=None, op0=ALU.mult),
                  reads=[sres, 'wsel'], writes=[dres])
            for r in range(1, group):
                em.op('dve', I('scalar_tensor_tensor', out=dst, in0=src(r), scalar=wsel[:, w0 + r:w0 + r + 1], in1=dst, op0=ALU.mult, op1=ALU.add),
                      reads=[sres, 'wsel', dres], writes=[dres])
        kp3 = kp.rearrange("p (c t) -> p c t", c=4)
        kn3 = kn.rearrange("p (c t) -> p c t", c=4)
        na_kT_asm = dram("na_kT_asm%d" % l, [128, 2, nkc_na * 128], BF16)
        na_v_asm = dram("na_v_asm%d" % l, [nkc_na * 128, 256], BF16)
        pin_asm = dram("pin_asm%d" % l, [128, 2, n_lat + 16], BF16)
        pinc_asm = dram("pinc_asm%d" % l, [128, 2, n_ctx + 16], BF16)
        em.dma('sp', I('dma_start', out=na_kT_asm[:, :, 0:256], in_=kp3[:, 0:2, 256:512]), reads=['kp'], writes=['na_kT_asm'])
        em.dma('sp', I('dma_start', out=na_kT_asm[:, :, 256 + n_lat:512 + n_lat], in_=kn3[:, 0:2, 0:256]), reads=['kn'], writes=['na_kT_asm'])
        em.dma('sp', I('dma_start', out=na_v_asm[0:256, :].rearrange("(a p) n -> p a n", p=128), in_=vp[:, 2:4, :]), reads=['vp'], writes=['na_v_asm'])
        em.dma('sp', I('dma_start', out=na_v_asm[256 + n_lat:512 + n_lat, :].rearrange("(a p) n -> p a n", p=128), in_=vn[:, 0:2, :]), reads=['vn'], writes=['na_v_asm'])
        em.dma('sp', I('dma_start', out=pin_asm[:, :, 0:8], in_=kp3[:, 2:4, 504:512]), reads=['kp'], writes=['pin_asm'])
        em.dma('sp', I('dma_start', out=pin_asm[:, :, 8 + n_lat:16 + n_lat], in_=kn3[:, 2:4, 0:8]), reads=['kn'], writes=['pin_asm'])
        em.dma('sp', I('dma_start', out=na_kT_asm[:, :, 256:256 + n_lat], in_=kvT_l3[:, 0:2, 0:n_lat]), reads=['kvT_o'], writes=['na_kT_asm'])
        em.dma('sp', I('dma_start', out=na_kT_asm[:, :, 512 + n_lat:], in_=kvT_l3[:, 0:2, n_lat:NT]), reads=['kvT_o'], writes=['na_kT_asm'])
        em.dma('sp', I('dma_start', out=na_v_asm[256:256 + n_lat, :], in_=v_l[0:n_lat, 0:256]), reads=['v_o'], writes=['na_v_asm'])
        em.dma('sp', I('dma_start', out=na_v_asm[512 + n_lat:, :], in_=v_l[n_lat:NT, 0:256]), reads=['v_o'], writes=['na_v_asm'])
        em.dma('sp', I('dma_start', out=pin_asm[:, :, 8:8 + n_lat], in_=kvT_l3[:, 2:4, 0:n_lat]), reads=['kvT_o'], writes=['pin_asm'])
        if ctx_out:
            for (a0, a1) in ((0, 8), (8 + n_ctx, 16 + n_ctx)):
                em.dma('sp', I('dma_start', out=pinc_asm[:, :, a0:a1], in_=zt[:, 0:16].rearrange("p (c t) -> p c t", c=2)), reads=['zeros'], writes=['pinc_asm'])
            em.dma('sp', I('dma_start', out=pinc_asm[:, :, 8:8 + n_ctx], in_=kvT_l3[:, 2:4, n_lat:NT]), reads=['kvT_o'], writes=['pinc_asm'])
        ar.reset(to_zero=True)
        mixT = ar.alloc("mixT", [128, 8, NT], BF16)
        ar.set_base()
        em.dma('sp', I('dma_start', out=pwbd[:], in_=W['pwbd']), writes=['pwbd'])
        em.dma('sp', I('dma_start', out=pscale[:], in_=W['pscale']), writes=['pscale'])
        em.dma('sp', I('dma_start', out=dlam[:], in_=W['dlam']), writes=['dlam'])
        em.dma('sp', I('dma_start', out=gsub[:], in_=W['subg']), writes=['gsub'])
        em.op('dve', I('tensor_scalar', out=gsub[:], in0=gsub[:], scalar1=float(1.0 - lam_init), scalar2=None, op0=ALU.mult), reads=['gsub'], writes=['gsub'])
        em.op('dve', I('tensor_tensor', out=dlam[:, 0:64], in0=dlam[:, 0:64], in1=dlam[:, 64:128], op=ALU.mult), reads=['dlam'], writes=['dlam'])
        em.op('dve', I('tensor_tensor', out=dlam[:, 128:192], in0=dlam[:, 128:192], in1=dlam[:, 192:256], op=ALU.mult), reads=['dlam'], writes=['dlam'])
        em.op('dve', I('reduce_sum', out=lamw[:, 0:2], in_=dlam[:].rearrange("p (a b) -> p a b", a=2)[:, :, 0:64], axis=AX.X), reads=['dlam'], writes=['lamw'])
        em.op('act', I('activation', out=lamw[:, 2:4], in_=lamw[:, 0:2], func=AF.Exp), reads=['lamw'], writes=['lamw'])
        em.op('dve', I('tensor_tensor', out=lamt[:], in0=lamw[:, 2:3], in1=lamw[:, 3:4], op=ALU.subtract), reads=['lamw'], writes=['lamt'])
        em.op('dve', I('tensor_scalar', out=lamt[:], in0=lamt[:], scalar1=float(lam_init), scalar2=None, op0=ALU.add), reads=['lamt'], writes=['lamt'])
        em.barrier()
        ctx_kcs = [n_halo_kc + i for i in range(n_ctx // 128)]
        qch = []
        for i in range(nq):
            k0, nl = na_local_chunks(i, nq)
            qch.append((i * 128, [k0 + j for j in range(nl)] + ctx_kcs, variants[i], nl))
        na_attention(c, ar, P, mixT, qT_l, na_kT_asm, na_v_asm, nkc_na, qch, W['bias'], ident, 0, shiftt)
        if ctx_out:
            ar.reset()
            qch = [(n_lat + i * 128, list(range(n_ctx // 128)), None, 0) for i in range(n_ctx // 128)]
            na_attention(c, ar, P, mixT, qT_l, kvT_l3[:, 0:2, n_lat:NT], v_l[n_lat:NT, 0:256], n_ctx // 128, qch, W['bias'], ident, n_lat, shiftt)
        ar.reset()
        pool_mixer(c, ar, P, mixT, pin_asm, rcnt_d, n_lat, pwbd, pscale, 0)
        if ctx_out:
            ar.reset()
            pool_mixer(c, ar, P, mixT, pinc_asm, rcntc_d, n_ctx, pwbd, pscale, n_lat)
        ar.reset()
        lat_kc = n_lat // 128

        def load_all(h, kb, vb, rk, rv):
            for r in range(group):
                em.dma('sp', I('dma_start', out=kb[:, r * n_lat:(r + 1) * n_lat], in_=dk_g[h][r * 128:(r + 1) * 128, :]), reads=['dk_g%d' % h], writes=[rk])
                em.dma('sp', I('dma_start', out=vb[:, r * lat_kc:(r + 1) * lat_kc, 0:128],
                               in_=dv_g[h][r * n_lat:(r + 1) * n_lat, :].rearrange("(c p) e -> p c e", p=128)),
                       reads=['dv_g%d' % h], writes=[rv])
            em.dma('sp', I('dma_start', out=kb[:, group * n_lat:], in_=kvT_l3[:, 4 + h, n_lat:NT]), reads=['kvT_o'], writes=[rk])
            em.dma('sp', I('dma_start', out=vb[:, group * lat_kc:, 0:128],
                           in_=v_l[n_lat:NT, 256 + h * 128:256 + (h + 1) * 128].rearrange("(c p) e -> p c e", p=128)), reads=['v_o'], writes=[rv])

        def load_ctx(h, kb, vb, rk, rv):
            em.dma('sp', I('dma_start', out=kb, in_=kvT_l3[:, 4 + h, n_lat:NT]), reads=['kvT_o'], writes=[rk])
            em.dma('sp', I('dma_start', out=vb[:, :, 0:128],
                           in_=v_l[n_lat:NT, 256 + h * 128:256 + (h + 1) * 128].rearrange("(c p) e -> p c e", p=128)), reads=['v_o'], writes=[rv])
        qtiles = [(t, min(512, n_lat - t)) for t in range(0, n_lat, 512)]
        diff_attention(c, ar, P, mixT, qT_l, qtiles, None, None, n_kc_diff, lamt, gsub, ident, shiftt, cm.epst, loaders=load_all)
        if ctx_out:
            ar.reset()
            diff_attention(c, ar, P, mixT, qT_l, [(n_lat, n_ctx)], None, None, n_ctx // 128, lamt, gsub, ident, shiftt, cm.epst, loaders=load_ctx)
        ar.reset()
        wo = ar.alloc("wo", [128, 8, D], BF16)
        em.dma('pool', I('dma_start', out=wo, in_=W['wout'].rearrange("(k p) n -> p k n", p=128)), writes=['wo'])

        class YB:
            pass
        yb = YB()
        yb.ysb = ar.alloc("ysb", [128, 8, 512], F32)
        yb.sq = ar.alloc("sq", [128, 8, 512], BF16)
        yb.tmp = [ar.alloc("tmpf%d" % i, [128, 512], F32) for i in range(2)]
        yb.rstd = ar.alloc("rstd", [128, 512], F32)
        yb.n_tmp = 0
        ny = 0
        for (t0, w, g) in [s_ for tile in make_tiles(n_lat, n_ctx if ctx_out else 0, 512) for s_ in tile]:
            for k in range(8):
                py = pss['y'][ny % 2]
                ry = psn['y'][ny % 2]
                ny += 1
                mm_group(em, py[:, :w], [(wo[:, kk, k * 128:(k + 1) * 128], mixT[:, kk, t0:t0 + w]) for kk in range(8)],
                         reads=['wo'] + ['mix%d' % i for i in range(t0 // 128, (t0 + w) // 128)], wres=ry)
                y_evac(c, yb, k, 0, w, py[:, :w], ry)
            sandwich_out(c, cm, yb, xT, 1, t0, w, g, 0, pss['ss'], ss_res=psn['ss'])
        ar.reset(to_zero=True)
        fb = FFNBufs(c, tw, alloc=ar.alloc, nw1=3, nw2=3)
        ffn(c, cm, fb, xT, 2, make_tiles(n_lat, n_ctx if ctx_out else 0, tw), W['w1b'], W['w2b'], pss, psnames=psn)
    em.dma('sp', I('dma_start', out=outT_o, in_=xT[:, :, 0:n_lat]), reads=xres(0, n_lat), writes=['outT_o'])
    print("fused instructions:", em.ninst)
    return c.done()


def fused_inputs(x, c, ctx, c_ctx, w_ada, b_ada, norm_g, ffn_w1, ffn_w2, w_in, w_out, na_rpb, pool_w, pool_scale, diff_lambda,
                 diff_subln_g, n_lat):
    f32 = np.float32
    x = np.asarray(x, f32)
    ctx = np.asarray(ctx, f32)
    cvals = np.asarray(c, f32)
    c_ctx = np.asarray(c_ctx, f32)
    seq = x.shape[1]
    n_ctx = ctx.shape[1]
    group = seq // n_lat
    ncores = x.shape[0] * group
    depth = w_ada.shape[0]
    rows_total = seq // GRID_W
    nq = n_lat // 128
    if nq > 4:
        vmap = ((0, 2), (1, 0), (2, 1), (3, nq - 2), (4, nq - 1))
    else:
        vmap = tuple((i + 1, i) for i in range(nq))
    shared = {"pmat": rope_pmat(), "ident": np.eye(128, dtype=f32).astype(NPBF), "rcntc": pool_rcount(np.arange(n_ctx), n_ctx)}
    for l in range(depth):
        shared["w_ada%d" % l] = np.ascontiguousarray(np.asarray(w_ada[l], f32))
        shared["badaT%d" % l] = np.ascontiguousarray(np.asarray(b_ada[l], f32).reshape(72, 128).T)
        shared["normgT%d" % l] = np.ascontiguousarray(np.asarray(norm_g[l], f32).reshape(6, 8, 128).transpose(2, 0, 1))
        shared["w1a%d" % l] = np.ascontiguousarray(np.asarray(ffn_w1[l, 0], f32))
        shared["w2a%d" % l] = np.ascontiguousarray(np.asarray(ffn_w2[l, 0], f32))
        shared["w1b%d" % l] = np.ascontiguousarray(np.asarray(ffn_w1[l, 1], f32))
        shared["w2b%d" % l] = np.ascontiguousarray(np.asarray(ffn_w2[l, 1], f32))
        shared["w_in%d" % l] = np.ascontiguousarray(np.asarray(w_in[l], f32))
        shared["w_out%d" % l] = np.ascontiguousarray(np.asarray(w_out[l], f32))
        shared["pwbd%d" % l] = pool_blockdiag(np.asarray(pool_w[l], f32))
        shared["pscaleT%d" % l] = np.ascontiguousarray(np.asarray(pool_scale[l], f32).reshape(2, 128).T)
        shared["dlam%d" % l] = np.ascontiguousarray(np.broadcast_to(np.asarray(diff_lambda[l], f32).reshape(1, 256), (128, 256)))
        shared["subg%d" % l] = np.ascontiguousarray(np.broadcast_to(np.asarray(diff_subln_g[l], f32)[None, :], (128, 128)))
    in_maps = []
    for i in range(ncores):
        b, j = i // group, i % group
        t_start = j * n_lat
        r0 = t_start // GRID_W
        m = dict(shared)
        m["xT"] = _fm(np.concatenate([x[b, t_start:t_start + n_lat], ctx[b]], axis=0))
        m["cvec"] = np.ascontiguousarray(np.stack([_lay_vec(cvals[b]), _lay_vec(c_ctx)], axis=-1))
        cos, sin = rope_tables(np.arange(t_start, t_start + n_lat))
        m["ropeC"], m["ropeS"] = rope_feature_major(cos, sin, n_ctx)
        m["rcnt"] = pool_rcount(np.arange(t_start, t_start + n_lat), seq)
        ws = np.zeros((128, 2 * group), f32)
        if j > 0:
            ws[:, j - 1] = 1.0
        if j < group - 1:
            ws[:, group + j + 1] = 1.0
        m["wsel"] = ws
        for l in range(depth):
            rpb = np.asarray(na_rpb[l], f32)
            bias = np.full((len(vmap) + (1 if nq <= 4 else 0), 4, 6, 128, 128), NEG, f32)
            for vi, ci in vmap:
                k0, nl = na_local_chunks(ci, nq)
                bias[vi] = na_bias_tiles(rpb, rows_total, r0 + 2 * ci, r0 - 4 + 2 * k0)
            m["na_bias%d" % l] = bias
        in_maps.append(m)
    return in_maps, ncores, group


def kernel_fused(n_lat=N_LAT, **inputs):
    in_maps, ncores, group = fused_inputs(n_lat=n_lat, **inputs)
    n_ctx = inputs["ctx"].shape[1]
    seq = inputs["x"].shape[1]
    tw = 512 if n_lat >= 2048 else 384
    nc = _prog(('F', n_lat, n_ctx), lambda: build_fused(n_lat, n_ctx, tw, depth=inputs["w_ada"].shape[0], group=group))
    res = run_bass_kernel_spmd(nc, in_maps, core_ids=list(range(ncores))).results
    out = np.zeros((inputs["x"].shape[0], seq, D), np.float32)
    for i in range(ncores):
        b, j = i // group, i % group
        out[b, j * n_lat:(j + 1) * n_lat] = _unfm(res[i]["outT"])
    return out


def kernel(x, c, ctx, c_ctx, w_ada, b_ada, norm_g, ffn_w1, ffn_w2, w_in, w_out, na_rpb, pool_w, pool_scale, diff_lambda,
           diff_subln_g):
    return kernel_fused(n_lat=N_LAT, x=x, c=c, ctx=ctx, c_ctx=c_ctx, w_ada=w_ada, b_ada=b_ada, norm_g=norm_g, ffn_w1=ffn_w1,
                        ffn_w2=ffn_w2, w_in=w_in, w_out=w_out, na_rpb=na_rpb, pool_w=pool_w, pool_scale=pool_scale,
                        diff_lambda=diff_lambda, diff_subln_g=diff_subln_g)
```
